# Optimizing a Trainium2 kernel written in Bass

```python
import math
import jax, jax.numpy as jnp
from jax import lax
import numpy as np

D_MODEL = 1024
BATCH = 4
SEQ = 4096
DEPTH = 1
DEC_BATCH = 32
DEC_SEQ = 8
PAST_LEN = 8192
PAGE_SIZE = 128

ATT_HEAD_DIM = 64
N_ATT_HEADS = (D_MODEL // 2) // ATT_HEAD_DIM
ATT_WIDTH = N_ATT_HEADS * ATT_HEAD_DIM
N_ML_HEADS = 4
ML_HEAD_DIM = (D_MODEL // 2) // N_ML_HEADS
ML_WIDTH = N_ML_HEADS * ML_HEAD_DIM
MIX_WIDTH = ATT_WIDTH + ML_WIDTH
PROJ_COLS = 3 * ATT_WIDTH + 4 * ML_WIDTH + 2 * N_ML_HEADS
MOBA_BLOCK = 256
MOBA_TOPK = 3
QUERY_BLOCK = 128
ML_CHUNK = 64
NUM_BUCKETS = 32
MAX_DISTANCE = 128
D_FF = 4 * D_MODEL
RMS_EPS = 1e-6

kernel_name = "hymba_moba_mlstm_decoder_step"


def rmsnorm(x, g):
    xf = x.astype(jnp.float32)
    y = xf * lax.rsqrt(jnp.mean(xf * xf, axis=-1, keepdims=True) + RMS_EPS) * g.astype(jnp.float32)
    return y.astype(x.dtype)


def t5_bucket(dist):
    n = jnp.maximum(dist, 0)
    max_exact = NUM_BUCKETS // 2
    nf = jnp.maximum(n, 1).astype(jnp.float32)
    large = max_exact + (jnp.log(nf / max_exact) / math.log(MAX_DISTANCE / max_exact)
                         * (NUM_BUCKETS - max_exact)).astype(jnp.int32)
    large = jnp.minimum(large, NUM_BUCKETS - 1)
    return jnp.where(n < max_exact, n, large)


def split_projection(xn, w_in, b_ig, b_fg):
    lead = xn.shape[:-1]
    p = xn @ w_in
    sizes = [ATT_WIDTH] * 3 + [ML_WIDTH] * 4 + [N_ML_HEADS] * 2
    cuts = np.cumsum(sizes)[:-1].tolist()
    aq, ak, av, mq, mk, mv, mo, ig, fg = jnp.split(p, cuts, axis=-1)
    att = lambda a: a.reshape(lead + (N_ATT_HEADS, ATT_HEAD_DIM))
    mh = lambda a: a.reshape(lead + (N_ML_HEADS, ML_HEAD_DIM))
    ig = ig.astype(jnp.float32) + b_ig.astype(jnp.float32)
    logf = jax.nn.log_sigmoid(fg.astype(jnp.float32) + b_fg.astype(jnp.float32))
    return (att(aq), att(ak), att(av), mh(mq), mh(mk) * (ML_HEAD_DIM ** -0.5), mh(mv), mo, ig, logf)


def key_blocks(k):
    B, T, H, dh = k.shape
    nb = -(-T // MOBA_BLOCK)
    k = jnp.pad(k, ((0, 0), (0, nb * MOBA_BLOCK - T), (0, 0), (0, 0)))
    return k.reshape(B, nb, MOBA_BLOCK, H, dh).transpose(0, 3, 1, 2, 4)


def moba_attend(q, k_blocks, v_blocks, k_means, q_pos, rel_bias):
    T, H = q.shape[0], q.shape[1]
    NB = k_blocks.shape[1]
    n_sel = min(MOBA_TOPK, NB)
    own = q_pos // MOBA_BLOCK
    scores = jnp.einsum('thd,hnd->thn', q.astype(jnp.float32), k_means)
    fully_past = jnp.arange(NB)[None, None, :] < own[:, None, None]
    scores = jnp.where(fully_past, scores, -jnp.inf)
    _, sel = lax.top_k(scores, n_sel)
    sel_ok = jnp.broadcast_to(jnp.arange(n_sel)[None, None, :] < own[:, None, None], (T, H, n_sel))
    idx = jnp.concatenate([sel, jnp.broadcast_to(own[:, None, None], (T, H, 1))], axis=-1)
    ok = jnp.concatenate([sel_ok, jnp.ones((T, H, 1), dtype=bool)], axis=-1)
    head = jnp.arange(H)[None, :, None]
    kg = k_blocks[head, idx]
    vg = v_blocks[head, idx]
    k_pos = idx[..., None] * MOBA_BLOCK + jnp.arange(MOBA_BLOCK)
    dist = q_pos[:, None, None, None] - k_pos
    bias = rel_bias[t5_bucket(dist), head[..., None]].astype(jnp.float32)
    logits = jnp.einsum('thd,thjpd->thjp', q, kg, preferred_element_type=jnp.float32) * (ATT_HEAD_DIM ** -0.5) + bias
    logits = jnp.where(ok[..., None] & (dist >= 0), logits, -jnp.inf)
    J = logits.shape[2]
    probs = jax.nn.softmax(logits.reshape(T, H, J * MOBA_BLOCK), axis=-1).reshape(T, H, J, MOBA_BLOCK)
    return jnp.einsum('thjp,thjpd->thd', probs.astype(vg.dtype), vg)


def moba_prompt(q, k, v, rel_bias):
    B, S, H, dh = q.shape
    kb, vb = key_blocks(k), key_blocks(v)
    km = jnp.mean(kb.astype(jnp.float32), axis=3)
    nq = S // QUERY_BLOCK
    qc = q.reshape(B * nq, QUERY_BLOCK, H, dh)
    bidx = jnp.repeat(jnp.arange(B), nq)
    p0 = jnp.tile(jnp.arange(nq) * QUERY_BLOCK, B)

    def step(args):
        qq, b, s0 = args
        return moba_attend(qq, kb[b], vb[b], km[b], s0 + jnp.arange(QUERY_BLOCK), rel_bias)

    return lax.map(step, (qc, bidx, p0)).reshape(B, S, H, dh)


def moba_sample(q, k, v, pool_k, pool_v, page_table, rel_bias):
    DB, T = q.shape[0], q.shape[1]
    past_k = pool_k[page_table].reshape(DB, -1, N_ATT_HEADS, ATT_HEAD_DIM)
    past_v = pool_v[page_table].reshape(DB, -1, N_ATT_HEADS, ATT_HEAD_DIM)
    past = past_k.shape[1]
    kb = key_blocks(jnp.concatenate([past_k.astype(k.dtype), k], axis=1))
    vb = key_blocks(jnp.concatenate([past_v.astype(v.dtype), v], axis=1))
    km = jnp.mean(kb.astype(jnp.float32), axis=3)
    q_pos = past + jnp.arange(T)
    return lax.map(lambda a: moba_attend(a[0], a[1], a[2], a[3], q_pos, rel_bias), (q, kb, vb, km))


def mlstm_chunk(carry, inp):
    C, n, m = carry
    q, k, v, ig, lf = (a.astype(jnp.float32) for a in inp)
    L = q.shape[1]
    b = jnp.cumsum(lf, axis=1)
    a = b + m[:, None, :]
    logD = b[:, :, None, :] - b[:, None, :, :] + ig[:, None, :, :]
    causal = jnp.tril(jnp.ones((L, L), dtype=bool))[None, :, :, None]
    logD = jnp.where(causal, logD, -jnp.inf)
    m_t = jnp.maximum(a, jnp.max(logD, axis=2))
    D = jnp.exp(logD - m_t[:, :, None, :])
    w_state = jnp.exp(a - m_t)
    s = jnp.einsum('bthd,bjhd->btjh', q, k) * D
    num = w_state[..., None] * jnp.einsum('bthd,bhde->bthe', q, C) + jnp.einsum('btjh,bjhe->bthe', s, v)
    den = w_state * jnp.einsum('bthd,bhd->bth', q, n) + jnp.sum(s, axis=2)
    h = num / jnp.maximum(jnp.abs(den), jnp.exp(-m_t))[..., None]
    bL = b[:, -1, :]
    m_new = m_t[:, -1, :]
    g_state = jnp.exp(bL + m - m_new)
    g_tok = jnp.exp(bL[:, None, :] - b + ig - m_new[:, None, :])
    C_new = g_state[..., None, None] * C + jnp.einsum('bjh,bjhd,bjhe->bhde', g_tok, k, v)
    n_new = g_state[..., None] * n + jnp.einsum('bjh,bjhd->bhd', g_tok, k)
    return (C_new, n_new, m_new), h


def mlstm_prompt(q, k, v, ig, lf):
    B, S, H, d = q.shape
    L = min(ML_CHUNK, S)
    nc = S // L
    chunks = lambda a: jnp.moveaxis(a.reshape((B, nc, L) + a.shape[2:]), 1, 0)
    init = (jnp.zeros((B, H, d, d), jnp.float32), jnp.zeros((B, H, d), jnp.float32), jnp.zeros((B, H), jnp.float32))
    state, hs = lax.scan(mlstm_chunk, init, (chunks(q), chunks(k), chunks(v), chunks(ig), chunks(lf)))
    return jnp.moveaxis(hs, 0, 1).reshape(B, S, H, d), state


def mlstm_sample(q, k, v, ig, lf, C, n, m):
    state, h = mlstm_chunk((C.astype(jnp.float32), n.astype(jnp.float32), m.astype(jnp.float32)), (q, k, v, ig, lf))
    return h, state


def trunk_layer(x, attend, recur, norm_mix, w_in, b_ig, b_fg, ml_norm, w_out, norm_ffn, w_up, w_down):
    lead = x.shape[:-1]
    xn = rmsnorm(x, norm_mix)
    aq, ak, av, mq, mk, mv, mo, ig, lf = split_projection(xn, w_in, b_ig, b_fg)
    att = attend(aq, ak, av).reshape(lead + (ATT_WIDTH,))
    h, ml_state = recur(mq, mk, mv, ig, lf)
    hn = rmsnorm(h, ml_norm.reshape(N_ML_HEADS, ML_HEAD_DIM)).reshape(lead + (ML_WIDTH,))
    ml = hn * jax.nn.sigmoid(mo.astype(jnp.float32))
    mix = jnp.concatenate([att.astype(x.dtype), ml.astype(x.dtype)], axis=-1)
    x = x + mix @ w_out
    u = jax.nn.relu(rmsnorm(x, norm_ffn) @ w_up)
    x = x + (u * u) @ w_down
    return x, (ak, av) + tuple(ml_state)


def setup_inputs(seed: int = 0) -> dict:
    key = jax.random.key(seed)
    ks = jax.random.split(key, 24)
    n_pages = PAST_LEN // PAGE_SIZE
    n_used = DEC_BATCH * n_pages
    n_phys = n_used + max(1, n_used // 4)
    f32 = jnp.float32
    nrm = lambda k, shape, s: jax.random.normal(k, shape, f32) * s
    page_table = jax.random.permutation(ks[0], n_phys)[:n_used].reshape(DEC_BATCH, n_pages).astype(jnp.int32)
    b_fg = jnp.linspace(3.0, 6.0, N_ML_HEADS, dtype=f32)[None, :] + nrm(ks[13], (DEPTH, N_ML_HEADS), 0.1)
    return {
        "x_prompt": nrm(ks[1], (BATCH, SEQ, D_MODEL), 1.0),
        "x_sample": nrm(ks[2], (DEC_BATCH, DEC_SEQ, D_MODEL), 1.0),
        "cache_k": nrm(ks[3], (DEPTH, n_phys, PAGE_SIZE, N_ATT_HEADS, ATT_HEAD_DIM), 1.0),
        "cache_v": nrm(ks[4], (DEPTH, n_phys, PAGE_SIZE, N_ATT_HEADS, ATT_HEAD_DIM), 1.0),
        "state_C": nrm(ks[5], (DEPTH, DEC_BATCH, N_ML_HEADS, ML_HEAD_DIM, ML_HEAD_DIM), 0.1),
        "state_n": nrm(ks[6], (DEPTH, DEC_BATCH, N_ML_HEADS, ML_HEAD_DIM), 0.1),
        "state_m": nrm(ks[7], (DEPTH, DEC_BATCH, N_ML_HEADS), 1.0),
        "page_table": page_table,
        "rel_bias": nrm(ks[8], (NUM_BUCKETS, N_ATT_HEADS), 0.5),
        "norm_mix": 1.0 + nrm(ks[9], (DEPTH, D_MODEL), 0.1),
        "w_in": nrm(ks[10], (DEPTH, D_MODEL, PROJ_COLS), D_MODEL ** -0.5),
        "b_ig": nrm(ks[11], (DEPTH, N_ML_HEADS), 0.1),
        "b_fg": b_fg,
        "ml_norm": 1.0 + nrm(ks[12], (DEPTH, ML_WIDTH), 0.1),
        "w_out": nrm(ks[14], (DEPTH, MIX_WIDTH, D_MODEL), MIX_WIDTH ** -0.5),
        "norm_ffn": 1.0 + nrm(ks[15], (DEPTH, D_MODEL), 0.1),
        "w_up": nrm(ks[16], (DEPTH, D_MODEL, D_FF), D_MODEL ** -0.5),
        "w_down": nrm(ks[17], (DEPTH, D_FF, D_MODEL), D_FF ** -0.5),
        "norm_final": 1.0 + nrm(ks[18], (D_MODEL,), 0.1),
    }


def reference(x_prompt, x_sample, cache_k, cache_v, state_C, state_n, state_m, page_table, rel_bias,
              norm_mix, w_in, b_ig, b_fg, ml_norm, w_out, norm_ffn, w_up, w_down, norm_final):
    yp, ys = x_prompt, x_sample
    sp, ss = [], []
    for l in range(DEPTH):
        wl = (norm_mix[l], w_in[l], b_ig[l], b_fg[l], ml_norm[l], w_out[l], norm_ffn[l], w_up[l], w_down[l])
        yp, st_p = trunk_layer(
            yp,
            lambda q, k, v: moba_prompt(q, k, v, rel_bias),
            mlstm_prompt,
            *wl)
        ys, st_s = trunk_layer(
            ys,
            lambda q, k, v: moba_sample(q, k, v, cache_k[l], cache_v[l], page_table, rel_bias),
            lambda q, k, v, ig, lf: mlstm_sample(q, k, v, ig, lf, state_C[l], state_n[l], state_m[l]),
            *wl)
        sp.append(st_p)
        ss.append(st_s)
    y_prompt = rmsnorm(yp, norm_final)
    y_sample = rmsnorm(ys, norm_final)
    new_k_prompt = jnp.stack([s[0] for s in sp])
    new_v_prompt = jnp.stack([s[1] for s in sp])
    new_C_prompt = jnp.stack([s[2] for s in sp])
    new_n_prompt = jnp.stack([s[3] for s in sp])
    new_m_prompt = jnp.stack([s[4] for s in sp])
    new_k_sample = jnp.stack([s[0] for s in ss])
    new_v_sample = jnp.stack([s[1] for s in ss])
    new_C_sample = jnp.stack([s[2] for s in ss])
    new_n_sample = jnp.stack([s[3] for s in ss])
    new_m_sample = jnp.stack([s[4] for s in ss])
    return (y_prompt, y_sample, new_k_prompt, new_v_prompt, new_C_prompt, new_n_prompt, new_m_prompt,
            new_k_sample, new_v_sample, new_C_sample, new_n_sample, new_m_sample)
```

```python
import contextlib
import math
import numpy as np
import concourse.bass as bass
import concourse.mybir as mybir
from concourse.alu_op_type import AluOpType as ALU
from concourse.bass_utils import run_bass_kernel_spmd

F32 = mybir.dt.float32
BF16 = mybir.dt.bfloat16
I32 = mybir.dt.int32
AF = mybir.ActivationFunctionType
AX = mybir.AxisListType

ENGS = ("pe", "act", "dve", "pool", "sp")
NEG = -30000.0
D = 1024
NM = 2048
NS = 32
EPS = 1e-6


class Buf:
    __slots__ = ("name", "w", "rs", "dsem", "dcnt")

    def __init__(self, name):
        self.name = name
        self.w = None
        self.rs = {}
        self.dsem = None
        self.dcnt = 0


class KB:
    def __init__(self, nc, stack):
        self.nc = nc
        self.stack = stack
        self.sems = {}
        self.cnt = {e: 0 for e in ENGS}
        self.known = {e: {} for e in ENGS}
        self.prog = {e: [] for e in ENGS}
        for e in ENGS:
            self._newsem("E_" + e)
        self.nd = 0
        self.dbufs = []

    def _newsem(self, key):
        self.sems[key] = self.stack.enter_context(self.nc.semaphore(key))
        return key

    def buf(self, name):
        return Buf(name)

    def _collect(self, e, reads, writes):
        waits = {}
        kn = self.known[e]
        own = "E_" + e

        def need(tok, same_ok):
            if tok is None:
                return
            k, v = tok
            if k == own and same_ok:
                return
            if kn.get(k, 0) >= v:
                return
            if waits.get(k, 0) < v:
                waits[k] = v

        for b in reads:
            need(b.w, False)
        for b in writes:
            need(b.w, True)
            for k, v in b.rs.items():
                need((k, v), True)
        for k, v in waits.items():
            kn[k] = v
        return list(waits.items())

    def op(self, e, fn, reads=(), writes=()):
        waits = self._collect(e, reads, writes)
        self.cnt[e] += 1
        tok = ("E_" + e, self.cnt[e])
        for b in reads:
            if b.rs.get(tok[0], 0) < tok[1]:
                b.rs[tok[0]] = tok[1]
        for b in writes:
            b.w = tok
            b.rs = {}
        self.prog[e].append((waits, fn, (tok[0], 1)))

    def dma(self, q, fn, reads=(), writes=()):
        waits = self._collect(q, reads, writes)
        tgt = writes[0] if writes else reads[0]
        if tgt.dsem is None:
            self.nd += 1
            tgt.dsem = self._newsem(f"D{self.nd}")
            self.dbufs.append(tgt)
        tgt.dcnt += 16
        tok = (tgt.dsem, tgt.dcnt)
        for b in reads:
            if b.rs.get(tok[0], 0) < tok[1]:
                b.rs[tok[0]] = tok[1]
        for b in writes:
            b.w = tok
            b.rs = {}
        self.prog[q].append((waits, fn, (tok[0], 16)))

    def barrier(self):
        for e in ENGS:
            waits = [("E_" + x, self.cnt[x]) for x in ENGS if x != e and self.cnt[x] > 0]
            waits += [(b.dsem, b.dcnt) for b in self.dbufs]
            for k, v in waits:
                self.known[e][k] = max(self.known[e].get(k, 0), v)
            self.prog[e].append((waits, None, None))

    def final_wait(self, e):
        waits = [(b.dsem, b.dcnt) for b in self.dbufs]
        self.prog[e].append((waits, None, None))

    def emit(self):
        nc = self.nc
        sems = self.sems
        prog = self.prog
        with nc.Block() as block:
            def run(name):
                def body(eng):
                    for waits, fn, inc in prog[name]:
                        for k, v in waits:
                            eng.wait_ge(sems[k], v)
                        if fn is not None:
                            fn(eng).then_inc(sems[inc[0]], inc[1])
                return body
            block.tensor(run("pe"))
            block.scalar(run("act"))
            block.vector(run("dve"))
            block.gpsimd(run("pool"))
            block.sync(run("sp"))


def t5_thresholds():
    n = np.arange(0, 600)
    nf = np.maximum(n, 1).astype(np.float32)
    large = 16 + (np.log(nf / np.float32(16)) / np.float32(math.log(128 / 16)) * np.float32(16)).astype(np.int32)
    large = np.minimum(large, 31)
    bucket = np.where(n < 16, n, large)
    return [int(np.argmax(bucket >= b)) for b in range(1, 32)]


def build_program(dbg=None, stop_after=None, cache_rows=2560 * 128):
    nc = bass.Bass("TRN2", target_bir_lowering=False)
    din = lambda name, shape, dt=F32: nc.dram_tensor(name, shape, dt, kind="ExternalInput").ap()
    dout = lambda name, shape, dt=F32: nc.dram_tensor(name, shape, dt, kind="ExternalOutput").ap()
    xm = din("xm", [NM, D]); xc = din("xc", [NM, D]); xs = din("xs", [NS, D])
    w_in = din("w_in", [D, 3592]); w_out = din("w_out", [D, D]); w_up = din("w_up", [D, 4096]); w_down = din("w_down", [4096, D])
    nmix = din("nmix", [1, D]); nffn = din("nffn", [1, D]); nfin = din("nfin", [1, D]); mlg = din("mlg", [1, 512])
    bg = din("bg", [8, 1]); rb = din("rb", [1, 256])
    ck = din("ck", [cache_rows, 512]); cv = din("cv", [cache_rows, 512])
    pt = din("pt", [1, 256], I32)
    sC = din("sC", [16 * 128, 128]); sn = din("sn", [16 * 128, 1]); smi = din("smi", [4, 4])
    cf = din("cf", [1, 4]); cand = din("cand", [1, 512])
    y_m = dout("y_m", [NM, D]); y_s = dout("y_s", [NS, D])
    k_m = dout("k_m", [NM, 512]); v_m = dout("v_m", [NM, 512]); k_s = dout("k_s", [NS, 512]); v_s = dout("v_s", [NS, 512])
    C_p = dout("C_p", [512, 128]); n_p = dout("n_p", [512, 1]); m_p = dout("m_p", [4, 1])
    C_s = dout("C_s", [2048, 128]); n_s = dout("n_s", [2048, 1]); m_s = dout("m_s", [16, 1])
    mix = nc.dram_tensor("mix", [NM + NS, D], BF16, kind="ExternalOutput").ap()
    dbg_outs = {}

    with contextlib.ExitStack() as st:
        kb = KB(nc, st)
        B = kb.buf
        out_bufs = []

        def sbt(name, shape, dt, stack=st):
            return stack.enter_context(nc.sbuf_tensor(name, shape, dt))

        PT = [st.enter_context(nc.psum_tensor(f"PT{i}", [128, 8, 128], BF16)) for i in range(2)]
        bPT = [B(f"PT{i}") for i in range(2)]
        PF = [st.enter_context(nc.psum_tensor(f"PF{i}", [128, 512], F32)) for i in range(6)]
        bPF = [B(f"PF{i}") for i in range(6)]

        def OP(e, fn, r=(), w=()):
            kb.op(e, fn, r, w)

        def store(dst_ap, src_ap, src_buf, q="sp"):
            import os
            if os.environ.get("NO_ST") is not None:
                return
            kb.dma(q, lambda e: e.dma_start(out=dst_ap, in_=src_ap), reads=[src_buf], writes=[])

        identf = sbt("identf", [128, 128], F32); b_idf = B("idf")
        ident = sbt("ident", [128, 128], BF16); b_id = B("id")
        OP("pool", lambda e: e.memset(identf[:], 1.0), w=[b_idf])
        OP("pool", lambda e: e.affine_select(out=identf[:], in_=identf[:], pattern=[[-1, 128]], compare_op=ALU.is_equal,
                                             fill=0.0, base=0, channel_multiplier=1), r=[b_idf], w=[b_idf])
        OP("dve", lambda e: e.tensor_copy(out=ident[:], in_=identf[:]), r=[b_idf], w=[b_id])
        g1 = sbt("g1", [128, D], F32); b_g1 = B("g1")
        kb.dma("sp", lambda e: e.dma_start(out=g1[:], in_=nmix[0:1, :].partition_broadcast(128)), writes=[b_g1])
        cfb = sbt("cfb", [128, 4], F32); b_cfb = B("cfb")
        kb.dma("sp", lambda e: e.dma_start(out=cfb[:], in_=cf[0:1, :].partition_broadcast(128)), writes=[b_cfb])
        rbb = sbt("rbb", [128, 32, 8], F32); b_rbb = B("rbb")
        kb.dma("sp", lambda e: e.dma_start(out=rbb[:].rearrange("p b h -> p (b h)"), in_=rb[0:1, :].partition_broadcast(128)), writes=[b_rbb])
        candb = sbt("candb", [128, 16, 2, 16], F32); b_cand = B("cand")
        kb.dma("sp", lambda e: e.dma_start(out=candb[:].rearrange("p a b c -> p (a b c)"), in_=cand[0:1, :].partition_broadcast(128)), writes=[b_cand])

        if stop_after == "c0":
            kb.final_wait("sp")
            kb.emit()
            return nc, dbg_outs
        xt = [sbt(f"xt{i}", [128, D], F32) for i in range(2)]; b_xt = [B(f"xt{i}") for i in range(2)]
        ssq = [sbt(f"ssq{i}", [128, 1], F32) for i in range(2)]; b_ssq = [B(f"ssq{i}") for i in range(2)]
        rstd = [sbt(f"rstd{i}", [128, 1], F32) for i in range(2)]; b_rstd = [B(f"rstd{i}") for i in range(2)]
        xnb = [sbt(f"xnb{i}", [128, D], BF16) for i in range(2)]; b_xnb = [B(f"xnb{i}") for i in range(2)]
        cnt = {"x": 0}

        def rms_scale(src, b_src, n, i, junk, b_junk, dim=D):
            OP("act", lambda e: e.activation(out=junk, in_=src, func=AF.Square, accum_out=ssq[i][0:n, :]),
               r=[b_src], w=[b_junk, b_ssq[i]])
            OP("act", lambda e: e.activation(out=rstd[i][0:n, :], in_=ssq[i][0:n, :], func=AF.Sqrt, scale=1.0 / dim, bias=EPS),
               r=[b_ssq[i]], w=[b_rstd[i]])
            OP("dve", lambda e: e.reciprocal(out=rstd[i][0:n, :], in_=rstd[i][0:n, :]), r=[b_rstd[i]], w=[b_rstd[i]])

        def to_featmajor(src_bf, b_src, n, dst, b_dst, ncol=8):
            i = cnt["x"] % 2
            cnt["x"] += 1
            for c in range(ncol):
                OP("pe", lambda e, c=c: e.transpose(out=PT[i][:, c, 0:n], in_=src_bf[0:n, c * 128:(c + 1) * 128], identity=ident[0:n, 0:n]),
                   r=[b_src, b_id], w=[bPT[i]])
            OP("act", lambda e: e.copy(out=dst, in_=PT[i][:, 0:ncol, 0:n]), r=[bPT[i]], w=[b_dst])

        def norm_tile(src_ap, n, gt, b_gt, dst, b_dst, q="sp"):
            i = cnt["x"] % 2
            kb.dma(q, lambda e: e.dma_start(out=xt[i][0:n, :], in_=src_ap), writes=[b_xt[i]])
            rms_scale(xt[i][0:n, :], b_xt[i], n, i, xnb[i][0:n, :], b_xnb[i])
            OP("dve", lambda e: e.scalar_tensor_tensor(out=xnb[i][0:n, :], in0=xt[i][0:n, :], scalar=rstd[i][0:n, 0:1], in1=gt[0:n, :],
                                                       op0=ALU.mult, op1=ALU.mult), r=[b_xt[i], b_rstd[i], b_gt], w=[b_xnb[i]])
            to_featmajor(xnb[i], b_xnb[i], n, dst, b_dst)

        def load_w(dst, src, b_dst, ncols):
            for c0 in range(0, ncols, 2048):
                c1 = min(ncols, c0 + 2048)
                kb.dma("pool", lambda e, c0=c0, c1=c1: e.dma_start(out=dst[:, c0:c1], in_=src[:, c0:c1]), writes=[b_dst])

        evq = {"i": 0}

        def evac(out, in_, r, w, scale=None):
            evq["i"] += 1
            if evq["i"] % 2:
                if scale is None:
                    OP("act", lambda e: e.copy(out=out, in_=in_), r=r, w=w)
                else:
                    OP("dve", lambda e: e.tensor_scalar(out=out, in0=in_, scalar1=scale, scalar2=None, op0=ALU.mult), r=r, w=w)
            else:
                if scale is None:
                    OP("dve", lambda e: e.tensor_copy(out=out, in_=in_), r=r, w=w)
                else:
                    OP("dve", lambda e: e.tensor_scalar(out=out, in0=in_, scalar1=scale, scalar2=None, op0=ALU.mult), r=r, w=w)

        qTs = sbt("qTs", [128, 4, NS], BF16); b_qTs = B("qTs")
        kTs_own = sbt("kTs_own", [128, 4, NS], BF16); b_kTs_own = B("kTs_own")
        Vs_own = sbt("Vs_own", [8, 4, 512], BF16); b_Vs_own = B("Vs_own")
        TS128 = sbt("TS128", [128, 8, 8], F32); b_TS128 = B("TS128")
        TS0 = sbt("TS0", [8, 8, 8], F32); b_TS0 = B("TS0")
        p1 = contextlib.ExitStack()
        qTa = [sbt(f"qTa{h}", [81, NM], BF16, p1) for h in range(8)]; b_qTa = [[B(f"qTa{h}_{g}") for g in range(4)] for h in range(8)]
        b_qTaP = [[B(f"qTaP{h}_{g}") for g in range(4)] for h in range(8)]
        kTa = [sbt(f"kTa{h}", [81, 4096], BF16, p1) for h in range(8)]; b_kTa = [[B(f"kTa{h}_{g}") for g in range(8)] for h in range(8)]
        b_kTaI = [B(f"kTaI{h}") for h in range(8)]
        Va = sbt("Va", [128, 32, 8, 65], BF16, p1); b_Va = [B(f"Va{t}") for t in range(32)]; b_Va1 = B("Va1")
        OP("pool", lambda e: e.memset(Va[:, :, :, 64:65], 1.0), w=[b_Va1])
        for h in range(8):
            OP("pool", lambda e, h=h: e.memset(kTa[h][64:81, :], 1.0), w=[b_kTaI[h]])
            OP("pool", lambda e, h=h: e.affine_select(out=kTa[h][64:80, :].rearrange("p (b k) -> p b k", k=256),
                                                      in_=kTa[h][64:80, :].rearrange("p (b k) -> p b k", k=256),
                                                      pattern=[[1, 16], [0, 256]], compare_op=ALU.is_equal, fill=0.0, base=0,
                                                      channel_multiplier=-1), r=[b_kTaI[h]], w=[b_kTaI[h]])

        if stop_after == "c1":
            kb.final_wait("sp")
            kb.emit()
            p1.close()
            return nc, dbg_outs
        thr = t5_thresholds()
        Tt = {0: sbt("T0", [128, 8, 128], F32, p1), 128: sbt("T128", [128, 8, 128], F32, p1)}
        b_Tt = {0: [B(f"T0_{h}") for h in range(8)], 128: [B(f"T128_{h}") for h in range(8)]}
        drb = sbt("drb", [128, 32, 8], F32, p1); b_drb = B("drb")
        OP("dve", lambda e: e.tensor_tensor(out=drb[:, 1:32, :], in0=rbb[:, 1:32, :], in1=rbb[:, 0:31, :], op=ALU.subtract), r=[b_rbb], w=[b_drb])
        OP("dve", lambda e: e.tensor_tensor(out=drb[:, 0:1, :], in0=rbb[:, 0:1, :], in1=rbb[:, 31:32, :], op=ALU.subtract), r=[b_rbb], w=[b_drb])
        rb31x8 = sbt("rb31x8", [128, 8], F32, p1); b_rb31 = B("rb31")
        OP("dve", lambda e: e.tensor_scalar(out=rb31x8[:], in0=rbb[:, 31, :], scalar1=8.0, scalar2=None, op0=ALU.mult), r=[b_rbb], w=[b_rb31])
        disti = sbt("disti", [128, 128], I32, p1); b_disti = B("disti")
        distf = sbt("distf", [128, 128], F32, p1); b_distf = B("distf")
        gef = sbt("gef", [128, 128], F32, p1); b_gef = B("gef")
        for delta in (0, 128):
            OP("pool", lambda e, delta=delta: e.iota(out=disti[:], pattern=[[1, 128]], base=delta, channel_multiplier=-1), w=[b_disti])
            OP("dve", lambda e: e.tensor_copy(out=distf[:], in_=disti[:]), r=[b_disti], w=[b_distf])
            for h in range(8):
                OP("dve", lambda e, h=h, delta=delta: e.tensor_scalar(out=Tt[delta][:, h, :], in0=distf[:], scalar1=0.0, scalar2=drb[:, 0, h:h + 1],
                                                                      op0=ALU.mult, op1=ALU.add), r=[b_distf, b_drb], w=[b_Tt[delta][h]])
            steps = [(float(thr[b - 1]), b) for b in range(1, 32)]
            for tv, b in steps:
                OP("dve", lambda e, tv=tv: e.tensor_scalar(out=gef[:], in0=distf[:], scalar1=tv, scalar2=None, op0=ALU.is_ge), r=[b_distf], w=[b_gef])
                for h in range(8):
                    OP("dve", lambda e, h=h, b=b, delta=delta: e.scalar_tensor_tensor(out=Tt[delta][:, h, :], in0=gef[:], scalar=drb[:, b, h:h + 1], in1=Tt[delta][:, h, :],
                                                                                     op0=ALU.mult, op1=ALU.add), r=[b_gef, b_drb, b_Tt[delta][h]], w=[b_Tt[delta][h]])
            if delta == 0:
                OP("dve", lambda e: e.tensor_scalar(out=gef[:], in0=distf[:], scalar1=0.0, scalar2=None, op0=ALU.is_lt), r=[b_distf], w=[b_gef])
                for h in range(8):
                    OP("dve", lambda e, h=h: e.scalar_tensor_tensor(out=Tt[0][:, h, :], in0=gef[:], scalar=NEG, in1=Tt[0][:, h, :],
                                                                    op0=ALU.mult, op1=ALU.add), r=[b_gef, b_Tt[0][h]], w=[b_Tt[0][h]])
        OP("dve", lambda e: e.tensor_copy(out=TS128[:], in_=Tt[128][:, :, 0:8]), r=b_Tt[128], w=[b_TS128])
        OP("dve", lambda e: e.tensor_copy(out=TS0[:], in_=Tt[0][0:8, :, 0:8]), r=b_Tt[0], w=[b_TS0])
        negT = sbt("negT", [128, 128], F32, p1); b_negT = B("negT")
        OP("pool", lambda e: e.memset(negT[:], NEG), w=[b_negT])

        if stop_after == "c2":
            kb.final_wait("sp")
            kb.emit()
            p1.close()
            return nc, dbg_outs
        p1a = contextlib.ExitStack()
        wA = sbt("wA", [128, 8, 1536], BF16, p1a); b_wA = B("wA")
        for c in range(8):
            load_w(wA[:, c, :], w_in[c * 128:(c + 1) * 128, 0:1536], b_wA, 1536)
        xnT = [sbt(f"xnT{i}", [128, 8, 512], BF16, p1a) for i in range(1)]; b_xnT = [B(f"xnT{i}") for i in range(1)]
        kvst = [sbt(f"kvst{i}", [128, 1024], F32, p1a) for i in range(2)]; b_kvst = [B(f"kvst{i}") for i in range(2)]

        if stop_after == "s0":
            kb.final_wait("sp"); kb.emit(); p1a.close(); p1.close(); return nc, dbg_outs
        for kind in ("ctx", "main"):
            src = xc if kind == "ctx" else xm
            for g in range(4):
                xi = 0
                for t in range(4):
                    r0 = g * 512 + t * 128
                    norm_tile(src[r0:r0 + 128, :], 128, g1, b_g1, xnT[xi][:, :, t * 128:(t + 1) * 128], b_xnT[xi])
                if stop_after == "s1":
                    kb.final_wait("sp"); kb.emit(); p1a.close(); p1.close(); return nc, dbg_outs
                kg = g if kind == "ctx" else 4 + g
                import os
                for h in (range(8) if os.environ.get("SKIP_FM") is None else ()):
                    for which in (("k",) if kind == "ctx" else ("q", "k")):
                        col0 = (0 if which == "q" else 512) + h * 64
                        pf = (2 * h + (which == "k")) % 2
                        for c in range(8):
                            OP("pe", lambda e, c=c, pf=pf, col0=col0, xi=xi: e.matmul(PF[pf][0:64, :], lhsT=wA[:, c, col0:col0 + 64], rhs=xnT[xi][:, c, :],
                                                                                       start=(c == 0), stop=(c == 7)),
                               r=[b_wA, b_xnT[xi]], w=[bPF[pf]])
                        if os.environ.get("NO_EV") is not None:
                            pass
                        elif which == "q":
                            evac(qTa[h][0:64, g * 512:(g + 1) * 512], PF[pf][0:64, :], [bPF[pf]], [b_qTa[h][g]])
                        else:
                            evac(kTa[h][0:64, kg * 512:(kg + 1) * 512], PF[pf][0:64, :], [bPF[pf]], [b_kTa[h][kg]])
                if stop_after == "s2":
                    kb.final_wait("sp"); kb.emit(); p1a.close(); p1.close(); return nc, dbg_outs
                for t in (range(4) if os.environ.get("SKIP_TM") is None else ()):
                    ta = kg * 4 + t
                    si = ta % 2
                    for which in (("v",) if kind == "ctx" else ("k", "v")):
                        col0 = 512 if which == "k" else 1024
                        pf = 3 if which == "v" else 4
                        for c in range(8):
                            OP("pe", lambda e, c=c, pf=pf, col0=col0, xi=xi, t=t: e.matmul(PF[pf][:, :], lhsT=xnT[xi][:, c, t * 128:(t + 1) * 128], rhs=wA[:, c, col0:col0 + 512],
                                                                                          start=(c == 0), stop=(c == 7)),
                               r=[b_wA, b_xnT[xi]], w=[bPF[pf]])
                        if which == "v" and os.environ.get("NO_VA") is None:
                            OP("dve", lambda e, ta=ta, pf=pf: e.tensor_copy(out=Va[:, ta, :, 0:64], in_=PF[pf][:, :].rearrange("p (h d) -> p h d", d=64)),
                               r=[bPF[pf]], w=[b_Va[ta]])
                        if kind == "main" and os.environ.get("NO_KV") is None:
                            o0 = 0 if which == "k" else 512
                            OP("dve", lambda e, si=si, pf=pf, o0=o0: e.tensor_copy(out=kvst[si][:, o0:o0 + 512], in_=PF[pf][:, :]), r=[bPF[pf]], w=[b_kvst[si]])
                    if kind == "main":
                        r0 = g * 512 + t * 128
                        store(k_m[r0:r0 + 128, :], kvst[si][:, 0:512], b_kvst[si])
                        store(v_m[r0:r0 + 128, :], kvst[si][:, 512:1024], b_kvst[si])
                if stop_after == "s3" or (stop_after == "s4" and kind == "main") or (stop_after == "s5" and kind == "ctx" and g == 3) or (stop_after == "s6" and kind == "ctx" and g == 1):
                    kb.final_wait("sp"); kb.emit(); p1a.close(); p1.close(); return nc, dbg_outs
        if stop_after == "c3":
            kb.final_wait("sp")
            kb.emit()
            p1a.close()
            p1.close()
            return nc, dbg_outs
        xi = 0
        norm_tile(xs[0:NS, :], NS, g1, b_g1, xnT[xi][:, :, 0:NS], b_xnT[xi])
        for which in ("q", "k"):
            for ch in range(4):
                col0 = (0 if which == "q" else 512) + ch * 128
                pf = ch % 2
                for c in range(8):
                    OP("pe", lambda e, c=c, pf=pf, col0=col0, xi=xi: e.matmul(PF[pf][:, 0:NS], lhsT=wA[:, c, col0:col0 + 128], rhs=xnT[xi][:, c, 0:NS],
                                                                               start=(c == 0), stop=(c == 7)), r=[b_wA, b_xnT[xi]], w=[bPF[pf]])
                dst = qTs if which == "q" else kTs_own
                bd = b_qTs if which == "q" else b_kTs_own
                evac(dst[:, ch, :], PF[pf][:, 0:NS], [bPF[pf]], [bd])
        for sbi in range(4):
            si = sbi % 2
            for which in ("k", "v"):
                col0 = 512 if which == "k" else 1024
                pf = 2 + (which == "v")
                for c in range(8):
                    OP("pe", lambda e, c=c, pf=pf, col0=col0, xi=xi, sbi=sbi: e.matmul(PF[pf][0:8, :], lhsT=xnT[xi][:, c, sbi * 8:(sbi + 1) * 8], rhs=wA[:, c, col0:col0 + 512],
                                                                                      start=(c == 0), stop=(c == 7)), r=[b_wA, b_xnT[xi]], w=[bPF[pf]])
                o0 = 0 if which == "k" else 512
                if which == "v":
                    OP("dve", lambda e, sbi=sbi, pf=pf: e.tensor_copy(out=Vs_own[:, sbi, :], in_=PF[pf][0:8, :]), r=[bPF[pf]], w=[b_Vs_own])
                OP("dve", lambda e, si=si, pf=pf, o0=o0: e.tensor_copy(out=kvst[si][0:8, o0:o0 + 512], in_=PF[pf][0:8, :]), r=[bPF[pf]], w=[b_kvst[si]])
            store(k_s[sbi * 8:(sbi + 1) * 8, :], kvst[si][0:8, 0:512], b_kvst[si])
            store(v_s[sbi * 8:(sbi + 1) * 8, :], kvst[si][0:8, 512:1024], b_kvst[si])
        kb.barrier()
        p1a.close()

        if stop_after == "p1":
            kb.final_wait("sp")
            kb.emit()
            p1.close()
            return nc, dbg_outs

        p1b = contextlib.ExitStack()
        kmf = sbt("kmf", [64, 8, 16], F32, p1b); b_kmf = B("kmf")
        kmT = sbt("kmT", [64, 8, 16], BF16, p1b); b_kmT = B("kmT")
        for h in range(8):
            OP("dve", lambda e, h=h: e.tensor_reduce(out=kmf[:, h, :], in_=kTa[h][0:64, :].rearrange("p (b k) -> p b k", k=256), axis=AX.X, op=ALU.add),
               r=b_kTa[h], w=[b_kmf])
        OP("dve", lambda e: e.tensor_copy(out=kmT[:], in_=kmf[:]), r=[b_kmf], w=[b_kmT])
        selm = sbt("selm", [128, 16, 16], F32, p1b); b_selm = B("selm")
        OP("dve", lambda e: e.tensor_scalar(out=selm[:], in0=candb[:, :, 0, :], scalar1=-1.0, scalar2=NEG, op0=ALU.mult, op1=ALU.add), r=[b_cand], w=[b_selm])
        selW = sbt("selW", [128, 4, 8, 81], F32, p1b); b_selW = [B(f"selW{j}") for j in range(4)]
        OP("pool", lambda e: e.memset(selW[:], 0.0), w=b_selW)
        for j in range(4):
            OP("dve", lambda e, j=j: e.tensor_copy(out=selW[:, j, :, 80:81], in_=rb31x8[:].rearrange("p (h o) -> p h o", o=1)), r=[b_rb31], w=[b_selW[j]])
        smk = sbt("smk", [128, 8, 16], F32, p1b); b_smk = B("smk")
        top8 = sbt("top8", [128, 8, 8], F32, p1b); b_top8 = B("top8")
        PTt = [sbt(f"PTt{i}", [128, 512], BF16, p1b) for i in range(2)]; b_PTt = [B(f"PTt{i}") for i in range(2)]
        tmpS = [sbt(f"tmpS{i}", [128, 512], F32, p1b) for i in range(2)]; b_tmpS = [B(f"tmpS{i}") for i in range(2)]
        ot = [sbt(f"ot{i}", [65, 512], F32, p1b) for i in range(2)]; b_ot = [B(f"ot{i}") for i in range(2)]
        rc4 = sbt("rc4", [128, 4, 1], F32, p1b); b_rc4 = B("rc4")
        attb = [sbt(f"attb{i}", [128, 4, 512], BF16, p1b) for i in range(2)]; b_attb = [B(f"attb{i}") for i in range(2)]
        b_mix = B("mix")
        sidx = 0
        for g in range(4):
            for j in range(4):
                qt = 4 * g + j
                for h in range(8):
                    OP("pe", lambda e, h=h, qt=qt: e.matmul(PF[4][:, h * 16:(h + 1) * 16], lhsT=qTa[h][0:64, qt * 128:(qt + 1) * 128], rhs=kmT[:, h, :],
                                                            start=True, stop=True), r=[b_qTa[h][g], b_kmT], w=[bPF[4]])
                OP("dve", lambda e, qt=qt: e.tensor_tensor(out=smk[:], in0=PF[4][:, 0:128].rearrange("p (h n) -> p h n", n=16),
                                                           in1=selm[:, qt:qt + 1, :].to_broadcast([128, 8, 16]), op=ALU.add), r=[bPF[4], b_selm], w=[b_smk])
                for h in range(8):
                    OP("dve", lambda e, h=h: e.max(out=top8[:, h, :], in_=smk[:, h, :]), r=[b_smk], w=[b_top8])
                for h in range(8):
                    OP("dve", lambda e, h=h, j=j, qt=qt: e.scalar_tensor_tensor(out=selW[:, j, h, 64:80], in0=smk[:, h, :], scalar=top8[:, h, 2:3], in1=candb[:, qt, 0, :],
                                                                               op0=ALU.is_lt, op1=ALU.mult), r=[b_smk, b_top8, b_cand], w=[b_selW[j]])
                OP("dve", lambda e, j=j, qt=qt: e.tensor_tensor(out=selW[:, j, :, 64:80], in0=selW[:, j, :, 64:80],
                                                                in1=candb[:, qt:qt + 1, 1, :].to_broadcast([128, 8, 16]), op=ALU.add), r=[b_cand, b_selW[j]], w=[b_selW[j]])
            for h in range(8):
                for j in range(4):
                    OP("pe", lambda e, h=h, j=j: e.transpose(out=PF[5][0:81, j * 128:(j + 1) * 128], in_=selW[:, j, h, :], identity=identf[:]),
                       r=[b_selW[j], b_idf], w=[bPF[5]])
                OP("act", lambda e, h=h, g=g: e.copy(out=qTa[h][64:81, g * 512:(g + 1) * 512], in_=PF[5][64:81, :]), r=[bPF[5]], w=[b_qTaP[h][g]])
            ab = g % 2
            for h in range(8):
                po = 2 + h % 2
                nk = 16 + 4 * g + 4
                for kt in range(nk):
                    si = sidx % 2
                    sidx += 1
                    OP("pe", lambda e, h=h, kt=kt, g=g, si=si: e.matmul(PF[si][:, :], lhsT=kTa[h][0:81, kt * 128:(kt + 1) * 128], rhs=qTa[h][0:81, g * 512:(g + 1) * 512],
                                                                         start=True, stop=True),
                       r=[b_kTa[h][kt // 4], b_kTaI[h], b_qTa[h][g], b_qTaP[h][g]], w=[bPF[si]])
                    rel = kt - (16 + 4 * g)
                    if rel < -1:
                        OP("act", lambda e, si=si: e.activation(out=PTt[si][:], in_=PF[si][:, :], func=AF.Exp, scale=0.125), r=[bPF[si]], w=[b_PTt[si]])
                    else:
                        for j in range(4):
                            d = j - rel
                            cs = slice(j * 128, (j + 1) * 128)
                            if d == 0:
                                Tm, bTm = Tt[0][:, h, :], b_Tt[0][h]
                            elif d == 1:
                                Tm, bTm = Tt[128][:, h, :], b_Tt[128][h]
                            elif d == -1 and j % 2 == 0:
                                Tm, bTm = negT[:], b_negT
                            else:
                                Tm = None
                            if Tm is not None:
                                OP("dve", lambda e, si=si, cs=cs, Tm=Tm: e.scalar_tensor_tensor(out=tmpS[si][:, cs], in0=PF[si][:, cs], scalar=0.125, in1=Tm,
                                                                                                 op0=ALU.mult, op1=ALU.add), r=[bPF[si], bTm], w=[b_tmpS[si]])
                            else:
                                OP("dve", lambda e, si=si, cs=cs: e.tensor_scalar(out=tmpS[si][:, cs], in0=PF[si][:, cs], scalar1=0.125, scalar2=None, op0=ALU.mult),
                                   r=[bPF[si]], w=[b_tmpS[si]])
                        OP("act", lambda e, si=si: e.activation(out=PTt[si][:], in_=tmpS[si][:], func=AF.Exp), r=[b_tmpS[si]], w=[b_PTt[si]])
                    OP("pe", lambda e, h=h, kt=kt, si=si, po=po, nk=nk: e.matmul(PF[po][0:65, :], lhsT=Va[:, kt, h, :], rhs=PTt[si][:], start=(kt == 0), stop=(kt == nk - 1)),
                       r=[b_Va[kt], b_Va1, b_PTt[si]], w=[bPF[po]])
                oi = h % 2
                OP("dve", lambda e, oi=oi, po=po: e.tensor_copy(out=ot[oi][:], in_=PF[po][0:65, :]), r=[bPF[po]], w=[b_ot[oi]])
                for j in range(4):
                    OP("pe", lambda e, oi=oi, j=j: e.transpose(out=PF[4][:, j * 128:j * 128 + 65], in_=ot[oi][0:65, j * 128:(j + 1) * 128], identity=identf[0:65, 0:65]),
                       r=[b_ot[oi], b_idf], w=[bPF[4]])
                pv = PF[4][:, :].rearrange("p (j c) -> p j c", c=128)
                OP("dve", lambda e, pv=pv: e.reciprocal(out=rc4[:], in_=pv[:, :, 64:65]), r=[bPF[4]], w=[b_rc4])
                OP("dve", lambda e, pv=pv, h=h, ab=ab: e.tensor_tensor(out=attb[ab][:, :, h * 64:(h + 1) * 64], in0=pv[:, :, 0:64], in1=rc4[:].to_broadcast([128, 4, 64]), op=ALU.mult),
                   r=[bPF[4], b_rc4], w=[b_attb[ab]])
            for j in range(4):
                r0 = g * 512 + j * 128
                kb.dma("sp", lambda e, ab=ab, j=j, r0=r0: e.dma_start(out=mix[r0:r0 + 128, 0:512], in_=attb[ab][:, j, :]), reads=[b_attb[ab]], writes=[b_mix])
        if dbg == "att":
            def dump(name, shape, dt, src, bufs):
                o = dout("d_" + name, shape, dt)
                bb = B("dmp_" + name)
                kb.dma("sp", lambda e: e.dma_start(out=o, in_=src), reads=bufs, writes=[bb])
            dump("qTa0", [81, NM], BF16, qTa[0][:, :], b_qTa[0] + b_qTaP[0])
            dump("kTa0", [81, 4096], BF16, kTa[0][:, :], b_kTa[0] + [b_kTaI[0]])
            dump("T0", [128, 128], F32, Tt[0][:, 0, :], [b_Tt[0][0]])
            dump("T128", [128, 128], F32, Tt[128][:, 0, :], [b_Tt[128][0]])
            dump("Va16", [128, 8 * 65], BF16, Va[:, 16, :, :].rearrange("p h d -> p (h d)"), [b_Va[16], b_Va1])
            dump("kmf", [64, 128], F32, kmf[:].rearrange("p h n -> p (h n)"), [b_kmf])
            dump("selW", [128, 4 * 8 * 81], F32, selW[:].rearrange("p a b c -> p (a b c)"), b_selW)
        kb.barrier()
        p1b.close()
        p1.close()
        if stop_after == "att":
            kb.final_wait("sp")
            kb.emit()
            return nc, dbg_outs

        ps_ = contextlib.ExitStack()
        ptb = sbt("ptb", [128, 256], I32, ps_); b_ptb = B("ptb")
        kb.dma("sp", lambda e: e.dma_start(out=ptb[:], in_=pt[0:1, :].partition_broadcast(128)), writes=[b_ptb])
        pio = sbt("pio", [128, 1], I32, ps_); b_pio = B("pio")
        OP("pool", lambda e: e.iota(out=pio[:], pattern=[[0, 1]], base=0, channel_multiplier=1), w=[b_pio])
        piof = sbt("piof", [128, 1], F32, ps_); b_piof = B("piof")
        OP("dve", lambda e: e.tensor_copy(out=piof[:], in_=pio[:]), r=[b_pio], w=[b_piof])
        ptf = sbt("ptf", [128, 256], F32, ps_); b_ptf = B("ptf")
        OP("dve", lambda e: e.tensor_copy(out=ptf[:], in_=ptb[:]), r=[b_ptb], w=[b_ptf])
        ridx = sbt("ridx", [128, 256], I32, ps_); b_ridx = B("ridx")
        OP("dve", lambda e: e.tensor_scalar(out=ridx[:], in0=ptf[:], scalar1=128.0, scalar2=piof[:, 0:1], op0=ALU.mult, op1=ALU.add), r=[b_ptf, b_piof], w=[b_ridx])
        onesb = sbt("onesb", [128, 1], BF16, ps_); b_onesb = B("onesb")
        OP("pool", lambda e: e.memset(onesb[:], 1.0), w=[b_onesb])
        ohS = sbt("ohS", [33, 33, 128], BF16, ps_); b_ohS = B("ohS")
        OP("pool", lambda e: e.memset(ohS[:], 1.0), w=[b_ohS])
        OP("pool", lambda e: e.affine_select(out=ohS[:], in_=ohS[:], pattern=[[1, 33], [0, 128]], compare_op=ALU.is_equal, fill=0.0, base=0, channel_multiplier=-1),
           r=[b_ohS], w=[b_ohS])
        rbcol = sbt("rbcol", [64, 1], F32, ps_); b_rbcol = B("rbcol")
        for h in range(8):
            kb.dma("sp", lambda e, h=h: e.dma_start(out=rbcol[h * 8:(h + 1) * 8, :], in_=rb[0:1, 248 + h:248 + h + 1].partition_broadcast(8)), writes=[b_rbcol])
        OP("dve", lambda e: e.tensor_scalar(out=rbcol[:], in0=rbcol[:], scalar1=8.0, scalar2=None, op0=ALU.mult), r=[b_rbcol], w=[b_rbcol])
        kTsp = sbt("kTsp", [128, 4, 8192], BF16, ps_); b_kTsp = B("kTsp")
        kpf = [sbt(f"kpf{i}", [128, 2, 512], F32, ps_) for i in range(2)]; b_kpf = [B(f"kpf{i}") for i in range(2)]
        kpb = [sbt(f"kpb{i}", [128, 2, 512], BF16, ps_) for i in range(2)]; b_kpb = [B(f"kpb{i}") for i in range(2)]
        kms = sbt("kms", [128, 4, 32], BF16, ps_); b_kms = B("kms")
        Qbd = sbt("Qbd", [128, 4, 64], BF16, ps_); b_Qbd = B("Qbd")
        scs = sbt("scs", [64, 32], F32, ps_); b_scs = B("scs")
        top8s = sbt("top8s", [64, 8], F32, ps_); b_top8s = B("top8s")
        penF = sbt("penF", [64, 33], F32, ps_); b_penF = B("penF")
        penTb = sbt("penTb", [33, 64], BF16, ps_); b_penTb = B("penTb")
        PTs = [sbt(f"PTs{i}", [128, 64], BF16, ps_) for i in range(2)]; b_PTs = [B(f"PTs{i}") for i in range(2)]
        tmS = sbt("tmS", [128, 64], F32, ps_); b_tmS = B("tmS")
        osb = sbt("osb", [64, 512], BF16, ps_); b_osb = B("osb")
        recs = sbt("recs", [64, 1], F32, ps_); b_recs = B("recs")
        b_mix3 = B("mix3")
        xcnt = 0
        for sbi in range(4):
            for pr in range(32):
                bi = pr % 2
                for a in range(2):
                    col = sbi * 64 + pr * 2 + a
                    kb.dma("pool", lambda e, bi=bi, a=a, col=col: e.indirect_dma_start(out=kpf[bi][:, a, :], out_offset=None, in_=ck[:, :],
                                                                                      in_offset=bass.IndirectOffsetOnAxis(ap=ridx[:, col:col + 1], axis=0)),
                           reads=[b_ridx], writes=[b_kpf[bi]])
                OP("dve", lambda e, bi=bi: e.tensor_copy(out=kpb[bi][:], in_=kpf[bi][:]), r=[b_kpf[bi]], w=[b_kpb[bi]])
                for a in range(2):
                    ti_ = xcnt % 2
                    xcnt += 1
                    pg = pr * 2 + a
                    for ch in range(4):
                        OP("pe", lambda e, ti_=ti_, bi=bi, a=a, ch=ch: e.transpose(out=PT[ti_][:, ch, :], in_=kpb[bi][:, a, ch * 128:(ch + 1) * 128], identity=ident[:]),
                           r=[b_kpb[bi], b_id], w=[bPT[ti_]])
                    OP("act", lambda e, ti_=ti_, pg=pg: e.copy(out=kTsp[:, :, pg * 128:(pg + 1) * 128], in_=PT[ti_][:, 0:4, :]), r=[bPT[ti_]], w=[b_kTsp])
                for ch in range(4):
                    for a in range(2):
                        OP("pe", lambda e, bi=bi, a=a, ch=ch, pr=pr: e.matmul(PF[5][:, ch * 32 + pr:ch * 32 + pr + 1], lhsT=kpb[bi][:, a, ch * 128:(ch + 1) * 128], rhs=onesb[:, 0:1],
                                                                             start=(a == 0), stop=(a == 1)), r=[b_kpb[bi], b_onesb], w=[bPF[5]])
            OP("dve", lambda e: e.tensor_copy(out=kms[:], in_=PF[5][:, 0:128].rearrange("p (c n) -> p c n", n=32)), r=[bPF[5]], w=[b_kms])
            OP("pool", lambda e: e.memset(Qbd[:], 0.0), w=[b_Qbd])
            for h in range(8):
                ch, hh = h // 2, h % 2
                OP("dve", lambda e, h=h, ch=ch, hh=hh, sbi=sbi: e.tensor_copy(out=Qbd[hh * 64:(hh + 1) * 64, ch, h * 8:(h + 1) * 8], in_=qTs[hh * 64:(hh + 1) * 64, ch, sbi * 8:(sbi + 1) * 8]),
                   r=[b_qTs], w=[b_Qbd])
            for ch in range(4):
                OP("pe", lambda e, ch=ch: e.matmul(PF[4][0:64, 0:32], lhsT=Qbd[:, ch, :], rhs=kms[:, ch, :], start=(ch == 0), stop=(ch == 3)), r=[b_Qbd, b_kms], w=[bPF[4]])
            OP("dve", lambda e: e.tensor_copy(out=scs[:], in_=PF[4][0:64, 0:32]), r=[bPF[4]], w=[b_scs])
            OP("dve", lambda e: e.max(out=top8s[:], in_=scs[:]), r=[b_scs], w=[b_top8s])
            OP("dve", lambda e: e.tensor_scalar(out=penF[:, 0:32], in0=scs[:], scalar1=top8s[:, 2:3], scalar2=NEG, op0=ALU.is_lt, op1=ALU.mult), r=[b_scs, b_top8s], w=[b_penF])
            OP("pool", lambda e: e.memset(penF[:, 32:33], 0.0), w=[b_penF])
            OP("dve", lambda e: e.tensor_scalar(out=penF[:], in0=penF[:], scalar1=rbcol[:, 0:1], scalar2=None, op0=ALU.add), r=[b_penF, b_rbcol], w=[b_penF])
            OP("pe", lambda e: e.transpose(out=PF[4][0:33, 64:128], in_=penF[:], identity=identf[0:64, 0:64]), r=[b_penF, b_idf], w=[bPF[4]])
            OP("act", lambda e: e.copy(out=penTb[:], in_=PF[4][0:33, 64:128]), r=[bPF[4]], w=[b_penTb])
            for kt in range(65):
                own = kt == 64
                si = kt % 2
                L = 8 if own else 128
                n = 32 if own else kt // 2
                if not own:
                    bi = kt % 2
                    col = sbi * 64 + kt
                    kb.dma("pool", lambda e, bi=bi, col=col: e.indirect_dma_start(out=kpf[bi][:, 0, :], out_offset=None, in_=cv[:, :],
                                                                                  in_offset=bass.IndirectOffsetOnAxis(ap=ridx[:, col:col + 1], axis=0)),
                           reads=[b_ridx], writes=[b_kpf[bi]])
                    OP("dve", lambda e, bi=bi: e.tensor_copy(out=kpb[bi][:, 0, :], in_=kpf[bi][:, 0, :]), r=[b_kpf[bi]], w=[b_kpb[bi]])
                for ch in range(4):
                    lhs = kTs_own[:, ch, sbi * 8:(sbi + 1) * 8] if own else kTsp[:, ch, kt * 128:(kt + 1) * 128]
                    OP("pe", lambda e, si=si, ch=ch, lhs=lhs, L=L: e.matmul(PF[si][0:L, 0:64], lhsT=lhs, rhs=Qbd[:, ch, :], start=(ch == 0), stop=False),
                       r=[b_kTsp, b_kTs_own, b_Qbd], w=[bPF[si]])
                OP("pe", lambda e, si=si, n=n, L=L: e.matmul(PF[si][0:L, 0:64], lhsT=ohS[:, n, 0:L], rhs=penTb[:], start=False, stop=True), r=[b_ohS, b_penTb], w=[bPF[si]])
                if kt >= 63:
                    Tm = TS0[:].rearrange("p h q -> p (h q)") if own else TS128[:].rearrange("p h q -> p (h q)")
                    OP("dve", lambda e, si=si, L=L, Tm=Tm: e.scalar_tensor_tensor(out=tmS[0:L, :], in0=PF[si][0:L, 0:64], scalar=0.125, in1=Tm, op0=ALU.mult, op1=ALU.add),
                       r=[bPF[si], b_TS0, b_TS128], w=[b_tmS])
                    OP("act", lambda e, si=si, L=L: e.activation(out=PTs[si][0:L, :], in_=tmS[0:L, :], func=AF.Exp), r=[b_tmS], w=[b_PTs[si]])
                else:
                    OP("act", lambda e, si=si: e.activation(out=PTs[si][:], in_=PF[si][:, 0:64], func=AF.Exp, scale=0.125), r=[bPF[si]], w=[b_PTs[si]])
                rhsv = Vs_own[:, sbi, :] if own else kpb[kt % 2][:, 0, :]
                OP("pe", lambda e, si=si, L=L, rhsv=rhsv, kt=kt: e.matmul(PF[2][0:64, :], lhsT=PTs[si][0:L, :], rhs=rhsv, start=(kt == 0), stop=(kt == 64)),
                   r=[b_PTs[si], b_Vs_own, b_kpb[kt % 2]], w=[bPF[2]])
                OP("pe", lambda e, si=si, L=L, kt=kt: e.matmul(PF[3][0:64, 0:1], lhsT=PTs[si][0:L, :], rhs=onesb[0:L, 0:1], start=(kt == 0), stop=(kt == 64)),
                   r=[b_PTs[si], b_onesb], w=[bPF[3]])
            OP("dve", lambda e: e.reciprocal(out=recs[:], in_=PF[3][0:64, 0:1]), r=[bPF[3]], w=[b_recs])
            OP("dve", lambda e: e.tensor_scalar(out=osb[:], in0=PF[2][0:64, :], scalar1=recs[:, 0:1], scalar2=None, op0=ALU.mult), r=[bPF[2], b_recs], w=[b_osb])
            for h in range(8):
                kb.dma("sp", lambda e, h=h, sbi=sbi: e.dma_start(out=mix[NM + sbi * 8:NM + sbi * 8 + 8, h * 64:(h + 1) * 64], in_=osb[h * 8:(h + 1) * 8, h * 64:(h + 1) * 64]),
                       reads=[b_osb], writes=[b_mix3])
        kb.barrier()
        ps_.close()
        if stop_after == "satt":
            kb.final_wait("sp")
            kb.emit()
            return nc, dbg_outs

        b_mix2 = B("mix2")
        NT = 4096 + NS
        def ml_pass(hp):
            h0 = 2 * hp
            p2 = contextlib.ExitStack()
            mqT = sbt(f"h{hp}_" "mqT", [128, 2, NM + NS], BF16, p2); b_mqT = B("mqT")
            mkT = sbt(f"h{hp}_" "mkT", [128, 2, NM + NS], BF16, p2); b_mkT = B("mkT")
            mkt = sbt(f"h{hp}_" "mkt", [128, 32, 256], BF16, p2); b_mkt = B("mkt")
            mva = sbt(f"h{hp}_" "mva", [128, 32, 2, 129], BF16, p2); b_mva = B("mva")
            sgm = sbt(f"h{hp}_" "sgm", [128, 16, 256], BF16, p2); b_sgm = B("sgm")
            mkt_s = sbt(f"h{hp}_" "mkt_s", [8, 4, 256], BF16, p2); b_mkt_s = B("mkt_s")
            mva_s = sbt(f"h{hp}_" "mva_s", [8, 4, 2, 129], BF16, p2); b_mva_s = B("mva_s")
            sgm_s = sbt(f"h{hp}_" "sgm_s", [8, 4, 256], BF16, p2); b_sgm_s = B("sgm_s")
            OP("pool", lambda e: e.memset(mva[:, :, :, 128:129], 1.0), w=[b_mva])
            OP("pool", lambda e: e.memset(mva_s[:, :, :, 128:129], 1.0), w=[b_mva_s])
            Grow = sbt(f"h{hp}_" "Grow", [2, NT], F32, p2); b_Grow = B("Grow")
            Urow = sbt(f"h{hp}_" "Urow", [2, NT], F32, p2); b_Urow = B("Urow")
            Brow = sbt(f"h{hp}_" "Brow", [2, NT], F32, p2); b_Brow = B("Brow")
            mlgb = sbt(f"h{hp}_" "mlgb", [128, 256], F32, p2); b_mlgb = B("mlgb")
            kb.dma("sp", lambda e, h0=h0: e.dma_start(out=mlgb[:], in_=mlg[0:1, h0 * 128:h0 * 128 + 256].partition_broadcast(128)), writes=[b_mlgb])
            bgt = sbt(f"h{hp}_" "bgt", [2, 2], F32, p2); b_bgt = B("bgt")
            kb.dma("sp", lambda e, h0=h0: e.dma_start(out=bgt[:, 0:1], in_=bg[h0:h0 + 2, :]), writes=[b_bgt])
            kb.dma("sp", lambda e, h0=h0: e.dma_start(out=bgt[:, 1:2], in_=bg[4 + h0:4 + h0 + 2, :]), writes=[b_bgt])
            sm0 = sbt(f"h{hp}_" "sm0", [2, 4], F32, p2); b_sm0 = B("sm0")
            kb.dma("sp", lambda e, h0=h0: e.dma_start(out=sm0[:], in_=smi[h0:h0 + 2, :]), writes=[b_sm0])
            ones2 = sbt(f"h{hp}_" "ones2", [2, 512], F32, p2); b_ones2 = B("ones2")
            OP("pool", lambda e: e.memset(ones2[:], 1.0), w=[b_ones2])
            oh2 = sbt(f"h{hp}_" "oh2", [2, 2, 128], F32, p2); b_oh2 = B("oh2")
            OP("pool", lambda e: e.memset(oh2[:], 1.0), w=[b_oh2])
            OP("pool", lambda e: e.affine_select(out=oh2[:], in_=oh2[:], pattern=[[1, 2], [0, 128]], compare_op=ALU.is_equal, fill=0.0, base=0,
                                                 channel_multiplier=-1), r=[b_oh2], w=[b_oh2])
            cmask = sbt(f"h{hp}_" "cmask", [128, 128], F32, p2); b_cmask = B("cmask")
            OP("pool", lambda e: e.memset(cmask[:], 0.0), w=[b_cmask])
            OP("pool", lambda e: e.affine_select(out=cmask[:], in_=cmask[:], pattern=[[1, 128]], compare_op=ALU.is_ge, fill=-NEG, base=0,
                                                 channel_multiplier=-1), r=[b_cmask], w=[b_cmask])
            UT = sbt(f"h{hp}_" "UT", [128, 33, 2], F32, p2); GT = sbt(f"h{hp}_" "GT", [128, 33, 2], F32, p2); mT = sbt(f"h{hp}_" "mT", [128, 33, 2], F32, p2); EM = sbt(f"h{hp}_" "EM", [128, 33, 2], F32, p2)
            b_UT = B("UT"); b_GT = B("GT"); b_mT = B("mT"); b_EM = B("EM")
            UTs = sbt(f"h{hp}_" "UTs", [8, 4, 2], F32, p2); GTs = sbt(f"h{hp}_" "GTs", [8, 4, 2], F32, p2); mTs = sbt(f"h{hp}_" "mTs", [8, 4, 2], F32, p2); EMs = sbt(f"h{hp}_" "EMs", [8, 4, 2], F32, p2)
            b_UTs = B("UTs"); b_GTs = B("GTs"); b_mTs = B("mTs"); b_EMs = B("EMs")

            p2a = contextlib.ExitStack()
            wB = sbt(f"h{hp}_" "wB", [128, 8, 1032], BF16, p2a); b_wB = B("wB")
            for c in range(8):
                rows = slice(c * 128, (c + 1) * 128)
                for k4 in range(4):
                    kb.dma("pool", lambda e, c=c, k4=k4, rows=rows, h0=h0: e.dma_start(out=wB[:, c, k4 * 256:(k4 + 1) * 256],
                                                                                    in_=w_in[rows, 1536 + k4 * 512 + h0 * 128:1536 + k4 * 512 + h0 * 128 + 256]), writes=[b_wB])
                kb.dma("pool", lambda e, c=c, rows=rows: e.dma_start(out=wB[:, c, 1024:1032], in_=w_in[rows, 3584:3592]), writes=[b_wB])
            xnT2 = sbt(f"h{hp}_" "xnT2", [128, 8, 512], BF16, p2a); b_xnT2 = B("xnT2")
            gTs = sbt(f"h{hp}_" "gTs", [8, 512], F32, p2a); b_gTs = B("gTs")
            sgt = sbt(f"h{hp}_" "sgt", [128, 256], F32, p2a); b_sgt = B("sgt")
            KS = 128.0 ** -0.5
            groups = [("ctx", g) for g in range(4)] + [("main", g) for g in range(4)] + [("smp", 0)]
            for kind, g in groups:
                ntok = NS if kind == "smp" else 512
                if kind == "smp":
                    norm_tile(xs[0:NS, :], NS, g1, b_g1, xnT2[:, :, 0:NS], b_xnT2)
                else:
                    src = xc if kind == "ctx" else xm
                    for t in range(4):
                        r0 = g * 512 + t * 128
                        norm_tile(src[r0:r0 + 128, :], 128, g1, b_g1, xnT2[:, :, t * 128:(t + 1) * 128], b_xnT2)
                gcol0 = {"ctx": g * 512, "main": 2048 + g * 512, "smp": 4096}[kind]
                for c in range(8):
                    OP("pe", lambda e, c=c, ntok=ntok: e.matmul(PF[0][0:8, 0:ntok], lhsT=wB[:, c, 1024:1032], rhs=xnT2[:, c, 0:ntok], start=(c == 0), stop=(c == 7)),
                       r=[b_wB, b_xnT2], w=[bPF[0]])
                OP("dve", lambda e, ntok=ntok: e.tensor_copy(out=gTs[:, 0:ntok], in_=PF[0][0:8, 0:ntok]), r=[bPF[0]], w=[b_gTs])
                kb.dma("sp", lambda e, ntok=ntok, gcol0=gcol0, h0=h0: e.dma_start(out=Urow[:, gcol0:gcol0 + ntok], in_=gTs[h0:h0 + 2, 0:ntok]), reads=[b_gTs], writes=[b_Urow])
                kb.dma("sp", lambda e, ntok=ntok, gcol0=gcol0, h0=h0: e.dma_start(out=Brow[:, gcol0:gcol0 + ntok], in_=gTs[4 + h0:4 + h0 + 2, 0:ntok]), reads=[b_gTs], writes=[b_Brow])
                if kind != "ctx":
                    fcol0 = g * 512 if kind == "main" else NM
                    for l in range(2):
                        for which in ("q", "k"):
                            wc0 = (0 if which == "q" else 256) + l * 128
                            pf = 1 + (which == "k")
                            for c in range(8):
                                OP("pe", lambda e, c=c, pf=pf, wc0=wc0, ntok=ntok: e.matmul(PF[pf][:, 0:ntok], lhsT=wB[:, c, wc0:wc0 + 128], rhs=xnT2[:, c, 0:ntok],
                                                                                          start=(c == 0), stop=(c == 7)), r=[b_wB, b_xnT2], w=[bPF[pf]])
                            if which == "q":
                                evac(mqT[:, l, fcol0:fcol0 + ntok], PF[pf][:, 0:ntok], [bPF[pf]], [b_mqT])
                            else:
                                evac(mkT[:, l, fcol0:fcol0 + ntok], PF[pf][:, 0:ntok], [bPF[pf]], [b_mkT], scale=KS)
                if kind == "smp":
                    tiles = [(sbi * 8, 8, sbi) for sbi in range(4)]
                else:
                    tiles = [(t * 128, 128, (g if kind == "ctx" else 4 + g) * 4 + t) for t in range(4)]
                for (c0, n, ta) in tiles:
                    for c in range(8):
                        OP("pe", lambda e, c=c, c0=c0, n=n: e.matmul(PF[3][0:n, :], lhsT=xnT2[:, c, c0:c0 + n], rhs=wB[:, c, 256:768], start=(c == 0), stop=(c == 7)),
                           r=[b_wB, b_xnT2], w=[bPF[3]])
                    if kind == "smp":
                        kdst, b_kd = mkt_s[:, ta, :], b_mkt_s
                        vdst, b_vd = mva_s[:, ta, :, 0:128], b_mva_s
                    else:
                        kdst, b_kd = mkt[:, ta, :], b_mkt
                        vdst, b_vd = mva[:, ta, :, 0:128], b_mva
                    OP("dve", lambda e, n=n, kdst=kdst: e.tensor_scalar(out=kdst, in0=PF[3][0:n, 0:256], scalar1=KS, scalar2=None, op0=ALU.mult), r=[bPF[3]], w=[b_kd])
                    OP("dve", lambda e, n=n, vdst=vdst: e.tensor_copy(out=vdst, in_=PF[3][0:n, 256:512].rearrange("p (l d) -> p l d", d=128)), r=[bPF[3]], w=[b_vd])
                    if kind != "ctx":
                        for c in range(8):
                            OP("pe", lambda e, c=c, c0=c0, n=n: e.matmul(PF[4][0:n, 0:256], lhsT=xnT2[:, c, c0:c0 + n], rhs=wB[:, c, 768:1024], start=(c == 0), stop=(c == 7)),
                               r=[b_wB, b_xnT2], w=[bPF[4]])
                        OP("dve", lambda e, n=n: e.tensor_copy(out=sgt[0:n, :], in_=PF[4][0:n, 0:256]), r=[bPF[4]], w=[b_sgt])
                        OP("act", lambda e, n=n: e.activation(out=sgt[0:n, :], in_=sgt[0:n, :], func=AF.Sigmoid), r=[b_sgt], w=[b_sgt])
                        if kind == "smp":
                            sdst, b_sd = sgm_s[:, ta, :], b_sgm_s
                        else:
                            sdst, b_sd = sgm[:, ta - 16, :], b_sgm
                        OP("dve", lambda e, n=n, sdst=sdst: e.tensor_tensor(out=sdst, in0=sgt[0:n, :], in1=mlgb[0:n, :], op=ALU.mult), r=[b_sgt, b_mlgb], w=[b_sd])
            kb.barrier()
            p2a.close()
            if stop_after == "ml_a":
                kb.final_wait("sp"); kb.emit(); p2.close(); raise StopIteration

            nbf = sbt(f"h{hp}_" "nbf", [2, 1], F32, p2); b_nbf = B("nbf")
            OP("dve", lambda e: e.tensor_scalar(out=nbf[:], in0=bgt[:, 1:2], scalar1=-1.0, scalar2=None, op0=ALU.mult), r=[b_bgt], w=[b_nbf])
            OP("act", lambda e: e.activation(out=Brow[:], in_=Brow[:], func=AF.Exp, scale=-1.0, bias=nbf[:, 0:1]), r=[b_Brow, b_nbf], w=[b_Brow])
            OP("act", lambda e: e.activation(out=Brow[:], in_=Brow[:], func=AF.Ln, scale=1.0, bias=1.0), r=[b_Brow], w=[b_Brow])
            OP("dve", lambda e: e.tensor_scalar(out=Brow[:], in0=Brow[:], scalar1=-1.0, scalar2=None, op0=ALU.mult), r=[b_Brow], w=[b_Brow])
            OP("dve", lambda e: e.tensor_scalar(out=Urow[:], in0=Urow[:], scalar1=bgt[:, 0:1], scalar2=None, op0=ALU.add), r=[b_Urow, b_bgt], w=[b_Urow])
            OP("dve", lambda e: e.tensor_scalar(out=Brow[:, 0:2048], in0=Brow[:, 0:2048], scalar1=cfb[0:2, 0:1], scalar2=None, op0=ALU.mult), r=[b_Brow, b_cfb], w=[b_Brow])
            OP("dve", lambda e: e.tensor_scalar(out=Urow[:, 0:2048], in0=Urow[:, 0:2048], scalar1=cfb[0:2, 0:1], scalar2=cfb[0:2, 1:2], op0=ALU.mult, op1=ALU.add),
               r=[b_Urow, b_cfb], w=[b_Urow])
            segs = [(i * 512, 512, None if i == 0 else i * 512 - 1) for i in range(8)] + [(4096 + sbi * 8, 8, None) for sbi in range(4)]
            for (c0, n, prev) in segs:
                init = 0.0 if prev is None else Brow[:, prev:prev + 1]
                OP("dve", lambda e, c0=c0, n=n, init=init: e.tensor_tensor_scan(out=Brow[:, c0:c0 + n], data0=ones2[:, 0:n], data1=Brow[:, c0:c0 + n], initial=init,
                                                                                op0=ALU.mult, op1=ALU.add), r=[b_Brow, b_ones2], w=[b_Brow])
            OP("dve", lambda e: e.tensor_tensor(out=Urow[:], in0=Urow[:], in1=Brow[:], op=ALU.subtract), r=[b_Urow, b_Brow], w=[b_Urow])
            for si_, (c0, n, prev) in enumerate(segs):
                if c0 >= 4096:
                    sbi = (c0 - 4096) // 8
                    init = sm0[:, sbi:sbi + 1]
                else:
                    init = 0.0 if prev is None else Grow[:, prev:prev + 1]
                OP("dve", lambda e, c0=c0, n=n, init=init: e.tensor_tensor_scan(out=Grow[:, c0:c0 + n], data0=Urow[:, c0:c0 + n], data1=Urow[:, c0:c0 + n], initial=init,
                                                                                op0=ALU.max, op1=ALU.max), r=[b_Urow, b_Grow, b_sm0], w=[b_Grow])
            OP("dve", lambda e: e.tensor_tensor(out=Brow[:], in0=Brow[:], in1=Grow[:], op=ALU.add), r=[b_Grow, b_Brow], w=[b_Brow])
            for (row, b_row, colt, b_colt, colts, b_colts) in ((Urow, b_Urow, UT, b_UT, UTs, b_UTs), (Grow, b_Grow, GT, b_GT, GTs, b_GTs), (Brow, b_Brow, mT, b_mT, mTs, b_mTs)):
                for ck in range(32):
                    OP("pe", lambda e, ck=ck, row=row: e.transpose(out=PF[5][:, ck * 2:ck * 2 + 2], in_=row[:, ck * 128:(ck + 1) * 128], identity=identf[0:2, 0:2]),
                       r=[b_row, b_idf], w=[bPF[5]])
                OP("dve", lambda e, colt=colt: e.tensor_copy(out=colt[:, 0:32, :], in_=PF[5][:, 0:64].rearrange("p (c l) -> p c l", l=2)), r=[bPF[5]], w=[b_colt])
                for sbi in range(4):
                    OP("pe", lambda e, sbi=sbi, row=row: e.transpose(out=PF[5][0:8, sbi * 2:sbi * 2 + 2], in_=row[:, 4096 + sbi * 8:4096 + sbi * 8 + 8], identity=identf[0:2, 0:2]),
                       r=[b_row, b_idf], w=[bPF[5]])
                OP("dve", lambda e, colts=colts: e.tensor_copy(out=colts[:], in_=PF[5][0:8, 0:8].rearrange("p (c l) -> p c l", l=2)), r=[bPF[5]], w=[b_colts])
            OP("act", lambda e: e.activation(out=EM[:, 0:32, :], in_=mT[:, 0:32, :], func=AF.Exp, scale=-1.0), r=[b_mT], w=[b_EM])
            OP("act", lambda e: e.activation(out=EMs[:], in_=mTs[:], func=AF.Exp, scale=-1.0), r=[b_mTs], w=[b_EMs])

            if stop_after == "ml_b":
                kb.final_wait("sp"); kb.emit(); p2.close(); raise StopIteration
            Cf = sbt(f"h{hp}_" "Cf", [128, 2, 129], F32, p2); b_Cf = B("Cf")
            Cb = sbt(f"h{hp}_" "Cb", [128, 2, 129], BF16, p2); b_Cb = B("Cb")
            gprev = sbt(f"h{hp}_" "gprev", [128, 2], F32, p2); b_gprev = B("gprev")
            gend = sbt(f"h{hp}_" "gend", [128, 2], F32, p2); b_gend = B("gend")
            g2 = sbt(f"h{hp}_" "g2", [128, 2], F32, p2); b_g2 = B("g2")
            gtok = sbt(f"h{hp}_" "gtok", [128, 2], F32, p2); b_gtok = B("gtok")
            gst = sbt(f"h{hp}_" "gst", [128, 2], F32, p2); b_gst = B("gst")
            wst = sbt(f"h{hp}_" "wst", [128, 2], F32, p2); b_wst = B("wst")
            tmpD = sbt(f"h{hp}_" "tmpD", [128, 2, 128], F32, p2); b_tmpD = B("tmpD")
            sT = sbt(f"h{hp}_" "sT", [128, 2, 128], BF16, p2); b_sT = B("sT")
            hs = sbt(f"h{hp}_" "hs", [128, 2, 129], F32, p2); b_hs = B("hs")
            nd = sbt(f"h{hp}_" "nd", [128, 2, 129], F32, p2); b_nd = B("nd")
            dab = sbt(f"h{hp}_" "dab", [128, 2], F32, p2); b_dab = B("dab")
            ssh = sbt(f"h{hp}_" "ssh", [128, 2], F32, p2); b_ssh = B("ssh")
            junk = sbt(f"h{hp}_" "junk", [128, 128], F32, p2); b_junk = B("junk")
            gv = sbt(f"h{hp}_" "gv", [128, 2, 129], BF16, p2); b_gv = B("gv")
            mlb = [sbt(f"h{hp}_" f"mlb{i}", [128, 256], BF16, p2) for i in range(2)]; b_mlb = [B(f"mlb{i}") for i in range(2)]

            def chunk(L, ck_cols, UTv, GTv, EMv, kT_v, qT_v, kt_v, va_v, sg_v, full, mix_rows, mi):
                for l in range(2):
                    OP("pe", lambda e, l=l: e.matmul(PF[4][0:L, l * 128:l * 128 + L], lhsT=oh2[:, l, 0:L], rhs=Grow[:, ck_cols], start=True, stop=True),
                       r=[b_oh2, b_Grow], w=[bPF[4]])
                OP("dve", lambda e: e.tensor_copy(out=gend[0:L, :].rearrange("p (l o) -> p l o", o=1), in_=PF[4][0:L, 0:256].rearrange("p (l t) -> p l t", t=128)[:, :, L - 1:L]),
                   r=[bPF[4]], w=[b_gend])
                if full:
                    for l in range(2):
                        OP("pe", lambda e, l=l: e.matmul(PF[0][0:L, l * 128:l * 128 + L], lhsT=kT_v(l), rhs=qT_v(l), start=True, stop=True), r=[b_mkT, b_mqT], w=[bPF[0]])
                        OP("dve", lambda e, l=l: e.scalar_tensor_tensor(out=tmpD[0:L, l, 0:L], in0=PF[4][0:L, l * 128:l * 128 + L], scalar=UTv[:, l:l + 1], in1=cmask[0:L, 0:L],
                                                                        op0=ALU.subtract, op1=ALU.add), r=[bPF[4], b_UT, b_UTs, b_cmask], w=[b_tmpD])
                    OP("act", lambda e: e.activation(out=tmpD[0:L, :, 0:L], in_=tmpD[0:L, :, 0:L], func=AF.Exp, scale=-1.0), r=[b_tmpD], w=[b_tmpD])
                    OP("dve", lambda e: e.tensor_tensor(out=sT[0:L, :, 0:L], in0=PF[0][0:L, 0:256].rearrange("p (l t) -> p l t", t=128)[:, :, 0:L], in1=tmpD[0:L, :, 0:L], op=ALU.mult),
                       r=[bPF[0], b_tmpD], w=[b_sT])
                    for l in range(2):
                        OP("pe", lambda e, l=l: e.matmul(PF[1][0:L, l * 129:(l + 1) * 129], lhsT=sT[0:L, l, 0:L], rhs=va_v(l), start=True, stop=True),
                           r=[b_sT, b_mva, b_mva_s], w=[bPF[1]])
                        OP("pe", lambda e, l=l: e.matmul(PF[2][0:L, l * 129:(l + 1) * 129], lhsT=qT_v(l), rhs=Cb[:, l, :], start=True, stop=True),
                           r=[b_mqT, b_Cb], w=[bPF[2]])
                    OP("dve", lambda e: e.tensor_copy(out=hs[0:L, :, :], in_=PF[1][0:L, 0:258].rearrange("p (l c) -> p l c", c=129)), r=[bPF[1]], w=[b_hs])
                    OP("dve", lambda e: e.tensor_tensor(out=g2[0:L, :], in0=gprev[0:L, :], in1=GTv, op=ALU.subtract), r=[b_gprev, b_GT, b_GTs], w=[b_g2])
                    OP("act", lambda e: e.activation(out=wst[0:L, :], in_=g2[0:L, :], func=AF.Exp), r=[b_g2], w=[b_wst])
                    for l in range(2):
                        OP("dve", lambda e, l=l: e.scalar_tensor_tensor(out=nd[0:L, l, :], in0=PF[2][0:L, l * 129:(l + 1) * 129], scalar=wst[0:L, l:l + 1], in1=hs[0:L, l, :],
                                                                        op0=ALU.mult, op1=ALU.add), r=[bPF[2], b_wst, b_hs], w=[b_nd])
                        OP("dve", lambda e, l=l: e.scalar_tensor_tensor(out=dab[0:L, l:l + 1], in0=nd[0:L, l, 128:129], scalar=-1.0, in1=nd[0:L, l, 128:129], op0=ALU.mult, op1=ALU.max),
                           r=[b_nd], w=[b_dab])
                        OP("dve", lambda e, l=l: e.tensor_scalar(out=dab[0:L, l:l + 1], in0=dab[0:L, l:l + 1], scalar1=EMv[:, l:l + 1], scalar2=None, op0=ALU.max),
                           r=[b_dab, b_EM, b_EMs], w=[b_dab])
                    OP("dve", lambda e: e.reciprocal(out=dab[0:L, :], in_=dab[0:L, :]), r=[b_dab], w=[b_dab])
                    for l in range(2):
                        OP("act", lambda e, l=l: e.activation(out=junk[0:L, :], in_=nd[0:L, l, 0:128], func=AF.Square, scale=dab[0:L, l:l + 1], accum_out=ssh[0:L, l:l + 1]),
                           r=[b_nd, b_dab], w=[b_junk, b_ssh])
                    OP("act", lambda e: e.activation(out=ssh[0:L, :], in_=ssh[0:L, :], func=AF.Sqrt, scale=1.0 / 128, bias=EPS), r=[b_ssh], w=[b_ssh])
                    OP("dve", lambda e: e.reciprocal(out=ssh[0:L, :], in_=ssh[0:L, :]), r=[b_ssh], w=[b_ssh])
                    OP("dve", lambda e: e.tensor_tensor(out=ssh[0:L, :], in0=ssh[0:L, :], in1=dab[0:L, :], op=ALU.mult), r=[b_ssh, b_dab], w=[b_ssh])
                    for l in range(2):
                        OP("dve", lambda e, l=l: e.scalar_tensor_tensor(out=mlb[mi][0:L, l * 128:(l + 1) * 128], in0=nd[0:L, l, 0:128], scalar=ssh[0:L, l:l + 1], in1=sg_v(l),
                                                                        op0=ALU.mult, op1=ALU.mult), r=[b_nd, b_ssh, b_sgm, b_sgm_s], w=[b_mlb[mi]])
                    kb.dma("sp", lambda e: e.dma_start(out=mix[mix_rows, 512 + h0 * 128:512 + h0 * 128 + 256], in_=mlb[mi][0:L, :]), reads=[b_mlb[mi]], writes=[b_mix2])
                OP("dve", lambda e: e.tensor_tensor(out=gtok[0:L, :], in0=UTv, in1=gend[0:L, :], op=ALU.subtract), r=[b_UT, b_UTs, b_gend], w=[b_gtok])
                OP("act", lambda e: e.activation(out=gtok[0:L, :], in_=gtok[0:L, :], func=AF.Exp), r=[b_gtok], w=[b_gtok])
                for l in range(2):
                    OP("dve", lambda e, l=l: e.tensor_scalar(out=gv[0:L, l, :], in0=va_v(l), scalar1=gtok[0:L, l:l + 1], scalar2=None, op0=ALU.mult),
                       r=[b_mva, b_mva_s, b_gtok], w=[b_gv])
                    OP("pe", lambda e, l=l: e.matmul(PF[3][:, l * 129:(l + 1) * 129], lhsT=kt_v(l), rhs=gv[0:L, l, :], start=True, stop=True), r=[b_mkt, b_mkt_s, b_gv], w=[bPF[3]])
                OP("dve", lambda e: e.tensor_tensor(out=g2[:, :], in0=gprev[:, :], in1=gend[0:1, :].to_broadcast([128, 2]) if False else gendb[:, :], op=ALU.subtract),
                   r=[b_gprev, b_gendb], w=[b_g2])
                OP("act", lambda e: e.activation(out=gst[:, :], in_=g2[:, :], func=AF.Exp), r=[b_g2], w=[b_gst])
                for l in range(2):
                    OP("dve", lambda e, l=l: e.scalar_tensor_tensor(out=Cf[:, l, :], in0=Cf[:, l, :], scalar=gst[:, l:l + 1], in1=PF[3][:, l * 129:(l + 1) * 129],
                                                                    op0=ALU.mult, op1=ALU.add), r=[b_Cf, b_gst, bPF[3]], w=[b_Cf])
                OP("act", lambda e: e.copy(out=Cb[:], in_=Cf[:]), r=[b_Cf], w=[b_Cb])
                OP("dve", lambda e: e.tensor_copy(out=gprev[:], in_=gendb[:]), r=[b_gendb], w=[b_gprev])

            gendb = sbt(f"h{hp}_" "gendb", [128, 2], F32, p2); b_gendb = B("gendb")

            def gend_bcast(col):
                for l in range(2):
                    OP("pe", lambda e, l=l: e.matmul(PF[5][:, l:l + 1], lhsT=oh2[:, l, :], rhs=Grow[:, col:col + 1], start=True, stop=True), r=[b_oh2, b_Grow], w=[bPF[5]])
                OP("dve", lambda e: e.tensor_copy(out=gendb[:], in_=PF[5][:, 0:2]), r=[bPF[5]], w=[b_gendb])

            OP("pool", lambda e: e.memset(Cf[:], 0.0), w=[b_Cf])
            OP("pool", lambda e: e.memset(Cb[:], 0.0), w=[b_Cb])
            OP("pool", lambda e: e.memset(gprev[:], 0.0), w=[b_gprev])
            for ck in range(32):
                full = ck >= 16
                tm = ck - 16
                cols = slice(ck * 128, (ck + 1) * 128)
                fc = slice(tm * 128, (tm + 1) * 128)
                gend_bcast(ck * 128 + 127)
                chunk(128, cols, UT[:, ck, :], GT[:, ck, :], EM[:, ck, :],
                      lambda l, fc=fc: mkT[:, l, fc], lambda l, fc=fc: mqT[:, l, fc], lambda l, ck=ck: mkt[:, ck, l * 128:(l + 1) * 128],
                      lambda l, ck=ck: mva[:, ck, l, :], lambda l, tm=tm: sgm[:, tm, l * 128:(l + 1) * 128], full,
                      slice(tm * 128, (tm + 1) * 128), ck % 2)
            if stop_after == "ml_c":
                kb.final_wait("sp"); kb.emit(); p2.close(); raise StopIteration
            for l in range(2):
                h = h0 + l
                store(C_p[h * 128:(h + 1) * 128, :], Cf[:, l, 0:128], b_Cf)
                store(n_p[h * 128:(h + 1) * 128, :], Cf[:, l, 128:129], b_Cf)
            store(m_p[h0:h0 + 2, :], Brow[:, 4095:4096], b_Brow)
            for sbi in range(4):
                for l in range(2):
                    h = h0 + l
                    r0 = (sbi * 4 + h) * 128
                    kb.dma("sp", lambda e, l=l, r0=r0: e.dma_start(out=Cf[:, l, 0:128], in_=sC[r0:r0 + 128, :]), writes=[b_Cf])
                    kb.dma("sp", lambda e, l=l, r0=r0: e.dma_start(out=Cf[:, l, 128:129], in_=sn[r0:r0 + 128, :]), writes=[b_Cf])
                    kb.dma("sp", lambda e, l=l, h=h, sbi=sbi: e.dma_start(out=gprev[:, l:l + 1], in_=smi[h:h + 1, sbi:sbi + 1].partition_broadcast(128)), writes=[b_gprev])
                OP("act", lambda e: e.copy(out=Cb[:], in_=Cf[:]), r=[b_Cf], w=[b_Cb])
                c0 = 4096 + sbi * 8
                fc = slice(NM + sbi * 8, NM + sbi * 8 + 8)
                gend_bcast(c0 + 7)
                chunk(8, slice(c0, c0 + 8), UTs[:, sbi, :], GTs[:, sbi, :], EMs[:, sbi, :],
                      lambda l, fc=fc: mkT[:, l, fc], lambda l, fc=fc: mqT[:, l, fc], lambda l, sbi=sbi: mkt_s[:, sbi, l * 128:(l + 1) * 128],
                      lambda l, sbi=sbi: mva_s[:, sbi, l, :], lambda l, sbi=sbi: sgm_s[:, sbi, l * 128:(l + 1) * 128], True,
                      slice(NM + sbi * 8, NM + sbi * 8 + 8), sbi % 2)
                for l in range(2):
                    h = h0 + l
                    r0 = (sbi * 4 + h) * 128
                    store(C_s[r0:r0 + 128, :], Cf[:, l, 0:128], b_Cf)
                    store(n_s[r0:r0 + 128, :], Cf[:, l, 128:129], b_Cf)
                store(m_s[sbi * 4 + h0:sbi * 4 + h0 + 2, :], Brow[:, c0 + 7:c0 + 8], b_Brow)
            kb.barrier()
            p2.close()
        try:
            for hp in range(2):
                ml_pass(hp)
        except StopIteration:
            return nc, dbg_outs
        if stop_after == "ml":
            kb.final_wait("sp")
            kb.emit()
            return nc, dbg_outs

        p3 = contextlib.ExitStack()
        wO = sbt("wO", [128, 8, 1024], BF16, p3); b_wO = B("wO")
        wU = sbt("wU", [128, 8, 4096], BF16, p3); b_wU = B("wU")
        wD = sbt("wD", [128, 32, 1024], BF16, p3); b_wD = B("wD")
        for c in range(8):
            load_w(wO[:, c, :], w_out[c * 128:(c + 1) * 128, :], b_wO, 1024)
        for c in range(8):
            load_w(wU[:, c, :], w_up[c * 128:(c + 1) * 128, :], b_wU, 4096)
        for f in range(32):
            load_w(wD[:, f, :], w_down[f * 128:(f + 1) * 128, :], b_wD, 1024)
        kb.dma("sp", lambda e: e.dma_start(out=g1[:], in_=nffn[0:1, :].partition_broadcast(128)), writes=[b_g1])
        g3 = sbt("g3", [128, D], F32, p3); b_g3 = B("g3")
        kb.dma("sp", lambda e: e.dma_start(out=g3[:], in_=nfin[0:1, :].partition_broadcast(128)), writes=[b_g3])
        mixb = [sbt(f"mixb{i}", [128, D], BF16, p3) for i in range(2)]; b_mixb = [B(f"mixb{i}") for i in range(2)]
        mixT = sbt("mixT", [128, 8, 256], BF16, p3); b_mixT = B("mixT")
        xn2T = sbt("xn2T", [128, 8, 256], BF16, p3); b_xn2T = B("xn2T")
        uT = sbt("uT", [128, 32, 256], BF16, p3); b_uT = B("uT")
        ur = [sbt(f"ur{i}", [128, 256], F32, p3) for i in range(2)]; b_ur = [B(f"ur{i}") for i in range(2)]
        yo = sbt("yo", [128, D], F32, p3); b_yo = B("yo")
        all_mix = [b_mix, b_mix2, b_mix3]
        fgroups = [("main", gi) for gi in range(8)] + [("smp", 0)]
        for kind, gi in fgroups:
            if kind == "main":
                tiles = [(gi * 256 + t * 128, 128, t * 128) for t in range(2)]
                xsrc, ydst = xm, y_m
            else:
                tiles = [(0, NS, 0)]
                xsrc, ydst = xs, y_s
            ntok = sum(n for _, n, _ in tiles)
            for ti, (r0, n, c0) in enumerate(tiles):
                mr0 = r0 if kind == "main" else NM
                kb.dma("sp", lambda e, ti=ti, r0=r0, n=n, xsrc=xsrc: e.dma_start(out=xt[ti][0:n, :], in_=xsrc[r0:r0 + n, :]), writes=[b_xt[ti]])
                kb.dma("sp", lambda e, ti=ti, mr0=mr0, n=n: e.dma_start(out=mixb[ti][0:n, :], in_=mix[mr0:mr0 + n, :]), reads=all_mix, writes=[b_mixb[ti]])
                to_featmajor(mixb[ti], b_mixb[ti], n, mixT[:, :, c0:c0 + n], b_mixT)
            for ti, (r0, n, c0) in enumerate(tiles):
                for hf in range(2):
                    for c in range(8):
                        OP("pe", lambda e, c=c, hf=hf, n=n, c0=c0: e.matmul(PF[hf][0:n, :], lhsT=mixT[:, c, c0:c0 + n], rhs=wO[:, c, hf * 512:(hf + 1) * 512], start=(c == 0), stop=(c == 7)),
                           r=[b_mixT, b_wO], w=[bPF[hf]])
                    OP("dve", lambda e, ti=ti, hf=hf, n=n: e.tensor_tensor(out=xt[ti][0:n, hf * 512:(hf + 1) * 512], in0=PF[hf][0:n, :], in1=xt[ti][0:n, hf * 512:(hf + 1) * 512], op=ALU.add),
                       r=[bPF[hf], b_xt[ti]], w=[b_xt[ti]])
                rms_scale(xt[ti][0:n, :], b_xt[ti], n, ti, xnb[ti][0:n, :], b_xnb[ti])
                OP("dve", lambda e, ti=ti, n=n: e.scalar_tensor_tensor(out=xnb[ti][0:n, :], in0=xt[ti][0:n, :], scalar=rstd[ti][0:n, 0:1], in1=g1[0:n, :], op0=ALU.mult, op1=ALU.mult),
                   r=[b_xt[ti], b_rstd[ti], b_g1], w=[b_xnb[ti]])
                to_featmajor(xnb[ti], b_xnb[ti], n, xn2T[:, :, c0:c0 + n], b_xn2T)
            for f in range(32):
                pf = 2 + f % 2
                ui = f % 2
                for c in range(8):
                    OP("pe", lambda e, c=c, f=f, pf=pf, ntok=ntok: e.matmul(PF[pf][:, 0:ntok], lhsT=wU[:, c, f * 128:(f + 1) * 128], rhs=xn2T[:, c, 0:ntok], start=(c == 0), stop=(c == 7)),
                       r=[b_wU, b_xn2T], w=[bPF[pf]])
                OP("dve", lambda e, pf=pf, ui=ui, ntok=ntok: e.tensor_scalar(out=ur[ui][:, 0:ntok], in0=PF[pf][:, 0:ntok], scalar1=0.0, scalar2=None, op0=ALU.max), r=[bPF[pf]], w=[b_ur[ui]])
                OP("act", lambda e, f=f, ui=ui, ntok=ntok: e.activation(out=uT[:, f, 0:ntok], in_=ur[ui][:, 0:ntok], func=AF.Square), r=[b_ur[ui]], w=[b_uT])
            for ti, (r0, n, c0) in enumerate(tiles):
                for hf in range(2):
                    for f in range(32):
                        OP("pe", lambda e, f=f, hf=hf, n=n, c0=c0: e.matmul(PF[hf][0:n, :], lhsT=uT[:, f, c0:c0 + n], rhs=wD[:, f, hf * 512:(hf + 1) * 512], start=(f == 0), stop=(f == 31)),
                           r=[b_uT, b_wD], w=[bPF[hf]])
                    OP("dve", lambda e, ti=ti, hf=hf, n=n: e.tensor_tensor(out=xt[ti][0:n, hf * 512:(hf + 1) * 512], in0=PF[hf][0:n, :], in1=xt[ti][0:n, hf * 512:(hf + 1) * 512], op=ALU.add),
                       r=[bPF[hf], b_xt[ti]], w=[b_xt[ti]])
                rms_scale(xt[ti][0:n, :], b_xt[ti], n, ti, xnb[ti][0:n, :], b_xnb[ti])
                OP("dve", lambda e, ti=ti, n=n: e.scalar_tensor_tensor(out=yo[0:n, :], in0=xt[ti][0:n, :], scalar=rstd[ti][0:n, 0:1], in1=g3[0:n, :], op0=ALU.mult, op1=ALU.mult),
                   r=[b_xt[ti], b_rstd[ti], b_g3], w=[b_yo])
                store(ydst[r0:r0 + n, :], yo[0:n, :], b_yo)
        kb.barrier()
        p3.close()
        kb.final_wait("sp")
        kb.emit()
        return nc, dbg_outs


def make_in_maps(inp):
    f = lambda a: np.ascontiguousarray(a, dtype=np.float32)
    xp = np.asarray(inp["x_prompt"]); xsamp = np.asarray(inp["x_sample"])
    ckf = f(np.asarray(inp["cache_k"]).reshape(2560 * 128, 512))
    cvf = f(np.asarray(inp["cache_v"]).reshape(2560 * 128, 512))
    ptab = np.asarray(inp["page_table"]).astype(np.int32)
    sCf = np.asarray(inp["state_C"])[0]; snf = np.asarray(inp["state_n"])[0]; smf = np.asarray(inp["state_m"])[0]
    shared = {
        "w_in": f(inp["w_in"][0]), "w_out": f(inp["w_out"][0]), "w_up": f(inp["w_up"][0]), "w_down": f(inp["w_down"][0]),
        "nmix": f(inp["norm_mix"]).reshape(1, D), "nffn": f(inp["norm_ffn"]).reshape(1, D), "nfin": f(inp["norm_final"]).reshape(1, D),
        "mlg": f(inp["ml_norm"]).reshape(1, 512),
        "bg": f(np.concatenate([np.asarray(inp["b_ig"]).reshape(-1), np.asarray(inp["b_fg"]).reshape(-1)])).reshape(8, 1),
        "rb": f(inp["rel_bias"]).reshape(1, 256), "ck": ckf, "cv": cvf,
    }
    maps = []
    for c in range(8):
        b, half = c // 2, c % 2
        m = dict(shared)
        m["xm"] = f(xp[b, half * NM:(half + 1) * NM])
        m["xc"] = f(xp[b, 0:NM]) if half else np.zeros((NM, D), np.float32)
        m["xs"] = f(xsamp[4 * c:4 * c + 4].reshape(NS, D))
        m["pt"] = np.ascontiguousarray(ptab[4 * c:4 * c + 4].reshape(1, 256))
        m["sC"] = f(sCf[4 * c:4 * c + 4].reshape(16 * 128, 128))
        m["sn"] = f(snf[4 * c:4 * c + 4].reshape(16 * 128, 1))
        m["smi"] = f(smf[4 * c:4 * c + 4].T)
        m["cf"] = np.array([[float(half), (float(half) - 1.0) * 30000.0, 0.0, 0.0]], np.float32)
        cd = np.zeros((16, 2, 16), np.float32)
        for qt in range(16):
            own = 8 + qt // 2
            for n in range(16):
                is_cand = (n < 8 and half == 1) or (8 <= n < own)
                cd[qt, 0, n] = NEG if is_cand else 0.0
                cd[qt, 1, n] = 0.0 if (is_cand or n == own) else NEG
        m["cand"] = cd.reshape(1, 512)
        maps.append(m)
    return maps


STOP_AFTER = "all"
CACHE_ROWS = 2560 * 128


def kernel(**inputs):
    maps = make_in_maps(inputs)
    for m in maps:
        m["ck"] = m["ck"][:CACHE_ROWS]
        m["cv"] = m["cv"][:CACHE_ROWS]
    nc, _ = build_program(stop_after=STOP_AFTER, cache_rows=CACHE_ROWS)
    res = run_bass_kernel_spmd(nc, maps, core_ids=list(range(8)))
    R = res.results
    f32 = np.float32
    y_prompt = np.zeros((4, 4096, D), f32); y_sample = np.zeros((32, 8, D), f32)
    nkp = np.zeros((1, 4, 4096, 8, 64), f32); nvp = np.zeros((1, 4, 4096, 8, 64), f32)
    nCp = np.zeros((1, 4, 4, 128, 128), f32); nnp_ = np.zeros((1, 4, 4, 128), f32); nmp = np.zeros((1, 4, 4), f32)
    nks = np.zeros((1, 32, 8, 8, 64), f32); nvs = np.zeros((1, 32, 8, 8, 64), f32)
    nCs = np.zeros((1, 32, 4, 128, 128), f32); nns = np.zeros((1, 32, 4, 128), f32); nms = np.zeros((1, 32, 4), f32)
    for c in range(8):
        b, half = c // 2, c % 2
        r = R[c]
        sl = slice(half * NM, (half + 1) * NM)
        y_prompt[b, sl] = r["y_m"]
        y_sample[4 * c:4 * c + 4] = r["y_s"].reshape(4, 8, D)
        nkp[0, b, sl] = r["k_m"].reshape(NM, 8, 64)
        nvp[0, b, sl] = r["v_m"].reshape(NM, 8, 64)
        if half == 1:
            nCp[0, b] = r["C_p"].reshape(4, 128, 128)
            nnp_[0, b] = r["n_p"].reshape(4, 128)
            nmp[0, b] = r["m_p"].reshape(4)
        nks[0, 4 * c:4 * c + 4] = r["k_s"].reshape(4, 8, 8, 64)
        nvs[0, 4 * c:4 * c + 4] = r["v_s"].reshape(4, 8, 8, 64)
        nCs[0, 4 * c:4 * c + 4] = r["C_s"].reshape(4, 4, 128, 128)
        nns[0, 4 * c:4 * c + 4] = r["n_s"].reshape(4, 4, 128)
        nms[0, 4 * c:4 * c + 4] = r["m_s"].reshape(4, 4)
    return (y_prompt, y_sample, nkp, nvp, nCp, nnp_, nmp, nks, nvs, nCs, nns, nms)
```

```python
import contextlib
import math
import numpy as np
import concourse.bass as bass
import concourse.mybir as mybir
from concourse.alu_op_type import AluOpType as ALU
from concourse.bass_utils import run_bass_kernel_spmd

F32 = mybir.dt.float32
BF16 = mybir.dt.bfloat16
I32 = mybir.dt.int32
AF = mybir.ActivationFunctionType
AX = mybir.AxisListType

ENGS = ("pe", "act", "dve", "pool", "sp")
NEG = -30000.0
D = 1024
NM = 2048
NS = 32
EPS = 1e-6


class Buf:
    __slots__ = ("name", "w", "rs", "dsem", "dcnt")

    def __init__(self, name):
        self.name = name
        self.w = None
        self.rs = {}
        self.dsem = None
        self.dcnt = 0


class KB:
    def __init__(self, nc, stack):
        self.nc = nc
        self.stack = stack
        self.sems = {}
        self.cnt = {e: 0 for e in ENGS}
        self.known = {e: {} for e in ENGS}
        self.prog = {e: [] for e in ENGS}
        for e in ENGS:
            self._newsem("E_" + e)
        self.nd = 0
        self.dbufs = []

    def _newsem(self, key):
        self.sems[key] = self.stack.enter_context(self.nc.semaphore(key))
        return key

    def buf(self, name):
        return Buf(name)

    def _collect(self, e, reads, writes):
        waits = {}
        kn = self.known[e]
        own = "E_" + e

        def need(tok, same_ok):
            if tok is None:
                return
            k, v = tok
            if k == own and same_ok:
                return
            if kn.get(k, 0) >= v:
                return
            if waits.get(k, 0) < v:
                waits[k] = v

        for b in reads:
            need(b.w, False)
        for b in writes:
            need(b.w, True)
            for k, v in b.rs.items():
                need((k, v), True)
        for k, v in waits.items():
            kn[k] = v
        return list(waits.items())

    def op(self, e, fn, reads=(), writes=()):
        waits = self._collect(e, reads, writes)
        self.cnt[e] += 1
        tok = ("E_" + e, self.cnt[e])
        for b in reads:
            if b.rs.get(tok[0], 0) < tok[1]:
                b.rs[tok[0]] = tok[1]
        for b in writes:
            b.w = tok
            b.rs = {}
        self.prog[e].append((waits, fn, (tok[0], 1)))

    def dma(self, q, fn, reads=(), writes=()):
        waits = self._collect(q, reads, writes)
        tgt = writes[0] if writes else reads[0]
        if tgt.dsem is None:
            self.nd += 1
            tgt.dsem = self._newsem(f"D{self.nd}")
            self.dbufs.append(tgt)
        tgt.dcnt += 16
        tok = (tgt.dsem, tgt.dcnt)
        for b in reads:
            if b.rs.get(tok[0], 0) < tok[1]:
                b.rs[tok[0]] = tok[1]
        for b in writes:
            b.w = tok
            b.rs = {}
        self.prog[q].append((waits, fn, (tok[0], 16)))

    def barrier(self):
        for e in ENGS:
            waits = [("E_" + x, self.cnt[x]) for x in ENGS if x != e and self.cnt[x] > 0]
            waits += [(b.dsem, b.dcnt) for b in self.dbufs]
            for k, v in waits:
                self.known[e][k] = max(self.known[e].get(k, 0), v)
            self.prog[e].append((waits, None, None))

    def final_wait(self, e):
        waits = [(b.dsem, b.dcnt) for b in self.dbufs]
        self.prog[e].append((waits, None, None))

    def emit(self):
        nc = self.nc
        sems = self.sems
        prog = self.prog
        with nc.Block() as block:
            def run(name):
                def body(eng):
                    for waits, fn, inc in prog[name]:
                        for k, v in waits:
                            eng.wait_ge(sems[k], v)
                        if fn is not None:
                            fn(eng).then_inc(sems[inc[0]], inc[1])
                return body
            block.tensor(run("pe"))
            block.scalar(run("act"))
            block.vector(run("dve"))
            block.gpsimd(run("pool"))
            block.sync(run("sp"))


def t5_thresholds():
    n = np.arange(0, 600)
    nf = np.maximum(n, 1).astype(np.float32)
    large = 16 + (np.log(nf / np.float32(16)) / np.float32(math.log(128 / 16)) * np.float32(16)).astype(np.int32)
    large = np.minimum(large, 31)
    bucket = np.where(n < 16, n, large)
    return [int(np.argmax(bucket >= b)) for b in range(1, 32)]


def build_program(dbg=None, stop_after=None, cache_rows=2560 * 128):
    nc = bass.Bass("TRN2", target_bir_lowering=False)
    din = lambda name, shape, dt=F32: nc.dram_tensor(name, shape, dt, kind="ExternalInput").ap()
    dout = lambda name, shape, dt=F32: nc.dram_tensor(name, shape, dt, kind="ExternalOutput").ap()
    xm = din("xm", [NM, D]); xc = din("xc", [NM, D]); xs = din("xs", [NS, D])
    w_in = din("w_in", [D, 3592]); w_out = din("w_out", [D, D]); w_up = din("w_up", [D, 4096]); w_down = din("w_down", [4096, D])
    nmix = din("nmix", [1, D]); nffn = din("nffn", [1, D]); nfin = din("nfin", [1, D]); mlg = din("mlg", [1, 512])
    bg = din("bg", [8, 1]); rb = din("rb", [1, 256])
    ck = din("ck", [cache_rows, 512]); cv = din("cv", [cache_rows, 512])
    pt = din("pt", [1, 256], I32)
    sC = din("sC", [16 * 128, 128]); sn = din("sn", [16 * 128, 1]); smi = din("smi", [4, 4])
    cf = din("cf", [1, 4]); cand = din("cand", [1, 512])
    y_m = dout("y_m", [NM, D]); y_s = dout("y_s", [NS, D])
    k_m = dout("k_m", [NM, 512]); v_m = dout("v_m", [NM, 512]); k_s = dout("k_s", [NS, 512]); v_s = dout("v_s", [NS, 512])
    C_p = dout("C_p", [512, 128]); n_p = dout("n_p", [512, 1]); m_p = dout("m_p", [4, 1])
    C_s = dout("C_s", [2048, 128]); n_s = dout("n_s", [2048, 1]); m_s = dout("m_s", [16, 1])
    mix = nc.dram_tensor("mix", [NM + NS, D], BF16, kind="ExternalOutput").ap()
    dbg_outs = {}

    with contextlib.ExitStack() as st:
        kb = KB(nc, st)
        B = kb.buf
        out_bufs = []

        def sbt(name, shape, dt, stack=st):
            return stack.enter_context(nc.sbuf_tensor(name, shape, dt))

        PT = [st.enter_context(nc.psum_tensor(f"PT{i}", [128, 8, 128], BF16)) for i in range(2)]
        bPT = [B(f"PT{i}") for i in range(2)]
        PF = [st.enter_context(nc.psum_tensor(f"PF{i}", [128, 512], F32)) for i in range(6)]
        bPF = [B(f"PF{i}") for i in range(6)]

        TQ = {"on": False, "q": []}

        def OP(e, fn, r=(), w=()):
            if TQ["on"]:
                TQ["q"].append((e, fn, list(r), list(w)))
            else:
                kb.op(e, fn, r, w)

        def flush_tq(n):
            for _ in range(min(n, len(TQ["q"]))):
                e, fn, r, w = TQ["q"].pop(0)
                kb.op(e, fn, r, w)

        def store(dst_ap, src_ap, src_buf, q="sp"):
            import os
            if os.environ.get("NO_ST") is not None:
                return
            kb.dma(q, lambda e: e.dma_start(out=dst_ap, in_=src_ap), reads=[src_buf], writes=[])

        identf = sbt("identf", [128, 128], F32); b_idf = B("idf")
        ident = sbt("ident", [128, 128], BF16); b_id = B("id")
        OP("pool", lambda e: e.memset(identf[:], 1.0), w=[b_idf])
        OP("pool", lambda e: e.affine_select(out=identf[:], in_=identf[:], pattern=[[-1, 128]], compare_op=ALU.is_equal,
                                             fill=0.0, base=0, channel_multiplier=1), r=[b_idf], w=[b_idf])
        OP("dve", lambda e: e.tensor_copy(out=ident[:], in_=identf[:]), r=[b_idf], w=[b_id])
        g1 = sbt("g1", [128, D], F32); b_g1 = B("g1")
        kb.dma("sp", lambda e: e.dma_start(out=g1[:], in_=nmix[0:1, :].partition_broadcast(128)), writes=[b_g1])
        cfb = sbt("cfb", [128, 4], F32); b_cfb = B("cfb")
        kb.dma("sp", lambda e: e.dma_start(out=cfb[:], in_=cf[0:1, :].partition_broadcast(128)), writes=[b_cfb])
        rbb = sbt("rbb", [128, 32, 8], F32); b_rbb = B("rbb")
        kb.dma("sp", lambda e: e.dma_start(out=rbb[:].rearrange("p b h -> p (b h)"), in_=rb[0:1, :].partition_broadcast(128)), writes=[b_rbb])
        candb = sbt("candb", [128, 16, 2, 16], F32); b_cand = B("cand")
        kb.dma("sp", lambda e: e.dma_start(out=candb[:].rearrange("p a b c -> p (a b c)"), in_=cand[0:1, :].partition_broadcast(128)), writes=[b_cand])

        if stop_after == "c0":
            kb.final_wait("sp")
            kb.emit()
            return nc, dbg_outs
        xt = [sbt(f"xt{i}", [128, D], F32) for i in range(2)]; b_xt = [B(f"xt{i}") for i in range(2)]
        ssq = [sbt(f"ssq{i}", [128, 1], F32) for i in range(2)]; b_ssq = [B(f"ssq{i}") for i in range(2)]
        rstd = [sbt(f"rstd{i}", [128, 1], F32) for i in range(2)]; b_rstd = [B(f"rstd{i}") for i in range(2)]
        xnb = [sbt(f"xnb{i}", [128, D], BF16) for i in range(2)]; b_xnb = [B(f"xnb{i}") for i in range(2)]
        cnt = {"x": 0}

        def rms_scale(src, b_src, n, i, junk, b_junk, dim=D):
            OP("act", lambda e: e.activation(out=junk, in_=src, func=AF.Square, accum_out=ssq[i][0:n, :]),
               r=[b_src], w=[b_junk, b_ssq[i]])
            OP("act", lambda e: e.activation(out=rstd[i][0:n, :], in_=ssq[i][0:n, :], func=AF.Sqrt, scale=1.0 / dim, bias=EPS),
               r=[b_ssq[i]], w=[b_rstd[i]])
            OP("dve", lambda e: e.reciprocal(out=rstd[i][0:n, :], in_=rstd[i][0:n, :]), r=[b_rstd[i]], w=[b_rstd[i]])

        def to_featmajor(src_bf, b_src, n, dst, b_dst, ncol=8):
            i = cnt["x"] % 2
            cnt["x"] += 1
            for c in range(ncol):
                OP("pe", lambda e, c=c: e.transpose(out=PT[i][:, c, 0:n], in_=src_bf[0:n, c * 128:(c + 1) * 128], identity=ident[0:n, 0:n]),
                   r=[b_src, b_id], w=[bPT[i]])
            OP("act", lambda e: e.copy(out=dst, in_=PT[i][:, 0:ncol, 0:n]), r=[bPT[i]], w=[b_dst])

        def norm_tile(src_ap, n, gt, b_gt, dst, b_dst, q="sp"):
            i = cnt["x"] % 2
            kb.dma(q, lambda e: e.dma_start(out=xt[i][0:n, :], in_=src_ap), writes=[b_xt[i]])
            rms_scale(xt[i][0:n, :], b_xt[i], n, i, xnb[i][0:n, :], b_xnb[i])
            OP("dve", lambda e: e.scalar_tensor_tensor(out=xnb[i][0:n, :], in0=xt[i][0:n, :], scalar=rstd[i][0:n, 0:1], in1=gt[0:n, :],
                                                       op0=ALU.mult, op1=ALU.mult), r=[b_xt[i], b_rstd[i], b_gt], w=[b_xnb[i]])
            to_featmajor(xnb[i], b_xnb[i], n, dst, b_dst)

        def load_w(dst, src, b_dst, ncols):
            for c0 in range(0, ncols, 2048):
                c1 = min(ncols, c0 + 2048)
                kb.dma("pool", lambda e, c0=c0, c1=c1: e.dma_start(out=dst[:, c0:c1], in_=src[:, c0:c1]), writes=[b_dst])

        evq = {"i": 0}

        def evac(out, in_, r, w, scale=None):
            evq["i"] += 1
            if evq["i"] % 2:
                if scale is None:
                    OP("act", lambda e: e.copy(out=out, in_=in_), r=r, w=w)
                else:
                    OP("dve", lambda e: e.tensor_scalar(out=out, in0=in_, scalar1=scale, scalar2=None, op0=ALU.mult), r=r, w=w)
            else:
                if scale is None:
                    OP("dve", lambda e: e.tensor_copy(out=out, in_=in_), r=r, w=w)
                else:
                    OP("dve", lambda e: e.tensor_scalar(out=out, in0=in_, scalar1=scale, scalar2=None, op0=ALU.mult), r=r, w=w)

        qTs = sbt("qTs", [128, 4, NS], BF16); b_qTs = B("qTs")
        kTs_own = sbt("kTs_own", [128, 4, NS], BF16); b_kTs_own = B("kTs_own")
        Vs_own = sbt("Vs_own", [8, 4, 512], BF16); b_Vs_own = B("Vs_own")
        TS128 = sbt("TS128", [128, 8, 8], F32); b_TS128 = B("TS128")
        TS0 = sbt("TS0", [8, 8, 8], F32); b_TS0 = B("TS0")
        p1 = contextlib.ExitStack()
        qTa = [sbt(f"qTa{h}", [81, NM], BF16, p1) for h in range(8)]; b_qTa = [[B(f"qTa{h}_{g}") for g in range(4)] for h in range(8)]
        b_qTaP = [[B(f"qTaP{h}_{g}") for g in range(4)] for h in range(8)]
        kTa = [sbt(f"kTa{h}", [81, 4096], BF16, p1) for h in range(8)]; b_kTa = [[B(f"kTa{h}_{g}") for g in range(8)] for h in range(8)]
        b_kTaI = [B(f"kTaI{h}") for h in range(8)]
        Va = sbt("Va", [128, 32, 8, 65], BF16, p1); b_Va = [B(f"Va{t}") for t in range(32)]; b_Va1 = B("Va1")
        OP("pool", lambda e: e.memset(Va[:, :, :, 64:65], 1.0), w=[b_Va1])
        for h in range(8):
            OP("pool", lambda e, h=h: e.memset(kTa[h][64:81, :], 1.0), w=[b_kTaI[h]])
            OP("pool", lambda e, h=h: e.affine_select(out=kTa[h][64:80, :].rearrange("p (b k) -> p b k", k=256),
                                                      in_=kTa[h][64:80, :].rearrange("p (b k) -> p b k", k=256),
                                                      pattern=[[1, 16], [0, 256]], compare_op=ALU.is_equal, fill=0.0, base=0,
                                                      channel_multiplier=-1), r=[b_kTaI[h]], w=[b_kTaI[h]])

        if stop_after == "c1":
            kb.final_wait("sp")
            kb.emit()
            p1.close()
            return nc, dbg_outs
        thr = t5_thresholds()
        Tt = {0: sbt("T0", [128, 8, 128], F32, p1), 128: sbt("T128", [128, 8, 128], F32, p1)}
        b_Tt = {0: [B(f"T0_{h}") for h in range(8)], 128: [B(f"T128_{h}") for h in range(8)]}
        drb = sbt("drb", [128, 32, 8], F32, p1); b_drb = B("drb")
        OP("dve", lambda e: e.tensor_tensor(out=drb[:, 1:32, :], in0=rbb[:, 1:32, :], in1=rbb[:, 0:31, :], op=ALU.subtract), r=[b_rbb], w=[b_drb])
        OP("dve", lambda e: e.tensor_tensor(out=drb[:, 0:1, :], in0=rbb[:, 0:1, :], in1=rbb[:, 31:32, :], op=ALU.subtract), r=[b_rbb], w=[b_drb])
        rb31x8 = sbt("rb31x8", [128, 8], F32, p1); b_rb31 = B("rb31")
        OP("dve", lambda e: e.tensor_scalar(out=rb31x8[:], in0=rbb[:, 31, :], scalar1=8.0, scalar2=None, op0=ALU.mult), r=[b_rbb], w=[b_rb31])
        disti = sbt("disti", [128, 128], I32, p1); b_disti = B("disti")
        distf = sbt("distf", [128, 128], F32, p1); b_distf = B("distf")
        gef = sbt("gef", [128, 128], F32, p1); b_gef = B("gef")
        TQ["on"] = True
        for delta in (0, 128):
            OP("pool", lambda e, delta=delta: e.iota(out=disti[:], pattern=[[1, 128]], base=delta, channel_multiplier=-1), w=[b_disti])
            OP("dve", lambda e: e.tensor_copy(out=distf[:], in_=disti[:]), r=[b_disti], w=[b_distf])
            for h in range(8):
                OP("dve", lambda e, h=h, delta=delta: e.tensor_scalar(out=Tt[delta][:, h, :], in0=distf[:], scalar1=0.0, scalar2=drb[:, 0, h:h + 1],
                                                                      op0=ALU.mult, op1=ALU.add), r=[b_distf, b_drb], w=[b_Tt[delta][h]])
            steps = [(float(thr[b - 1]), b) for b in range(1, 32)]
            for tv, b in steps:
                OP("dve", lambda e, tv=tv: e.tensor_scalar(out=gef[:], in0=distf[:], scalar1=tv, scalar2=None, op0=ALU.is_ge), r=[b_distf], w=[b_gef])
                for h in range(8):
                    OP("dve", lambda e, h=h, b=b, delta=delta: e.scalar_tensor_tensor(out=Tt[delta][:, h, :], in0=gef[:], scalar=drb[:, b, h:h + 1], in1=Tt[delta][:, h, :],
                                                                                     op0=ALU.mult, op1=ALU.add), r=[b_gef, b_drb, b_Tt[delta][h]], w=[b_Tt[delta][h]])
            if delta == 0:
                OP("dve", lambda e: e.tensor_scalar(out=gef[:], in0=distf[:], scalar1=0.0, scalar2=None, op0=ALU.is_lt), r=[b_distf], w=[b_gef])
                for h in range(8):
                    OP("dve", lambda e, h=h: e.scalar_tensor_tensor(out=Tt[0][:, h, :], in0=gef[:], scalar=NEG, in1=Tt[0][:, h, :],
                                                                    op0=ALU.mult, op1=ALU.add), r=[b_gef, b_Tt[0][h]], w=[b_Tt[0][h]])
        OP("dve", lambda e: e.tensor_copy(out=TS128[:], in_=Tt[128][:, :, 0:8]), r=b_Tt[128], w=[b_TS128])
        OP("dve", lambda e: e.tensor_copy(out=TS0[:], in_=Tt[0][0:8, :, 0:8]), r=b_Tt[0], w=[b_TS0])
        TQ["on"] = False
        tq_slice = (len(TQ["q"]) + 7) // 8
        negT = sbt("negT", [128, 128], F32, p1); b_negT = B("negT")
        OP("pool", lambda e: e.memset(negT[:], NEG), w=[b_negT])

        if stop_after == "c2":
            kb.final_wait("sp")
            kb.emit()
            p1.close()
            return nc, dbg_outs
        p1a = contextlib.ExitStack()
        wA = sbt("wA", [128, 8, 1536], BF16, p1a); b_wA = B("wA")
        for c in range(8):
            load_w(wA[:, c, :], w_in[c * 128:(c + 1) * 128, 0:1536], b_wA, 1536)
        xnT = [sbt(f"xnT{i}", [128, 8, 512], BF16, p1a) for i in range(1)]; b_xnT = [B(f"xnT{i}") for i in range(1)]
        kvst = [sbt(f"kvst{i}", [128, 1024], F32, p1a) for i in range(2)]; b_kvst = [B(f"kvst{i}") for i in range(2)]

        if stop_after == "s0":
            kb.final_wait("sp"); kb.emit(); p1a.close(); p1.close(); return nc, dbg_outs
        for kind in ("ctx", "main"):
            src = xc if kind == "ctx" else xm
            for g in range(4):
                xi = 0
                for t in range(4):
                    r0 = g * 512 + t * 128
                    norm_tile(src[r0:r0 + 128, :], 128, g1, b_g1, xnT[xi][:, :, t * 128:(t + 1) * 128], b_xnT[xi])
                if stop_after == "s1":
                    kb.final_wait("sp"); kb.emit(); p1a.close(); p1.close(); return nc, dbg_outs
                flush_tq(tq_slice)
                kg = g if kind == "ctx" else 4 + g
                import os
                for h in (range(8) if os.environ.get("SKIP_FM") is None else ()):
                    for which in (("k",) if kind == "ctx" else ("q", "k")):
                        col0 = (0 if which == "q" else 512) + h * 64
                        pf = (2 * h + (which == "k")) % 2
                        for c in range(8):
                            OP("pe", lambda e, c=c, pf=pf, col0=col0, xi=xi: e.matmul(PF[pf][0:64, :], lhsT=wA[:, c, col0:col0 + 64], rhs=xnT[xi][:, c, :],
                                                                                       start=(c == 0), stop=(c == 7)),
                               r=[b_wA, b_xnT[xi]], w=[bPF[pf]])
                        if os.environ.get("NO_EV") is not None:
                            pass
                        elif which == "q":
                            evac(qTa[h][0:64, g * 512:(g + 1) * 512], PF[pf][0:64, :], [bPF[pf]], [b_qTa[h][g]])
                        else:
                            evac(kTa[h][0:64, kg * 512:(kg + 1) * 512], PF[pf][0:64, :], [bPF[pf]], [b_kTa[h][kg]])
                if stop_after == "s2":
                    kb.final_wait("sp"); kb.emit(); p1a.close(); p1.close(); return nc, dbg_outs
                for t in (range(4) if os.environ.get("SKIP_TM") is None else ()):
                    ta = kg * 4 + t
                    si = ta % 2
                    for which in (("v",) if kind == "ctx" else ("k", "v")):
                        col0 = 512 if which == "k" else 1024
                        pf = 3 if which == "v" else 4
                        for c in range(8):
                            OP("pe", lambda e, c=c, pf=pf, col0=col0, xi=xi, t=t: e.matmul(PF[pf][:, :], lhsT=xnT[xi][:, c, t * 128:(t + 1) * 128], rhs=wA[:, c, col0:col0 + 512],
                                                                                          start=(c == 0), stop=(c == 7)),
                               r=[b_wA, b_xnT[xi]], w=[bPF[pf]])
                        if which == "v" and os.environ.get("NO_VA") is None:
                            OP("dve", lambda e, ta=ta, pf=pf: e.tensor_copy(out=Va[:, ta, :, 0:64], in_=PF[pf][:, :].rearrange("p (h d) -> p h d", d=64)),
                               r=[bPF[pf]], w=[b_Va[ta]])
                        if kind == "main" and os.environ.get("NO_KV") is None:
                            o0 = 0 if which == "k" else 512
                            OP("dve", lambda e, si=si, pf=pf, o0=o0: e.tensor_copy(out=kvst[si][:, o0:o0 + 512], in_=PF[pf][:, :]), r=[bPF[pf]], w=[b_kvst[si]])
                    if kind == "main":
                        r0 = g * 512 + t * 128
                        store(k_m[r0:r0 + 128, :], kvst[si][:, 0:512], b_kvst[si])
                        store(v_m[r0:r0 + 128, :], kvst[si][:, 512:1024], b_kvst[si])
                if stop_after == "s3" or (stop_after == "s4" and kind == "main") or (stop_after == "s5" and kind == "ctx" and g == 3) or (stop_after == "s6" and kind == "ctx" and g == 1):
                    kb.final_wait("sp"); kb.emit(); p1a.close(); p1.close(); return nc, dbg_outs
        if stop_after == "c3":
            kb.final_wait("sp")
            kb.emit()
            p1a.close()
            p1.close()
            return nc, dbg_outs
        flush_tq(10 ** 9)
        xi = 0
        norm_tile(xs[0:NS, :], NS, g1, b_g1, xnT[xi][:, :, 0:NS], b_xnT[xi])
        for which in ("q", "k"):
            for ch in range(4):
                col0 = (0 if which == "q" else 512) + ch * 128
                pf = ch % 2
                for c in range(8):
                    OP("pe", lambda e, c=c, pf=pf, col0=col0, xi=xi: e.matmul(PF[pf][:, 0:NS], lhsT=wA[:, c, col0:col0 + 128], rhs=xnT[xi][:, c, 0:NS],
                                                                               start=(c == 0), stop=(c == 7)), r=[b_wA, b_xnT[xi]], w=[bPF[pf]])
                dst = qTs if which == "q" else kTs_own
                bd = b_qTs if which == "q" else b_kTs_own
                evac(dst[:, ch, :], PF[pf][:, 0:NS], [bPF[pf]], [bd])
        for sbi in range(4):
            si = sbi % 2
            for which in ("k", "v"):
                col0 = 512 if which == "k" else 1024
                pf = 2 + (which == "v")
                for c in range(8):
                    OP("pe", lambda e, c=c, pf=pf, col0=col0, xi=xi, sbi=sbi: e.matmul(PF[pf][0:8, :], lhsT=xnT[xi][:, c, sbi * 8:(sbi + 1) * 8], rhs=wA[:, c, col0:col0 + 512],
                                                                                      start=(c == 0), stop=(c == 7)), r=[b_wA, b_xnT[xi]], w=[bPF[pf]])
                o0 = 0 if which == "k" else 512
                if which == "v":
                    OP("dve", lambda e, sbi=sbi, pf=pf: e.tensor_copy(out=Vs_own[:, sbi, :], in_=PF[pf][0:8, :]), r=[bPF[pf]], w=[b_Vs_own])
                OP("dve", lambda e, si=si, pf=pf, o0=o0: e.tensor_copy(out=kvst[si][0:8, o0:o0 + 512], in_=PF[pf][0:8, :]), r=[bPF[pf]], w=[b_kvst[si]])
            store(k_s[sbi * 8:(sbi + 1) * 8, :], kvst[si][0:8, 0:512], b_kvst[si])
            store(v_s[sbi * 8:(sbi + 1) * 8, :], kvst[si][0:8, 512:1024], b_kvst[si])
        kb.barrier()
        p1a.close()

        if stop_after == "p1":
            kb.final_wait("sp")
            kb.emit()
            p1.close()
            return nc, dbg_outs

        p1b = contextlib.ExitStack()
        kmf = sbt("kmf", [64, 8, 16], F32, p1b); b_kmf = B("kmf")
        kmT = sbt("kmT", [64, 8, 16], BF16, p1b); b_kmT = B("kmT")
        for h in range(8):
            OP("dve", lambda e, h=h: e.tensor_reduce(out=kmf[:, h, :], in_=kTa[h][0:64, :].rearrange("p (b k) -> p b k", k=256), axis=AX.X, op=ALU.add),
               r=b_kTa[h], w=[b_kmf])
        OP("dve", lambda e: e.tensor_copy(out=kmT[:], in_=kmf[:]), r=[b_kmf], w=[b_kmT])
        selm = sbt("selm", [128, 16, 16], F32, p1b); b_selm = B("selm")
        OP("dve", lambda e: e.tensor_scalar(out=selm[:], in0=candb[:, :, 0, :], scalar1=-1.0, scalar2=NEG, op0=ALU.mult, op1=ALU.add), r=[b_cand], w=[b_selm])
        selW = sbt("selW", [128, 4, 8, 81], F32, p1b); b_selW = [B(f"selW{j}") for j in range(4)]
        OP("pool", lambda e: e.memset(selW[:], 0.0), w=b_selW)
        for j in range(4):
            OP("dve", lambda e, j=j: e.tensor_copy(out=selW[:, j, :, 80:81], in_=rb31x8[:].rearrange("p (h o) -> p h o", o=1)), r=[b_rb31], w=[b_selW[j]])
        smk = sbt("smk", [128, 8, 16], F32, p1b); b_smk = B("smk")
        top8 = sbt("top8", [128, 8, 8], F32, p1b); b_top8 = B("top8")
        PTt = [sbt(f"PTt{i}", [128, 512], BF16, p1b) for i in range(2)]; b_PTt = [B(f"PTt{i}") for i in range(2)]
        tmpS = [sbt(f"tmpS{i}", [128, 512], F32, p1b) for i in range(2)]; b_tmpS = [B(f"tmpS{i}") for i in range(2)]
        ot = [sbt(f"ot{i}", [65, 512], F32, p1b) for i in range(2)]; b_ot = [B(f"ot{i}") for i in range(2)]
        rc4 = sbt("rc4", [128, 4, 1], F32, p1b); b_rc4 = B("rc4")
        attb = [sbt(f"attb{i}", [128, 4, 512], BF16, p1b) for i in range(2)]; b_attb = [B(f"attb{i}") for i in range(2)]
        b_mix = B("mix")
        sidx = 0
        for g in range(4):
            for j in range(4):
                qt = 4 * g + j
                for h in range(8):
                    OP("pe", lambda e, h=h, qt=qt: e.matmul(PF[4][:, h * 16:(h + 1) * 16], lhsT=qTa[h][0:64, qt * 128:(qt + 1) * 128], rhs=kmT[:, h, :],
                                                            start=True, stop=True), r=[b_qTa[h][g], b_kmT], w=[bPF[4]])
                OP("dve", lambda e, qt=qt: e.tensor_tensor(out=smk[:], in0=PF[4][:, 0:128].rearrange("p (h n) -> p h n", n=16),
                                                           in1=selm[:, qt:qt + 1, :].to_broadcast([128, 8, 16]), op=ALU.add), r=[bPF[4], b_selm], w=[b_smk])
                for h in range(8):
                    OP("dve", lambda e, h=h: e.max(out=top8[:, h, :], in_=smk[:, h, :]), r=[b_smk], w=[b_top8])
                for h in range(8):
                    OP("dve", lambda e, h=h, j=j, qt=qt: e.scalar_tensor_tensor(out=selW[:, j, h, 64:80], in0=smk[:, h, :], scalar=top8[:, h, 2:3], in1=candb[:, qt, 0, :],
                                                                               op0=ALU.is_lt, op1=ALU.mult), r=[b_smk, b_top8, b_cand], w=[b_selW[j]])
                OP("dve", lambda e, j=j, qt=qt: e.tensor_tensor(out=selW[:, j, :, 64:80], in0=selW[:, j, :, 64:80],
                                                                in1=candb[:, qt:qt + 1, 1, :].to_broadcast([128, 8, 16]), op=ALU.add), r=[b_cand, b_selW[j]], w=[b_selW[j]])
            for h in range(8):
                for j in range(4):
                    OP("pe", lambda e, h=h, j=j: e.transpose(out=PF[5][0:81, j * 128:(j + 1) * 128], in_=selW[:, j, h, :], identity=identf[:]),
                       r=[b_selW[j], b_idf], w=[bPF[5]])
                OP("act", lambda e, h=h, g=g: e.copy(out=qTa[h][64:81, g * 512:(g + 1) * 512], in_=PF[5][64:81, :]), r=[bPF[5]], w=[b_qTaP[h][g]])
            ab = g % 2
            for h in range(8):
                po = 2 + h % 2
                nk = 16 + 4 * g + 4

                def emit_S(kt, h=h, g=g):
                    si = kt % 2
                    OP("pe", lambda e, h=h, kt=kt, g=g, si=si: e.matmul(PF[si][:, :], lhsT=kTa[h][0:81, kt * 128:(kt + 1) * 128], rhs=qTa[h][0:81, g * 512:(g + 1) * 512],
                                                                         start=True, stop=True),
                       r=[b_kTa[h][kt // 4], b_kTaI[h], b_qTa[h][g], b_qTaP[h][g]], w=[bPF[si]])

                def emit_exp(kt, h=h, g=g):
                    si = kt % 2
                    rel = kt - (16 + 4 * g)
                    if rel < -1:
                        OP("act", lambda e, si=si: e.activation(out=PTt[si][:], in_=PF[si][:, :], func=AF.Exp, scale=0.125), r=[bPF[si]], w=[b_PTt[si]])
                    else:
                        for j in range(4):
                            d = j - rel
                            cs = slice(j * 128, (j + 1) * 128)
                            if d == 0:
                                Tm, bTm = Tt[0][:, h, :], b_Tt[0][h]
                            elif d == 1:
                                Tm, bTm = Tt[128][:, h, :], b_Tt[128][h]
                            elif d == -1 and j % 2 == 0:
                                Tm, bTm = negT[:], b_negT
                            else:
                                Tm = None
                            if Tm is not None:
                                OP("dve", lambda e, si=si, cs=cs, Tm=Tm: e.scalar_tensor_tensor(out=tmpS[si][:, cs], in0=PF[si][:, cs], scalar=0.125, in1=Tm,
                                                                                                 op0=ALU.mult, op1=ALU.add), r=[bPF[si], bTm], w=[b_tmpS[si]])
                            else:
                                OP("dve", lambda e, si=si, cs=cs: e.tensor_scalar(out=tmpS[si][:, cs], in0=PF[si][:, cs], scalar1=0.125, scalar2=None, op0=ALU.mult),
                                   r=[bPF[si]], w=[b_tmpS[si]])
                        OP("act", lambda e, si=si: e.activation(out=PTt[si][:], in_=tmpS[si][:], func=AF.Exp), r=[b_tmpS[si]], w=[b_PTt[si]])

                def emit_PV(kt, h=h, po=po, nk=nk):
                    si = kt % 2
                    OP("pe", lambda e, h=h, kt=kt, si=si, po=po, nk=nk: e.matmul(PF[po][0:65, :], lhsT=Va[:, kt, h, :], rhs=PTt[si][:], start=(kt == 0), stop=(kt == nk - 1)),
                       r=[b_Va[kt], b_Va1, b_PTt[si]], w=[bPF[po]])

                emit_S(0)
                for kt in range(nk):
                    if kt + 1 < nk:
                        emit_S(kt + 1)
                    emit_exp(kt)
                    emit_PV(kt)
                oi = h % 2
                OP("dve", lambda e, oi=oi, po=po: e.tensor_copy(out=ot[oi][:], in_=PF[po][0:65, :]), r=[bPF[po]], w=[b_ot[oi]])
                for j in range(4):
                    OP("pe", lambda e, oi=oi, j=j: e.transpose(out=PF[4][:, j * 128:j * 128 + 65], in_=ot[oi][0:65, j * 128:(j + 1) * 128], identity=identf[0:65, 0:65]),
                       r=[b_ot[oi], b_idf], w=[bPF[4]])
                pv = PF[4][:, :].rearrange("p (j c) -> p j c", c=128)
                OP("dve", lambda e, pv=pv: e.reciprocal(out=rc4[:], in_=pv[:, :, 64:65]), r=[bPF[4]], w=[b_rc4])
                OP("dve", lambda e, pv=pv, h=h, ab=ab: e.tensor_tensor(out=attb[ab][:, :, h * 64:(h + 1) * 64], in0=pv[:, :, 0:64], in1=rc4[:].to_broadcast([128, 4, 64]), op=ALU.mult),
                   r=[bPF[4], b_rc4], w=[b_attb[ab]])
            for j in range(4):
                r0 = g * 512 + j * 128
                kb.dma("sp", lambda e, ab=ab, j=j, r0=r0: e.dma_start(out=mix[r0:r0 + 128, 0:512], in_=attb[ab][:, j, :]), reads=[b_attb[ab]], writes=[b_mix])
        if dbg == "att":
            def dump(name, shape, dt, src, bufs):
                o = dout("d_" + name, shape, dt)
                bb = B("dmp_" + name)
                kb.dma("sp", lambda e: e.dma_start(out=o, in_=src), reads=bufs, writes=[bb])
            dump("qTa0", [81, NM], BF16, qTa[0][:, :], b_qTa[0] + b_qTaP[0])
            dump("kTa0", [81, 4096], BF16, kTa[0][:, :], b_kTa[0] + [b_kTaI[0]])
            dump("T0", [128, 128], F32, Tt[0][:, 0, :], [b_Tt[0][0]])
            dump("T128", [128, 128], F32, Tt[128][:, 0, :], [b_Tt[128][0]])
            dump("Va16", [128, 8 * 65], BF16, Va[:, 16, :, :].rearrange("p h d -> p (h d)"), [b_Va[16], b_Va1])
            dump("kmf", [64, 128], F32, kmf[:].rearrange("p h n -> p (h n)"), [b_kmf])
            dump("selW", [128, 4 * 8 * 81], F32, selW[:].rearrange("p a b c -> p (a b c)"), b_selW)
        kb.barrier()
        p1b.close()
        p1.close()
        if stop_after == "att":
            kb.final_wait("sp")
            kb.emit()
            return nc, dbg_outs

        ps_ = contextlib.ExitStack()
        ptb = sbt("ptb", [128, 256], I32, ps_); b_ptb = B("ptb")
        kb.dma("sp", lambda e: e.dma_start(out=ptb[:], in_=pt[0:1, :].partition_broadcast(128)), writes=[b_ptb])
        pio = sbt("pio", [128, 1], I32, ps_); b_pio = B("pio")
        OP("pool", lambda e: e.iota(out=pio[:], pattern=[[0, 1]], base=0, channel_multiplier=1), w=[b_pio])
        piof = sbt("piof", [128, 1], F32, ps_); b_piof = B("piof")
        OP("dve", lambda e: e.tensor_copy(out=piof[:], in_=pio[:]), r=[b_pio], w=[b_piof])
        ptf = sbt("ptf", [128, 256], F32, ps_); b_ptf = B("ptf")
        OP("dve", lambda e: e.tensor_copy(out=ptf[:], in_=ptb[:]), r=[b_ptb], w=[b_ptf])
        ridx = sbt("ridx", [128, 256], I32, ps_); b_ridx = B("ridx")
        OP("dve", lambda e: e.tensor_scalar(out=ridx[:], in0=ptf[:], scalar1=128.0, scalar2=piof[:, 0:1], op0=ALU.mult, op1=ALU.add), r=[b_ptf, b_piof], w=[b_ridx])
        onesb = sbt("onesb", [128, 1], BF16, ps_); b_onesb = B("onesb")
        OP("pool", lambda e: e.memset(onesb[:], 1.0), w=[b_onesb])
        ohS = sbt("ohS", [33, 33, 128], BF16, ps_); b_ohS = B("ohS")
        OP("pool", lambda e: e.memset(ohS[:], 1.0), w=[b_ohS])
        OP("pool", lambda e: e.affine_select(out=ohS[:], in_=ohS[:], pattern=[[1, 33], [0, 128]], compare_op=ALU.is_equal, fill=0.0, base=0, channel_multiplier=-1),
           r=[b_ohS], w=[b_ohS])
        rbcol = sbt("rbcol", [64, 1], F32, ps_); b_rbcol = B("rbcol")
        for h in range(8):
            kb.dma("sp", lambda e, h=h: e.dma_start(out=rbcol[h * 8:(h + 1) * 8, :], in_=rb[0:1, 248 + h:248 + h + 1].partition_broadcast(8)), writes=[b_rbcol])
        OP("dve", lambda e: e.tensor_scalar(out=rbcol[:], in0=rbcol[:], scalar1=8.0, scalar2=None, op0=ALU.mult), r=[b_rbcol], w=[b_rbcol])
        kTsp = sbt("kTsp", [128, 4, 8192], BF16, ps_); b_kTsp = B("kTsp")
        kpf = [sbt(f"kpf{i}", [128, 2, 512], F32, ps_) for i in range(4)]; b_kpf = [B(f"kpf{i}") for i in range(4)]
        kpb = [sbt(f"kpb{i}", [128, 2, 512], BF16, ps_) for i in range(4)]; b_kpb = [B(f"kpb{i}") for i in range(4)]
        kms = sbt("kms", [128, 4, 32], BF16, ps_); b_kms = B("kms")
        Qbd = sbt("Qbd", [128, 4, 64], BF16, ps_); b_Qbd = B("Qbd")
        scs = sbt("scs", [64, 32], F32, ps_); b_scs = B("scs")
        top8s = sbt("top8s", [64, 8], F32, ps_); b_top8s = B("top8s")
        penF = sbt("penF", [64, 33], F32, ps_); b_penF = B("penF")
        penTb = sbt("penTb", [33, 64], BF16, ps_); b_penTb = B("penTb")
        PTs = [sbt(f"PTs{i}", [128, 64], BF16, ps_) for i in range(2)]; b_PTs = [B(f"PTs{i}") for i in range(2)]
        tmS = sbt("tmS", [128, 64], F32, ps_); b_tmS = B("tmS")
        osb = sbt("osb", [64, 512], BF16, ps_); b_osb = B("osb")
        recs = sbt("recs", [64, 1], F32, ps_); b_recs = B("recs")
        b_mix3 = B("mix3")
        xcnt = 0
        for sbi in range(4):
            for pr in range(32):
                bi = pr % 4
                for a in range(2):
                    col = sbi * 64 + pr * 2 + a
                    kb.dma("pool", lambda e, bi=bi, a=a, col=col: e.indirect_dma_start(out=kpf[bi][:, a, :], out_offset=None, in_=ck[:, :],
                                                                                      in_offset=bass.IndirectOffsetOnAxis(ap=ridx[:, col:col + 1], axis=0)),
                           reads=[b_ridx], writes=[b_kpf[bi]])
                OP("dve", lambda e, bi=bi: e.tensor_copy(out=kpb[bi][:], in_=kpf[bi][:]), r=[b_kpf[bi]], w=[b_kpb[bi]])
                for a in range(2):
                    ti_ = xcnt % 2
                    xcnt += 1
                    pg = pr * 2 + a
                    for ch in range(4):
                        OP("pe", lambda e, ti_=ti_, bi=bi, a=a, ch=ch: e.transpose(out=PT[ti_][:, ch, :], in_=kpb[bi][:, a, ch * 128:(ch + 1) * 128], identity=ident[:]),
                           r=[b_kpb[bi], b_id], w=[bPT[ti_]])
                    OP("act", lambda e, ti_=ti_, pg=pg: e.copy(out=kTsp[:, :, pg * 128:(pg + 1) * 128], in_=PT[ti_][:, 0:4, :]), r=[bPT[ti_]], w=[b_kTsp])
                for ch in range(4):
                    for a in range(2):
                        OP("pe", lambda e, bi=bi, a=a, ch=ch, pr=pr: e.matmul(PF[5][:, ch * 32 + pr:ch * 32 + pr + 1], lhsT=kpb[bi][:, a, ch * 128:(ch + 1) * 128], rhs=onesb[:, 0:1],
                                                                             start=(a == 0), stop=(a == 1)), r=[b_kpb[bi], b_onesb], w=[bPF[5]])
            OP("dve", lambda e: e.tensor_copy(out=kms[:], in_=PF[5][:, 0:128].rearrange("p (c n) -> p c n", n=32)), r=[bPF[5]], w=[b_kms])
            OP("pool", lambda e: e.memset(Qbd[:], 0.0), w=[b_Qbd])
            for h in range(8):
                ch, hh = h // 2, h % 2
                OP("dve", lambda e, h=h, ch=ch, hh=hh, sbi=sbi: e.tensor_copy(out=Qbd[hh * 64:(hh + 1) * 64, ch, h * 8:(h + 1) * 8], in_=qTs[hh * 64:(hh + 1) * 64, ch, sbi * 8:(sbi + 1) * 8]),
                   r=[b_qTs], w=[b_Qbd])
            for ch in range(4):
                OP("pe", lambda e, ch=ch: e.matmul(PF[4][0:64, 0:32], lhsT=Qbd[:, ch, :], rhs=kms[:, ch, :], start=(ch == 0), stop=(ch == 3)), r=[b_Qbd, b_kms], w=[bPF[4]])
            OP("dve", lambda e: e.tensor_copy(out=scs[:], in_=PF[4][0:64, 0:32]), r=[bPF[4]], w=[b_scs])
            OP("dve", lambda e: e.max(out=top8s[:], in_=scs[:]), r=[b_scs], w=[b_top8s])
            OP("dve", lambda e: e.tensor_scalar(out=penF[:, 0:32], in0=scs[:], scalar1=top8s[:, 2:3], scalar2=NEG, op0=ALU.is_lt, op1=ALU.mult), r=[b_scs, b_top8s], w=[b_penF])
            OP("pool", lambda e: e.memset(penF[:, 32:33], 0.0), w=[b_penF])
            OP("dve", lambda e: e.tensor_scalar(out=penF[:], in0=penF[:], scalar1=rbcol[:, 0:1], scalar2=None, op0=ALU.add), r=[b_penF, b_rbcol], w=[b_penF])
            OP("pe", lambda e: e.transpose(out=PF[4][0:33, 64:128], in_=penF[:], identity=identf[0:64, 0:64]), r=[b_penF, b_idf], w=[bPF[4]])
            OP("act", lambda e: e.copy(out=penTb[:], in_=PF[4][0:33, 64:128]), r=[bPF[4]], w=[b_penTb])
            def s_load(kt, sbi=sbi):
                if kt >= 64:
                    return
                bi = kt % 4
                col = sbi * 64 + kt
                kb.dma("pool", lambda e, bi=bi, col=col: e.indirect_dma_start(out=kpf[bi][:, 0, :], out_offset=None, in_=cv[:, :],
                                                                              in_offset=bass.IndirectOffsetOnAxis(ap=ridx[:, col:col + 1], axis=0)),
                       reads=[b_ridx], writes=[b_kpf[bi]])
                OP("dve", lambda e, bi=bi: e.tensor_copy(out=kpb[bi][:, 0, :], in_=kpf[bi][:, 0, :]), r=[b_kpf[bi]], w=[b_kpb[bi]])

            def s_S(kt, sbi=sbi):
                own = kt == 64
                si = kt % 2
                L = 8 if own else 128
                n = 32 if own else kt // 2
                for ch in range(4):
                    lhs = kTs_own[:, ch, sbi * 8:(sbi + 1) * 8] if own else kTsp[:, ch, kt * 128:(kt + 1) * 128]
                    OP("pe", lambda e, si=si, ch=ch, lhs=lhs, L=L: e.matmul(PF[si][0:L, 0:64], lhsT=lhs, rhs=Qbd[:, ch, :], start=(ch == 0), stop=False),
                       r=[b_kTsp, b_kTs_own, b_Qbd], w=[bPF[si]])
                OP("pe", lambda e, si=si, n=n, L=L: e.matmul(PF[si][0:L, 0:64], lhsT=ohS[:, n, 0:L], rhs=penTb[:], start=False, stop=True), r=[b_ohS, b_penTb], w=[bPF[si]])

            def s_exp(kt):
                own = kt == 64
                si = kt % 2
                L = 8 if own else 128
                if kt >= 63:
                    Tm = TS0[:].rearrange("p h q -> p (h q)") if own else TS128[:].rearrange("p h q -> p (h q)")
                    OP("dve", lambda e, si=si, L=L, Tm=Tm: e.scalar_tensor_tensor(out=tmS[0:L, :], in0=PF[si][0:L, 0:64], scalar=0.125, in1=Tm, op0=ALU.mult, op1=ALU.add),
                       r=[bPF[si], b_TS0, b_TS128], w=[b_tmS])
                    OP("act", lambda e, si=si, L=L: e.activation(out=PTs[si][0:L, :], in_=tmS[0:L, :], func=AF.Exp), r=[b_tmS], w=[b_PTs[si]])
                else:
                    OP("act", lambda e, si=si: e.activation(out=PTs[si][:], in_=PF[si][:, 0:64], func=AF.Exp, scale=0.125), r=[bPF[si]], w=[b_PTs[si]])

            def s_PV(kt, sbi=sbi):
                own = kt == 64
                si = kt % 2
                L = 8 if own else 128
                rhsv = Vs_own[:, sbi, :] if own else kpb[kt % 4][:, 0, :]
                OP("pe", lambda e, si=si, L=L, rhsv=rhsv, kt=kt: e.matmul(PF[2][0:64, :], lhsT=PTs[si][0:L, :], rhs=rhsv, start=(kt == 0), stop=(kt == 64)),
                   r=[b_PTs[si], b_Vs_own, b_kpb[kt % 4]], w=[bPF[2]])
                OP("pe", lambda e, si=si, L=L, kt=kt: e.matmul(PF[3][0:64, 0:1], lhsT=PTs[si][0:L, :], rhs=onesb[0:L, 0:1], start=(kt == 0), stop=(kt == 64)),
                   r=[b_PTs[si], b_onesb], w=[bPF[3]])

            s_load(0); s_load(1); s_load(2)
            s_S(0)
            for kt in range(65):
                s_load(kt + 3)
                if kt + 1 < 65:
                    s_S(kt + 1)
                s_exp(kt)
                s_PV(kt)
            OP("dve", lambda e: e.reciprocal(out=recs[:], in_=PF[3][0:64, 0:1]), r=[bPF[3]], w=[b_recs])
            OP("dve", lambda e: e.tensor_scalar(out=osb[:], in0=PF[2][0:64, :], scalar1=recs[:, 0:1], scalar2=None, op0=ALU.mult), r=[bPF[2], b_recs], w=[b_osb])
            for h in range(8):
                kb.dma("sp", lambda e, h=h, sbi=sbi: e.dma_start(out=mix[NM + sbi * 8:NM + sbi * 8 + 8, h * 64:(h + 1) * 64], in_=osb[h * 8:(h + 1) * 8, h * 64:(h + 1) * 64]),
                       reads=[b_osb], writes=[b_mix3])
        kb.barrier()
        ps_.close()
        if stop_after == "satt":
            kb.final_wait("sp")
            kb.emit()
            return nc, dbg_outs

        b_mix2 = B("mix2")
        NT = 4096 + NS
        def ml_pass(hp):
            h0 = 2 * hp
            p2 = contextlib.ExitStack()
            mqT = sbt(f"h{hp}_" "mqT", [128, 2, NM + NS], BF16, p2); b_mqT = B("mqT")
            mkT = sbt(f"h{hp}_" "mkT", [128, 2, NM + NS], BF16, p2); b_mkT = B("mkT")
            mkt = sbt(f"h{hp}_" "mkt", [128, 32, 256], BF16, p2); b_mkt = B("mkt")
            mva = sbt(f"h{hp}_" "mva", [128, 32, 2, 129], BF16, p2); b_mva = B("mva")
            sgm = sbt(f"h{hp}_" "sgm", [128, 16, 256], BF16, p2); b_sgm = B("sgm")
            mkt_s = sbt(f"h{hp}_" "mkt_s", [8, 4, 256], BF16, p2); b_mkt_s = B("mkt_s")
            mva_s = sbt(f"h{hp}_" "mva_s", [8, 4, 2, 129], BF16, p2); b_mva_s = B("mva_s")
            sgm_s = sbt(f"h{hp}_" "sgm_s", [8, 4, 256], BF16, p2); b_sgm_s = B("sgm_s")
            OP("pool", lambda e: e.memset(mva[:, :, :, 128:129], 1.0), w=[b_mva])
            OP("pool", lambda e: e.memset(mva_s[:, :, :, 128:129], 1.0), w=[b_mva_s])
            Grow = sbt(f"h{hp}_" "Grow", [2, NT], F32, p2); b_Grow = B("Grow")
            Urow = sbt(f"h{hp}_" "Urow", [2, NT], F32, p2); b_Urow = B("Urow")
            Brow = sbt(f"h{hp}_" "Brow", [2, NT], F32, p2); b_Brow = B("Brow")
            mlgb = sbt(f"h{hp}_" "mlgb", [128, 256], F32, p2); b_mlgb = B("mlgb")
            kb.dma("sp", lambda e, h0=h0: e.dma_start(out=mlgb[:], in_=mlg[0:1, h0 * 128:h0 * 128 + 256].partition_broadcast(128)), writes=[b_mlgb])
            bgt = sbt(f"h{hp}_" "bgt", [2, 2], F32, p2); b_bgt = B("bgt")
            kb.dma("sp", lambda e, h0=h0: e.dma_start(out=bgt[:, 0:1], in_=bg[h0:h0 + 2, :]), writes=[b_bgt])
            kb.dma("sp", lambda e, h0=h0: e.dma_start(out=bgt[:, 1:2], in_=bg[4 + h0:4 + h0 + 2, :]), writes=[b_bgt])
            sm0 = sbt(f"h{hp}_" "sm0", [2, 4], F32, p2); b_sm0 = B("sm0")
            kb.dma("sp", lambda e, h0=h0: e.dma_start(out=sm0[:], in_=smi[h0:h0 + 2, :]), writes=[b_sm0])
            ones2 = sbt(f"h{hp}_" "ones2", [2, 512], F32, p2); b_ones2 = B("ones2")
            OP("pool", lambda e: e.memset(ones2[:], 1.0), w=[b_ones2])
            oh2 = sbt(f"h{hp}_" "oh2", [2, 2, 128], F32, p2); b_oh2 = B("oh2")
            OP("pool", lambda e: e.memset(oh2[:], 1.0), w=[b_oh2])
            OP("pool", lambda e: e.affine_select(out=oh2[:], in_=oh2[:], pattern=[[1, 2], [0, 128]], compare_op=ALU.is_equal, fill=0.0, base=0,
                                                 channel_multiplier=-1), r=[b_oh2], w=[b_oh2])
            cmask = sbt(f"h{hp}_" "cmask", [128, 128], F32, p2); b_cmask = B("cmask")
            OP("pool", lambda e: e.memset(cmask[:], 0.0), w=[b_cmask])
            OP("pool", lambda e: e.affine_select(out=cmask[:], in_=cmask[:], pattern=[[1, 128]], compare_op=ALU.is_ge, fill=-NEG, base=0,
                                                 channel_multiplier=-1), r=[b_cmask], w=[b_cmask])
            UT = sbt(f"h{hp}_" "UT", [128, 33, 2], F32, p2); GT = sbt(f"h{hp}_" "GT", [128, 33, 2], F32, p2); mT = sbt(f"h{hp}_" "mT", [128, 33, 2], F32, p2); EM = sbt(f"h{hp}_" "EM", [128, 33, 2], F32, p2)
            b_UT = B("UT"); b_GT = B("GT"); b_mT = B("mT"); b_EM = B("EM")
            UTs = sbt(f"h{hp}_" "UTs", [8, 4, 2], F32, p2); GTs = sbt(f"h{hp}_" "GTs", [8, 4, 2], F32, p2); mTs = sbt(f"h{hp}_" "mTs", [8, 4, 2], F32, p2); EMs = sbt(f"h{hp}_" "EMs", [8, 4, 2], F32, p2)
            b_UTs = B("UTs"); b_GTs = B("GTs"); b_mTs = B("mTs"); b_EMs = B("EMs")

            p2a = contextlib.ExitStack()
            wB = sbt(f"h{hp}_" "wB", [128, 8, 1032], BF16, p2a); b_wB = B("wB")
            for c in range(8):
                rows = slice(c * 128, (c + 1) * 128)
                for k4 in range(4):
                    kb.dma("pool", lambda e, c=c, k4=k4, rows=rows, h0=h0: e.dma_start(out=wB[:, c, k4 * 256:(k4 + 1) * 256],
                                                                                    in_=w_in[rows, 1536 + k4 * 512 + h0 * 128:1536 + k4 * 512 + h0 * 128 + 256]), writes=[b_wB])
                kb.dma("pool", lambda e, c=c, rows=rows: e.dma_start(out=wB[:, c, 1024:1032], in_=w_in[rows, 3584:3592]), writes=[b_wB])
            xnT2 = sbt(f"h{hp}_" "xnT2", [128, 8, 512], BF16, p2a); b_xnT2 = B("xnT2")
            gTs = sbt(f"h{hp}_" "gTs", [8, 512], F32, p2a); b_gTs = B("gTs")
            sgt = sbt(f"h{hp}_" "sgt", [128, 256], F32, p2a); b_sgt = B("sgt")
            KS = 128.0 ** -0.5
            groups = [("ctx", g) for g in range(4)] + [("main", g) for g in range(4)] + [("smp", 0)]
            for kind, g in groups:
                ntok = NS if kind == "smp" else 512
                if kind == "smp":
                    norm_tile(xs[0:NS, :], NS, g1, b_g1, xnT2[:, :, 0:NS], b_xnT2)
                else:
                    src = xc if kind == "ctx" else xm
                    for t in range(4):
                        r0 = g * 512 + t * 128
                        norm_tile(src[r0:r0 + 128, :], 128, g1, b_g1, xnT2[:, :, t * 128:(t + 1) * 128], b_xnT2)
                gcol0 = {"ctx": g * 512, "main": 2048 + g * 512, "smp": 4096}[kind]
                for c in range(8):
                    OP("pe", lambda e, c=c, ntok=ntok: e.matmul(PF[0][0:8, 0:ntok], lhsT=wB[:, c, 1024:1032], rhs=xnT2[:, c, 0:ntok], start=(c == 0), stop=(c == 7)),
                       r=[b_wB, b_xnT2], w=[bPF[0]])
                OP("dve", lambda e, ntok=ntok: e.tensor_copy(out=gTs[:, 0:ntok], in_=PF[0][0:8, 0:ntok]), r=[bPF[0]], w=[b_gTs])
                kb.dma("sp", lambda e, ntok=ntok, gcol0=gcol0, h0=h0: e.dma_start(out=Urow[:, gcol0:gcol0 + ntok], in_=gTs[h0:h0 + 2, 0:ntok]), reads=[b_gTs], writes=[b_Urow])
                kb.dma("sp", lambda e, ntok=ntok, gcol0=gcol0, h0=h0: e.dma_start(out=Brow[:, gcol0:gcol0 + ntok], in_=gTs[4 + h0:4 + h0 + 2, 0:ntok]), reads=[b_gTs], writes=[b_Brow])
                if kind != "ctx":
                    fcol0 = g * 512 if kind == "main" else NM
                    for l in range(2):
                        for which in ("q", "k"):
                            wc0 = (0 if which == "q" else 256) + l * 128
                            pf = 1 + (which == "k")
                            for c in range(8):
                                OP("pe", lambda e, c=c, pf=pf, wc0=wc0, ntok=ntok: e.matmul(PF[pf][:, 0:ntok], lhsT=wB[:, c, wc0:wc0 + 128], rhs=xnT2[:, c, 0:ntok],
                                                                                          start=(c == 0), stop=(c == 7)), r=[b_wB, b_xnT2], w=[bPF[pf]])
                            if which == "q":
                                evac(mqT[:, l, fcol0:fcol0 + ntok], PF[pf][:, 0:ntok], [bPF[pf]], [b_mqT])
                            else:
                                evac(mkT[:, l, fcol0:fcol0 + ntok], PF[pf][:, 0:ntok], [bPF[pf]], [b_mkT], scale=KS)
                if kind == "smp":
                    tiles = [(sbi * 8, 8, sbi) for sbi in range(4)]
                else:
                    tiles = [(t * 128, 128, (g if kind == "ctx" else 4 + g) * 4 + t) for t in range(4)]
                for (c0, n, ta) in tiles:
                    for c in range(8):
                        OP("pe", lambda e, c=c, c0=c0, n=n: e.matmul(PF[3][0:n, :], lhsT=xnT2[:, c, c0:c0 + n], rhs=wB[:, c, 256:768], start=(c == 0), stop=(c == 7)),
                           r=[b_wB, b_xnT2], w=[bPF[3]])
                    if kind == "smp":
                        kdst, b_kd = mkt_s[:, ta, :], b_mkt_s
                        vdst, b_vd = mva_s[:, ta, :, 0:128], b_mva_s
                    else:
                        kdst, b_kd = mkt[:, ta, :], b_mkt
                        vdst, b_vd = mva[:, ta, :, 0:128], b_mva
                    OP("dve", lambda e, n=n, kdst=kdst: e.tensor_scalar(out=kdst, in0=PF[3][0:n, 0:256], scalar1=KS, scalar2=None, op0=ALU.mult), r=[bPF[3]], w=[b_kd])
                    OP("dve", lambda e, n=n, vdst=vdst: e.tensor_copy(out=vdst, in_=PF[3][0:n, 256:512].rearrange("p (l d) -> p l d", d=128)), r=[bPF[3]], w=[b_vd])
                    if kind != "ctx":
                        for c in range(8):
                            OP("pe", lambda e, c=c, c0=c0, n=n: e.matmul(PF[4][0:n, 0:256], lhsT=xnT2[:, c, c0:c0 + n], rhs=wB[:, c, 768:1024], start=(c == 0), stop=(c == 7)),
                               r=[b_wB, b_xnT2], w=[bPF[4]])
                        OP("dve", lambda e, n=n: e.tensor_copy(out=sgt[0:n, :], in_=PF[4][0:n, 0:256]), r=[bPF[4]], w=[b_sgt])
                        OP("act", lambda e, n=n: e.activation(out=sgt[0:n, :], in_=sgt[0:n, :], func=AF.Sigmoid), r=[b_sgt], w=[b_sgt])
                        if kind == "smp":
                            sdst, b_sd = sgm_s[:, ta, :], b_sgm_s
                        else:
                            sdst, b_sd = sgm[:, ta - 16, :], b_sgm
                        OP("dve", lambda e, n=n, sdst=sdst: e.tensor_tensor(out=sdst, in0=sgt[0:n, :], in1=mlgb[0:n, :], op=ALU.mult), r=[b_sgt, b_mlgb], w=[b_sd])
            kb.barrier()
            p2a.close()
            if stop_after == "ml_a":
                kb.final_wait("sp"); kb.emit(); p2.close(); raise StopIteration

            nbf = sbt(f"h{hp}_" "nbf", [2, 1], F32, p2); b_nbf = B("nbf")
            OP("dve", lambda e: e.tensor_scalar(out=nbf[:], in0=bgt[:, 1:2], scalar1=-1.0, scalar2=None, op0=ALU.mult), r=[b_bgt], w=[b_nbf])
            OP("act", lambda e: e.activation(out=Brow[:], in_=Brow[:], func=AF.Exp, scale=-1.0, bias=nbf[:, 0:1]), r=[b_Brow, b_nbf], w=[b_Brow])
            OP("act", lambda e: e.activation(out=Brow[:], in_=Brow[:], func=AF.Ln, scale=1.0, bias=1.0), r=[b_Brow], w=[b_Brow])
            OP("dve", lambda e: e.tensor_scalar(out=Brow[:], in0=Brow[:], scalar1=-1.0, scalar2=None, op0=ALU.mult), r=[b_Brow], w=[b_Brow])
            OP("dve", lambda e: e.tensor_scalar(out=Urow[:], in0=Urow[:], scalar1=bgt[:, 0:1], scalar2=None, op0=ALU.add), r=[b_Urow, b_bgt], w=[b_Urow])
            OP("dve", lambda e: e.tensor_scalar(out=Brow[:, 0:2048], in0=Brow[:, 0:2048], scalar1=cfb[0:2, 0:1], scalar2=None, op0=ALU.mult), r=[b_Brow, b_cfb], w=[b_Brow])
            OP("dve", lambda e: e.tensor_scalar(out=Urow[:, 0:2048], in0=Urow[:, 0:2048], scalar1=cfb[0:2, 0:1], scalar2=cfb[0:2, 1:2], op0=ALU.mult, op1=ALU.add),
               r=[b_Urow, b_cfb], w=[b_Urow])
            segs = [(i * 512, 512, None if i == 0 else i * 512 - 1) for i in range(8)] + [(4096 + sbi * 8, 8, None) for sbi in range(4)]
            for (c0, n, prev) in segs:
                init = 0.0 if prev is None else Brow[:, prev:prev + 1]
                OP("dve", lambda e, c0=c0, n=n, init=init: e.tensor_tensor_scan(out=Brow[:, c0:c0 + n], data0=ones2[:, 0:n], data1=Brow[:, c0:c0 + n], initial=init,
                                                                                op0=ALU.mult, op1=ALU.add), r=[b_Brow, b_ones2], w=[b_Brow])
            OP("dve", lambda e: e.tensor_tensor(out=Urow[:], in0=Urow[:], in1=Brow[:], op=ALU.subtract), r=[b_Urow, b_Brow], w=[b_Urow])
            for si_, (c0, n, prev) in enumerate(segs):
                if c0 >= 4096:
                    sbi = (c0 - 4096) // 8
                    init = sm0[:, sbi:sbi + 1]
                else:
                    init = 0.0 if prev is None else Grow[:, prev:prev + 1]
                OP("dve", lambda e, c0=c0, n=n, init=init: e.tensor_tensor_scan(out=Grow[:, c0:c0 + n], data0=Urow[:, c0:c0 + n], data1=Urow[:, c0:c0 + n], initial=init,
                                                                                op0=ALU.max, op1=ALU.max), r=[b_Urow, b_Grow, b_sm0], w=[b_Grow])
            OP("dve", lambda e: e.tensor_tensor(out=Brow[:], in0=Brow[:], in1=Grow[:], op=ALU.add), r=[b_Grow, b_Brow], w=[b_Brow])
            for (row, b_row, colt, b_colt, colts, b_colts) in ((Urow, b_Urow, UT, b_UT, UTs, b_UTs), (Grow, b_Grow, GT, b_GT, GTs, b_GTs), (Brow, b_Brow, mT, b_mT, mTs, b_mTs)):
                for ck in range(32):
                    OP("pe", lambda e, ck=ck, row=row: e.transpose(out=PF[5][:, ck * 2:ck * 2 + 2], in_=row[:, ck * 128:(ck + 1) * 128], identity=identf[0:2, 0:2]),
                       r=[b_row, b_idf], w=[bPF[5]])
                OP("dve", lambda e, colt=colt: e.tensor_copy(out=colt[:, 0:32, :], in_=PF[5][:, 0:64].rearrange("p (c l) -> p c l", l=2)), r=[bPF[5]], w=[b_colt])
                for sbi in range(4):
                    OP("pe", lambda e, sbi=sbi, row=row: e.transpose(out=PF[5][0:8, sbi * 2:sbi * 2 + 2], in_=row[:, 4096 + sbi * 8:4096 + sbi * 8 + 8], identity=identf[0:2, 0:2]),
                       r=[b_row, b_idf], w=[bPF[5]])
                OP("dve", lambda e, colts=colts: e.tensor_copy(out=colts[:], in_=PF[5][0:8, 0:8].rearrange("p (c l) -> p c l", l=2)), r=[bPF[5]], w=[b_colts])
            OP("act", lambda e: e.activation(out=EM[:, 0:32, :], in_=mT[:, 0:32, :], func=AF.Exp, scale=-1.0), r=[b_mT], w=[b_EM])
            OP("act", lambda e: e.activation(out=EMs[:], in_=mTs[:], func=AF.Exp, scale=-1.0), r=[b_mTs], w=[b_EMs])

            if stop_after == "ml_b":
                kb.final_wait("sp"); kb.emit(); p2.close(); raise StopIteration
            Cf = sbt(f"h{hp}_" "Cf", [128, 2, 129], F32, p2); b_Cf = B("Cf")
            Cb = sbt(f"h{hp}_" "Cb", [128, 2, 129], BF16, p2); b_Cb = B("Cb")
            gprev = sbt(f"h{hp}_" "gprev", [128, 2], F32, p2); b_gprev = B("gprev")
            gend = [sbt(f"h{hp}_" f"gend{i}", [128, 2], F32, p2) for i in range(2)]; b_gend = [B(f"gend{i}") for i in range(2)]
            g2 = sbt(f"h{hp}_" "g2", [128, 2], F32, p2); b_g2 = B("g2")
            gtok = sbt(f"h{hp}_" "gtok", [128, 2], F32, p2); b_gtok = B("gtok")
            gst = sbt(f"h{hp}_" "gst", [128, 2], F32, p2); b_gst = B("gst")
            wst = sbt(f"h{hp}_" "wst", [128, 2], F32, p2); b_wst = B("wst")
            tmpD = sbt(f"h{hp}_" "tmpD", [128, 2, 128], F32, p2); b_tmpD = B("tmpD")
            sT = sbt(f"h{hp}_" "sT", [128, 2, 128], BF16, p2); b_sT = B("sT")
            hs = [sbt(f"h{hp}_" f"hs{i}", [128, 2, 129], F32, p2) for i in range(2)]; b_hs = [B(f"hs{i}") for i in range(2)]
            nd = sbt(f"h{hp}_" "nd", [128, 2, 129], F32, p2); b_nd = B("nd")
            dab = sbt(f"h{hp}_" "dab", [128, 2], F32, p2); b_dab = B("dab")
            ssh = sbt(f"h{hp}_" "ssh", [128, 2], F32, p2); b_ssh = B("ssh")
            junk = sbt(f"h{hp}_" "junk", [128, 128], F32, p2); b_junk = B("junk")
            gv = sbt(f"h{hp}_" "gv", [128, 2, 129], BF16, p2); b_gv = B("gv")
            mlb = [sbt(f"h{hp}_" f"mlb{i}", [128, 256], BF16, p2) for i in range(2)]; b_mlb = [B(f"mlb{i}") for i in range(2)]

            def chunkA(L, ck_cols, UTv, GTv, EMv, kT_v, qT_v, kt_v, va_v, sg_v, full, mix_rows, mi, gcol):
                for l in range(2):
                    OP("pe", lambda e, l=l: e.matmul(PF[4][0:L, l * 128:l * 128 + L], lhsT=oh2[:, l, 0:L], rhs=Grow[:, ck_cols], start=True, stop=True),
                       r=[b_oh2, b_Grow], w=[bPF[4]])
                OP("dve", lambda e: e.tensor_copy(out=gend[mi][0:L, :].rearrange("p (l o) -> p l o", o=1), in_=PF[4][0:L, 0:256].rearrange("p (l t) -> p l t", t=128)[:, :, L - 1:L]),
                   r=[bPF[4]], w=[b_gend[mi]])
                if not full:
                    return
                for l in range(2):
                    OP("pe", lambda e, l=l: e.matmul(PF[0][0:L, l * 128:l * 128 + L], lhsT=kT_v(l), rhs=qT_v(l), start=True, stop=True), r=[b_mkT, b_mqT], w=[bPF[0]])
                    OP("dve", lambda e, l=l: e.scalar_tensor_tensor(out=tmpD[0:L, l, 0:L], in0=PF[4][0:L, l * 128:l * 128 + L], scalar=UTv[:, l:l + 1], in1=cmask[0:L, 0:L],
                                                                    op0=ALU.subtract, op1=ALU.add), r=[bPF[4], b_UT, b_UTs, b_cmask], w=[b_tmpD])
                OP("act", lambda e: e.activation(out=tmpD[0:L, :, 0:L], in_=tmpD[0:L, :, 0:L], func=AF.Exp, scale=-1.0), r=[b_tmpD], w=[b_tmpD])
                OP("dve", lambda e: e.tensor_tensor(out=sT[0:L, :, 0:L], in0=PF[0][0:L, 0:256].rearrange("p (l t) -> p l t", t=128)[:, :, 0:L], in1=tmpD[0:L, :, 0:L], op=ALU.mult),
                   r=[bPF[0], b_tmpD], w=[b_sT])
                for l in range(2):
                    OP("pe", lambda e, l=l: e.matmul(PF[1][0:L, l * 129:(l + 1) * 129], lhsT=sT[0:L, l, 0:L], rhs=va_v(l), start=True, stop=True),
                       r=[b_sT, b_mva, b_mva_s], w=[bPF[1]])
                OP("dve", lambda e: e.tensor_copy(out=hs[mi][0:L, :, :], in_=PF[1][0:L, 0:258].rearrange("p (l c) -> p l c", c=129)), r=[bPF[1]], w=[b_hs[mi]])

            def chunkB(L, ck_cols, UTv, GTv, EMv, kT_v, qT_v, kt_v, va_v, sg_v, full, mix_rows, mi, gcol):
                if not full:
                    return
                for l in range(2):
                    OP("pe", lambda e, l=l: e.matmul(PF[2][0:L, l * 129:(l + 1) * 129], lhsT=qT_v(l), rhs=Cb[:, l, :], start=True, stop=True),
                       r=[b_mqT, b_Cb], w=[bPF[2]])
                OP("dve", lambda e: e.tensor_tensor(out=g2[0:L, :], in0=gprev[0:L, :], in1=GTv, op=ALU.subtract), r=[b_gprev, b_GT, b_GTs], w=[b_g2])
                OP("act", lambda e: e.activation(out=wst[0:L, :], in_=g2[0:L, :], func=AF.Exp), r=[b_g2], w=[b_wst])
                for l in range(2):
                    OP("dve", lambda e, l=l: e.scalar_tensor_tensor(out=nd[0:L, l, :], in0=PF[2][0:L, l * 129:(l + 1) * 129], scalar=wst[0:L, l:l + 1], in1=hs[mi][0:L, l, :],
                                                                    op0=ALU.mult, op1=ALU.add), r=[bPF[2], b_wst, b_hs[mi]], w=[b_nd])
                    OP("dve", lambda e, l=l: e.scalar_tensor_tensor(out=dab[0:L, l:l + 1], in0=nd[0:L, l, 128:129], scalar=-1.0, in1=nd[0:L, l, 128:129], op0=ALU.mult, op1=ALU.max),
                       r=[b_nd], w=[b_dab])
                    OP("dve", lambda e, l=l: e.tensor_scalar(out=dab[0:L, l:l + 1], in0=dab[0:L, l:l + 1], scalar1=EMv[:, l:l + 1], scalar2=None, op0=ALU.max),
                       r=[b_dab, b_EM, b_EMs], w=[b_dab])
                OP("dve", lambda e: e.reciprocal(out=dab[0:L, :], in_=dab[0:L, :]), r=[b_dab], w=[b_dab])
                for l in range(2):
                    OP("act", lambda e, l=l: e.activation(out=junk[0:L, :], in_=nd[0:L, l, 0:128], func=AF.Square, scale=dab[0:L, l:l + 1], accum_out=ssh[0:L, l:l + 1]),
                       r=[b_nd, b_dab], w=[b_junk, b_ssh])
                OP("act", lambda e: e.activation(out=ssh[0:L, :], in_=ssh[0:L, :], func=AF.Sqrt, scale=1.0 / 128, bias=EPS), r=[b_ssh], w=[b_ssh])
                OP("dve", lambda e: e.reciprocal(out=ssh[0:L, :], in_=ssh[0:L, :]), r=[b_ssh], w=[b_ssh])
                OP("dve", lambda e: e.tensor_tensor(out=ssh[0:L, :], in0=ssh[0:L, :], in1=dab[0:L, :], op=ALU.mult), r=[b_ssh, b_dab], w=[b_ssh])
                for l in range(2):
                    OP("dve", lambda e, l=l: e.scalar_tensor_tensor(out=mlb[mi][0:L, l * 128:(l + 1) * 128], in0=nd[0:L, l, 0:128], scalar=ssh[0:L, l:l + 1], in1=sg_v(l),
                                                                    op0=ALU.mult, op1=ALU.mult), r=[b_nd, b_ssh, b_sgm, b_sgm_s], w=[b_mlb[mi]])
                kb.dma("sp", lambda e: e.dma_start(out=mix[mix_rows, 512 + h0 * 128:512 + h0 * 128 + 256], in_=mlb[mi][0:L, :]), reads=[b_mlb[mi]], writes=[b_mix2])

            def chunkC(L, ck_cols, UTv, GTv, EMv, kT_v, qT_v, kt_v, va_v, sg_v, full, mix_rows, mi, gcol):
                gend_bcast(gcol)
                OP("dve", lambda e: e.tensor_tensor(out=gtok[0:L, :], in0=UTv, in1=gend[mi][0:L, :], op=ALU.subtract), r=[b_UT, b_UTs, b_gend[mi]], w=[b_gtok])
                OP("act", lambda e: e.activation(out=gtok[0:L, :], in_=gtok[0:L, :], func=AF.Exp), r=[b_gtok], w=[b_gtok])
                for l in range(2):
                    OP("dve", lambda e, l=l: e.tensor_scalar(out=gv[0:L, l, :], in0=va_v(l), scalar1=gtok[0:L, l:l + 1], scalar2=None, op0=ALU.mult),
                       r=[b_mva, b_mva_s, b_gtok], w=[b_gv])
                    OP("pe", lambda e, l=l: e.matmul(PF[3][:, l * 129:(l + 1) * 129], lhsT=kt_v(l), rhs=gv[0:L, l, :], start=True, stop=True), r=[b_mkt, b_mkt_s, b_gv], w=[bPF[3]])
                OP("dve", lambda e: e.tensor_tensor(out=g2[:, :], in0=gprev[:, :], in1=gendb[:, :], op=ALU.subtract), r=[b_gprev, b_gendb], w=[b_g2])
                OP("act", lambda e: e.activation(out=gst[:, :], in_=g2[:, :], func=AF.Exp), r=[b_g2], w=[b_gst])
                for l in range(2):
                    OP("dve", lambda e, l=l: e.scalar_tensor_tensor(out=Cf[:, l, :], in0=Cf[:, l, :], scalar=gst[:, l:l + 1], in1=PF[3][:, l * 129:(l + 1) * 129],
                                                                    op0=ALU.mult, op1=ALU.add), r=[b_Cf, b_gst, bPF[3]], w=[b_Cf])
                OP("act", lambda e: e.copy(out=Cb[:], in_=Cf[:]), r=[b_Cf], w=[b_Cb])
                OP("dve", lambda e: e.tensor_copy(out=gprev[:], in_=gendb[:]), r=[b_gendb], w=[b_gprev])

            gendb = sbt(f"h{hp}_" "gendb", [128, 2], F32, p2); b_gendb = B("gendb")

            def gend_bcast(col):
                for l in range(2):
                    OP("pe", lambda e, l=l: e.matmul(PF[5][:, l:l + 1], lhsT=oh2[:, l, :], rhs=Grow[:, col:col + 1], start=True, stop=True), r=[b_oh2, b_Grow], w=[bPF[5]])
                OP("dve", lambda e: e.tensor_copy(out=gendb[:], in_=PF[5][:, 0:2]), r=[bPF[5]], w=[b_gendb])

            OP("pool", lambda e: e.memset(Cf[:], 0.0), w=[b_Cf])
            OP("pool", lambda e: e.memset(Cb[:], 0.0), w=[b_Cb])
            OP("pool", lambda e: e.memset(gprev[:], 0.0), w=[b_gprev])
            def pargs(ck):
                full = ck >= 16
                tm = ck - 16
                cols = slice(ck * 128, (ck + 1) * 128)
                fc = slice(tm * 128, (tm + 1) * 128)
                return (128, cols, UT[:, ck, :], GT[:, ck, :], EM[:, ck, :],
                        lambda l, fc=fc: mkT[:, l, fc], lambda l, fc=fc: mqT[:, l, fc], lambda l, ck=ck: mkt[:, ck, l * 128:(l + 1) * 128],
                        lambda l, ck=ck: mva[:, ck, l, :], lambda l, tm=tm: sgm[:, tm, l * 128:(l + 1) * 128], full,
                        slice(tm * 128, (tm + 1) * 128), ck % 2, ck * 128 + 127)
            chunkA(*pargs(0))
            for ck in range(32):
                if ck + 1 < 32:
                    chunkA(*pargs(ck + 1))
                chunkB(*pargs(ck))
                chunkC(*pargs(ck))
            if stop_after == "ml_c":
                kb.final_wait("sp"); kb.emit(); p2.close(); raise StopIteration
            for l in range(2):
                h = h0 + l
                store(C_p[h * 128:(h + 1) * 128, :], Cf[:, l, 0:128], b_Cf)
                store(n_p[h * 128:(h + 1) * 128, :], Cf[:, l, 128:129], b_Cf)
            store(m_p[h0:h0 + 2, :], Brow[:, 4095:4096], b_Brow)
            for sbi in range(4):
                for l in range(2):
                    h = h0 + l
                    r0 = (sbi * 4 + h) * 128
                    kb.dma("sp", lambda e, l=l, r0=r0: e.dma_start(out=Cf[:, l, 0:128], in_=sC[r0:r0 + 128, :]), writes=[b_Cf])
                    kb.dma("sp", lambda e, l=l, r0=r0: e.dma_start(out=Cf[:, l, 128:129], in_=sn[r0:r0 + 128, :]), writes=[b_Cf])
                    kb.dma("sp", lambda e, l=l, h=h, sbi=sbi: e.dma_start(out=gprev[:, l:l + 1], in_=smi[h:h + 1, sbi:sbi + 1].partition_broadcast(128)), writes=[b_gprev])
                OP("act", lambda e: e.copy(out=Cb[:], in_=Cf[:]), r=[b_Cf], w=[b_Cb])
                c0 = 4096 + sbi * 8
                fc = slice(NM + sbi * 8, NM + sbi * 8 + 8)
                sargs = (8, slice(c0, c0 + 8), UTs[:, sbi, :], GTs[:, sbi, :], EMs[:, sbi, :],
                         lambda l, fc=fc: mkT[:, l, fc], lambda l, fc=fc: mqT[:, l, fc], lambda l, sbi=sbi: mkt_s[:, sbi, l * 128:(l + 1) * 128],
                         lambda l, sbi=sbi: mva_s[:, sbi, l, :], lambda l, sbi=sbi: sgm_s[:, sbi, l * 128:(l + 1) * 128], True,
                         slice(NM + sbi * 8, NM + sbi * 8 + 8), sbi % 2, c0 + 7)
                chunkA(*sargs)
                chunkB(*sargs)
                chunkC(*sargs)
                for l in range(2):
                    h = h0 + l
                    r0 = (sbi * 4 + h) * 128
                    store(C_s[r0:r0 + 128, :], Cf[:, l, 0:128], b_Cf)
                    store(n_s[r0:r0 + 128, :], Cf[:, l, 128:129], b_Cf)
                store(m_s[sbi * 4 + h0:sbi * 4 + h0 + 2, :], Brow[:, c0 + 7:c0 + 8], b_Brow)
            kb.barrier()
            p2.close()
        try:
            for hp in range(2):
                ml_pass(hp)
        except StopIteration:
            return nc, dbg_outs
        if stop_after == "ml":
            kb.final_wait("sp")
            kb.emit()
            return nc, dbg_outs

        p3 = contextlib.ExitStack()
        wO = sbt("wO", [128, 8, 1024], BF16, p3); b_wO = B("wO")
        wU = sbt("wU", [128, 8, 4096], BF16, p3); b_wU = B("wU")
        wD = sbt("wD", [128, 32, 1024], BF16, p3); b_wD = B("wD")
        for c in range(8):
            load_w(wO[:, c, :], w_out[c * 128:(c + 1) * 128, :], b_wO, 1024)
        for c in range(8):
            load_w(wU[:, c, :], w_up[c * 128:(c + 1) * 128, :], b_wU, 4096)
        for f in range(32):
            load_w(wD[:, f, :], w_down[f * 128:(f + 1) * 128, :], b_wD, 1024)
        kb.dma("sp", lambda e: e.dma_start(out=g1[:], in_=nffn[0:1, :].partition_broadcast(128)), writes=[b_g1])
        g3 = sbt("g3", [128, D], F32, p3); b_g3 = B("g3")
        kb.dma("sp", lambda e: e.dma_start(out=g3[:], in_=nfin[0:1, :].partition_broadcast(128)), writes=[b_g3])
        mixb = [sbt(f"mixb{i}", [128, D], BF16, p3) for i in range(2)]; b_mixb = [B(f"mixb{i}") for i in range(2)]
        mixT = sbt("mixT", [128, 8, 256], BF16, p3); b_mixT = B("mixT")
        xn2T = sbt("xn2T", [128, 8, 256], BF16, p3); b_xn2T = B("xn2T")
        uT = sbt("uT", [128, 32, 256], BF16, p3); b_uT = B("uT")
        ur = [sbt(f"ur{i}", [128, 256], F32, p3) for i in range(2)]; b_ur = [B(f"ur{i}") for i in range(2)]
        yo = sbt("yo", [128, D], F32, p3); b_yo = B("yo")
        all_mix = [b_mix, b_mix2, b_mix3]
        fgroups = [("main", gi) for gi in range(8)] + [("smp", 0)]
        for kind, gi in fgroups:
            if kind == "main":
                tiles = [(gi * 256 + t * 128, 128, t * 128) for t in range(2)]
                xsrc, ydst = xm, y_m
            else:
                tiles = [(0, NS, 0)]
                xsrc, ydst = xs, y_s
            ntok = sum(n for _, n, _ in tiles)
            for ti, (r0, n, c0) in enumerate(tiles):
                mr0 = r0 if kind == "main" else NM
                kb.dma("sp", lambda e, ti=ti, r0=r0, n=n, xsrc=xsrc: e.dma_start(out=xt[ti][0:n, :], in_=xsrc[r0:r0 + n, :]), writes=[b_xt[ti]])
                kb.dma("sp", lambda e, ti=ti, mr0=mr0, n=n: e.dma_start(out=mixb[ti][0:n, :], in_=mix[mr0:mr0 + n, :]), reads=all_mix, writes=[b_mixb[ti]])
                to_featmajor(mixb[ti], b_mixb[ti], n, mixT[:, :, c0:c0 + n], b_mixT)
            for ti, (r0, n, c0) in enumerate(tiles):
                for hf in range(2):
                    for c in range(8):
                        OP("pe", lambda e, c=c, hf=hf, n=n, c0=c0: e.matmul(PF[hf][0:n, :], lhsT=mixT[:, c, c0:c0 + n], rhs=wO[:, c, hf * 512:(hf + 1) * 512], start=(c == 0), stop=(c == 7)),
                           r=[b_mixT, b_wO], w=[bPF[hf]])
                    OP("dve", lambda e, ti=ti, hf=hf, n=n: e.tensor_tensor(out=xt[ti][0:n, hf * 512:(hf + 1) * 512], in0=PF[hf][0:n, :], in1=xt[ti][0:n, hf * 512:(hf + 1) * 512], op=ALU.add),
                       r=[bPF[hf], b_xt[ti]], w=[b_xt[ti]])
                rms_scale(xt[ti][0:n, :], b_xt[ti], n, ti, xnb[ti][0:n, :], b_xnb[ti])
                OP("dve", lambda e, ti=ti, n=n: e.scalar_tensor_tensor(out=xnb[ti][0:n, :], in0=xt[ti][0:n, :], scalar=rstd[ti][0:n, 0:1], in1=g1[0:n, :], op0=ALU.mult, op1=ALU.mult),
                   r=[b_xt[ti], b_rstd[ti], b_g1], w=[b_xnb[ti]])
                to_featmajor(xnb[ti], b_xnb[ti], n, xn2T[:, :, c0:c0 + n], b_xn2T)
            for f in range(32):
                pf = 2 + f % 2
                ui = f % 2
                for c in range(8):
                    OP("pe", lambda e, c=c, f=f, pf=pf, ntok=ntok: e.matmul(PF[pf][:, 0:ntok], lhsT=wU[:, c, f * 128:(f + 1) * 128], rhs=xn2T[:, c, 0:ntok], start=(c == 0), stop=(c == 7)),
                       r=[b_wU, b_xn2T], w=[bPF[pf]])
                OP("dve", lambda e, pf=pf, ui=ui, ntok=ntok: e.tensor_scalar(out=ur[ui][:, 0:ntok], in0=PF[pf][:, 0:ntok], scalar1=0.0, scalar2=None, op0=ALU.max), r=[bPF[pf]], w=[b_ur[ui]])
                OP("act", lambda e, f=f, ui=ui, ntok=ntok: e.activation(out=uT[:, f, 0:ntok], in_=ur[ui][:, 0:ntok], func=AF.Square), r=[b_ur[ui]], w=[b_uT])
            for ti, (r0, n, c0) in enumerate(tiles):
                for hf in range(2):
                    for f in range(32):
                        OP("pe", lambda e, f=f, hf=hf, n=n, c0=c0: e.matmul(PF[hf][0:n, :], lhsT=uT[:, f, c0:c0 + n], rhs=wD[:, f, hf * 512:(hf + 1) * 512], start=(f == 0), stop=(f == 31)),
                           r=[b_uT, b_wD], w=[bPF[hf]])
                    OP("dve", lambda e, ti=ti, hf=hf, n=n: e.tensor_tensor(out=xt[ti][0:n, hf * 512:(hf + 1) * 512], in0=PF[hf][0:n, :], in1=xt[ti][0:n, hf * 512:(hf + 1) * 512], op=ALU.add),
                       r=[bPF[hf], b_xt[ti]], w=[b_xt[ti]])
                rms_scale(xt[ti][0:n, :], b_xt[ti], n, ti, xnb[ti][0:n, :], b_xnb[ti])
                OP("dve", lambda e, ti=ti, n=n: e.scalar_tensor_tensor(out=yo[0:n, :], in0=xt[ti][0:n, :], scalar=rstd[ti][0:n, 0:1], in1=g3[0:n, :], op0=ALU.mult, op1=ALU.mult),
                   r=[b_xt[ti], b_rstd[ti], b_g3], w=[b_yo])
                store(ydst[r0:r0 + n, :], yo[0:n, :], b_yo)
        kb.barrier()
        p3.close()
        kb.final_wait("sp")
        kb.emit()
        return nc, dbg_outs


def make_in_maps(inp):
    f = lambda a: np.ascontiguousarray(a, dtype=np.float32)
    xp = np.asarray(inp["x_prompt"]); xsamp = np.asarray(inp["x_sample"])
    ckf = f(np.asarray(inp["cache_k"]).reshape(2560 * 128, 512))
    cvf = f(np.asarray(inp["cache_v"]).reshape(2560 * 128, 512))
    ptab = np.asarray(inp["page_table"]).astype(np.int32)
    sCf = np.asarray(inp["state_C"])[0]; snf = np.asarray(inp["state_n"])[0]; smf = np.asarray(inp["state_m"])[0]
    shared = {
        "w_in": f(inp["w_in"][0]), "w_out": f(inp["w_out"][0]), "w_up": f(inp["w_up"][0]), "w_down": f(inp["w_down"][0]),
        "nmix": f(inp["norm_mix"]).reshape(1, D), "nffn": f(inp["norm_ffn"]).reshape(1, D), "nfin": f(inp["norm_final"]).reshape(1, D),
        "mlg": f(inp["ml_norm"]).reshape(1, 512),
        "bg": f(np.concatenate([np.asarray(inp["b_ig"]).reshape(-1), np.asarray(inp["b_fg"]).reshape(-1)])).reshape(8, 1),
        "rb": f(inp["rel_bias"]).reshape(1, 256), "ck": ckf, "cv": cvf,
    }
    maps = []
    for c in range(8):
        b, half = c // 2, c % 2
        m = dict(shared)
        m["xm"] = f(xp[b, half * NM:(half + 1) * NM])
        m["xc"] = f(xp[b, 0:NM]) if half else np.zeros((NM, D), np.float32)
        m["xs"] = f(xsamp[4 * c:4 * c + 4].reshape(NS, D))
        m["pt"] = np.ascontiguousarray(ptab[4 * c:4 * c + 4].reshape(1, 256))
        m["sC"] = f(sCf[4 * c:4 * c + 4].reshape(16 * 128, 128))
        m["sn"] = f(snf[4 * c:4 * c + 4].reshape(16 * 128, 1))
        m["smi"] = f(smf[4 * c:4 * c + 4].T)
        m["cf"] = np.array([[float(half), (float(half) - 1.0) * 30000.0, 0.0, 0.0]], np.float32)
        cd = np.zeros((16, 2, 16), np.float32)
        for qt in range(16):
            own = 8 + qt // 2
            for n in range(16):
                is_cand = (n < 8 and half == 1) or (8 <= n < own)
                cd[qt, 0, n] = NEG if is_cand else 0.0
                cd[qt, 1, n] = 0.0 if (is_cand or n == own) else NEG
        m["cand"] = cd.reshape(1, 512)
        maps.append(m)
    return maps


STOP_AFTER = "all"
CACHE_ROWS = 2560 * 128


def kernel(**inputs):
    maps = make_in_maps(inputs)
    for m in maps:
        m["ck"] = m["ck"][:CACHE_ROWS]
        m["cv"] = m["cv"][:CACHE_ROWS]
    nc, _ = build_program(stop_after=STOP_AFTER, cache_rows=CACHE_ROWS)
    res = run_bass_kernel_spmd(nc, maps, core_ids=list(range(8)))
    R = res.results
    f32 = np.float32
    y_prompt = np.zeros((4, 4096, D), f32); y_sample = np.zeros((32, 8, D), f32)
    nkp = np.zeros((1, 4, 4096, 8, 64), f32); nvp = np.zeros((1, 4, 4096, 8, 64), f32)
    nCp = np.zeros((1, 4, 4, 128, 128), f32); nnp_ = np.zeros((1, 4, 4, 128), f32); nmp = np.zeros((1, 4, 4), f32)
    nks = np.zeros((1, 32, 8, 8, 64), f32); nvs = np.zeros((1, 32, 8, 8, 64), f32)
    nCs = np.zeros((1, 32, 4, 128, 128), f32); nns = np.zeros((1, 32, 4, 128), f32); nms = np.zeros((1, 32, 4), f32)
    for c in range(8):
        b, half = c // 2, c % 2
        r = R[c]
        sl = slice(half * NM, (half + 1) * NM)
        y_prompt[b, sl] = r["y_m"]
        y_sample[4 * c:4 * c + 4] = r["y_s"].reshape(4, 8, D)
        nkp[0, b, sl] = r["k_m"].reshape(NM, 8, 64)
        nvp[0, b, sl] = r["v_m"].reshape(NM, 8, 64)
        if half == 1:
            nCp[0, b] = r["C_p"].reshape(4, 128, 128)
            nnp_[0, b] = r["n_p"].reshape(4, 128)
            nmp[0, b] = r["m_p"].reshape(4)
        nks[0, 4 * c:4 * c + 4] = r["k_s"].reshape(4, 8, 8, 64)
        nvs[0, 4 * c:4 * c + 4] = r["v_s"].reshape(4, 8, 8, 64)
        nCs[0, 4 * c:4 * c + 4] = r["C_s"].reshape(4, 4, 128, 128)
        nns[0, 4 * c:4 * c + 4] = r["n_s"].reshape(4, 4, 128)
        nms[0, 4 * c:4 * c + 4] = r["m_s"].reshape(4, 4)
    return (y_prompt, y_sample, nkp, nvp, nCp, nnp_, nmp, nks, nvs, nCs, nns, nms)
```

```python
import contextlib
import math
import numpy as np
import concourse.bass as bass
import concourse.mybir as mybir
from concourse.alu_op_type import AluOpType as ALU
from concourse.bass_utils import run_bass_kernel_spmd

F32 = mybir.dt.float32
BF16 = mybir.dt.bfloat16
I32 = mybir.dt.int32
AF = mybir.ActivationFunctionType
AX = mybir.AxisListType

ENGS = ("pe", "act", "dve", "pool", "sp")
NEG = -30000.0
D = 1024
NM = 2048
NS = 32
EPS = 1e-6


class Buf:
    __slots__ = ("name", "w", "rs", "dsem", "dcnt")

    def __init__(self, name):
        self.name = name
        self.w = None
        self.rs = {}
        self.dsem = None
        self.dcnt = 0


class KB:
    def __init__(self, nc, stack):
        self.nc = nc
        self.stack = stack
        self.sems = {}
        self.cnt = {e: 0 for e in ENGS}
        self.known = {e: {} for e in ENGS}
        self.prog = {e: [] for e in ENGS}
        for e in ENGS:
            self._newsem("E_" + e)
        self.nd = 0
        self.dbufs = []

    def _newsem(self, key):
        self.sems[key] = self.stack.enter_context(self.nc.semaphore(key))
        return key

    def buf(self, name):
        return Buf(name)

    def _collect(self, e, reads, writes):
        waits = {}
        kn = self.known[e]
        own = "E_" + e

        def need(tok, same_ok):
            if tok is None:
                return
            k, v = tok
            if k == own and same_ok:
                return
            if kn.get(k, 0) >= v:
                return
            if waits.get(k, 0) < v:
                waits[k] = v

        for b in reads:
            need(b.w, False)
        for b in writes:
            need(b.w, True)
            for k, v in b.rs.items():
                need((k, v), True)
        for k, v in waits.items():
            kn[k] = v
        return list(waits.items())

    def op(self, e, fn, reads=(), writes=()):
        waits = self._collect(e, reads, writes)
        self.cnt[e] += 1
        tok = ("E_" + e, self.cnt[e])
        for b in reads:
            if b.rs.get(tok[0], 0) < tok[1]:
                b.rs[tok[0]] = tok[1]
        for b in writes:
            b.w = tok
            b.rs = {}
        self.prog[e].append((waits, fn, (tok[0], 1)))

    def dma(self, q, fn, reads=(), writes=()):
        waits = self._collect(q, reads, writes)
        tgt = writes[0] if writes else reads[0]
        if tgt.dsem is None:
            self.nd += 1
            tgt.dsem = self._newsem(f"D{self.nd}")
            self.dbufs.append(tgt)
        tgt.dcnt += 16
        tok = (tgt.dsem, tgt.dcnt)
        for b in reads:
            if b.rs.get(tok[0], 0) < tok[1]:
                b.rs[tok[0]] = tok[1]
        for b in writes:
            b.w = tok
            b.rs = {}
        self.prog[q].append((waits, fn, (tok[0], 16)))

    def barrier(self):
        for e in ENGS:
            waits = [("E_" + x, self.cnt[x]) for x in ENGS if x != e and self.cnt[x] > 0]
            waits += [(b.dsem, b.dcnt) for b in self.dbufs]
            for k, v in waits:
                self.known[e][k] = max(self.known[e].get(k, 0), v)
            self.prog[e].append((waits, None, None))

    def final_wait(self, e):
        waits = [(b.dsem, b.dcnt) for b in self.dbufs]
        self.prog[e].append((waits, None, None))

    def emit(self):
        nc = self.nc
        sems = self.sems
        prog = self.prog
        with nc.Block() as block:
            def run(name):
                def body(eng):
                    for waits, fn, inc in prog[name]:
                        for k, v in waits:
                            eng.wait_ge(sems[k], v)
                        if fn is not None:
                            fn(eng).then_inc(sems[inc[0]], inc[1])
                return body
            block.tensor(run("pe"))
            block.scalar(run("act"))
            block.vector(run("dve"))
            block.gpsimd(run("pool"))
            block.sync(run("sp"))


def t5_thresholds():
    n = np.arange(0, 600)
    nf = np.maximum(n, 1).astype(np.float32)
    large = 16 + (np.log(nf / np.float32(16)) / np.float32(math.log(128 / 16)) * np.float32(16)).astype(np.int32)
    large = np.minimum(large, 31)
    bucket = np.where(n < 16, n, large)
    return [int(np.argmax(bucket >= b)) for b in range(1, 32)]


def build_program(dbg=None, stop_after=None, cache_rows=2560 * 128):
    nc = bass.Bass("TRN2", target_bir_lowering=False)
    din = lambda name, shape, dt=F32: nc.dram_tensor(name, shape, dt, kind="ExternalInput").ap()
    dout = lambda name, shape, dt=F32: nc.dram_tensor(name, shape, dt, kind="ExternalOutput").ap()
    xm = din("xm", [NM, D]); xc = din("xc", [NM, D]); xs = din("xs", [NS, D])
    w_in = din("w_in", [D, 3592]); w_out = din("w_out", [D, D]); w_up = din("w_up", [D, 4096]); w_down = din("w_down", [4096, D])
    nmix = din("nmix", [1, D]); nffn = din("nffn", [1, D]); nfin = din("nfin", [1, D]); mlg = din("mlg", [1, 512])
    bg = din("bg", [8, 1]); rb = din("rb", [1, 256])
    ck = din("ck", [cache_rows, 512]); cv = din("cv", [cache_rows, 512])
    pt = din("pt", [1, 256], I32)
    sC = din("sC", [16 * 128, 128]); sn = din("sn", [16 * 128, 1]); smi = din("smi", [4, 4])
    cf = din("cf", [1, 4]); cand = din("cand", [1, 512])
    y_m = dout("y_m", [NM, D]); y_s = dout("y_s", [NS, D])
    k_m = dout("k_m", [NM, 512]); v_m = dout("v_m", [NM, 512]); k_s = dout("k_s", [NS, 512]); v_s = dout("v_s", [NS, 512])
    C_p = dout("C_p", [512, 128]); n_p = dout("n_p", [512, 1]); m_p = dout("m_p", [4, 1])
    C_s = dout("C_s", [2048, 128]); n_s = dout("n_s", [2048, 1]); m_s = dout("m_s", [16, 1])
    mix = nc.dram_tensor("mix", [NM + NS, D], BF16, kind="ExternalOutput").ap()
    xnTd = nc.dram_tensor("xnTd", [9, 128, 4096], BF16).ap()
    dbg_outs = {}

    with contextlib.ExitStack() as st:
        kb = KB(nc, st)
        B = kb.buf
        out_bufs = []

        def sbt(name, shape, dt, stack=st):
            return stack.enter_context(nc.sbuf_tensor(name, shape, dt))

        PT = [st.enter_context(nc.psum_tensor(f"PT{i}", [128, 8, 128], BF16)) for i in range(2)]
        bPT = [B(f"PT{i}") for i in range(2)]
        PF = [st.enter_context(nc.psum_tensor(f"PF{i}", [128, 512], F32)) for i in range(6)]
        bPF = [B(f"PF{i}") for i in range(6)]

        TQ = {"on": False, "q": []}

        def OP(e, fn, r=(), w=()):
            if TQ["on"]:
                TQ["q"].append((e, fn, list(r), list(w)))
            else:
                kb.op(e, fn, r, w)

        def flush_tq(n):
            for _ in range(min(n, len(TQ["q"]))):
                e, fn, r, w = TQ["q"].pop(0)
                kb.op(e, fn, r, w)

        def store(dst_ap, src_ap, src_buf, q="sp"):
            import os
            if os.environ.get("NO_ST") is not None:
                return
            kb.dma(q, lambda e: e.dma_start(out=dst_ap, in_=src_ap), reads=[src_buf], writes=[])

        identf = sbt("identf", [128, 128], F32); b_idf = B("idf")
        ident = sbt("ident", [128, 128], BF16); b_id = B("id")
        OP("pool", lambda e: e.memset(identf[:], 1.0), w=[b_idf])
        OP("pool", lambda e: e.affine_select(out=identf[:], in_=identf[:], pattern=[[-1, 128]], compare_op=ALU.is_equal,
                                             fill=0.0, base=0, channel_multiplier=1), r=[b_idf], w=[b_idf])
        OP("dve", lambda e: e.tensor_copy(out=ident[:], in_=identf[:]), r=[b_idf], w=[b_id])
        g1 = sbt("g1", [128, D], F32); b_g1 = B("g1")
        kb.dma("sp", lambda e: e.dma_start(out=g1[:], in_=nmix[0:1, :].partition_broadcast(128)), writes=[b_g1])
        cfb = sbt("cfb", [128, 4], F32); b_cfb = B("cfb")
        kb.dma("sp", lambda e: e.dma_start(out=cfb[:], in_=cf[0:1, :].partition_broadcast(128)), writes=[b_cfb])
        rbb = sbt("rbb", [128, 32, 8], F32); b_rbb = B("rbb")
        kb.dma("sp", lambda e: e.dma_start(out=rbb[:].rearrange("p b h -> p (b h)"), in_=rb[0:1, :].partition_broadcast(128)), writes=[b_rbb])
        candb = sbt("candb", [128, 16, 2, 16], F32); b_cand = B("cand")
        kb.dma("sp", lambda e: e.dma_start(out=candb[:].rearrange("p a b c -> p (a b c)"), in_=cand[0:1, :].partition_broadcast(128)), writes=[b_cand])

        if stop_after == "c0":
            kb.final_wait("sp")
            kb.emit()
            return nc, dbg_outs
        xt = [sbt(f"xt{i}", [128, D], F32) for i in range(2)]; b_xt = [B(f"xt{i}") for i in range(2)]
        ssq = [sbt(f"ssq{i}", [128, 1], F32) for i in range(2)]; b_ssq = [B(f"ssq{i}") for i in range(2)]
        rstd = [sbt(f"rstd{i}", [128, 1], F32) for i in range(2)]; b_rstd = [B(f"rstd{i}") for i in range(2)]
        xnb = [sbt(f"xnb{i}", [128, D], BF16) for i in range(2)]; b_xnb = [B(f"xnb{i}") for i in range(2)]
        cnt = {"x": 0}

        def rms_scale(src, b_src, n, i, junk, b_junk, dim=D):
            OP("act", lambda e: e.activation(out=junk, in_=src, func=AF.Square, accum_out=ssq[i][0:n, :]),
               r=[b_src], w=[b_junk, b_ssq[i]])
            OP("act", lambda e: e.activation(out=rstd[i][0:n, :], in_=ssq[i][0:n, :], func=AF.Sqrt, scale=1.0 / dim, bias=EPS),
               r=[b_ssq[i]], w=[b_rstd[i]])
            OP("dve", lambda e: e.reciprocal(out=rstd[i][0:n, :], in_=rstd[i][0:n, :]), r=[b_rstd[i]], w=[b_rstd[i]])

        def to_featmajor(src_bf, b_src, n, dst, b_dst, ncol=8):
            i = cnt["x"] % 2
            cnt["x"] += 1
            for c in range(ncol):
                OP("pe", lambda e, c=c: e.transpose(out=PT[i][:, c, 0:n], in_=src_bf[0:n, c * 128:(c + 1) * 128], identity=ident[0:n, 0:n]),
                   r=[b_src, b_id], w=[bPT[i]])
            OP("act", lambda e: e.copy(out=dst, in_=PT[i][:, 0:ncol, 0:n]), r=[bPT[i]], w=[b_dst])

        def norm_tile(src_ap, n, gt, b_gt, dst, b_dst, q="sp"):
            i = cnt["x"] % 2
            kb.dma(q, lambda e: e.dma_start(out=xt[i][0:n, :], in_=src_ap), writes=[b_xt[i]])
            rms_scale(xt[i][0:n, :], b_xt[i], n, i, xnb[i][0:n, :], b_xnb[i])
            OP("dve", lambda e: e.scalar_tensor_tensor(out=xnb[i][0:n, :], in0=xt[i][0:n, :], scalar=rstd[i][0:n, 0:1], in1=gt[0:n, :],
                                                       op0=ALU.mult, op1=ALU.mult), r=[b_xt[i], b_rstd[i], b_gt], w=[b_xnb[i]])
            to_featmajor(xnb[i], b_xnb[i], n, dst, b_dst)

        def load_w(dst, src, b_dst, ncols):
            for c0 in range(0, ncols, 2048):
                c1 = min(ncols, c0 + 2048)
                kb.dma("pool", lambda e, c0=c0, c1=c1: e.dma_start(out=dst[:, c0:c1], in_=src[:, c0:c1]), writes=[b_dst])

        evq = {"i": 0}

        def evac(out, in_, r, w, scale=None):
            evq["i"] += 1
            if evq["i"] % 2:
                if scale is None:
                    OP("act", lambda e: e.copy(out=out, in_=in_), r=r, w=w)
                else:
                    OP("dve", lambda e: e.tensor_scalar(out=out, in0=in_, scalar1=scale, scalar2=None, op0=ALU.mult), r=r, w=w)
            else:
                if scale is None:
                    OP("dve", lambda e: e.tensor_copy(out=out, in_=in_), r=r, w=w)
                else:
                    OP("dve", lambda e: e.tensor_scalar(out=out, in0=in_, scalar1=scale, scalar2=None, op0=ALU.mult), r=r, w=w)

        qTs = sbt("qTs", [128, 4, NS], BF16); b_qTs = B("qTs")
        kTs_own = sbt("kTs_own", [128, 4, NS], BF16); b_kTs_own = B("kTs_own")
        Vs_own = sbt("Vs_own", [8, 4, 512], BF16); b_Vs_own = B("Vs_own")
        TS128 = sbt("TS128", [128, 8, 8], F32); b_TS128 = B("TS128")
        TS0 = sbt("TS0", [8, 8, 8], F32); b_TS0 = B("TS0")
        p1 = contextlib.ExitStack()
        qTa = [sbt(f"qTa{h}", [81, NM], BF16, p1) for h in range(8)]; b_qTa = [[B(f"qTa{h}_{g}") for g in range(4)] for h in range(8)]
        b_qTaP = [[B(f"qTaP{h}_{g}") for g in range(4)] for h in range(8)]
        kTa = [sbt(f"kTa{h}", [81, 4096], BF16, p1) for h in range(8)]; b_kTa = [[B(f"kTa{h}_{g}") for g in range(8)] for h in range(8)]
        b_kTaI = [B(f"kTaI{h}") for h in range(8)]
        Va = sbt("Va", [128, 32, 8, 65], BF16, p1); b_Va = [B(f"Va{t}") for t in range(32)]; b_Va1 = B("Va1")
        OP("pool", lambda e: e.memset(Va[:, :, :, 64:65], 1.0), w=[b_Va1])
        for h in range(8):
            OP("pool", lambda e, h=h: e.memset(kTa[h][64:81, :], 1.0), w=[b_kTaI[h]])
            OP("pool", lambda e, h=h: e.affine_select(out=kTa[h][64:80, :].rearrange("p (b k) -> p b k", k=256),
                                                      in_=kTa[h][64:80, :].rearrange("p (b k) -> p b k", k=256),
                                                      pattern=[[1, 16], [0, 256]], compare_op=ALU.is_equal, fill=0.0, base=0,
                                                      channel_multiplier=-1), r=[b_kTaI[h]], w=[b_kTaI[h]])

        if stop_after == "c1":
            kb.final_wait("sp")
            kb.emit()
            p1.close()
            return nc, dbg_outs
        thr = t5_thresholds()
        Tt = {0: sbt("T0", [128, 8, 128], F32, p1), 128: sbt("T128", [128, 8, 128], F32, p1)}
        b_Tt = {0: [B(f"T0_{h}") for h in range(8)], 128: [B(f"T128_{h}") for h in range(8)]}
        drb = sbt("drb", [128, 32, 8], F32, p1); b_drb = B("drb")
        OP("dve", lambda e: e.tensor_tensor(out=drb[:, 1:32, :], in0=rbb[:, 1:32, :], in1=rbb[:, 0:31, :], op=ALU.subtract), r=[b_rbb], w=[b_drb])
        OP("dve", lambda e: e.tensor_tensor(out=drb[:, 0:1, :], in0=rbb[:, 0:1, :], in1=rbb[:, 31:32, :], op=ALU.subtract), r=[b_rbb], w=[b_drb])
        rb31x8 = sbt("rb31x8", [128, 8], F32, p1); b_rb31 = B("rb31")
        OP("dve", lambda e: e.tensor_scalar(out=rb31x8[:], in0=rbb[:, 31, :], scalar1=8.0, scalar2=None, op0=ALU.mult), r=[b_rbb], w=[b_rb31])
        disti = sbt("disti", [128, 128], I32, p1); b_disti = B("disti")
        distf = sbt("distf", [128, 128], F32, p1); b_distf = B("distf")
        gef = sbt("gef", [128, 128], F32, p1); b_gef = B("gef")
        TQ["on"] = True
        for delta in (0, 128):
            OP("pool", lambda e, delta=delta: e.iota(out=disti[:], pattern=[[1, 128]], base=delta, channel_multiplier=-1), w=[b_disti])
            OP("dve", lambda e: e.tensor_copy(out=distf[:], in_=disti[:]), r=[b_disti], w=[b_distf])
            for h in range(8):
                OP("dve", lambda e, h=h, delta=delta: e.tensor_scalar(out=Tt[delta][:, h, :], in0=distf[:], scalar1=0.0, scalar2=drb[:, 0, h:h + 1],
                                                                      op0=ALU.mult, op1=ALU.add), r=[b_distf, b_drb], w=[b_Tt[delta][h]])
            steps = [(float(thr[b - 1]), b) for b in range(1, 32)]
            for tv, b in steps:
                OP("dve", lambda e, tv=tv: e.tensor_scalar(out=gef[:], in0=distf[:], scalar1=tv, scalar2=None, op0=ALU.is_ge), r=[b_distf], w=[b_gef])
                for h in range(8):
                    OP("dve", lambda e, h=h, b=b, delta=delta: e.scalar_tensor_tensor(out=Tt[delta][:, h, :], in0=gef[:], scalar=drb[:, b, h:h + 1], in1=Tt[delta][:, h, :],
                                                                                     op0=ALU.mult, op1=ALU.add), r=[b_gef, b_drb, b_Tt[delta][h]], w=[b_Tt[delta][h]])
            if delta == 0:
                OP("dve", lambda e: e.tensor_scalar(out=gef[:], in0=distf[:], scalar1=0.0, scalar2=None, op0=ALU.is_lt), r=[b_distf], w=[b_gef])
                for h in range(8):
                    OP("dve", lambda e, h=h: e.scalar_tensor_tensor(out=Tt[0][:, h, :], in0=gef[:], scalar=NEG, in1=Tt[0][:, h, :],
                                                                    op0=ALU.mult, op1=ALU.add), r=[b_gef, b_Tt[0][h]], w=[b_Tt[0][h]])
        OP("dve", lambda e: e.tensor_copy(out=TS128[:], in_=Tt[128][:, :, 0:8]), r=b_Tt[128], w=[b_TS128])
        OP("dve", lambda e: e.tensor_copy(out=TS0[:], in_=Tt[0][0:8, :, 0:8]), r=b_Tt[0], w=[b_TS0])
        TQ["on"] = False
        tq_slice = (len(TQ["q"]) + 7) // 8
        negT = sbt("negT", [128, 128], F32, p1); b_negT = B("negT")
        OP("pool", lambda e: e.memset(negT[:], NEG), w=[b_negT])

        if stop_after == "c2":
            kb.final_wait("sp")
            kb.emit()
            p1.close()
            return nc, dbg_outs
        p1a = contextlib.ExitStack()
        wA = sbt("wA", [128, 8, 1536], BF16, p1a); b_wA = B("wA")
        for c in range(8):
            load_w(wA[:, c, :], w_in[c * 128:(c + 1) * 128, 0:1536], b_wA, 1536)
        xnT = [sbt(f"xnT{i}", [128, 8, 512], BF16, p1a) for i in range(1)]; b_xnT = [B(f"xnT{i}") for i in range(1)]
        kvst = [sbt(f"kvst{i}", [128, 1024], F32, p1a) for i in range(2)]; b_kvst = [B(f"kvst{i}") for i in range(2)]

        if stop_after == "s0":
            kb.final_wait("sp"); kb.emit(); p1a.close(); p1.close(); return nc, dbg_outs
        b_xnTd = B("xnTd")
        for kind in ("ctx", "main"):
            src = xc if kind == "ctx" else xm
            for g in range(4):
                xi = 0
                for t in range(4):
                    r0 = g * 512 + t * 128
                    norm_tile(src[r0:r0 + 128, :], 128, g1, b_g1, xnT[xi][:, :, t * 128:(t + 1) * 128], b_xnT[xi])
                if stop_after == "s1":
                    kb.final_wait("sp"); kb.emit(); p1a.close(); p1.close(); return nc, dbg_outs
                flush_tq(tq_slice)
                kg = g if kind == "ctx" else 4 + g
                kb.dma("sp", lambda e, kg=kg, xi=xi: e.dma_start(out=xnTd[kg], in_=xnT[xi][:].rearrange("p c t -> p (c t)")), reads=[b_xnT[xi]], writes=[b_xnTd])
                import os
                for h in (range(8) if os.environ.get("SKIP_FM") is None else ()):
                    for which in (("k",) if kind == "ctx" else ("q", "k")):
                        col0 = (0 if which == "q" else 512) + h * 64
                        pf = (2 * h + (which == "k")) % 2
                        for c in range(8):
                            OP("pe", lambda e, c=c, pf=pf, col0=col0, xi=xi: e.matmul(PF[pf][0:64, :], lhsT=wA[:, c, col0:col0 + 64], rhs=xnT[xi][:, c, :],
                                                                                       start=(c == 0), stop=(c == 7)),
                               r=[b_wA, b_xnT[xi]], w=[bPF[pf]])
                        if os.environ.get("NO_EV") is not None:
                            pass
                        elif which == "q":
                            evac(qTa[h][0:64, g * 512:(g + 1) * 512], PF[pf][0:64, :], [bPF[pf]], [b_qTa[h][g]])
                        else:
                            evac(kTa[h][0:64, kg * 512:(kg + 1) * 512], PF[pf][0:64, :], [bPF[pf]], [b_kTa[h][kg]])
                if stop_after == "s2":
                    kb.final_wait("sp"); kb.emit(); p1a.close(); p1.close(); return nc, dbg_outs
                for t in (range(4) if os.environ.get("SKIP_TM") is None else ()):
                    ta = kg * 4 + t
                    si = ta % 2
                    for which in (("v",) if kind == "ctx" else ("k", "v")):
                        col0 = 512 if which == "k" else 1024
                        pf = 3 if which == "v" else 4
                        for c in range(8):
                            OP("pe", lambda e, c=c, pf=pf, col0=col0, xi=xi, t=t: e.matmul(PF[pf][:, :], lhsT=xnT[xi][:, c, t * 128:(t + 1) * 128], rhs=wA[:, c, col0:col0 + 512],
                                                                                          start=(c == 0), stop=(c == 7)),
                               r=[b_wA, b_xnT[xi]], w=[bPF[pf]])
                        if which == "v" and os.environ.get("NO_VA") is None:
                            OP("dve", lambda e, ta=ta, pf=pf: e.tensor_copy(out=Va[:, ta, :, 0:64], in_=PF[pf][:, :].rearrange("p (h d) -> p h d", d=64)),
                               r=[bPF[pf]], w=[b_Va[ta]])
                        if kind == "main" and os.environ.get("NO_KV") is None:
                            o0 = 0 if which == "k" else 512
                            OP("dve", lambda e, si=si, pf=pf, o0=o0: e.tensor_copy(out=kvst[si][:, o0:o0 + 512], in_=PF[pf][:, :]), r=[bPF[pf]], w=[b_kvst[si]])
                    if kind == "main":
                        r0 = g * 512 + t * 128
                        store(k_m[r0:r0 + 128, :], kvst[si][:, 0:512], b_kvst[si])
                        store(v_m[r0:r0 + 128, :], kvst[si][:, 512:1024], b_kvst[si])
                if stop_after == "s3" or (stop_after == "s4" and kind == "main") or (stop_after == "s5" and kind == "ctx" and g == 3) or (stop_after == "s6" and kind == "ctx" and g == 1):
                    kb.final_wait("sp"); kb.emit(); p1a.close(); p1.close(); return nc, dbg_outs
        if stop_after == "c3":
            kb.final_wait("sp")
            kb.emit()
            p1a.close()
            p1.close()
            return nc, dbg_outs
        flush_tq(10 ** 9)
        xi = 0
        norm_tile(xs[0:NS, :], NS, g1, b_g1, xnT[xi][:, :, 0:NS], b_xnT[xi])
        kb.dma("sp", lambda e, xi=xi: e.dma_start(out=xnTd[8].rearrange("p (c t) -> p c t", t=512)[:, :, 0:NS], in_=xnT[xi][:, :, 0:NS]), reads=[b_xnT[xi]], writes=[b_xnTd])
        for which in ("q", "k"):
            for ch in range(4):
                col0 = (0 if which == "q" else 512) + ch * 128
                pf = ch % 2
                for c in range(8):
                    OP("pe", lambda e, c=c, pf=pf, col0=col0, xi=xi: e.matmul(PF[pf][:, 0:NS], lhsT=wA[:, c, col0:col0 + 128], rhs=xnT[xi][:, c, 0:NS],
                                                                               start=(c == 0), stop=(c == 7)), r=[b_wA, b_xnT[xi]], w=[bPF[pf]])
                dst = qTs if which == "q" else kTs_own
                bd = b_qTs if which == "q" else b_kTs_own
                evac(dst[:, ch, :], PF[pf][:, 0:NS], [bPF[pf]], [bd])
        for sbi in range(4):
            si = sbi % 2
            for which in ("k", "v"):
                col0 = 512 if which == "k" else 1024
                pf = 2 + (which == "v")
                for c in range(8):
                    OP("pe", lambda e, c=c, pf=pf, col0=col0, xi=xi, sbi=sbi: e.matmul(PF[pf][0:8, :], lhsT=xnT[xi][:, c, sbi * 8:(sbi + 1) * 8], rhs=wA[:, c, col0:col0 + 512],
                                                                                      start=(c == 0), stop=(c == 7)), r=[b_wA, b_xnT[xi]], w=[bPF[pf]])
                o0 = 0 if which == "k" else 512
                if which == "v":
                    OP("dve", lambda e, sbi=sbi, pf=pf: e.tensor_copy(out=Vs_own[:, sbi, :], in_=PF[pf][0:8, :]), r=[bPF[pf]], w=[b_Vs_own])
                OP("dve", lambda e, si=si, pf=pf, o0=o0: e.tensor_copy(out=kvst[si][0:8, o0:o0 + 512], in_=PF[pf][0:8, :]), r=[bPF[pf]], w=[b_kvst[si]])
            store(k_s[sbi * 8:(sbi + 1) * 8, :], kvst[si][0:8, 0:512], b_kvst[si])
            store(v_s[sbi * 8:(sbi + 1) * 8, :], kvst[si][0:8, 512:1024], b_kvst[si])
        kb.barrier()
        p1a.close()

        if stop_after == "p1":
            kb.final_wait("sp")
            kb.emit()
            p1.close()
            return nc, dbg_outs

        p1b = contextlib.ExitStack()
        kmf = sbt("kmf", [64, 8, 16], F32, p1b); b_kmf = B("kmf")
        kmT = sbt("kmT", [64, 8, 16], BF16, p1b); b_kmT = B("kmT")
        for h in range(8):
            OP("dve", lambda e, h=h: e.tensor_reduce(out=kmf[:, h, :], in_=kTa[h][0:64, :].rearrange("p (b k) -> p b k", k=256), axis=AX.X, op=ALU.add),
               r=b_kTa[h], w=[b_kmf])
        OP("dve", lambda e: e.tensor_copy(out=kmT[:], in_=kmf[:]), r=[b_kmf], w=[b_kmT])
        selm = sbt("selm", [128, 16, 16], F32, p1b); b_selm = B("selm")
        OP("dve", lambda e: e.tensor_scalar(out=selm[:], in0=candb[:, :, 0, :], scalar1=-1.0, scalar2=NEG, op0=ALU.mult, op1=ALU.add), r=[b_cand], w=[b_selm])
        selW = sbt("selW", [128, 4, 8, 81], F32, p1b); b_selW = [B(f"selW{j}") for j in range(4)]
        OP("pool", lambda e: e.memset(selW[:], 0.0), w=b_selW)
        for j in range(4):
            OP("dve", lambda e, j=j: e.tensor_copy(out=selW[:, j, :, 80:81], in_=rb31x8[:].rearrange("p (h o) -> p h o", o=1)), r=[b_rb31], w=[b_selW[j]])
        smk = sbt("smk", [128, 8, 16], F32, p1b); b_smk = B("smk")
        top8 = sbt("top8", [128, 8, 8], F32, p1b); b_top8 = B("top8")
        PTt = [sbt(f"PTt{i}", [128, 512], BF16, p1b) for i in range(2)]; b_PTt = [B(f"PTt{i}") for i in range(2)]
        tmpS = [sbt(f"tmpS{i}", [128, 512], F32, p1b) for i in range(2)]; b_tmpS = [B(f"tmpS{i}") for i in range(2)]
        ot = [sbt(f"ot{i}", [65, 512], F32, p1b) for i in range(2)]; b_ot = [B(f"ot{i}") for i in range(2)]
        rc4 = sbt("rc4", [128, 4, 1], F32, p1b); b_rc4 = B("rc4")
        attb = [sbt(f"attb{i}", [128, 4, 512], BF16, p1b) for i in range(2)]; b_attb = [B(f"attb{i}") for i in range(2)]
        b_mix = B("mix")
        sidx = 0
        for g in range(4):
            for j in range(4):
                qt = 4 * g + j
                for h in range(8):
                    OP("pe", lambda e, h=h, qt=qt: e.matmul(PF[4][:, h * 16:(h + 1) * 16], lhsT=qTa[h][0:64, qt * 128:(qt + 1) * 128], rhs=kmT[:, h, :],
                                                            start=True, stop=True), r=[b_qTa[h][g], b_kmT], w=[bPF[4]])
                OP("dve", lambda e, qt=qt: e.tensor_tensor(out=smk[:], in0=PF[4][:, 0:128].rearrange("p (h n) -> p h n", n=16),
                                                           in1=selm[:, qt:qt + 1, :].to_broadcast([128, 8, 16]), op=ALU.add), r=[bPF[4], b_selm], w=[b_smk])
                for h in range(8):
                    OP("dve", lambda e, h=h: e.max(out=top8[:, h, :], in_=smk[:, h, :]), r=[b_smk], w=[b_top8])
                for h in range(8):
                    OP("dve", lambda e, h=h, j=j, qt=qt: e.scalar_tensor_tensor(out=selW[:, j, h, 64:80], in0=smk[:, h, :], scalar=top8[:, h, 2:3], in1=candb[:, qt, 0, :],
                                                                               op0=ALU.is_lt, op1=ALU.mult), r=[b_smk, b_top8, b_cand], w=[b_selW[j]])
                OP("dve", lambda e, j=j, qt=qt: e.tensor_tensor(out=selW[:, j, :, 64:80], in0=selW[:, j, :, 64:80],
                                                                in1=candb[:, qt:qt + 1, 1, :].to_broadcast([128, 8, 16]), op=ALU.add), r=[b_cand, b_selW[j]], w=[b_selW[j]])
            for h in range(8):
                for j in range(4):
                    OP("pe", lambda e, h=h, j=j: e.transpose(out=PF[5][0:81, j * 128:(j + 1) * 128], in_=selW[:, j, h, :], identity=identf[:]),
                       r=[b_selW[j], b_idf], w=[bPF[5]])
                OP("act", lambda e, h=h, g=g: e.copy(out=qTa[h][64:81, g * 512:(g + 1) * 512], in_=PF[5][64:81, :]), r=[bPF[5]], w=[b_qTaP[h][g]])
            ab = g % 2
            for h in range(8):
                po = 2 + h % 2
                nk = 16 + 4 * g + 4

                def emit_S(kt, h=h, g=g):
                    si = kt % 2
                    OP("pe", lambda e, h=h, kt=kt, g=g, si=si: e.matmul(PF[si][:, :], lhsT=kTa[h][0:81, kt * 128:(kt + 1) * 128], rhs=qTa[h][0:81, g * 512:(g + 1) * 512],
                                                                         start=True, stop=True),
                       r=[b_kTa[h][kt // 4], b_kTaI[h], b_qTa[h][g], b_qTaP[h][g]], w=[bPF[si]])

                def emit_exp(kt, h=h, g=g):
                    si = kt % 2
                    rel = kt - (16 + 4 * g)
                    if rel < -1:
                        OP("act", lambda e, si=si: e.activation(out=PTt[si][:], in_=PF[si][:, :], func=AF.Exp, scale=0.125), r=[bPF[si]], w=[b_PTt[si]])
                    else:
                        for j in range(4):
                            d = j - rel
                            cs = slice(j * 128, (j + 1) * 128)
                            if d == 0:
                                Tm, bTm = Tt[0][:, h, :], b_Tt[0][h]
                            elif d == 1:
                                Tm, bTm = Tt[128][:, h, :], b_Tt[128][h]
                            elif d == -1 and j % 2 == 0:
                                Tm, bTm = negT[:], b_negT
                            else:
                                Tm = None
                            if Tm is not None:
                                OP("dve", lambda e, si=si, cs=cs, Tm=Tm: e.scalar_tensor_tensor(out=tmpS[si][:, cs], in0=PF[si][:, cs], scalar=0.125, in1=Tm,
                                                                                                 op0=ALU.mult, op1=ALU.add), r=[bPF[si], bTm], w=[b_tmpS[si]])
                            else:
                                OP("dve", lambda e, si=si, cs=cs: e.tensor_scalar(out=tmpS[si][:, cs], in0=PF[si][:, cs], scalar1=0.125, scalar2=None, op0=ALU.mult),
                                   r=[bPF[si]], w=[b_tmpS[si]])
                        OP("act", lambda e, si=si: e.activation(out=PTt[si][:], in_=tmpS[si][:], func=AF.Exp), r=[b_tmpS[si]], w=[b_PTt[si]])

                def emit_PV(kt, h=h, po=po, nk=nk):
                    si = kt % 2
                    OP("pe", lambda e, h=h, kt=kt, si=si, po=po, nk=nk: e.matmul(PF[po][0:65, :], lhsT=Va[:, kt, h, :], rhs=PTt[si][:], start=(kt == 0), stop=(kt == nk - 1)),
                       r=[b_Va[kt], b_Va1, b_PTt[si]], w=[bPF[po]])

                emit_S(0)
                for kt in range(nk):
                    if kt + 1 < nk:
                        emit_S(kt + 1)
                    emit_exp(kt)
                    emit_PV(kt)
                oi = h % 2
                OP("dve", lambda e, oi=oi, po=po: e.tensor_copy(out=ot[oi][:], in_=PF[po][0:65, :]), r=[bPF[po]], w=[b_ot[oi]])
                for j in range(4):
                    OP("pe", lambda e, oi=oi, j=j: e.transpose(out=PF[4][:, j * 128:j * 128 + 65], in_=ot[oi][0:65, j * 128:(j + 1) * 128], identity=identf[0:65, 0:65]),
                       r=[b_ot[oi], b_idf], w=[bPF[4]])
                pv = PF[4][:, :].rearrange("p (j c) -> p j c", c=128)
                OP("dve", lambda e, pv=pv: e.reciprocal(out=rc4[:], in_=pv[:, :, 64:65]), r=[bPF[4]], w=[b_rc4])
                OP("dve", lambda e, pv=pv, h=h, ab=ab: e.tensor_tensor(out=attb[ab][:, :, h * 64:(h + 1) * 64], in0=pv[:, :, 0:64], in1=rc4[:].to_broadcast([128, 4, 64]), op=ALU.mult),
                   r=[bPF[4], b_rc4], w=[b_attb[ab]])
            for j in range(4):
                r0 = g * 512 + j * 128
                kb.dma("sp", lambda e, ab=ab, j=j, r0=r0: e.dma_start(out=mix[r0:r0 + 128, 0:512], in_=attb[ab][:, j, :]), reads=[b_attb[ab]], writes=[b_mix])
        if dbg == "att":
            def dump(name, shape, dt, src, bufs):
                o = dout("d_" + name, shape, dt)
                bb = B("dmp_" + name)
                kb.dma("sp", lambda e: e.dma_start(out=o, in_=src), reads=bufs, writes=[bb])
            dump("qTa0", [81, NM], BF16, qTa[0][:, :], b_qTa[0] + b_qTaP[0])
            dump("kTa0", [81, 4096], BF16, kTa[0][:, :], b_kTa[0] + [b_kTaI[0]])
            dump("T0", [128, 128], F32, Tt[0][:, 0, :], [b_Tt[0][0]])
            dump("T128", [128, 128], F32, Tt[128][:, 0, :], [b_Tt[128][0]])
            dump("Va16", [128, 8 * 65], BF16, Va[:, 16, :, :].rearrange("p h d -> p (h d)"), [b_Va[16], b_Va1])
            dump("kmf", [64, 128], F32, kmf[:].rearrange("p h n -> p (h n)"), [b_kmf])
            dump("selW", [128, 4 * 8 * 81], F32, selW[:].rearrange("p a b c -> p (a b c)"), b_selW)
        kb.barrier()
        p1b.close()
        p1.close()
        if stop_after == "att":
            kb.final_wait("sp")
            kb.emit()
            return nc, dbg_outs

        ps_ = contextlib.ExitStack()
        ptb = sbt("ptb", [128, 256], I32, ps_); b_ptb = B("ptb")
        kb.dma("sp", lambda e: e.dma_start(out=ptb[:], in_=pt[0:1, :].partition_broadcast(128)), writes=[b_ptb])
        pio = sbt("pio", [128, 1], I32, ps_); b_pio = B("pio")
        OP("pool", lambda e: e.iota(out=pio[:], pattern=[[0, 1]], base=0, channel_multiplier=1), w=[b_pio])
        piof = sbt("piof", [128, 1], F32, ps_); b_piof = B("piof")
        OP("dve", lambda e: e.tensor_copy(out=piof[:], in_=pio[:]), r=[b_pio], w=[b_piof])
        ptf = sbt("ptf", [128, 256], F32, ps_); b_ptf = B("ptf")
        OP("dve", lambda e: e.tensor_copy(out=ptf[:], in_=ptb[:]), r=[b_ptb], w=[b_ptf])
        ridx = sbt("ridx", [128, 256], I32, ps_); b_ridx = B("ridx")
        OP("dve", lambda e: e.tensor_scalar(out=ridx[:], in0=ptf[:], scalar1=128.0, scalar2=piof[:, 0:1], op0=ALU.mult, op1=ALU.add), r=[b_ptf, b_piof], w=[b_ridx])
        onesb = sbt("onesb", [128, 1], BF16, ps_); b_onesb = B("onesb")
        OP("pool", lambda e: e.memset(onesb[:], 1.0), w=[b_onesb])
        ohS = sbt("ohS", [33, 33, 128], BF16, ps_); b_ohS = B("ohS")
        OP("pool", lambda e: e.memset(ohS[:], 1.0), w=[b_ohS])
        OP("pool", lambda e: e.affine_select(out=ohS[:], in_=ohS[:], pattern=[[1, 33], [0, 128]], compare_op=ALU.is_equal, fill=0.0, base=0, channel_multiplier=-1),
           r=[b_ohS], w=[b_ohS])
        rbcol = sbt("rbcol", [64, 1], F32, ps_); b_rbcol = B("rbcol")
        for h in range(8):
            kb.dma("sp", lambda e, h=h: e.dma_start(out=rbcol[h * 8:(h + 1) * 8, :], in_=rb[0:1, 248 + h:248 + h + 1].partition_broadcast(8)), writes=[b_rbcol])
        OP("dve", lambda e: e.tensor_scalar(out=rbcol[:], in0=rbcol[:], scalar1=8.0, scalar2=None, op0=ALU.mult), r=[b_rbcol], w=[b_rbcol])
        kTsp = sbt("kTsp", [128, 4, 8192], BF16, ps_); b_kTsp = B("kTsp")
        kpf = [sbt(f"kpf{i}", [128, 2, 512], F32, ps_) for i in range(4)]; b_kpf = [B(f"kpf{i}") for i in range(4)]
        kpb = [sbt(f"kpb{i}", [128, 2, 512], BF16, ps_) for i in range(4)]; b_kpb = [B(f"kpb{i}") for i in range(4)]
        kms = sbt("kms", [128, 4, 32], BF16, ps_); b_kms = B("kms")
        kmsf = sbt("kmsf", [128, 4, 32], F32, ps_); b_kmsf = B("kmsf")
        Qbd = sbt("Qbd", [128, 4, 64], BF16, ps_); b_Qbd = B("Qbd")
        scs = sbt("scs", [64, 32], F32, ps_); b_scs = B("scs")
        top8s = sbt("top8s", [64, 8], F32, ps_); b_top8s = B("top8s")
        penF = sbt("penF", [64, 33], F32, ps_); b_penF = B("penF")
        penTb = sbt("penTb", [33, 64], BF16, ps_); b_penTb = B("penTb")
        PTs = [sbt(f"PTs{i}", [128, 64], BF16, ps_) for i in range(2)]; b_PTs = [B(f"PTs{i}") for i in range(2)]
        tmS = sbt("tmS", [128, 64], F32, ps_); b_tmS = B("tmS")
        osb = sbt("osb", [64, 512], BF16, ps_); b_osb = B("osb")
        recs = sbt("recs", [64, 1], F32, ps_); b_recs = B("recs")
        b_mix3 = B("mix3")
        xcnt = 0
        for sbi in range(4):
            for pr in range(32):
                bi = pr % 4
                for a in range(2):
                    col = sbi * 64 + pr * 2 + a
                    kb.dma("pool", lambda e, bi=bi, a=a, col=col: e.indirect_dma_start(out=kpf[bi][:, a, :], out_offset=None, in_=ck[:, :],
                                                                                      in_offset=bass.IndirectOffsetOnAxis(ap=ridx[:, col:col + 1], axis=0)),
                           reads=[b_ridx], writes=[b_kpf[bi]])
                OP("dve", lambda e, bi=bi: e.tensor_copy(out=kpb[bi][:], in_=kpf[bi][:]), r=[b_kpf[bi]], w=[b_kpb[bi]])
                for a in range(2):
                    ti_ = xcnt % 2
                    xcnt += 1
                    pg = pr * 2 + a
                    for ch in range(4):
                        OP("pe", lambda e, ti_=ti_, bi=bi, a=a, ch=ch: e.transpose(out=PT[ti_][:, ch, :], in_=kpb[bi][:, a, ch * 128:(ch + 1) * 128], identity=ident[:]),
                           r=[b_kpb[bi], b_id], w=[bPT[ti_]])
                    OP("act", lambda e, ti_=ti_, pg=pg: e.copy(out=kTsp[:, :, pg * 128:(pg + 1) * 128], in_=PT[ti_][:, 0:4, :]), r=[bPT[ti_]], w=[b_kTsp])
            for ch in range(4):
                OP("dve", lambda e, ch=ch: e.tensor_reduce(out=kmsf[:, ch, :], in_=kTsp[:, ch, :].rearrange("p (n k) -> p n k", k=256), axis=AX.X, op=ALU.add), r=[b_kTsp], w=[b_kmsf])
            OP("dve", lambda e: e.tensor_copy(out=kms[:], in_=kmsf[:]), r=[b_kmsf], w=[b_kms])
            OP("pool", lambda e: e.memset(Qbd[:], 0.0), w=[b_Qbd])
            for h in range(8):
                ch, hh = h // 2, h % 2
                OP("dve", lambda e, h=h, ch=ch, hh=hh, sbi=sbi: e.tensor_copy(out=Qbd[hh * 64:(hh + 1) * 64, ch, h * 8:(h + 1) * 8], in_=qTs[hh * 64:(hh + 1) * 64, ch, sbi * 8:(sbi + 1) * 8]),
                   r=[b_qTs], w=[b_Qbd])
            for ch in range(4):
                OP("pe", lambda e, ch=ch: e.matmul(PF[4][0:64, 0:32], lhsT=Qbd[:, ch, :], rhs=kms[:, ch, :], start=(ch == 0), stop=(ch == 3)), r=[b_Qbd, b_kms], w=[bPF[4]])
            OP("dve", lambda e: e.tensor_copy(out=scs[:], in_=PF[4][0:64, 0:32]), r=[bPF[4]], w=[b_scs])
            OP("dve", lambda e: e.max(out=top8s[:], in_=scs[:]), r=[b_scs], w=[b_top8s])
            OP("dve", lambda e: e.tensor_scalar(out=penF[:, 0:32], in0=scs[:], scalar1=top8s[:, 2:3], scalar2=NEG, op0=ALU.is_lt, op1=ALU.mult), r=[b_scs, b_top8s], w=[b_penF])
            OP("pool", lambda e: e.memset(penF[:, 32:33], 0.0), w=[b_penF])
            OP("dve", lambda e: e.tensor_scalar(out=penF[:], in0=penF[:], scalar1=rbcol[:, 0:1], scalar2=None, op0=ALU.add), r=[b_penF, b_rbcol], w=[b_penF])
            OP("pe", lambda e: e.transpose(out=PF[4][0:33, 64:128], in_=penF[:], identity=identf[0:64, 0:64]), r=[b_penF, b_idf], w=[bPF[4]])
            OP("act", lambda e: e.copy(out=penTb[:], in_=PF[4][0:33, 64:128]), r=[bPF[4]], w=[b_penTb])
            def s_load(kt, sbi=sbi):
                if kt >= 64:
                    return
                bi = kt % 4
                col = sbi * 64 + kt
                kb.dma("pool", lambda e, bi=bi, col=col: e.indirect_dma_start(out=kpf[bi][:, 0, :], out_offset=None, in_=cv[:, :],
                                                                              in_offset=bass.IndirectOffsetOnAxis(ap=ridx[:, col:col + 1], axis=0)),
                       reads=[b_ridx], writes=[b_kpf[bi]])
                OP("dve", lambda e, bi=bi: e.tensor_copy(out=kpb[bi][:, 0, :], in_=kpf[bi][:, 0, :]), r=[b_kpf[bi]], w=[b_kpb[bi]])

            def s_S(kt, sbi=sbi):
                own = kt == 64
                si = kt % 2
                L = 8 if own else 128
                n = 32 if own else kt // 2
                for ch in range(4):
                    lhs = kTs_own[:, ch, sbi * 8:(sbi + 1) * 8] if own else kTsp[:, ch, kt * 128:(kt + 1) * 128]
                    OP("pe", lambda e, si=si, ch=ch, lhs=lhs, L=L: e.matmul(PF[si][0:L, 0:64], lhsT=lhs, rhs=Qbd[:, ch, :], start=(ch == 0), stop=False),
                       r=[b_kTsp, b_kTs_own, b_Qbd], w=[bPF[si]])
                OP("pe", lambda e, si=si, n=n, L=L: e.matmul(PF[si][0:L, 0:64], lhsT=ohS[:, n, 0:L], rhs=penTb[:], start=False, stop=True), r=[b_ohS, b_penTb], w=[bPF[si]])

            def s_exp(kt):
                own = kt == 64
                si = kt % 2
                L = 8 if own else 128
                if kt >= 63:
                    Tm = TS0[:].rearrange("p h q -> p (h q)") if own else TS128[:].rearrange("p h q -> p (h q)")
                    OP("dve", lambda e, si=si, L=L, Tm=Tm: e.scalar_tensor_tensor(out=tmS[0:L, :], in0=PF[si][0:L, 0:64], scalar=0.125, in1=Tm, op0=ALU.mult, op1=ALU.add),
                       r=[bPF[si], b_TS0, b_TS128], w=[b_tmS])
                    OP("act", lambda e, si=si, L=L: e.activation(out=PTs[si][0:L, :], in_=tmS[0:L, :], func=AF.Exp), r=[b_tmS], w=[b_PTs[si]])
                else:
                    OP("act", lambda e, si=si: e.activation(out=PTs[si][:], in_=PF[si][:, 0:64], func=AF.Exp, scale=0.125), r=[bPF[si]], w=[b_PTs[si]])

            def s_PV(kt, sbi=sbi):
                own = kt == 64
                si = kt % 2
                L = 8 if own else 128
                rhsv = Vs_own[:, sbi, :] if own else kpb[kt % 4][:, 0, :]
                OP("pe", lambda e, si=si, L=L, rhsv=rhsv, kt=kt: e.matmul(PF[2][0:64, :], lhsT=PTs[si][0:L, :], rhs=rhsv, start=(kt == 0), stop=(kt == 64)),
                   r=[b_PTs[si], b_Vs_own, b_kpb[kt % 4]], w=[bPF[2]])
                OP("pe", lambda e, si=si, L=L, kt=kt: e.matmul(PF[3][0:64, 0:1], lhsT=PTs[si][0:L, :], rhs=onesb[0:L, 0:1], start=(kt == 0), stop=(kt == 64)),
                   r=[b_PTs[si], b_onesb], w=[bPF[3]])

            s_load(0); s_load(1); s_load(2)
            s_S(0)
            for kt in range(65):
                s_load(kt + 3)
                if kt + 1 < 65:
                    s_S(kt + 1)
                s_exp(kt)
                s_PV(kt)
            OP("dve", lambda e: e.reciprocal(out=recs[:], in_=PF[3][0:64, 0:1]), r=[bPF[3]], w=[b_recs])
            OP("dve", lambda e: e.tensor_scalar(out=osb[:], in0=PF[2][0:64, :], scalar1=recs[:, 0:1], scalar2=None, op0=ALU.mult), r=[bPF[2], b_recs], w=[b_osb])
            for h in range(8):
                kb.dma("sp", lambda e, h=h, sbi=sbi: e.dma_start(out=mix[NM + sbi * 8:NM + sbi * 8 + 8, h * 64:(h + 1) * 64], in_=osb[h * 8:(h + 1) * 8, h * 64:(h + 1) * 64]),
                       reads=[b_osb], writes=[b_mix3])
        kb.barrier()
        ps_.close()
        if stop_after == "satt":
            kb.final_wait("sp")
            kb.emit()
            return nc, dbg_outs

        b_mix2 = B("mix2")
        NT = 4096 + NS
        def ml_pass(hp):
            h0 = 2 * hp
            p2 = contextlib.ExitStack()
            mqT = sbt(f"h{hp}_" "mqT", [128, 2, NM + NS], BF16, p2); b_mqT = B("mqT")
            mkT = sbt(f"h{hp}_" "mkT", [128, 2, NM + NS], BF16, p2); b_mkT = B("mkT")
            mkt = sbt(f"h{hp}_" "mkt", [128, 32, 256], BF16, p2); b_mkt = B("mkt")
            mva = sbt(f"h{hp}_" "mva", [128, 32, 2, 129], BF16, p2); b_mva = B("mva")
            sgm = sbt(f"h{hp}_" "sgm", [128, 16, 256], BF16, p2); b_sgm = B("sgm")
            mkt_s = sbt(f"h{hp}_" "mkt_s", [8, 4, 256], BF16, p2); b_mkt_s = B("mkt_s")
            mva_s = sbt(f"h{hp}_" "mva_s", [8, 4, 2, 129], BF16, p2); b_mva_s = B("mva_s")
            sgm_s = sbt(f"h{hp}_" "sgm_s", [8, 4, 256], BF16, p2); b_sgm_s = B("sgm_s")
            OP("pool", lambda e: e.memset(mva[:, :, :, 128:129], 1.0), w=[b_mva])
            OP("pool", lambda e: e.memset(mva_s[:, :, :, 128:129], 1.0), w=[b_mva_s])
            Grow = sbt(f"h{hp}_" "Grow", [2, NT], F32, p2); b_Grow = B("Grow")
            Urow = sbt(f"h{hp}_" "Urow", [2, NT], F32, p2); b_Urow = B("Urow")
            Brow = sbt(f"h{hp}_" "Brow", [2, NT], F32, p2); b_Brow = B("Brow")
            mlgb = sbt(f"h{hp}_" "mlgb", [128, 256], F32, p2); b_mlgb = B("mlgb")
            kb.dma("sp", lambda e, h0=h0: e.dma_start(out=mlgb[:], in_=mlg[0:1, h0 * 128:h0 * 128 + 256].partition_broadcast(128)), writes=[b_mlgb])
            bgt = sbt(f"h{hp}_" "bgt", [2, 2], F32, p2); b_bgt = B("bgt")
            kb.dma("sp", lambda e, h0=h0: e.dma_start(out=bgt[:, 0:1], in_=bg[h0:h0 + 2, :]), writes=[b_bgt])
            kb.dma("sp", lambda e, h0=h0: e.dma_start(out=bgt[:, 1:2], in_=bg[4 + h0:4 + h0 + 2, :]), writes=[b_bgt])
            sm0 = sbt(f"h{hp}_" "sm0", [2, 4], F32, p2); b_sm0 = B("sm0")
            kb.dma("sp", lambda e, h0=h0: e.dma_start(out=sm0[:], in_=smi[h0:h0 + 2, :]), writes=[b_sm0])
            ones2 = sbt(f"h{hp}_" "ones2", [2, 512], F32, p2); b_ones2 = B("ones2")
            OP("pool", lambda e: e.memset(ones2[:], 1.0), w=[b_ones2])
            oh2 = sbt(f"h{hp}_" "oh2", [2, 2, 128], F32, p2); b_oh2 = B("oh2")
            OP("pool", lambda e: e.memset(oh2[:], 1.0), w=[b_oh2])
            OP("pool", lambda e: e.affine_select(out=oh2[:], in_=oh2[:], pattern=[[1, 2], [0, 128]], compare_op=ALU.is_equal, fill=0.0, base=0,
                                                 channel_multiplier=-1), r=[b_oh2], w=[b_oh2])
            cmask = sbt(f"h{hp}_" "cmask", [128, 128], F32, p2); b_cmask = B("cmask")
            OP("pool", lambda e: e.memset(cmask[:], 0.0), w=[b_cmask])
            OP("pool", lambda e: e.affine_select(out=cmask[:], in_=cmask[:], pattern=[[1, 128]], compare_op=ALU.is_ge, fill=-NEG, base=0,
                                                 channel_multiplier=-1), r=[b_cmask], w=[b_cmask])
            UT = sbt(f"h{hp}_" "UT", [128, 33, 2], F32, p2); GT = sbt(f"h{hp}_" "GT", [128, 33, 2], F32, p2); mT = sbt(f"h{hp}_" "mT", [128, 33, 2], F32, p2); EM = sbt(f"h{hp}_" "EM", [128, 33, 2], F32, p2)
            b_UT = B("UT"); b_GT = B("GT"); b_mT = B("mT"); b_EM = B("EM")
            UTs = sbt(f"h{hp}_" "UTs", [8, 4, 2], F32, p2); GTs = sbt(f"h{hp}_" "GTs", [8, 4, 2], F32, p2); mTs = sbt(f"h{hp}_" "mTs", [8, 4, 2], F32, p2); EMs = sbt(f"h{hp}_" "EMs", [8, 4, 2], F32, p2)
            b_UTs = B("UTs"); b_GTs = B("GTs"); b_mTs = B("mTs"); b_EMs = B("EMs")

            p2a = contextlib.ExitStack()
            wB = sbt(f"h{hp}_" "wB", [128, 8, 1032], BF16, p2a); b_wB = B("wB")
            for c in range(8):
                rows = slice(c * 128, (c + 1) * 128)
                for k4 in range(4):
                    kb.dma("pool", lambda e, c=c, k4=k4, rows=rows, h0=h0: e.dma_start(out=wB[:, c, k4 * 256:(k4 + 1) * 256],
                                                                                    in_=w_in[rows, 1536 + k4 * 512 + h0 * 128:1536 + k4 * 512 + h0 * 128 + 256]), writes=[b_wB])
                kb.dma("pool", lambda e, c=c, rows=rows: e.dma_start(out=wB[:, c, 1024:1032], in_=w_in[rows, 3584:3592]), writes=[b_wB])
            xnT2 = sbt(f"h{hp}_" "xnT2", [128, 8, 512], BF16, p2a); b_xnT2 = B("xnT2")
            gTs = sbt(f"h{hp}_" "gTs", [8, 512], F32, p2a); b_gTs = B("gTs")
            sgt = sbt(f"h{hp}_" "sgt", [128, 256], F32, p2a); b_sgt = B("sgt")
            KS = 128.0 ** -0.5
            groups = [("ctx", g) for g in range(4)] + [("main", g) for g in range(4)] + [("smp", 0)]
            for kind, g in groups:
                ntok = NS if kind == "smp" else 512
                gidx = {"ctx": g, "main": 4 + g, "smp": 8}[kind]
                kb.dma("sp", lambda e, gidx=gidx, ntok=ntok: e.dma_start(out=xnT2[:, :, 0:ntok], in_=xnTd[gidx].rearrange("p (c t) -> p c t", t=512)[:, :, 0:ntok]),
                       reads=[b_xnTd], writes=[b_xnT2])
                gcol0 = {"ctx": g * 512, "main": 2048 + g * 512, "smp": 4096}[kind]
                for c in range(8):
                    OP("pe", lambda e, c=c, ntok=ntok: e.matmul(PF[0][0:8, 0:ntok], lhsT=wB[:, c, 1024:1032], rhs=xnT2[:, c, 0:ntok], start=(c == 0), stop=(c == 7)),
                       r=[b_wB, b_xnT2], w=[bPF[0]])
                OP("dve", lambda e, ntok=ntok: e.tensor_copy(out=gTs[:, 0:ntok], in_=PF[0][0:8, 0:ntok]), r=[bPF[0]], w=[b_gTs])
                kb.dma("sp", lambda e, ntok=ntok, gcol0=gcol0, h0=h0: e.dma_start(out=Urow[:, gcol0:gcol0 + ntok], in_=gTs[h0:h0 + 2, 0:ntok]), reads=[b_gTs], writes=[b_Urow])
                kb.dma("sp", lambda e, ntok=ntok, gcol0=gcol0, h0=h0: e.dma_start(out=Brow[:, gcol0:gcol0 + ntok], in_=gTs[4 + h0:4 + h0 + 2, 0:ntok]), reads=[b_gTs], writes=[b_Brow])
                if kind != "ctx":
                    fcol0 = g * 512 if kind == "main" else NM
                    for l in range(2):
                        for which in ("q", "k"):
                            wc0 = (0 if which == "q" else 256) + l * 128
                            pf = 1 + (which == "k")
                            for c in range(8):
                                OP("pe", lambda e, c=c, pf=pf, wc0=wc0, ntok=ntok: e.matmul(PF[pf][:, 0:ntok], lhsT=wB[:, c, wc0:wc0 + 128], rhs=xnT2[:, c, 0:ntok],
                                                                                          start=(c == 0), stop=(c == 7)), r=[b_wB, b_xnT2], w=[bPF[pf]])
                            if which == "q":
                                evac(mqT[:, l, fcol0:fcol0 + ntok], PF[pf][:, 0:ntok], [bPF[pf]], [b_mqT])
                            else:
                                evac(mkT[:, l, fcol0:fcol0 + ntok], PF[pf][:, 0:ntok], [bPF[pf]], [b_mkT], scale=KS)
                if kind == "smp":
                    tiles = [(sbi * 8, 8, sbi) for sbi in range(4)]
                else:
                    tiles = [(t * 128, 128, (g if kind == "ctx" else 4 + g) * 4 + t) for t in range(4)]
                for (c0, n, ta) in tiles:
                    for c in range(8):
                        OP("pe", lambda e, c=c, c0=c0, n=n: e.matmul(PF[3][0:n, :], lhsT=xnT2[:, c, c0:c0 + n], rhs=wB[:, c, 256:768], start=(c == 0), stop=(c == 7)),
                           r=[b_wB, b_xnT2], w=[bPF[3]])
                    if kind == "smp":
                        kdst, b_kd = mkt_s[:, ta, :], b_mkt_s
                        vdst, b_vd = mva_s[:, ta, :, 0:128], b_mva_s
                    else:
                        kdst, b_kd = mkt[:, ta, :], b_mkt
                        vdst, b_vd = mva[:, ta, :, 0:128], b_mva
                    OP("dve", lambda e, n=n, kdst=kdst: e.tensor_scalar(out=kdst, in0=PF[3][0:n, 0:256], scalar1=KS, scalar2=None, op0=ALU.mult), r=[bPF[3]], w=[b_kd])
                    OP("dve", lambda e, n=n, vdst=vdst: e.tensor_copy(out=vdst, in_=PF[3][0:n, 256:512].rearrange("p (l d) -> p l d", d=128)), r=[bPF[3]], w=[b_vd])
                    if kind != "ctx":
                        for c in range(8):
                            OP("pe", lambda e, c=c, c0=c0, n=n: e.matmul(PF[4][0:n, 0:256], lhsT=xnT2[:, c, c0:c0 + n], rhs=wB[:, c, 768:1024], start=(c == 0), stop=(c == 7)),
                               r=[b_wB, b_xnT2], w=[bPF[4]])
                        OP("dve", lambda e, n=n: e.tensor_copy(out=sgt[0:n, :], in_=PF[4][0:n, 0:256]), r=[bPF[4]], w=[b_sgt])
                        OP("act", lambda e, n=n: e.activation(out=sgt[0:n, :], in_=sgt[0:n, :], func=AF.Sigmoid), r=[b_sgt], w=[b_sgt])
                        if kind == "smp":
                            sdst, b_sd = sgm_s[:, ta, :], b_sgm_s
                        else:
                            sdst, b_sd = sgm[:, ta - 16, :], b_sgm
                        OP("dve", lambda e, n=n, sdst=sdst: e.tensor_tensor(out=sdst, in0=sgt[0:n, :], in1=mlgb[0:n, :], op=ALU.mult), r=[b_sgt, b_mlgb], w=[b_sd])
            kb.barrier()
            p2a.close()
            if stop_after == "ml_a":
                kb.final_wait("sp"); kb.emit(); p2.close(); raise StopIteration

            nbf = sbt(f"h{hp}_" "nbf", [2, 1], F32, p2); b_nbf = B("nbf")
            OP("dve", lambda e: e.tensor_scalar(out=nbf[:], in0=bgt[:, 1:2], scalar1=-1.0, scalar2=None, op0=ALU.mult), r=[b_bgt], w=[b_nbf])
            OP("act", lambda e: e.activation(out=Brow[:], in_=Brow[:], func=AF.Exp, scale=-1.0, bias=nbf[:, 0:1]), r=[b_Brow, b_nbf], w=[b_Brow])
            OP("act", lambda e: e.activation(out=Brow[:], in_=Brow[:], func=AF.Ln, scale=1.0, bias=1.0), r=[b_Brow], w=[b_Brow])
            OP("dve", lambda e: e.tensor_scalar(out=Brow[:], in0=Brow[:], scalar1=-1.0, scalar2=None, op0=ALU.mult), r=[b_Brow], w=[b_Brow])
            OP("dve", lambda e: e.tensor_scalar(out=Urow[:], in0=Urow[:], scalar1=bgt[:, 0:1], scalar2=None, op0=ALU.add), r=[b_Urow, b_bgt], w=[b_Urow])
            OP("dve", lambda e: e.tensor_scalar(out=Brow[:, 0:2048], in0=Brow[:, 0:2048], scalar1=cfb[0:2, 0:1], scalar2=None, op0=ALU.mult), r=[b_Brow, b_cfb], w=[b_Brow])
            OP("dve", lambda e: e.tensor_scalar(out=Urow[:, 0:2048], in0=Urow[:, 0:2048], scalar1=cfb[0:2, 0:1], scalar2=cfb[0:2, 1:2], op0=ALU.mult, op1=ALU.add),
               r=[b_Urow, b_cfb], w=[b_Urow])
            segs = [(i * 512, 512, None if i == 0 else i * 512 - 1) for i in range(8)] + [(4096 + sbi * 8, 8, None) for sbi in range(4)]
            for (c0, n, prev) in segs:
                init = 0.0 if prev is None else Brow[:, prev:prev + 1]
                OP("dve", lambda e, c0=c0, n=n, init=init: e.tensor_tensor_scan(out=Brow[:, c0:c0 + n], data0=ones2[:, 0:n], data1=Brow[:, c0:c0 + n], initial=init,
                                                                                op0=ALU.mult, op1=ALU.add), r=[b_Brow, b_ones2], w=[b_Brow])
            OP("dve", lambda e: e.tensor_tensor(out=Urow[:], in0=Urow[:], in1=Brow[:], op=ALU.subtract), r=[b_Urow, b_Brow], w=[b_Urow])
            for si_, (c0, n, prev) in enumerate(segs):
                if c0 >= 4096:
                    sbi = (c0 - 4096) // 8
                    init = sm0[:, sbi:sbi + 1]
                else:
                    init = 0.0 if prev is None else Grow[:, prev:prev + 1]
                OP("dve", lambda e, c0=c0, n=n, init=init: e.tensor_tensor_scan(out=Grow[:, c0:c0 + n], data0=Urow[:, c0:c0 + n], data1=Urow[:, c0:c0 + n], initial=init,
                                                                                op0=ALU.max, op1=ALU.max), r=[b_Urow, b_Grow, b_sm0], w=[b_Grow])
            OP("dve", lambda e: e.tensor_tensor(out=Brow[:], in0=Brow[:], in1=Grow[:], op=ALU.add), r=[b_Grow, b_Brow], w=[b_Brow])
            for (row, b_row, colt, b_colt, colts, b_colts) in ((Urow, b_Urow, UT, b_UT, UTs, b_UTs), (Grow, b_Grow, GT, b_GT, GTs, b_GTs), (Brow, b_Brow, mT, b_mT, mTs, b_mTs)):
                for ck in range(32):
                    OP("pe", lambda e, ck=ck, row=row: e.transpose(out=PF[5][:, ck * 2:ck * 2 + 2], in_=row[:, ck * 128:(ck + 1) * 128], identity=identf[0:2, 0:2]),
                       r=[b_row, b_idf], w=[bPF[5]])
                OP("dve", lambda e, colt=colt: e.tensor_copy(out=colt[:, 0:32, :], in_=PF[5][:, 0:64].rearrange("p (c l) -> p c l", l=2)), r=[bPF[5]], w=[b_colt])
                for sbi in range(4):
                    OP("pe", lambda e, sbi=sbi, row=row: e.transpose(out=PF[5][0:8, sbi * 2:sbi * 2 + 2], in_=row[:, 4096 + sbi * 8:4096 + sbi * 8 + 8], identity=identf[0:2, 0:2]),
                       r=[b_row, b_idf], w=[bPF[5]])
                OP("dve", lambda e, colts=colts: e.tensor_copy(out=colts[:], in_=PF[5][0:8, 0:8].rearrange("p (c l) -> p c l", l=2)), r=[bPF[5]], w=[b_colts])
            OP("act", lambda e: e.activation(out=EM[:, 0:32, :], in_=mT[:, 0:32, :], func=AF.Exp, scale=-1.0), r=[b_mT], w=[b_EM])
            OP("act", lambda e: e.activation(out=EMs[:], in_=mTs[:], func=AF.Exp, scale=-1.0), r=[b_mTs], w=[b_EMs])

            if stop_after == "ml_b":
                kb.final_wait("sp"); kb.emit(); p2.close(); raise StopIteration
            Cf = sbt(f"h{hp}_" "Cf", [128, 2, 129], F32, p2); b_Cf = B("Cf")
            Cb = sbt(f"h{hp}_" "Cb", [128, 2, 129], BF16, p2); b_Cb = B("Cb")
            gprev = sbt(f"h{hp}_" "gprev", [128, 2], F32, p2); b_gprev = B("gprev")
            gend = [sbt(f"h{hp}_" f"gend{i}", [128, 2], F32, p2) for i in range(2)]; b_gend = [B(f"gend{i}") for i in range(2)]
            g2 = sbt(f"h{hp}_" "g2", [128, 2], F32, p2); b_g2 = B("g2")
            gtok = sbt(f"h{hp}_" "gtok", [128, 2], F32, p2); b_gtok = B("gtok")
            gst = sbt(f"h{hp}_" "gst", [128, 2], F32, p2); b_gst = B("gst")
            wst = sbt(f"h{hp}_" "wst", [128, 2], F32, p2); b_wst = B("wst")
            tmpD = sbt(f"h{hp}_" "tmpD", [128, 2, 128], F32, p2); b_tmpD = B("tmpD")
            sT = sbt(f"h{hp}_" "sT", [128, 2, 128], BF16, p2); b_sT = B("sT")
            hs = [sbt(f"h{hp}_" f"hs{i}", [128, 2, 129], F32, p2) for i in range(2)]; b_hs = [B(f"hs{i}") for i in range(2)]
            nd = sbt(f"h{hp}_" "nd", [128, 2, 129], F32, p2); b_nd = B("nd")
            dab = sbt(f"h{hp}_" "dab", [128, 2], F32, p2); b_dab = B("dab")
            ssh = sbt(f"h{hp}_" "ssh", [128, 2], F32, p2); b_ssh = B("ssh")
            junk = sbt(f"h{hp}_" "junk", [128, 128], F32, p2); b_junk = B("junk")
            gv = sbt(f"h{hp}_" "gv", [128, 2, 129], BF16, p2); b_gv = B("gv")
            mlb = [sbt(f"h{hp}_" f"mlb{i}", [128, 256], BF16, p2) for i in range(2)]; b_mlb = [B(f"mlb{i}") for i in range(2)]

            def chunkA(L, ck_cols, UTv, GTv, EMv, kT_v, qT_v, kt_v, va_v, sg_v, full, mix_rows, mi, gcol):
                for l in range(2):
                    OP("pe", lambda e, l=l: e.matmul(PF[4][0:L, l * 128:l * 128 + L], lhsT=oh2[:, l, 0:L], rhs=Grow[:, ck_cols], start=True, stop=True),
                       r=[b_oh2, b_Grow], w=[bPF[4]])
                OP("dve", lambda e: e.tensor_copy(out=gend[mi][0:L, :].rearrange("p (l o) -> p l o", o=1), in_=PF[4][0:L, 0:256].rearrange("p (l t) -> p l t", t=128)[:, :, L - 1:L]),
                   r=[bPF[4]], w=[b_gend[mi]])
                if not full:
                    return
                for l in range(2):
                    OP("pe", lambda e, l=l: e.matmul(PF[0][0:L, l * 128:l * 128 + L], lhsT=kT_v(l), rhs=qT_v(l), start=True, stop=True), r=[b_mkT, b_mqT], w=[bPF[0]])
                    OP("dve", lambda e, l=l: e.scalar_tensor_tensor(out=tmpD[0:L, l, 0:L], in0=PF[4][0:L, l * 128:l * 128 + L], scalar=UTv[:, l:l + 1], in1=cmask[0:L, 0:L],
                                                                    op0=ALU.subtract, op1=ALU.add), r=[bPF[4], b_UT, b_UTs, b_cmask], w=[b_tmpD])
                OP("act", lambda e: e.activation(out=tmpD[0:L, :, 0:L], in_=tmpD[0:L, :, 0:L], func=AF.Exp, scale=-1.0), r=[b_tmpD], w=[b_tmpD])
                OP("dve", lambda e: e.tensor_tensor(out=sT[0:L, :, 0:L], in0=PF[0][0:L, 0:256].rearrange("p (l t) -> p l t", t=128)[:, :, 0:L], in1=tmpD[0:L, :, 0:L], op=ALU.mult),
                   r=[bPF[0], b_tmpD], w=[b_sT])
                for l in range(2):
                    OP("pe", lambda e, l=l: e.matmul(PF[1][0:L, l * 129:(l + 1) * 129], lhsT=sT[0:L, l, 0:L], rhs=va_v(l), start=True, stop=True),
                       r=[b_sT, b_mva, b_mva_s], w=[bPF[1]])
                OP("dve", lambda e: e.tensor_copy(out=hs[mi][0:L, :, :], in_=PF[1][0:L, 0:258].rearrange("p (l c) -> p l c", c=129)), r=[bPF[1]], w=[b_hs[mi]])

            def chunkB(L, ck_cols, UTv, GTv, EMv, kT_v, qT_v, kt_v, va_v, sg_v, full, mix_rows, mi, gcol):
                if not full:
                    return
                for l in range(2):
                    OP("pe", lambda e, l=l: e.matmul(PF[2][0:L, l * 129:(l + 1) * 129], lhsT=qT_v(l), rhs=Cb[:, l, :], start=True, stop=True),
                       r=[b_mqT, b_Cb], w=[bPF[2]])
                OP("dve", lambda e: e.tensor_tensor(out=g2[0:L, :], in0=gprev[0:L, :], in1=GTv, op=ALU.subtract), r=[b_gprev, b_GT, b_GTs], w=[b_g2])
                OP("act", lambda e: e.activation(out=wst[0:L, :], in_=g2[0:L, :], func=AF.Exp), r=[b_g2], w=[b_wst])
                for l in range(2):
                    OP("dve", lambda e, l=l: e.scalar_tensor_tensor(out=nd[0:L, l, :], in0=PF[2][0:L, l * 129:(l + 1) * 129], scalar=wst[0:L, l:l + 1], in1=hs[mi][0:L, l, :],
                                                                    op0=ALU.mult, op1=ALU.add), r=[bPF[2], b_wst, b_hs[mi]], w=[b_nd])
                    OP("dve", lambda e, l=l: e.scalar_tensor_tensor(out=dab[0:L, l:l + 1], in0=nd[0:L, l, 128:129], scalar=-1.0, in1=nd[0:L, l, 128:129], op0=ALU.mult, op1=ALU.max),
                       r=[b_nd], w=[b_dab])
                    OP("dve", lambda e, l=l: e.tensor_scalar(out=dab[0:L, l:l + 1], in0=dab[0:L, l:l + 1], scalar1=EMv[:, l:l + 1], scalar2=None, op0=ALU.max),
                       r=[b_dab, b_EM, b_EMs], w=[b_dab])
                OP("dve", lambda e: e.reciprocal(out=dab[0:L, :], in_=dab[0:L, :]), r=[b_dab], w=[b_dab])
                for l in range(2):
                    OP("act", lambda e, l=l: e.activation(out=junk[0:L, :], in_=nd[0:L, l, 0:128], func=AF.Square, scale=dab[0:L, l:l + 1], accum_out=ssh[0:L, l:l + 1]),
                       r=[b_nd, b_dab], w=[b_junk, b_ssh])
                OP("act", lambda e: e.activation(out=ssh[0:L, :], in_=ssh[0:L, :], func=AF.Sqrt, scale=1.0 / 128, bias=EPS), r=[b_ssh], w=[b_ssh])
                OP("dve", lambda e: e.reciprocal(out=ssh[0:L, :], in_=ssh[0:L, :]), r=[b_ssh], w=[b_ssh])
                OP("dve", lambda e: e.tensor_tensor(out=ssh[0:L, :], in0=ssh[0:L, :], in1=dab[0:L, :], op=ALU.mult), r=[b_ssh, b_dab], w=[b_ssh])
                for l in range(2):
                    OP("dve", lambda e, l=l: e.scalar_tensor_tensor(out=mlb[mi][0:L, l * 128:(l + 1) * 128], in0=nd[0:L, l, 0:128], scalar=ssh[0:L, l:l + 1], in1=sg_v(l),
                                                                    op0=ALU.mult, op1=ALU.mult), r=[b_nd, b_ssh, b_sgm, b_sgm_s], w=[b_mlb[mi]])
                kb.dma("sp", lambda e: e.dma_start(out=mix[mix_rows, 512 + h0 * 128:512 + h0 * 128 + 256], in_=mlb[mi][0:L, :]), reads=[b_mlb[mi]], writes=[b_mix2])

            def chunkC(L, ck_cols, UTv, GTv, EMv, kT_v, qT_v, kt_v, va_v, sg_v, full, mix_rows, mi, gcol):
                gend_bcast(gcol)
                OP("dve", lambda e: e.tensor_tensor(out=gtok[0:L, :], in0=UTv, in1=gend[mi][0:L, :], op=ALU.subtract), r=[b_UT, b_UTs, b_gend[mi]], w=[b_gtok])
                OP("act", lambda e: e.activation(out=gtok[0:L, :], in_=gtok[0:L, :], func=AF.Exp), r=[b_gtok], w=[b_gtok])
                for l in range(2):
                    OP("dve", lambda e, l=l: e.tensor_scalar(out=gv[0:L, l, :], in0=va_v(l), scalar1=gtok[0:L, l:l + 1], scalar2=None, op0=ALU.mult),
                       r=[b_mva, b_mva_s, b_gtok], w=[b_gv])
                    OP("pe", lambda e, l=l: e.matmul(PF[3][:, l * 129:(l + 1) * 129], lhsT=kt_v(l), rhs=gv[0:L, l, :], start=True, stop=True), r=[b_mkt, b_mkt_s, b_gv], w=[bPF[3]])
                OP("dve", lambda e: e.tensor_tensor(out=g2[:, :], in0=gprev[:, :], in1=gendb[:, :], op=ALU.subtract), r=[b_gprev, b_gendb], w=[b_g2])
                OP("act", lambda e: e.activation(out=gst[:, :], in_=g2[:, :], func=AF.Exp), r=[b_g2], w=[b_gst])
                for l in range(2):
                    OP("dve", lambda e, l=l: e.scalar_tensor_tensor(out=Cf[:, l, :], in0=Cf[:, l, :], scalar=gst[:, l:l + 1], in1=PF[3][:, l * 129:(l + 1) * 129],
                                                                    op0=ALU.mult, op1=ALU.add), r=[b_Cf, b_gst, bPF[3]], w=[b_Cf])
                OP("act", lambda e: e.copy(out=Cb[:], in_=Cf[:]), r=[b_Cf], w=[b_Cb])
                OP("dve", lambda e: e.tensor_copy(out=gprev[:], in_=gendb[:]), r=[b_gendb], w=[b_gprev])

            gendb = sbt(f"h{hp}_" "gendb", [128, 2], F32, p2); b_gendb = B("gendb")

            def gend_bcast(col):
                for l in range(2):
                    OP("pe", lambda e, l=l: e.matmul(PF[5][:, l:l + 1], lhsT=oh2[:, l, :], rhs=Grow[:, col:col + 1], start=True, stop=True), r=[b_oh2, b_Grow], w=[bPF[5]])
                OP("dve", lambda e: e.tensor_copy(out=gendb[:], in_=PF[5][:, 0:2]), r=[bPF[5]], w=[b_gendb])

            OP("pool", lambda e: e.memset(Cf[:], 0.0), w=[b_Cf])
            OP("pool", lambda e: e.memset(Cb[:], 0.0), w=[b_Cb])
            OP("pool", lambda e: e.memset(gprev[:], 0.0), w=[b_gprev])
            def pargs(ck):
                full = ck >= 16
                tm = ck - 16
                cols = slice(ck * 128, (ck + 1) * 128)
                fc = slice(tm * 128, (tm + 1) * 128)
                return (128, cols, UT[:, ck, :], GT[:, ck, :], EM[:, ck, :],
                        lambda l, fc=fc: mkT[:, l, fc], lambda l, fc=fc: mqT[:, l, fc], lambda l, ck=ck: mkt[:, ck, l * 128:(l + 1) * 128],
                        lambda l, ck=ck: mva[:, ck, l, :], lambda l, tm=tm: sgm[:, tm, l * 128:(l + 1) * 128], full,
                        slice(tm * 128, (tm + 1) * 128), ck % 2, ck * 128 + 127)
            chunkA(*pargs(0))
            for ck in range(32):
                if ck + 1 < 32:
                    chunkA(*pargs(ck + 1))
                chunkB(*pargs(ck))
                chunkC(*pargs(ck))
            if stop_after == "ml_c":
                kb.final_wait("sp"); kb.emit(); p2.close(); raise StopIteration
            for l in range(2):
                h = h0 + l
                store(C_p[h * 128:(h + 1) * 128, :], Cf[:, l, 0:128], b_Cf)
                store(n_p[h * 128:(h + 1) * 128, :], Cf[:, l, 128:129], b_Cf)
            store(m_p[h0:h0 + 2, :], Brow[:, 4095:4096], b_Brow)
            for sbi in range(4):
                for l in range(2):
                    h = h0 + l
                    r0 = (sbi * 4 + h) * 128
                    kb.dma("sp", lambda e, l=l, r0=r0: e.dma_start(out=Cf[:, l, 0:128], in_=sC[r0:r0 + 128, :]), writes=[b_Cf])
                    kb.dma("sp", lambda e, l=l, r0=r0: e.dma_start(out=Cf[:, l, 128:129], in_=sn[r0:r0 + 128, :]), writes=[b_Cf])
                    kb.dma("sp", lambda e, l=l, h=h, sbi=sbi: e.dma_start(out=gprev[:, l:l + 1], in_=smi[h:h + 1, sbi:sbi + 1].partition_broadcast(128)), writes=[b_gprev])
                OP("act", lambda e: e.copy(out=Cb[:], in_=Cf[:]), r=[b_Cf], w=[b_Cb])
                c0 = 4096 + sbi * 8
                fc = slice(NM + sbi * 8, NM + sbi * 8 + 8)
                sargs = (8, slice(c0, c0 + 8), UTs[:, sbi, :], GTs[:, sbi, :], EMs[:, sbi, :],
                         lambda l, fc=fc: mkT[:, l, fc], lambda l, fc=fc: mqT[:, l, fc], lambda l, sbi=sbi: mkt_s[:, sbi, l * 128:(l + 1) * 128],
                         lambda l, sbi=sbi: mva_s[:, sbi, l, :], lambda l, sbi=sbi: sgm_s[:, sbi, l * 128:(l + 1) * 128], True,
                         slice(NM + sbi * 8, NM + sbi * 8 + 8), sbi % 2, c0 + 7)
                chunkA(*sargs)
                chunkB(*sargs)
                chunkC(*sargs)
                for l in range(2):
                    h = h0 + l
                    r0 = (sbi * 4 + h) * 128
                    store(C_s[r0:r0 + 128, :], Cf[:, l, 0:128], b_Cf)
                    store(n_s[r0:r0 + 128, :], Cf[:, l, 128:129], b_Cf)
                store(m_s[sbi * 4 + h0:sbi * 4 + h0 + 2, :], Brow[:, c0 + 7:c0 + 8], b_Brow)
            kb.barrier()
            p2.close()
        try:
            for hp in range(2):
                ml_pass(hp)
        except StopIteration:
            return nc, dbg_outs
        if stop_after == "ml":
            kb.final_wait("sp")
            kb.emit()
            return nc, dbg_outs

        p3 = contextlib.ExitStack()
        wO = sbt("wO", [128, 8, 1024], BF16, p3); b_wO = B("wO")
        wU = sbt("wU", [128, 8, 4096], BF16, p3); b_wU = B("wU")
        wD = sbt("wD", [128, 32, 1024], BF16, p3); b_wD = B("wD")
        for c in range(8):
            load_w(wO[:, c, :], w_out[c * 128:(c + 1) * 128, :], b_wO, 1024)
        for c in range(8):
            load_w(wU[:, c, :], w_up[c * 128:(c + 1) * 128, :], b_wU, 4096)
        for f in range(32):
            load_w(wD[:, f, :], w_down[f * 128:(f + 1) * 128, :], b_wD, 1024)
        kb.dma("sp", lambda e: e.dma_start(out=g1[:], in_=nffn[0:1, :].partition_broadcast(128)), writes=[b_g1])
        g3 = sbt("g3", [128, D], F32, p3); b_g3 = B("g3")
        kb.dma("sp", lambda e: e.dma_start(out=g3[:], in_=nfin[0:1, :].partition_broadcast(128)), writes=[b_g3])
        mixb = [sbt(f"mixb{i}", [128, D], BF16, p3) for i in range(2)]; b_mixb = [B(f"mixb{i}") for i in range(2)]
        mixT = sbt("mixT", [128, 8, 256], BF16, p3); b_mixT = B("mixT")
        xn2T = sbt("xn2T", [128, 8, 256], BF16, p3); b_xn2T = B("xn2T")
        uT = sbt("uT", [128, 32, 256], BF16, p3); b_uT = B("uT")
        ur = [sbt(f"ur{i}", [128, 256], F32, p3) for i in range(2)]; b_ur = [B(f"ur{i}") for i in range(2)]
        yo = sbt("yo", [128, D], F32, p3); b_yo = B("yo")
        all_mix = [b_mix, b_mix2, b_mix3]
        fgroups = [("main", gi) for gi in range(8)] + [("smp", 0)]
        for kind, gi in fgroups:
            if kind == "main":
                tiles = [(gi * 256 + t * 128, 128, t * 128) for t in range(2)]
                xsrc, ydst = xm, y_m
            else:
                tiles = [(0, NS, 0)]
                xsrc, ydst = xs, y_s
            ntok = sum(n for _, n, _ in tiles)
            for ti, (r0, n, c0) in enumerate(tiles):
                mr0 = r0 if kind == "main" else NM
                kb.dma("sp", lambda e, ti=ti, r0=r0, n=n, xsrc=xsrc: e.dma_start(out=xt[ti][0:n, :], in_=xsrc[r0:r0 + n, :]), writes=[b_xt[ti]])
                kb.dma("sp", lambda e, ti=ti, mr0=mr0, n=n: e.dma_start(out=mixb[ti][0:n, :], in_=mix[mr0:mr0 + n, :]), reads=all_mix, writes=[b_mixb[ti]])
                to_featmajor(mixb[ti], b_mixb[ti], n, mixT[:, :, c0:c0 + n], b_mixT)
            for ti, (r0, n, c0) in enumerate(tiles):
                for hf in range(2):
                    for c in range(8):
                        OP("pe", lambda e, c=c, hf=hf, n=n, c0=c0: e.matmul(PF[hf][0:n, :], lhsT=mixT[:, c, c0:c0 + n], rhs=wO[:, c, hf * 512:(hf + 1) * 512], start=(c == 0), stop=(c == 7)),
                           r=[b_mixT, b_wO], w=[bPF[hf]])
                    OP("dve", lambda e, ti=ti, hf=hf, n=n: e.tensor_tensor(out=xt[ti][0:n, hf * 512:(hf + 1) * 512], in0=PF[hf][0:n, :], in1=xt[ti][0:n, hf * 512:(hf + 1) * 512], op=ALU.add),
                       r=[bPF[hf], b_xt[ti]], w=[b_xt[ti]])
                rms_scale(xt[ti][0:n, :], b_xt[ti], n, ti, xnb[ti][0:n, :], b_xnb[ti])
                OP("dve", lambda e, ti=ti, n=n: e.scalar_tensor_tensor(out=xnb[ti][0:n, :], in0=xt[ti][0:n, :], scalar=rstd[ti][0:n, 0:1], in1=g1[0:n, :], op0=ALU.mult, op1=ALU.mult),
                   r=[b_xt[ti], b_rstd[ti], b_g1], w=[b_xnb[ti]])
                to_featmajor(xnb[ti], b_xnb[ti], n, xn2T[:, :, c0:c0 + n], b_xn2T)
            for f in range(32):
                pf = 2 + f % 2
                ui = f % 2
                for c in range(8):
                    OP("pe", lambda e, c=c, f=f, pf=pf, ntok=ntok: e.matmul(PF[pf][:, 0:ntok], lhsT=wU[:, c, f * 128:(f + 1) * 128], rhs=xn2T[:, c, 0:ntok], start=(c == 0), stop=(c == 7)),
                       r=[b_wU, b_xn2T], w=[bPF[pf]])
                OP("dve", lambda e, pf=pf, ui=ui, ntok=ntok: e.tensor_scalar(out=ur[ui][:, 0:ntok], in0=PF[pf][:, 0:ntok], scalar1=0.0, scalar2=None, op0=ALU.max), r=[bPF[pf]], w=[b_ur[ui]])
                OP("act", lambda e, f=f, ui=ui, ntok=ntok: e.activation(out=uT[:, f, 0:ntok], in_=ur[ui][:, 0:ntok], func=AF.Square), r=[b_ur[ui]], w=[b_uT])
            for ti, (r0, n, c0) in enumerate(tiles):
                for hf in range(2):
                    for f in range(32):
                        OP("pe", lambda e, f=f, hf=hf, n=n, c0=c0: e.matmul(PF[hf][0:n, :], lhsT=uT[:, f, c0:c0 + n], rhs=wD[:, f, hf * 512:(hf + 1) * 512], start=(f == 0), stop=(f == 31)),
                           r=[b_uT, b_wD], w=[bPF[hf]])
                    OP("dve", lambda e, ti=ti, hf=hf, n=n: e.tensor_tensor(out=xt[ti][0:n, hf * 512:(hf + 1) * 512], in0=PF[hf][0:n, :], in1=xt[ti][0:n, hf * 512:(hf + 1) * 512], op=ALU.add),
                       r=[bPF[hf], b_xt[ti]], w=[b_xt[ti]])
                rms_scale(xt[ti][0:n, :], b_xt[ti], n, ti, xnb[ti][0:n, :], b_xnb[ti])
                OP("dve", lambda e, ti=ti, n=n: e.scalar_tensor_tensor(out=yo[0:n, :], in0=xt[ti][0:n, :], scalar=rstd[ti][0:n, 0:1], in1=g3[0:n, :], op0=ALU.mult, op1=ALU.mult),
                   r=[b_xt[ti], b_rstd[ti], b_g3], w=[b_yo])
                store(ydst[r0:r0 + n, :], yo[0:n, :], b_yo)
        kb.barrier()
        p3.close()
        kb.final_wait("sp")
        kb.emit()
        return nc, dbg_outs


def make_in_maps(inp):
    f = lambda a: np.ascontiguousarray(a, dtype=np.float32)
    xp = np.asarray(inp["x_prompt"]); xsamp = np.asarray(inp["x_sample"])
    ckf = f(np.asarray(inp["cache_k"]).reshape(2560 * 128, 512))
    cvf = f(np.asarray(inp["cache_v"]).reshape(2560 * 128, 512))
    ptab = np.asarray(inp["page_table"]).astype(np.int32)
    sCf = np.asarray(inp["state_C"])[0]; snf = np.asarray(inp["state_n"])[0]; smf = np.asarray(inp["state_m"])[0]
    shared = {
        "w_in": f(inp["w_in"][0]), "w_out": f(inp["w_out"][0]), "w_up": f(inp["w_up"][0]), "w_down": f(inp["w_down"][0]),
        "nmix": f(inp["norm_mix"]).reshape(1, D), "nffn": f(inp["norm_ffn"]).reshape(1, D), "nfin": f(inp["norm_final"]).reshape(1, D),
        "mlg": f(inp["ml_norm"]).reshape(1, 512),
        "bg": f(np.concatenate([np.asarray(inp["b_ig"]).reshape(-1), np.asarray(inp["b_fg"]).reshape(-1)])).reshape(8, 1),
        "rb": f(inp["rel_bias"]).reshape(1, 256), "ck": ckf, "cv": cvf,
    }
    maps = []
    for c in range(8):
        b, half = c // 2, c % 2
        m = dict(shared)
        m["xm"] = f(xp[b, half * NM:(half + 1) * NM])
        m["xc"] = f(xp[b, 0:NM]) if half else np.zeros((NM, D), np.float32)
        m["xs"] = f(xsamp[4 * c:4 * c + 4].reshape(NS, D))
        m["pt"] = np.ascontiguousarray(ptab[4 * c:4 * c + 4].reshape(1, 256))
        m["sC"] = f(sCf[4 * c:4 * c + 4].reshape(16 * 128, 128))
        m["sn"] = f(snf[4 * c:4 * c + 4].reshape(16 * 128, 1))
        m["smi"] = f(smf[4 * c:4 * c + 4].T)
        m["cf"] = np.array([[float(half), (float(half) - 1.0) * 30000.0, 0.0, 0.0]], np.float32)
        cd = np.zeros((16, 2, 16), np.float32)
        for qt in range(16):
            own = 8 + qt // 2
            for n in range(16):
                is_cand = (n < 8 and half == 1) or (8 <= n < own)
                cd[qt, 0, n] = NEG if is_cand else 0.0
                cd[qt, 1, n] = 0.0 if (is_cand or n == own) else NEG
        m["cand"] = cd.reshape(1, 512)
        maps.append(m)
    return maps


STOP_AFTER = "all"
CACHE_ROWS = 2560 * 128


def kernel(**inputs):
    maps = make_in_maps(inputs)
    for m in maps:
        m["ck"] = m["ck"][:CACHE_ROWS]
        m["cv"] = m["cv"][:CACHE_ROWS]
    nc, _ = build_program(stop_after=STOP_AFTER, cache_rows=CACHE_ROWS)
    res = run_bass_kernel_spmd(nc, maps, core_ids=list(range(8)))
    R = res.results
    f32 = np.float32
    y_prompt = np.zeros((4, 4096, D), f32); y_sample = np.zeros((32, 8, D), f32)
    nkp = np.zeros((1, 4, 4096, 8, 64), f32); nvp = np.zeros((1, 4, 4096, 8, 64), f32)
    nCp = np.zeros((1, 4, 4, 128, 128), f32); nnp_ = np.zeros((1, 4, 4, 128), f32); nmp = np.zeros((1, 4, 4), f32)
    nks = np.zeros((1, 32, 8, 8, 64), f32); nvs = np.zeros((1, 32, 8, 8, 64), f32)
    nCs = np.zeros((1, 32, 4, 128, 128), f32); nns = np.zeros((1, 32, 4, 128), f32); nms = np.zeros((1, 32, 4), f32)
    for c in range(8):
        b, half = c // 2, c % 2
        r = R[c]
        sl = slice(half * NM, (half + 1) * NM)
        y_prompt[b, sl] = r["y_m"]
        y_sample[4 * c:4 * c + 4] = r["y_s"].reshape(4, 8, D)
        nkp[0, b, sl] = r["k_m"].reshape(NM, 8, 64)
        nvp[0, b, sl] = r["v_m"].reshape(NM, 8, 64)
        if half == 1:
            nCp[0, b] = r["C_p"].reshape(4, 128, 128)
            nnp_[0, b] = r["n_p"].reshape(4, 128)
            nmp[0, b] = r["m_p"].reshape(4)
        nks[0, 4 * c:4 * c + 4] = r["k_s"].reshape(4, 8, 8, 64)
        nvs[0, 4 * c:4 * c + 4] = r["v_s"].reshape(4, 8, 8, 64)
        nCs[0, 4 * c:4 * c + 4] = r["C_s"].reshape(4, 4, 128, 128)
        nns[0, 4 * c:4 * c + 4] = r["n_s"].reshape(4, 4, 128)
        nms[0, 4 * c:4 * c + 4] = r["m_s"].reshape(4, 4)
    return (y_prompt, y_sample, nkp, nvp, nCp, nnp_, nmp, nks, nvs, nCs, nns, nms)
```

```python
import contextlib
import math
import numpy as np
import concourse.bass as bass
import concourse.mybir as mybir
from concourse.alu_op_type import AluOpType as ALU
from concourse.bass_utils import run_bass_kernel_spmd

F32 = mybir.dt.float32
BF16 = mybir.dt.bfloat16
I32 = mybir.dt.int32
AF = mybir.ActivationFunctionType
AX = mybir.AxisListType

ENGS = ("pe", "act", "dve", "pool", "sp")
NEG = -30000.0
D = 1024
NM = 2048
NS = 32
EPS = 1e-6


class Buf:
    __slots__ = ("name", "w", "rs", "dsem", "dcnt")

    def __init__(self, name):
        self.name = name
        self.w = None
        self.rs = {}
        self.dsem = None
        self.dcnt = 0


class KB:
    def __init__(self, nc, stack):
        self.nc = nc
        self.stack = stack
        self.sems = {}
        self.cnt = {e: 0 for e in ENGS}
        self.known = {e: {} for e in ENGS}
        self.prog = {e: [] for e in ENGS}
        for e in ENGS:
            self._newsem("E_" + e)
        self.nd = 0
        self.dbufs = []

    def _newsem(self, key):
        self.sems[key] = self.stack.enter_context(self.nc.semaphore(key))
        return key

    def buf(self, name):
        return Buf(name)

    def _collect(self, e, reads, writes):
        waits = {}
        kn = self.known[e]
        own = "E_" + e

        def need(tok, same_ok):
            if tok is None:
                return
            k, v = tok
            if k == own and same_ok:
                return
            if kn.get(k, 0) >= v:
                return
            if waits.get(k, 0) < v:
                waits[k] = v

        for b in reads:
            need(b.w, False)
        for b in writes:
            need(b.w, True)
            for k, v in b.rs.items():
                need((k, v), True)
        for k, v in waits.items():
            kn[k] = v
        return list(waits.items())

    def op(self, e, fn, reads=(), writes=()):
        waits = self._collect(e, reads, writes)
        self.cnt[e] += 1
        tok = ("E_" + e, self.cnt[e])
        for b in reads:
            if b.rs.get(tok[0], 0) < tok[1]:
                b.rs[tok[0]] = tok[1]
        for b in writes:
            b.w = tok
            b.rs = {}
        self.prog[e].append((waits, fn, (tok[0], 1)))

    def dma(self, q, fn, reads=(), writes=()):
        waits = self._collect(q, reads, writes)
        tgt = writes[0] if writes else reads[0]
        if tgt.dsem is None:
            self.nd += 1
            tgt.dsem = self._newsem(f"D{self.nd}")
            self.dbufs.append(tgt)
        tgt.dcnt += 16
        tok = (tgt.dsem, tgt.dcnt)
        for b in reads:
            if b.rs.get(tok[0], 0) < tok[1]:
                b.rs[tok[0]] = tok[1]
        for b in writes:
            b.w = tok
            b.rs = {}
        self.prog[q].append((waits, fn, (tok[0], 16)))

    def barrier(self):
        for e in ENGS:
            waits = [("E_" + x, self.cnt[x]) for x in ENGS if x != e and self.cnt[x] > 0]
            waits += [(b.dsem, b.dcnt) for b in self.dbufs]
            for k, v in waits:
                self.known[e][k] = max(self.known[e].get(k, 0), v)
            self.prog[e].append((waits, None, None))

    def final_wait(self, e):
        waits = [(b.dsem, b.dcnt) for b in self.dbufs]
        self.prog[e].append((waits, None, None))

    def emit(self):
        nc = self.nc
        sems = self.sems
        prog = self.prog
        with nc.Block() as block:
            def run(name):
                def body(eng):
                    for waits, fn, inc in prog[name]:
                        for k, v in waits:
                            eng.wait_ge(sems[k], v)
                        if fn is not None:
                            fn(eng).then_inc(sems[inc[0]], inc[1])
                return body
            block.tensor(run("pe"))
            block.scalar(run("act"))
            block.vector(run("dve"))
            block.gpsimd(run("pool"))
            block.sync(run("sp"))


def t5_thresholds():
    n = np.arange(0, 600)
    nf = np.maximum(n, 1).astype(np.float32)
    large = 16 + (np.log(nf / np.float32(16)) / np.float32(math.log(128 / 16)) * np.float32(16)).astype(np.int32)
    large = np.minimum(large, 31)
    bucket = np.where(n < 16, n, large)
    return [int(np.argmax(bucket >= b)) for b in range(1, 32)]


def build_program(dbg=None, stop_after=None, cache_rows=2560 * 128):
    nc = bass.Bass("TRN2", target_bir_lowering=False)
    din = lambda name, shape, dt=F32: nc.dram_tensor(name, shape, dt, kind="ExternalInput").ap()
    dout = lambda name, shape, dt=F32: nc.dram_tensor(name, shape, dt, kind="ExternalOutput").ap()
    xm = din("xm", [NM, D]); xc = din("xc", [NM, D]); xs = din("xs", [NS, D])
    w_in = din("w_in", [D, 3592]); w_out = din("w_out", [D, D]); w_up = din("w_up", [D, 4096]); w_down = din("w_down", [4096, D])
    nmix = din("nmix", [1, D]); nffn = din("nffn", [1, D]); nfin = din("nfin", [1, D]); mlg = din("mlg", [1, 512])
    bg = din("bg", [8, 1]); rb = din("rb", [1, 256])
    ck = din("ck", [cache_rows, 512]); cv = din("cv", [cache_rows, 512])
    pt = din("pt", [1, 256], I32)
    sC = din("sC", [16 * 128, 128]); sn = din("sn", [16 * 128, 1]); smi = din("smi", [4, 4])
    cf = din("cf", [1, 4]); cand = din("cand", [1, 512])
    y_m = dout("y_m", [NM, D]); y_s = dout("y_s", [NS, D])
    k_m = dout("k_m", [NM, 512]); v_m = dout("v_m", [NM, 512]); k_s = dout("k_s", [NS, 512]); v_s = dout("v_s", [NS, 512])
    C_p = dout("C_p", [512, 128]); n_p = dout("n_p", [512, 1]); m_p = dout("m_p", [4, 1])
    C_s = dout("C_s", [2048, 128]); n_s = dout("n_s", [2048, 1]); m_s = dout("m_s", [16, 1])
    mix = nc.dram_tensor("mix", [NM + NS, D], BF16, kind="ExternalOutput").ap()
    xnTd = nc.dram_tensor("xnTd", [9, 128, 4096], BF16).ap()
    dbg_outs = {}

    with contextlib.ExitStack() as st:
        kb = KB(nc, st)
        B = kb.buf
        out_bufs = []

        def sbt(name, shape, dt, stack=st):
            return stack.enter_context(nc.sbuf_tensor(name, shape, dt))

        PT = [st.enter_context(nc.psum_tensor(f"PT{i}", [128, 8, 128], BF16)) for i in range(2)]
        bPT = [B(f"PT{i}") for i in range(2)]
        PF = [st.enter_context(nc.psum_tensor(f"PF{i}", [128, 512], F32)) for i in range(6)]
        bPF = [B(f"PF{i}") for i in range(6)]

        TQ = {"on": False, "q": []}

        def OP(e, fn, r=(), w=()):
            if TQ["on"]:
                TQ["q"].append((e, fn, list(r), list(w)))
            else:
                kb.op(e, fn, r, w)

        def DMA(q, fn, reads=(), writes=()):
            if TQ["on"]:
                TQ["q"].append(("dma", q, fn, list(reads), list(writes)))
            else:
                kb.dma(q, fn, reads, writes)

        def flush_tq(n):
            for _ in range(min(n, len(TQ["q"]))):
                it = TQ["q"].pop(0)
                if it[0] == "dma":
                    kb.dma(it[1], it[2], it[3], it[4])
                else:
                    kb.op(it[0], it[1], it[2], it[3])

        def store(dst_ap, src_ap, src_buf, q="sp"):
            import os
            if os.environ.get("NO_ST") is not None:
                return
            kb.dma(q, lambda e: e.dma_start(out=dst_ap, in_=src_ap), reads=[src_buf], writes=[])

        identf = sbt("identf", [128, 128], F32); b_idf = B("idf")
        ident = sbt("ident", [128, 128], BF16); b_id = B("id")
        OP("pool", lambda e: e.memset(identf[:], 1.0), w=[b_idf])
        OP("pool", lambda e: e.affine_select(out=identf[:], in_=identf[:], pattern=[[-1, 128]], compare_op=ALU.is_equal,
                                             fill=0.0, base=0, channel_multiplier=1), r=[b_idf], w=[b_idf])
        OP("dve", lambda e: e.tensor_copy(out=ident[:], in_=identf[:]), r=[b_idf], w=[b_id])
        g1 = sbt("g1", [128, D], F32); b_g1 = B("g1")
        kb.dma("sp", lambda e: e.dma_start(out=g1[:], in_=nmix[0:1, :].partition_broadcast(128)), writes=[b_g1])
        cfb = sbt("cfb", [128, 4], F32); b_cfb = B("cfb")
        kb.dma("sp", lambda e: e.dma_start(out=cfb[:], in_=cf[0:1, :].partition_broadcast(128)), writes=[b_cfb])
        rbb = sbt("rbb", [128, 32, 8], F32); b_rbb = B("rbb")
        kb.dma("sp", lambda e: e.dma_start(out=rbb[:].rearrange("p b h -> p (b h)"), in_=rb[0:1, :].partition_broadcast(128)), writes=[b_rbb])
        candb = sbt("candb", [128, 16, 2, 16], F32); b_cand = B("cand")
        kb.dma("sp", lambda e: e.dma_start(out=candb[:].rearrange("p a b c -> p (a b c)"), in_=cand[0:1, :].partition_broadcast(128)), writes=[b_cand])

        if stop_after == "c0":
            kb.final_wait("sp")
            kb.emit()
            return nc, dbg_outs
        xt = [sbt(f"xt{i}", [128, D], F32) for i in range(2)]; b_xt = [B(f"xt{i}") for i in range(2)]
        ssq = [sbt(f"ssq{i}", [128, 1], F32) for i in range(2)]; b_ssq = [B(f"ssq{i}") for i in range(2)]
        rstd = [sbt(f"rstd{i}", [128, 1], F32) for i in range(2)]; b_rstd = [B(f"rstd{i}") for i in range(2)]
        xnb = [sbt(f"xnb{i}", [128, D], BF16) for i in range(2)]; b_xnb = [B(f"xnb{i}") for i in range(2)]
        cnt = {"x": 0}

        def rms_scale(src, b_src, n, i, junk, b_junk, dim=D):
            OP("act", lambda e: e.activation(out=junk, in_=src, func=AF.Square, accum_out=ssq[i][0:n, :]),
               r=[b_src], w=[b_junk, b_ssq[i]])
            OP("act", lambda e: e.activation(out=rstd[i][0:n, :], in_=ssq[i][0:n, :], func=AF.Sqrt, scale=1.0 / dim, bias=EPS),
               r=[b_ssq[i]], w=[b_rstd[i]])
            OP("dve", lambda e: e.reciprocal(out=rstd[i][0:n, :], in_=rstd[i][0:n, :]), r=[b_rstd[i]], w=[b_rstd[i]])

        def to_featmajor(src_bf, b_src, n, dst, b_dst, ncol=8):
            i = cnt["x"] % 2
            cnt["x"] += 1
            for c in range(ncol):
                OP("pe", lambda e, c=c: e.transpose(out=PT[i][:, c, 0:n], in_=src_bf[0:n, c * 128:(c + 1) * 128], identity=ident[0:n, 0:n]),
                   r=[b_src, b_id], w=[bPT[i]])
            OP("act", lambda e: e.copy(out=dst, in_=PT[i][:, 0:ncol, 0:n]), r=[bPT[i]], w=[b_dst])

        def norm_tile(src_ap, n, gt, b_gt, dst, b_dst, q="sp"):
            i = cnt["x"] % 2
            kb.dma(q, lambda e: e.dma_start(out=xt[i][0:n, :], in_=src_ap), writes=[b_xt[i]])
            rms_scale(xt[i][0:n, :], b_xt[i], n, i, xnb[i][0:n, :], b_xnb[i])
            OP("dve", lambda e: e.scalar_tensor_tensor(out=xnb[i][0:n, :], in0=xt[i][0:n, :], scalar=rstd[i][0:n, 0:1], in1=gt[0:n, :],
                                                       op0=ALU.mult, op1=ALU.mult), r=[b_xt[i], b_rstd[i], b_gt], w=[b_xnb[i]])
            to_featmajor(xnb[i], b_xnb[i], n, dst, b_dst)

        def load_w(dst, src, b_dst, ncols):
            for c0 in range(0, ncols, 2048):
                c1 = min(ncols, c0 + 2048)
                kb.dma("pool", lambda e, c0=c0, c1=c1: e.dma_start(out=dst[:, c0:c1], in_=src[:, c0:c1]), writes=[b_dst])

        evq = {"i": 0}

        def evac(out, in_, r, w, scale=None):
            evq["i"] += 1
            if evq["i"] % 2:
                if scale is None:
                    OP("act", lambda e: e.copy(out=out, in_=in_), r=r, w=w)
                else:
                    OP("dve", lambda e: e.tensor_scalar(out=out, in0=in_, scalar1=scale, scalar2=None, op0=ALU.mult), r=r, w=w)
            else:
                if scale is None:
                    OP("dve", lambda e: e.tensor_copy(out=out, in_=in_), r=r, w=w)
                else:
                    OP("dve", lambda e: e.tensor_scalar(out=out, in0=in_, scalar1=scale, scalar2=None, op0=ALU.mult), r=r, w=w)

        qTs = sbt("qTs", [128, 4, NS], BF16); b_qTs = B("qTs")
        kTs_own = sbt("kTs_own", [128, 4, NS], BF16); b_kTs_own = B("kTs_own")
        Vs_own = sbt("Vs_own", [8, 4, 512], BF16); b_Vs_own = B("Vs_own")
        TS128 = sbt("TS128", [128, 8, 8], F32); b_TS128 = B("TS128")
        TS0 = sbt("TS0", [8, 8, 8], F32); b_TS0 = B("TS0")
        p1 = contextlib.ExitStack()
        qTa = [sbt(f"qTa{h}", [81, NM], BF16, p1) for h in range(8)]; b_qTa = [[B(f"qTa{h}_{g}") for g in range(4)] for h in range(8)]
        b_qTaP = [[B(f"qTaP{h}_{g}") for g in range(4)] for h in range(8)]
        kTa = [sbt(f"kTa{h}", [81, 4096], BF16, p1) for h in range(8)]; b_kTa = [[B(f"kTa{h}_{g}") for g in range(8)] for h in range(8)]
        b_kTaI = [B(f"kTaI{h}") for h in range(8)]
        Va = sbt("Va", [128, 32, 8, 65], BF16, p1); b_Va = [B(f"Va{t}") for t in range(32)]; b_Va1 = B("Va1")
        OP("pool", lambda e: e.memset(Va[:, :, :, 64:65], 1.0), w=[b_Va1])
        for h in range(8):
            OP("pool", lambda e, h=h: e.memset(kTa[h][64:81, :], 1.0), w=[b_kTaI[h]])
            OP("pool", lambda e, h=h: e.affine_select(out=kTa[h][64:80, :].rearrange("p (b k) -> p b k", k=256),
                                                      in_=kTa[h][64:80, :].rearrange("p (b k) -> p b k", k=256),
                                                      pattern=[[1, 16], [0, 256]], compare_op=ALU.is_equal, fill=0.0, base=0,
                                                      channel_multiplier=-1), r=[b_kTaI[h]], w=[b_kTaI[h]])

        if stop_after == "c1":
            kb.final_wait("sp")
            kb.emit()
            p1.close()
            return nc, dbg_outs
        thr = t5_thresholds()
        Tt = {0: sbt("T0", [128, 8, 128], F32, p1), 128: sbt("T128", [128, 8, 128], F32, p1)}
        b_Tt = {0: [B(f"T0_{h}") for h in range(8)], 128: [B(f"T128_{h}") for h in range(8)]}
        drb = sbt("drb", [128, 32, 8], F32, p1); b_drb = B("drb")
        OP("dve", lambda e: e.tensor_tensor(out=drb[:, 1:32, :], in0=rbb[:, 1:32, :], in1=rbb[:, 0:31, :], op=ALU.subtract), r=[b_rbb], w=[b_drb])
        OP("dve", lambda e: e.tensor_tensor(out=drb[:, 0:1, :], in0=rbb[:, 0:1, :], in1=rbb[:, 31:32, :], op=ALU.subtract), r=[b_rbb], w=[b_drb])
        rb31x8 = sbt("rb31x8", [128, 8], F32, p1); b_rb31 = B("rb31")
        OP("dve", lambda e: e.tensor_scalar(out=rb31x8[:], in0=rbb[:, 31, :], scalar1=8.0, scalar2=None, op0=ALU.mult), r=[b_rbb], w=[b_rb31])
        disti = sbt("disti", [128, 128], I32, p1); b_disti = B("disti")
        distf = sbt("distf", [128, 128], F32, p1); b_distf = B("distf")
        gef = sbt("gef", [128, 128], F32, p1); b_gef = B("gef")
        TQ["on"] = True
        for delta in (0, 128):
            OP("pool", lambda e, delta=delta: e.iota(out=disti[:], pattern=[[1, 128]], base=delta, channel_multiplier=-1), w=[b_disti])
            OP("dve", lambda e: e.tensor_copy(out=distf[:], in_=disti[:]), r=[b_disti], w=[b_distf])
            for h in range(8):
                OP("dve", lambda e, h=h, delta=delta: e.tensor_scalar(out=Tt[delta][:, h, :], in0=distf[:], scalar1=0.0, scalar2=drb[:, 0, h:h + 1],
                                                                      op0=ALU.mult, op1=ALU.add), r=[b_distf, b_drb], w=[b_Tt[delta][h]])
            steps = [(float(thr[b - 1]), b) for b in range(1, 32)]
            for tv, b in steps:
                OP("dve", lambda e, tv=tv: e.tensor_scalar(out=gef[:], in0=distf[:], scalar1=tv, scalar2=None, op0=ALU.is_ge), r=[b_distf], w=[b_gef])
                for h in range(8):
                    OP("dve", lambda e, h=h, b=b, delta=delta: e.scalar_tensor_tensor(out=Tt[delta][:, h, :], in0=gef[:], scalar=drb[:, b, h:h + 1], in1=Tt[delta][:, h, :],
                                                                                     op0=ALU.mult, op1=ALU.add), r=[b_gef, b_drb, b_Tt[delta][h]], w=[b_Tt[delta][h]])
            if delta == 0:
                OP("dve", lambda e: e.tensor_scalar(out=gef[:], in0=distf[:], scalar1=0.0, scalar2=None, op0=ALU.is_lt), r=[b_distf], w=[b_gef])
                for h in range(8):
                    OP("dve", lambda e, h=h: e.scalar_tensor_tensor(out=Tt[0][:, h, :], in0=gef[:], scalar=NEG, in1=Tt[0][:, h, :],
                                                                    op0=ALU.mult, op1=ALU.add), r=[b_gef, b_Tt[0][h]], w=[b_Tt[0][h]])
        OP("dve", lambda e: e.tensor_copy(out=TS128[:], in_=Tt[128][:, :, 0:8]), r=b_Tt[128], w=[b_TS128])
        OP("dve", lambda e: e.tensor_copy(out=TS0[:], in_=Tt[0][0:8, :, 0:8]), r=b_Tt[0], w=[b_TS0])
        TQ["on"] = False
        tq_slice = (len(TQ["q"]) + 7) // 8
        negT = sbt("negT", [128, 128], F32, p1); b_negT = B("negT")
        OP("pool", lambda e: e.memset(negT[:], NEG), w=[b_negT])

        if stop_after == "c2":
            kb.final_wait("sp")
            kb.emit()
            p1.close()
            return nc, dbg_outs
        p1a = contextlib.ExitStack()
        wA = sbt("wA", [128, 8, 1536], BF16, p1a); b_wA = B("wA")
        for c in range(8):
            load_w(wA[:, c, :], w_in[c * 128:(c + 1) * 128, 0:1536], b_wA, 1536)
        xnT = [sbt(f"xnT{i}", [128, 8, 512], BF16, p1a) for i in range(1)]; b_xnT = [B(f"xnT{i}") for i in range(1)]
        kvst = [sbt(f"kvst{i}", [128, 1024], F32, p1a) for i in range(2)]; b_kvst = [B(f"kvst{i}") for i in range(2)]

        if stop_after == "s0":
            kb.final_wait("sp"); kb.emit(); p1a.close(); p1.close(); return nc, dbg_outs
        b_xnTd = B("xnTd")
        for kind in ("ctx", "main"):
            src = xc if kind == "ctx" else xm
            for g in range(4):
                xi = 0
                for t in range(4):
                    r0 = g * 512 + t * 128
                    norm_tile(src[r0:r0 + 128, :], 128, g1, b_g1, xnT[xi][:, :, t * 128:(t + 1) * 128], b_xnT[xi])
                if stop_after == "s1":
                    kb.final_wait("sp"); kb.emit(); p1a.close(); p1.close(); return nc, dbg_outs
                flush_tq(tq_slice)
                kg = g if kind == "ctx" else 4 + g
                kb.dma("sp", lambda e, kg=kg, xi=xi: e.dma_start(out=xnTd[kg], in_=xnT[xi][:].rearrange("p c t -> p (c t)")), reads=[b_xnT[xi]], writes=[b_xnTd])
                import os
                for h in (range(8) if os.environ.get("SKIP_FM") is None else ()):
                    for which in (("k",) if kind == "ctx" else ("q", "k")):
                        col0 = (0 if which == "q" else 512) + h * 64
                        pf = (2 * h + (which == "k")) % 2
                        for c in range(8):
                            OP("pe", lambda e, c=c, pf=pf, col0=col0, xi=xi: e.matmul(PF[pf][0:64, :], lhsT=wA[:, c, col0:col0 + 64], rhs=xnT[xi][:, c, :],
                                                                                       start=(c == 0), stop=(c == 7)),
                               r=[b_wA, b_xnT[xi]], w=[bPF[pf]])
                        if os.environ.get("NO_EV") is not None:
                            pass
                        elif which == "q":
                            evac(qTa[h][0:64, g * 512:(g + 1) * 512], PF[pf][0:64, :], [bPF[pf]], [b_qTa[h][g]])
                        else:
                            evac(kTa[h][0:64, kg * 512:(kg + 1) * 512], PF[pf][0:64, :], [bPF[pf]], [b_kTa[h][kg]])
                if stop_after == "s2":
                    kb.final_wait("sp"); kb.emit(); p1a.close(); p1.close(); return nc, dbg_outs
                for t in (range(4) if os.environ.get("SKIP_TM") is None else ()):
                    ta = kg * 4 + t
                    si = ta % 2
                    for which in (("v",) if kind == "ctx" else ("k", "v")):
                        col0 = 512 if which == "k" else 1024
                        pf = 3 if which == "v" else 4
                        for c in range(8):
                            OP("pe", lambda e, c=c, pf=pf, col0=col0, xi=xi, t=t: e.matmul(PF[pf][:, :], lhsT=xnT[xi][:, c, t * 128:(t + 1) * 128], rhs=wA[:, c, col0:col0 + 512],
                                                                                          start=(c == 0), stop=(c == 7)),
                               r=[b_wA, b_xnT[xi]], w=[bPF[pf]])
                        if which == "v" and os.environ.get("NO_VA") is None:
                            OP("dve", lambda e, ta=ta, pf=pf: e.tensor_copy(out=Va[:, ta, :, 0:64], in_=PF[pf][:, :].rearrange("p (h d) -> p h d", d=64)),
                               r=[bPF[pf]], w=[b_Va[ta]])
                        if kind == "main" and os.environ.get("NO_KV") is None:
                            o0 = 0 if which == "k" else 512
                            OP("dve", lambda e, si=si, pf=pf, o0=o0: e.tensor_copy(out=kvst[si][:, o0:o0 + 512], in_=PF[pf][:, :]), r=[bPF[pf]], w=[b_kvst[si]])
                    if kind == "main":
                        r0 = g * 512 + t * 128
                        store(k_m[r0:r0 + 128, :], kvst[si][:, 0:512], b_kvst[si])
                        store(v_m[r0:r0 + 128, :], kvst[si][:, 512:1024], b_kvst[si])
                if stop_after == "s3" or (stop_after == "s4" and kind == "main") or (stop_after == "s5" and kind == "ctx" and g == 3) or (stop_after == "s6" and kind == "ctx" and g == 1):
                    kb.final_wait("sp"); kb.emit(); p1a.close(); p1.close(); return nc, dbg_outs
        if stop_after == "c3":
            kb.final_wait("sp")
            kb.emit()
            p1a.close()
            p1.close()
            return nc, dbg_outs
        flush_tq(10 ** 9)
        xi = 0
        norm_tile(xs[0:NS, :], NS, g1, b_g1, xnT[xi][:, :, 0:NS], b_xnT[xi])
        kb.dma("sp", lambda e, xi=xi: e.dma_start(out=xnTd[8].rearrange("p (c t) -> p c t", t=512)[:, :, 0:NS], in_=xnT[xi][:, :, 0:NS]), reads=[b_xnT[xi]], writes=[b_xnTd])
        for which in ("q", "k"):
            for ch in range(4):
                col0 = (0 if which == "q" else 512) + ch * 128
                pf = ch % 2
                for c in range(8):
                    OP("pe", lambda e, c=c, pf=pf, col0=col0, xi=xi: e.matmul(PF[pf][:, 0:NS], lhsT=wA[:, c, col0:col0 + 128], rhs=xnT[xi][:, c, 0:NS],
                                                                               start=(c == 0), stop=(c == 7)), r=[b_wA, b_xnT[xi]], w=[bPF[pf]])
                dst = qTs if which == "q" else kTs_own
                bd = b_qTs if which == "q" else b_kTs_own
                evac(dst[:, ch, :], PF[pf][:, 0:NS], [bPF[pf]], [bd])
        for sbi in range(4):
            si = sbi % 2
            for which in ("k", "v"):
                col0 = 512 if which == "k" else 1024
                pf = 2 + (which == "v")
                for c in range(8):
                    OP("pe", lambda e, c=c, pf=pf, col0=col0, xi=xi, sbi=sbi: e.matmul(PF[pf][0:8, :], lhsT=xnT[xi][:, c, sbi * 8:(sbi + 1) * 8], rhs=wA[:, c, col0:col0 + 512],
                                                                                      start=(c == 0), stop=(c == 7)), r=[b_wA, b_xnT[xi]], w=[bPF[pf]])
                o0 = 0 if which == "k" else 512
                if which == "v":
                    OP("dve", lambda e, sbi=sbi, pf=pf: e.tensor_copy(out=Vs_own[:, sbi, :], in_=PF[pf][0:8, :]), r=[bPF[pf]], w=[b_Vs_own])
                OP("dve", lambda e, si=si, pf=pf, o0=o0: e.tensor_copy(out=kvst[si][0:8, o0:o0 + 512], in_=PF[pf][0:8, :]), r=[bPF[pf]], w=[b_kvst[si]])
            store(k_s[sbi * 8:(sbi + 1) * 8, :], kvst[si][0:8, 0:512], b_kvst[si])
            store(v_s[sbi * 8:(sbi + 1) * 8, :], kvst[si][0:8, 512:1024], b_kvst[si])
        kb.barrier()
        p1a.close()

        if stop_after == "p1":
            kb.final_wait("sp")
            kb.emit()
            p1.close()
            return nc, dbg_outs

        p1b = contextlib.ExitStack()
        kmf = sbt("kmf", [64, 8, 16], F32, p1b); b_kmf = B("kmf")
        kmT = sbt("kmT", [64, 8, 16], BF16, p1b); b_kmT = B("kmT")
        for h in range(8):
            OP("dve", lambda e, h=h: e.tensor_reduce(out=kmf[:, h, :], in_=kTa[h][0:64, :].rearrange("p (b k) -> p b k", k=256), axis=AX.X, op=ALU.add),
               r=b_kTa[h], w=[b_kmf])
        OP("dve", lambda e: e.tensor_copy(out=kmT[:], in_=kmf[:]), r=[b_kmf], w=[b_kmT])
        selm = sbt("selm", [128, 16, 16], F32, p1b); b_selm = B("selm")
        OP("dve", lambda e: e.tensor_scalar(out=selm[:], in0=candb[:, :, 0, :], scalar1=-1.0, scalar2=NEG, op0=ALU.mult, op1=ALU.add), r=[b_cand], w=[b_selm])
        selW = sbt("selW", [128, 4, 8, 81], F32, p1b); b_selW = [B(f"selW{j}") for j in range(4)]
        OP("pool", lambda e: e.memset(selW[:], 0.0), w=b_selW)
        for j in range(4):
            OP("dve", lambda e, j=j: e.tensor_copy(out=selW[:, j, :, 80:81], in_=rb31x8[:].rearrange("p (h o) -> p h o", o=1)), r=[b_rb31], w=[b_selW[j]])
        smk = sbt("smk", [128, 8, 16], F32, p1b); b_smk = B("smk")
        top8 = sbt("top8", [128, 8, 8], F32, p1b); b_top8 = B("top8")
        PTt = [sbt(f"PTt{i}", [128, 512], BF16, p1b) for i in range(2)]; b_PTt = [B(f"PTt{i}") for i in range(2)]
        tmpS = [sbt(f"tmpS{i}", [128, 512], F32, p1b) for i in range(2)]; b_tmpS = [B(f"tmpS{i}") for i in range(2)]
        ot = [sbt(f"ot{i}", [65, 512], F32, p1b) for i in range(2)]; b_ot = [B(f"ot{i}") for i in range(2)]
        rc4 = sbt("rc4", [128, 4, 1], F32, p1b); b_rc4 = B("rc4")
        attb = [sbt(f"attb{i}", [128, 4, 512], BF16, p1b) for i in range(2)]; b_attb = [B(f"attb{i}") for i in range(2)]
        b_mix = B("mix")
        sidx = 0
        for g in range(4):
            for j in range(4):
                qt = 4 * g + j
                for h in range(8):
                    OP("pe", lambda e, h=h, qt=qt: e.matmul(PF[4][:, h * 16:(h + 1) * 16], lhsT=qTa[h][0:64, qt * 128:(qt + 1) * 128], rhs=kmT[:, h, :],
                                                            start=True, stop=True), r=[b_qTa[h][g], b_kmT], w=[bPF[4]])
                OP("dve", lambda e, qt=qt: e.tensor_tensor(out=smk[:], in0=PF[4][:, 0:128].rearrange("p (h n) -> p h n", n=16),
                                                           in1=selm[:, qt:qt + 1, :].to_broadcast([128, 8, 16]), op=ALU.add), r=[bPF[4], b_selm], w=[b_smk])
                for h in range(8):
                    OP("dve", lambda e, h=h: e.max(out=top8[:, h, :], in_=smk[:, h, :]), r=[b_smk], w=[b_top8])
                for h in range(8):
                    OP("dve", lambda e, h=h, j=j, qt=qt: e.scalar_tensor_tensor(out=selW[:, j, h, 64:80], in0=smk[:, h, :], scalar=top8[:, h, 2:3], in1=candb[:, qt, 0, :],
                                                                               op0=ALU.is_lt, op1=ALU.mult), r=[b_smk, b_top8, b_cand], w=[b_selW[j]])
                OP("dve", lambda e, j=j, qt=qt: e.tensor_tensor(out=selW[:, j, :, 64:80], in0=selW[:, j, :, 64:80],
                                                                in1=candb[:, qt:qt + 1, 1, :].to_broadcast([128, 8, 16]), op=ALU.add), r=[b_cand, b_selW[j]], w=[b_selW[j]])
            for h in range(8):
                for j in range(4):
                    OP("pe", lambda e, h=h, j=j: e.transpose(out=PF[5][0:81, j * 128:(j + 1) * 128], in_=selW[:, j, h, :], identity=identf[:]),
                       r=[b_selW[j], b_idf], w=[bPF[5]])
                OP("act", lambda e, h=h, g=g: e.copy(out=qTa[h][64:81, g * 512:(g + 1) * 512], in_=PF[5][64:81, :]), r=[bPF[5]], w=[b_qTaP[h][g]])
            ab = g % 2
            for h in range(8):
                po = 2 + h % 2
                nk = 16 + 4 * g + 4

                def emit_S(kt, h=h, g=g):
                    si = kt % 2
                    OP("pe", lambda e, h=h, kt=kt, g=g, si=si: e.matmul(PF[si][:, :], lhsT=kTa[h][0:81, kt * 128:(kt + 1) * 128], rhs=qTa[h][0:81, g * 512:(g + 1) * 512],
                                                                         start=True, stop=True),
                       r=[b_kTa[h][kt // 4], b_kTaI[h], b_qTa[h][g], b_qTaP[h][g]], w=[bPF[si]])

                def emit_exp(kt, h=h, g=g):
                    si = kt % 2
                    rel = kt - (16 + 4 * g)
                    if rel < -1:
                        OP("act", lambda e, si=si: e.activation(out=PTt[si][:], in_=PF[si][:, :], func=AF.Exp, scale=0.125), r=[bPF[si]], w=[b_PTt[si]])
                    else:
                        for j in range(4):
                            d = j - rel
                            cs = slice(j * 128, (j + 1) * 128)
                            if d == 0:
                                Tm, bTm = Tt[0][:, h, :], b_Tt[0][h]
                            elif d == 1:
                                Tm, bTm = Tt[128][:, h, :], b_Tt[128][h]
                            elif d == -1 and j % 2 == 0:
                                Tm, bTm = negT[:], b_negT
                            else:
                                Tm = None
                            if Tm is not None:
                                OP("dve", lambda e, si=si, cs=cs, Tm=Tm: e.scalar_tensor_tensor(out=tmpS[si][:, cs], in0=PF[si][:, cs], scalar=0.125, in1=Tm,
                                                                                                 op0=ALU.mult, op1=ALU.add), r=[bPF[si], bTm], w=[b_tmpS[si]])
                            else:
                                OP("dve", lambda e, si=si, cs=cs: e.tensor_scalar(out=tmpS[si][:, cs], in0=PF[si][:, cs], scalar1=0.125, scalar2=None, op0=ALU.mult),
                                   r=[bPF[si]], w=[b_tmpS[si]])
                        OP("act", lambda e, si=si: e.activation(out=PTt[si][:], in_=tmpS[si][:], func=AF.Exp), r=[b_tmpS[si]], w=[b_PTt[si]])

                def emit_PV(kt, h=h, po=po, nk=nk):
                    si = kt % 2
                    OP("pe", lambda e, h=h, kt=kt, si=si, po=po, nk=nk: e.matmul(PF[po][0:65, :], lhsT=Va[:, kt, h, :], rhs=PTt[si][:], start=(kt == 0), stop=(kt == nk - 1)),
                       r=[b_Va[kt], b_Va1, b_PTt[si]], w=[bPF[po]])

                emit_S(0)
                for kt in range(nk):
                    if kt + 1 < nk:
                        emit_S(kt + 1)
                    emit_exp(kt)
                    emit_PV(kt)
                oi = h % 2
                OP("dve", lambda e, oi=oi, po=po: e.tensor_copy(out=ot[oi][:], in_=PF[po][0:65, :]), r=[bPF[po]], w=[b_ot[oi]])
                for j in range(4):
                    OP("pe", lambda e, oi=oi, j=j: e.transpose(out=PF[4][:, j * 128:j * 128 + 65], in_=ot[oi][0:65, j * 128:(j + 1) * 128], identity=identf[0:65, 0:65]),
                       r=[b_ot[oi], b_idf], w=[bPF[4]])
                pv = PF[4][:, :].rearrange("p (j c) -> p j c", c=128)
                OP("dve", lambda e, pv=pv: e.reciprocal(out=rc4[:], in_=pv[:, :, 64:65]), r=[bPF[4]], w=[b_rc4])
                OP("dve", lambda e, pv=pv, h=h, ab=ab: e.tensor_tensor(out=attb[ab][:, :, h * 64:(h + 1) * 64], in0=pv[:, :, 0:64], in1=rc4[:].to_broadcast([128, 4, 64]), op=ALU.mult),
                   r=[bPF[4], b_rc4], w=[b_attb[ab]])
            for j in range(4):
                r0 = g * 512 + j * 128
                kb.dma("sp", lambda e, ab=ab, j=j, r0=r0: e.dma_start(out=mix[r0:r0 + 128, 0:512], in_=attb[ab][:, j, :]), reads=[b_attb[ab]], writes=[b_mix])
        if dbg == "att":
            def dump(name, shape, dt, src, bufs):
                o = dout("d_" + name, shape, dt)
                bb = B("dmp_" + name)
                kb.dma("sp", lambda e: e.dma_start(out=o, in_=src), reads=bufs, writes=[bb])
            dump("qTa0", [81, NM], BF16, qTa[0][:, :], b_qTa[0] + b_qTaP[0])
            dump("kTa0", [81, 4096], BF16, kTa[0][:, :], b_kTa[0] + [b_kTaI[0]])
            dump("T0", [128, 128], F32, Tt[0][:, 0, :], [b_Tt[0][0]])
            dump("T128", [128, 128], F32, Tt[128][:, 0, :], [b_Tt[128][0]])
            dump("Va16", [128, 8 * 65], BF16, Va[:, 16, :, :].rearrange("p h d -> p (h d)"), [b_Va[16], b_Va1])
            dump("kmf", [64, 128], F32, kmf[:].rearrange("p h n -> p (h n)"), [b_kmf])
            dump("selW", [128, 4 * 8 * 81], F32, selW[:].rearrange("p a b c -> p (a b c)"), b_selW)
        kb.barrier()
        p1b.close()
        p1.close()
        if stop_after == "att":
            kb.final_wait("sp")
            kb.emit()
            return nc, dbg_outs

        ps_ = contextlib.ExitStack()
        ptb = sbt("ptb", [128, 256], I32, ps_); b_ptb = B("ptb")
        DMA("sp", lambda e: e.dma_start(out=ptb[:], in_=pt[0:1, :].partition_broadcast(128)), writes=[b_ptb])
        pio = sbt("pio", [128, 1], I32, ps_); b_pio = B("pio")
        OP("pool", lambda e: e.iota(out=pio[:], pattern=[[0, 1]], base=0, channel_multiplier=1), w=[b_pio])
        piof = sbt("piof", [128, 1], F32, ps_); b_piof = B("piof")
        OP("dve", lambda e: e.tensor_copy(out=piof[:], in_=pio[:]), r=[b_pio], w=[b_piof])
        ptf = sbt("ptf", [128, 256], F32, ps_); b_ptf = B("ptf")
        OP("dve", lambda e: e.tensor_copy(out=ptf[:], in_=ptb[:]), r=[b_ptb], w=[b_ptf])
        ridx = sbt("ridx", [128, 256], I32, ps_); b_ridx = B("ridx")
        OP("dve", lambda e: e.tensor_scalar(out=ridx[:], in0=ptf[:], scalar1=128.0, scalar2=piof[:, 0:1], op0=ALU.mult, op1=ALU.add), r=[b_ptf, b_piof], w=[b_ridx])
        onesb = sbt("onesb", [128, 1], BF16, ps_); b_onesb = B("onesb")
        OP("pool", lambda e: e.memset(onesb[:], 1.0), w=[b_onesb])
        ohS = sbt("ohS", [33, 33, 128], BF16, ps_); b_ohS = B("ohS")
        OP("pool", lambda e: e.memset(ohS[:], 1.0), w=[b_ohS])
        OP("pool", lambda e: e.affine_select(out=ohS[:], in_=ohS[:], pattern=[[1, 33], [0, 128]], compare_op=ALU.is_equal, fill=0.0, base=0, channel_multiplier=-1),
           r=[b_ohS], w=[b_ohS])
        rbcol = sbt("rbcol", [64, 1], F32, ps_); b_rbcol = B("rbcol")
        for h in range(8):
            DMA("sp", lambda e, h=h: e.dma_start(out=rbcol[h * 8:(h + 1) * 8, :], in_=rb[0:1, 248 + h:248 + h + 1].partition_broadcast(8)), writes=[b_rbcol])
        OP("dve", lambda e: e.tensor_scalar(out=rbcol[:], in0=rbcol[:], scalar1=8.0, scalar2=None, op0=ALU.mult), r=[b_rbcol], w=[b_rbcol])
        kTsp2 = [sbt(f"kTsp{i}", [128, 4, 8192], BF16, ps_) for i in range(2)]; b_kTsp2 = [B(f"kTsp{i}") for i in range(2)]
        kpf = [sbt(f"kpf{i}", [128, 2, 512], F32, ps_) for i in range(4)]; b_kpf = [B(f"kpf{i}") for i in range(4)]
        kpb = [sbt(f"kpb{i}", [128, 2, 512], BF16, ps_) for i in range(4)]; b_kpb = [B(f"kpb{i}") for i in range(4)]
        vpf = [sbt(f"vpf{i}", [128, 512], F32, ps_) for i in range(4)]; b_vpf = [B(f"vpf{i}") for i in range(4)]
        vpb = [sbt(f"vpb{i}", [128, 512], BF16, ps_) for i in range(4)]; b_vpb = [B(f"vpb{i}") for i in range(4)]
        kms = sbt("kms", [128, 4, 32], BF16, ps_); b_kms = B("kms")
        kmsf = sbt("kmsf", [128, 4, 32], F32, ps_); b_kmsf = B("kmsf")
        Qbd = sbt("Qbd", [128, 4, 64], BF16, ps_); b_Qbd = B("Qbd")
        scs = sbt("scs", [64, 32], F32, ps_); b_scs = B("scs")
        top8s = sbt("top8s", [64, 8], F32, ps_); b_top8s = B("top8s")
        penF = sbt("penF", [64, 33], F32, ps_); b_penF = B("penF")
        penTb = sbt("penTb", [33, 64], BF16, ps_); b_penTb = B("penTb")
        PTs = [sbt(f"PTs{i}", [128, 64], BF16, ps_) for i in range(2)]; b_PTs = [B(f"PTs{i}") for i in range(2)]
        tmS = sbt("tmS", [128, 64], F32, ps_); b_tmS = B("tmS")
        osb = sbt("osb", [64, 512], BF16, ps_); b_osb = B("osb")
        recs = sbt("recs", [64, 1], F32, ps_); b_recs = B("recs")
        b_mix3 = B("mix3")
        xc_ = {"n": 0}
        def k_phase(sbi):
            kTsp = kTsp2[sbi % 2]; b_kTsp = b_kTsp2[sbi % 2]
            for pr in range(32):
                bi = pr % 4
                for a in range(2):
                    col = sbi * 64 + pr * 2 + a
                    DMA("pool", lambda e, bi=bi, a=a, col=col: e.indirect_dma_start(out=kpf[bi][:, a, :], out_offset=None, in_=ck[:, :],
                                                                                      in_offset=bass.IndirectOffsetOnAxis(ap=ridx[:, col:col + 1], axis=0)),
                           reads=[b_ridx], writes=[b_kpf[bi]])
                OP("dve", lambda e, bi=bi: e.tensor_copy(out=kpb[bi][:], in_=kpf[bi][:]), r=[b_kpf[bi]], w=[b_kpb[bi]])
                for a in range(2):
                    ti_ = xc_["n"] % 2
                    xc_["n"] += 1
                    pg = pr * 2 + a
                    for ch in range(4):
                        OP("pe", lambda e, ti_=ti_, bi=bi, a=a, ch=ch: e.transpose(out=PT[ti_][:, ch, :], in_=kpb[bi][:, a, ch * 128:(ch + 1) * 128], identity=ident[:]),
                           r=[b_kpb[bi], b_id], w=[bPT[ti_]])
                    OP("act", lambda e, ti_=ti_, pg=pg: e.copy(out=kTsp[:, :, pg * 128:(pg + 1) * 128], in_=PT[ti_][:, 0:4, :]), r=[bPT[ti_]], w=[b_kTsp])
            for ch in range(4):
                OP("dve", lambda e, ch=ch: e.tensor_reduce(out=kmsf[:, ch, :], in_=kTsp[:, ch, :].rearrange("p (n k) -> p n k", k=256), axis=AX.X, op=ALU.add), r=[b_kTsp], w=[b_kmsf])
            OP("dve", lambda e: e.tensor_copy(out=kms[:], in_=kmsf[:]), r=[b_kmsf], w=[b_kms])

        TQ["on"] = False
        k_phase(0)
        for sbi in range(4):
            kTsp = kTsp2[sbi % 2]; b_kTsp = b_kTsp2[sbi % 2]
            if sbi + 1 < 4:
                TQ["on"] = True
                k_phase(sbi + 1)
                TQ["on"] = False
            kq_slice = (len(TQ["q"]) + 64) // 65
            OP("pool", lambda e: e.memset(Qbd[:], 0.0), w=[b_Qbd])
            for h in range(8):
                ch, hh = h // 2, h % 2
                OP("dve", lambda e, h=h, ch=ch, hh=hh, sbi=sbi: e.tensor_copy(out=Qbd[hh * 64:(hh + 1) * 64, ch, h * 8:(h + 1) * 8], in_=qTs[hh * 64:(hh + 1) * 64, ch, sbi * 8:(sbi + 1) * 8]),
                   r=[b_qTs], w=[b_Qbd])
            for ch in range(4):
                OP("pe", lambda e, ch=ch: e.matmul(PF[4][0:64, 0:32], lhsT=Qbd[:, ch, :], rhs=kms[:, ch, :], start=(ch == 0), stop=(ch == 3)), r=[b_Qbd, b_kms], w=[bPF[4]])
            OP("dve", lambda e: e.tensor_copy(out=scs[:], in_=PF[4][0:64, 0:32]), r=[bPF[4]], w=[b_scs])
            OP("dve", lambda e: e.max(out=top8s[:], in_=scs[:]), r=[b_scs], w=[b_top8s])
            OP("dve", lambda e: e.tensor_scalar(out=penF[:, 0:32], in0=scs[:], scalar1=top8s[:, 2:3], scalar2=NEG, op0=ALU.is_lt, op1=ALU.mult), r=[b_scs, b_top8s], w=[b_penF])
            OP("pool", lambda e: e.memset(penF[:, 32:33], 0.0), w=[b_penF])
            OP("dve", lambda e: e.tensor_scalar(out=penF[:], in0=penF[:], scalar1=rbcol[:, 0:1], scalar2=None, op0=ALU.add), r=[b_penF, b_rbcol], w=[b_penF])
            OP("pe", lambda e: e.transpose(out=PF[4][0:33, 64:128], in_=penF[:], identity=identf[0:64, 0:64]), r=[b_penF, b_idf], w=[bPF[4]])
            OP("act", lambda e: e.copy(out=penTb[:], in_=PF[4][0:33, 64:128]), r=[bPF[4]], w=[b_penTb])
            def s_load(kt, sbi=sbi):
                if kt >= 64:
                    return
                bi = kt % 4
                col = sbi * 64 + kt
                DMA("pool", lambda e, bi=bi, col=col: e.indirect_dma_start(out=vpf[bi][:, :], out_offset=None, in_=cv[:, :],
                                                                              in_offset=bass.IndirectOffsetOnAxis(ap=ridx[:, col:col + 1], axis=0)),
                       reads=[b_ridx], writes=[b_vpf[bi]])
                OP("dve", lambda e, bi=bi: e.tensor_copy(out=vpb[bi][:, :], in_=vpf[bi][:, :]), r=[b_vpf[bi]], w=[b_vpb[bi]])

            def s_S(kt, sbi=sbi, kTsp=kTsp, b_kTsp=b_kTsp):
                own = kt == 64
                si = kt % 2
                L = 8 if own else 128
                n = 32 if own else kt // 2
                for ch in range(4):
                    lhs = kTs_own[:, ch, sbi * 8:(sbi + 1) * 8] if own else kTsp[:, ch, kt * 128:(kt + 1) * 128]
                    OP("pe", lambda e, si=si, ch=ch, lhs=lhs, L=L: e.matmul(PF[si][0:L, 0:64], lhsT=lhs, rhs=Qbd[:, ch, :], start=(ch == 0), stop=False),
                       r=[b_kTsp, b_kTs_own, b_Qbd], w=[bPF[si]])
                OP("pe", lambda e, si=si, n=n, L=L: e.matmul(PF[si][0:L, 0:64], lhsT=ohS[:, n, 0:L], rhs=penTb[:], start=False, stop=True), r=[b_ohS, b_penTb], w=[bPF[si]])

            def s_exp(kt):
                own = kt == 64
                si = kt % 2
                L = 8 if own else 128
                if kt >= 63:
                    Tm = TS0[:].rearrange("p h q -> p (h q)") if own else TS128[:].rearrange("p h q -> p (h q)")
                    OP("dve", lambda e, si=si, L=L, Tm=Tm: e.scalar_tensor_tensor(out=tmS[0:L, :], in0=PF[si][0:L, 0:64], scalar=0.125, in1=Tm, op0=ALU.mult, op1=ALU.add),
                       r=[bPF[si], b_TS0, b_TS128], w=[b_tmS])
                    OP("act", lambda e, si=si, L=L: e.activation(out=PTs[si][0:L, :], in_=tmS[0:L, :], func=AF.Exp), r=[b_tmS], w=[b_PTs[si]])
                else:
                    OP("act", lambda e, si=si: e.activation(out=PTs[si][:], in_=PF[si][:, 0:64], func=AF.Exp, scale=0.125), r=[bPF[si]], w=[b_PTs[si]])

            def s_PV(kt, sbi=sbi):
                own = kt == 64
                si = kt % 2
                L = 8 if own else 128
                rhsv = Vs_own[:, sbi, :] if own else vpb[kt % 4][:, :]
                OP("pe", lambda e, si=si, L=L, rhsv=rhsv, kt=kt: e.matmul(PF[2][0:64, :], lhsT=PTs[si][0:L, :], rhs=rhsv, start=(kt == 0), stop=(kt == 64)),
                   r=[b_PTs[si], b_Vs_own, b_vpb[kt % 4]], w=[bPF[2]])
                OP("pe", lambda e, si=si, L=L, kt=kt: e.matmul(PF[3][0:64, 0:1], lhsT=PTs[si][0:L, :], rhs=onesb[0:L, 0:1], start=(kt == 0), stop=(kt == 64)),
                   r=[b_PTs[si], b_onesb], w=[bPF[3]])

            s_load(0); s_load(1); s_load(2)
            s_S(0)
            for kt in range(65):
                flush_tq(kq_slice)
                s_load(kt + 3)
                if kt + 1 < 65:
                    s_S(kt + 1)
                s_exp(kt)
                s_PV(kt)
            flush_tq(10 ** 9)
            OP("dve", lambda e: e.reciprocal(out=recs[:], in_=PF[3][0:64, 0:1]), r=[bPF[3]], w=[b_recs])
            OP("dve", lambda e: e.tensor_scalar(out=osb[:], in0=PF[2][0:64, :], scalar1=recs[:, 0:1], scalar2=None, op0=ALU.mult), r=[bPF[2], b_recs], w=[b_osb])
            for h in range(8):
                DMA("sp", lambda e, h=h, sbi=sbi: e.dma_start(out=mix[NM + sbi * 8:NM + sbi * 8 + 8, h * 64:(h + 1) * 64], in_=osb[h * 8:(h + 1) * 8, h * 64:(h + 1) * 64]),
                       reads=[b_osb], writes=[b_mix3])
        kb.barrier()
        ps_.close()
        if stop_after == "satt":
            kb.final_wait("sp")
            kb.emit()
            return nc, dbg_outs

        b_mix2 = B("mix2")
        NT = 4096 + NS
        def ml_pass(hp):
            h0 = 2 * hp
            p2 = contextlib.ExitStack()
            mqT = sbt(f"h{hp}_" "mqT", [128, 2, NM + NS], BF16, p2); b_mqT = B("mqT")
            mkT = sbt(f"h{hp}_" "mkT", [128, 2, NM + NS], BF16, p2); b_mkT = B("mkT")
            mkt = sbt(f"h{hp}_" "mkt", [128, 32, 256], BF16, p2); b_mkt = B("mkt")
            mva = sbt(f"h{hp}_" "mva", [128, 32, 2, 129], BF16, p2); b_mva = B("mva")
            sgm = sbt(f"h{hp}_" "sgm", [128, 16, 256], BF16, p2); b_sgm = B("sgm")
            mkt_s = sbt(f"h{hp}_" "mkt_s", [8, 4, 256], BF16, p2); b_mkt_s = B("mkt_s")
            mva_s = sbt(f"h{hp}_" "mva_s", [8, 4, 2, 129], BF16, p2); b_mva_s = B("mva_s")
            sgm_s = sbt(f"h{hp}_" "sgm_s", [8, 4, 256], BF16, p2); b_sgm_s = B("sgm_s")
            OP("pool", lambda e: e.memset(mva[:, :, :, 128:129], 1.0), w=[b_mva])
            OP("pool", lambda e: e.memset(mva_s[:, :, :, 128:129], 1.0), w=[b_mva_s])
            Grow = sbt(f"h{hp}_" "Grow", [2, NT], F32, p2); b_Grow = B("Grow")
            Urow = sbt(f"h{hp}_" "Urow", [2, NT], F32, p2); b_Urow = B("Urow")
            Brow = sbt(f"h{hp}_" "Brow", [2, NT], F32, p2); b_Brow = B("Brow")
            mlgb = sbt(f"h{hp}_" "mlgb", [128, 256], F32, p2); b_mlgb = B("mlgb")
            kb.dma("sp", lambda e, h0=h0: e.dma_start(out=mlgb[:], in_=mlg[0:1, h0 * 128:h0 * 128 + 256].partition_broadcast(128)), writes=[b_mlgb])
            bgt = sbt(f"h{hp}_" "bgt", [2, 2], F32, p2); b_bgt = B("bgt")
            kb.dma("sp", lambda e, h0=h0: e.dma_start(out=bgt[:, 0:1], in_=bg[h0:h0 + 2, :]), writes=[b_bgt])
            kb.dma("sp", lambda e, h0=h0: e.dma_start(out=bgt[:, 1:2], in_=bg[4 + h0:4 + h0 + 2, :]), writes=[b_bgt])
            sm0 = sbt(f"h{hp}_" "sm0", [2, 4], F32, p2); b_sm0 = B("sm0")
            kb.dma("sp", lambda e, h0=h0: e.dma_start(out=sm0[:], in_=smi[h0:h0 + 2, :]), writes=[b_sm0])
            ones2 = sbt(f"h{hp}_" "ones2", [2, 512], F32, p2); b_ones2 = B("ones2")
            OP("pool", lambda e: e.memset(ones2[:], 1.0), w=[b_ones2])
            oh2 = sbt(f"h{hp}_" "oh2", [2, 2, 128], F32, p2); b_oh2 = B("oh2")
            OP("pool", lambda e: e.memset(oh2[:], 1.0), w=[b_oh2])
            OP("pool", lambda e: e.affine_select(out=oh2[:], in_=oh2[:], pattern=[[1, 2], [0, 128]], compare_op=ALU.is_equal, fill=0.0, base=0,
                                                 channel_multiplier=-1), r=[b_oh2], w=[b_oh2])
            cmask = sbt(f"h{hp}_" "cmask", [128, 128], F32, p2); b_cmask = B("cmask")
            OP("pool", lambda e: e.memset(cmask[:], 0.0), w=[b_cmask])
            OP("pool", lambda e: e.affine_select(out=cmask[:], in_=cmask[:], pattern=[[1, 128]], compare_op=ALU.is_ge, fill=-NEG, base=0,
                                                 channel_multiplier=-1), r=[b_cmask], w=[b_cmask])
            UT = sbt(f"h{hp}_" "UT", [128, 33, 2], F32, p2); GT = sbt(f"h{hp}_" "GT", [128, 33, 2], F32, p2); mT = sbt(f"h{hp}_" "mT", [128, 33, 2], F32, p2); EM = sbt(f"h{hp}_" "EM", [128, 33, 2], F32, p2)
            b_UT = B("UT"); b_GT = B("GT"); b_mT = B("mT"); b_EM = B("EM")
            UTs = sbt(f"h{hp}_" "UTs", [8, 4, 2], F32, p2); GTs = sbt(f"h{hp}_" "GTs", [8, 4, 2], F32, p2); mTs = sbt(f"h{hp}_" "mTs", [8, 4, 2], F32, p2); EMs = sbt(f"h{hp}_" "EMs", [8, 4, 2], F32, p2)
            b_UTs = B("UTs"); b_GTs = B("GTs"); b_mTs = B("mTs"); b_EMs = B("EMs")

            p2a = contextlib.ExitStack()
            wB = sbt(f"h{hp}_" "wB", [128, 8, 1032], BF16, p2a); b_wB = B("wB")
            for c in range(8):
                rows = slice(c * 128, (c + 1) * 128)
                for k4 in range(4):
                    kb.dma("pool", lambda e, c=c, k4=k4, rows=rows, h0=h0: e.dma_start(out=wB[:, c, k4 * 256:(k4 + 1) * 256],
                                                                                    in_=w_in[rows, 1536 + k4 * 512 + h0 * 128:1536 + k4 * 512 + h0 * 128 + 256]), writes=[b_wB])
                kb.dma("pool", lambda e, c=c, rows=rows: e.dma_start(out=wB[:, c, 1024:1032], in_=w_in[rows, 3584:3592]), writes=[b_wB])
            xnT2 = sbt(f"h{hp}_" "xnT2", [128, 8, 512], BF16, p2a); b_xnT2 = B("xnT2")
            gTs = sbt(f"h{hp}_" "gTs", [8, 512], F32, p2a); b_gTs = B("gTs")
            sgt = sbt(f"h{hp}_" "sgt", [128, 256], F32, p2a); b_sgt = B("sgt")
            KS = 128.0 ** -0.5
            groups = [("ctx", g) for g in range(4)] + [("main", g) for g in range(4)] + [("smp", 0)]
            for kind, g in groups:
                ntok = NS if kind == "smp" else 512
                gidx = {"ctx": g, "main": 4 + g, "smp": 8}[kind]
                kb.dma("sp", lambda e, gidx=gidx, ntok=ntok: e.dma_start(out=xnT2[:, :, 0:ntok], in_=xnTd[gidx].rearrange("p (c t) -> p c t", t=512)[:, :, 0:ntok]),
                       reads=[b_xnTd], writes=[b_xnT2])
                gcol0 = {"ctx": g * 512, "main": 2048 + g * 512, "smp": 4096}[kind]
                for c in range(8):
                    OP("pe", lambda e, c=c, ntok=ntok: e.matmul(PF[0][0:8, 0:ntok], lhsT=wB[:, c, 1024:1032], rhs=xnT2[:, c, 0:ntok], start=(c == 0), stop=(c == 7)),
                       r=[b_wB, b_xnT2], w=[bPF[0]])
                OP("dve", lambda e, ntok=ntok: e.tensor_copy(out=gTs[:, 0:ntok], in_=PF[0][0:8, 0:ntok]), r=[bPF[0]], w=[b_gTs])
                kb.dma("sp", lambda e, ntok=ntok, gcol0=gcol0, h0=h0: e.dma_start(out=Urow[:, gcol0:gcol0 + ntok], in_=gTs[h0:h0 + 2, 0:ntok]), reads=[b_gTs], writes=[b_Urow])
                kb.dma("sp", lambda e, ntok=ntok, gcol0=gcol0, h0=h0: e.dma_start(out=Brow[:, gcol0:gcol0 + ntok], in_=gTs[4 + h0:4 + h0 + 2, 0:ntok]), reads=[b_gTs], writes=[b_Brow])
                if kind != "ctx":
                    fcol0 = g * 512 if kind == "main" else NM
                    for l in range(2):
                        for which in ("q", "k"):
                            wc0 = (0 if which == "q" else 256) + l * 128
                            pf = 1 + (which == "k")
                            for c in range(8):
                                OP("pe", lambda e, c=c, pf=pf, wc0=wc0, ntok=ntok: e.matmul(PF[pf][:, 0:ntok], lhsT=wB[:, c, wc0:wc0 + 128], rhs=xnT2[:, c, 0:ntok],
                                                                                          start=(c == 0), stop=(c == 7)), r=[b_wB, b_xnT2], w=[bPF[pf]])
                            if which == "q":
                                evac(mqT[:, l, fcol0:fcol0 + ntok], PF[pf][:, 0:ntok], [bPF[pf]], [b_mqT])
                            else:
                                evac(mkT[:, l, fcol0:fcol0 + ntok], PF[pf][:, 0:ntok], [bPF[pf]], [b_mkT], scale=KS)
                if kind == "smp":
                    tiles = [(sbi * 8, 8, sbi) for sbi in range(4)]
                else:
                    tiles = [(t * 128, 128, (g if kind == "ctx" else 4 + g) * 4 + t) for t in range(4)]
                for (c0, n, ta) in tiles:
                    for c in range(8):
                        OP("pe", lambda e, c=c, c0=c0, n=n: e.matmul(PF[3][0:n, :], lhsT=xnT2[:, c, c0:c0 + n], rhs=wB[:, c, 256:768], start=(c == 0), stop=(c == 7)),
                           r=[b_wB, b_xnT2], w=[bPF[3]])
                    if kind == "smp":
                        kdst, b_kd = mkt_s[:, ta, :], b_mkt_s
                        vdst, b_vd = mva_s[:, ta, :, 0:128], b_mva_s
                    else:
                        kdst, b_kd = mkt[:, ta, :], b_mkt
                        vdst, b_vd = mva[:, ta, :, 0:128], b_mva
                    OP("dve", lambda e, n=n, kdst=kdst: e.tensor_scalar(out=kdst, in0=PF[3][0:n, 0:256], scalar1=KS, scalar2=None, op0=ALU.mult), r=[bPF[3]], w=[b_kd])
                    OP("dve", lambda e, n=n, vdst=vdst: e.tensor_copy(out=vdst, in_=PF[3][0:n, 256:512].rearrange("p (l d) -> p l d", d=128)), r=[bPF[3]], w=[b_vd])
                    if kind != "ctx":
                        for c in range(8):
                            OP("pe", lambda e, c=c, c0=c0, n=n: e.matmul(PF[4][0:n, 0:256], lhsT=xnT2[:, c, c0:c0 + n], rhs=wB[:, c, 768:1024], start=(c == 0), stop=(c == 7)),
                               r=[b_wB, b_xnT2], w=[bPF[4]])
                        OP("dve", lambda e, n=n: e.tensor_copy(out=sgt[0:n, :], in_=PF[4][0:n, 0:256]), r=[bPF[4]], w=[b_sgt])
                        OP("act", lambda e, n=n: e.activation(out=sgt[0:n, :], in_=sgt[0:n, :], func=AF.Sigmoid), r=[b_sgt], w=[b_sgt])
                        if kind == "smp":
                            sdst, b_sd = sgm_s[:, ta, :], b_sgm_s
                        else:
                            sdst, b_sd = sgm[:, ta - 16, :], b_sgm
                        OP("dve", lambda e, n=n, sdst=sdst: e.tensor_tensor(out=sdst, in0=sgt[0:n, :], in1=mlgb[0:n, :], op=ALU.mult), r=[b_sgt, b_mlgb], w=[b_sd])
            kb.barrier()
            p2a.close()
            if stop_after == "ml_a":
                kb.final_wait("sp"); kb.emit(); p2.close(); raise StopIteration

            nbf = sbt(f"h{hp}_" "nbf", [2, 1], F32, p2); b_nbf = B("nbf")
            OP("dve", lambda e: e.tensor_scalar(out=nbf[:], in0=bgt[:, 1:2], scalar1=-1.0, scalar2=None, op0=ALU.mult), r=[b_bgt], w=[b_nbf])
            OP("act", lambda e: e.activation(out=Brow[:], in_=Brow[:], func=AF.Exp, scale=-1.0, bias=nbf[:, 0:1]), r=[b_Brow, b_nbf], w=[b_Brow])
            OP("act", lambda e: e.activation(out=Brow[:], in_=Brow[:], func=AF.Ln, scale=1.0, bias=1.0), r=[b_Brow], w=[b_Brow])
            OP("dve", lambda e: e.tensor_scalar(out=Brow[:], in0=Brow[:], scalar1=-1.0, scalar2=None, op0=ALU.mult), r=[b_Brow], w=[b_Brow])
            OP("dve", lambda e: e.tensor_scalar(out=Urow[:], in0=Urow[:], scalar1=bgt[:, 0:1], scalar2=None, op0=ALU.add), r=[b_Urow, b_bgt], w=[b_Urow])
            OP("dve", lambda e: e.tensor_scalar(out=Brow[:, 0:2048], in0=Brow[:, 0:2048], scalar1=cfb[0:2, 0:1], scalar2=None, op0=ALU.mult), r=[b_Brow, b_cfb], w=[b_Brow])
            OP("dve", lambda e: e.tensor_scalar(out=Urow[:, 0:2048], in0=Urow[:, 0:2048], scalar1=cfb[0:2, 0:1], scalar2=cfb[0:2, 1:2], op0=ALU.mult, op1=ALU.add),
               r=[b_Urow, b_cfb], w=[b_Urow])
            segs = [(i * 512, 512, None if i == 0 else i * 512 - 1) for i in range(8)] + [(4096 + sbi * 8, 8, None) for sbi in range(4)]
            for (c0, n, prev) in segs:
                init = 0.0 if prev is None else Brow[:, prev:prev + 1]
                OP("dve", lambda e, c0=c0, n=n, init=init: e.tensor_tensor_scan(out=Brow[:, c0:c0 + n], data0=ones2[:, 0:n], data1=Brow[:, c0:c0 + n], initial=init,
                                                                                op0=ALU.mult, op1=ALU.add), r=[b_Brow, b_ones2], w=[b_Brow])
            OP("dve", lambda e: e.tensor_tensor(out=Urow[:], in0=Urow[:], in1=Brow[:], op=ALU.subtract), r=[b_Urow, b_Brow], w=[b_Urow])
            for si_, (c0, n, prev) in enumerate(segs):
                if c0 >= 4096:
                    sbi = (c0 - 4096) // 8
                    init = sm0[:, sbi:sbi + 1]
                else:
                    init = 0.0 if prev is None else Grow[:, prev:prev + 1]
                OP("dve", lambda e, c0=c0, n=n, init=init: e.tensor_tensor_scan(out=Grow[:, c0:c0 + n], data0=Urow[:, c0:c0 + n], data1=Urow[:, c0:c0 + n], initial=init,
                                                                                op0=ALU.max, op1=ALU.max), r=[b_Urow, b_Grow, b_sm0], w=[b_Grow])
            OP("dve", lambda e: e.tensor_tensor(out=Brow[:], in0=Brow[:], in1=Grow[:], op=ALU.add), r=[b_Grow, b_Brow], w=[b_Brow])
            for (row, b_row, colt, b_colt, colts, b_colts) in ((Urow, b_Urow, UT, b_UT, UTs, b_UTs), (Grow, b_Grow, GT, b_GT, GTs, b_GTs), (Brow, b_Brow, mT, b_mT, mTs, b_mTs)):
                for ck in range(32):
                    OP("pe", lambda e, ck=ck, row=row: e.transpose(out=PF[5][:, ck * 2:ck * 2 + 2], in_=row[:, ck * 128:(ck + 1) * 128], identity=identf[0:2, 0:2]),
                       r=[b_row, b_idf], w=[bPF[5]])
                OP("dve", lambda e, colt=colt: e.tensor_copy(out=colt[:, 0:32, :], in_=PF[5][:, 0:64].rearrange("p (c l) -> p c l", l=2)), r=[bPF[5]], w=[b_colt])
                for sbi in range(4):
                    OP("pe", lambda e, sbi=sbi, row=row: e.transpose(out=PF[5][0:8, sbi * 2:sbi * 2 + 2], in_=row[:, 4096 + sbi * 8:4096 + sbi * 8 + 8], identity=identf[0:2, 0:2]),
                       r=[b_row, b_idf], w=[bPF[5]])
                OP("dve", lambda e, colts=colts: e.tensor_copy(out=colts[:], in_=PF[5][0:8, 0:8].rearrange("p (c l) -> p c l", l=2)), r=[bPF[5]], w=[b_colts])
            OP("act", lambda e: e.activation(out=EM[:, 0:32, :], in_=mT[:, 0:32, :], func=AF.Exp, scale=-1.0), r=[b_mT], w=[b_EM])
            OP("act", lambda e: e.activation(out=EMs[:], in_=mTs[:], func=AF.Exp, scale=-1.0), r=[b_mTs], w=[b_EMs])

            if stop_after == "ml_b":
                kb.final_wait("sp"); kb.emit(); p2.close(); raise StopIteration
            Cf = sbt(f"h{hp}_" "Cf", [128, 2, 129], F32, p2); b_Cf = B("Cf")
            Cb = sbt(f"h{hp}_" "Cb", [128, 2, 129], BF16, p2); b_Cb = B("Cb")
            gprev = sbt(f"h{hp}_" "gprev", [128, 2], F32, p2); b_gprev = B("gprev")
            gend = [sbt(f"h{hp}_" f"gend{i}", [128, 2], F32, p2) for i in range(2)]; b_gend = [B(f"gend{i}") for i in range(2)]
            g2 = sbt(f"h{hp}_" "g2", [128, 2], F32, p2); b_g2 = B("g2")
            gtok = sbt(f"h{hp}_" "gtok", [128, 2], F32, p2); b_gtok = B("gtok")
            gst = sbt(f"h{hp}_" "gst", [128, 2], F32, p2); b_gst = B("gst")
            wst = sbt(f"h{hp}_" "wst", [128, 2], F32, p2); b_wst = B("wst")
            tmpD = sbt(f"h{hp}_" "tmpD", [128, 2, 128], F32, p2); b_tmpD = B("tmpD")
            sT = sbt(f"h{hp}_" "sT", [128, 2, 128], BF16, p2); b_sT = B("sT")
            hs = [sbt(f"h{hp}_" f"hs{i}", [128, 2, 129], F32, p2) for i in range(2)]; b_hs = [B(f"hs{i}") for i in range(2)]
            nd = sbt(f"h{hp}_" "nd", [128, 2, 129], F32, p2); b_nd = B("nd")
            dab = sbt(f"h{hp}_" "dab", [128, 2], F32, p2); b_dab = B("dab")
            ssh = sbt(f"h{hp}_" "ssh", [128, 2], F32, p2); b_ssh = B("ssh")
            junk = sbt(f"h{hp}_" "junk", [128, 128], F32, p2); b_junk = B("junk")
            gv = sbt(f"h{hp}_" "gv", [128, 2, 129], BF16, p2); b_gv = B("gv")
            mlb = [sbt(f"h{hp}_" f"mlb{i}", [128, 256], BF16, p2) for i in range(2)]; b_mlb = [B(f"mlb{i}") for i in range(2)]

            def chunkA(L, ck_cols, UTv, GTv, EMv, kT_v, qT_v, kt_v, va_v, sg_v, full, mix_rows, mi, gcol):
                for l in range(2):
                    OP("pe", lambda e, l=l: e.matmul(PF[4][0:L, l * 128:l * 128 + L], lhsT=oh2[:, l, 0:L], rhs=Grow[:, ck_cols], start=True, stop=True),
                       r=[b_oh2, b_Grow], w=[bPF[4]])
                OP("dve", lambda e: e.tensor_copy(out=gend[mi][0:L, :].rearrange("p (l o) -> p l o", o=1), in_=PF[4][0:L, 0:256].rearrange("p (l t) -> p l t", t=128)[:, :, L - 1:L]),
                   r=[bPF[4]], w=[b_gend[mi]])
                if not full:
                    return
                for l in range(2):
                    OP("pe", lambda e, l=l: e.matmul(PF[0][0:L, l * 128:l * 128 + L], lhsT=kT_v(l), rhs=qT_v(l), start=True, stop=True), r=[b_mkT, b_mqT], w=[bPF[0]])
                    OP("dve", lambda e, l=l: e.scalar_tensor_tensor(out=tmpD[0:L, l, 0:L], in0=PF[4][0:L, l * 128:l * 128 + L], scalar=UTv[:, l:l + 1], in1=cmask[0:L, 0:L],
                                                                    op0=ALU.subtract, op1=ALU.add), r=[bPF[4], b_UT, b_UTs, b_cmask], w=[b_tmpD])
                OP("act", lambda e: e.activation(out=tmpD[0:L, :, 0:L], in_=tmpD[0:L, :, 0:L], func=AF.Exp, scale=-1.0), r=[b_tmpD], w=[b_tmpD])
                OP("dve", lambda e: e.tensor_tensor(out=sT[0:L, :, 0:L], in0=PF[0][0:L, 0:256].rearrange("p (l t) -> p l t", t=128)[:, :, 0:L], in1=tmpD[0:L, :, 0:L], op=ALU.mult),
                   r=[bPF[0], b_tmpD], w=[b_sT])
                for l in range(2):
                    OP("pe", lambda e, l=l: e.matmul(PF[1][0:L, l * 129:(l + 1) * 129], lhsT=sT[0:L, l, 0:L], rhs=va_v(l), start=True, stop=True),
                       r=[b_sT, b_mva, b_mva_s], w=[bPF[1]])
                OP("dve", lambda e: e.tensor_copy(out=hs[mi][0:L, :, :], in_=PF[1][0:L, 0:258].rearrange("p (l c) -> p l c", c=129)), r=[bPF[1]], w=[b_hs[mi]])

            def chunkB(L, ck_cols, UTv, GTv, EMv, kT_v, qT_v, kt_v, va_v, sg_v, full, mix_rows, mi, gcol):
                if not full:
                    return
                for l in range(2):
                    OP("pe", lambda e, l=l: e.matmul(PF[2][0:L, l * 129:(l + 1) * 129], lhsT=qT_v(l), rhs=Cb[:, l, :], start=True, stop=True),
                       r=[b_mqT, b_Cb], w=[bPF[2]])
                OP("dve", lambda e: e.tensor_tensor(out=g2[0:L, :], in0=gprev[0:L, :], in1=GTv, op=ALU.subtract), r=[b_gprev, b_GT, b_GTs], w=[b_g2])
                OP("act", lambda e: e.activation(out=wst[0:L, :], in_=g2[0:L, :], func=AF.Exp), r=[b_g2], w=[b_wst])
                for l in range(2):
                    OP("dve", lambda e, l=l: e.scalar_tensor_tensor(out=nd[0:L, l, :], in0=PF[2][0:L, l * 129:(l + 1) * 129], scalar=wst[0:L, l:l + 1], in1=hs[mi][0:L, l, :],
                                                                    op0=ALU.mult, op1=ALU.add), r=[bPF[2], b_wst, b_hs[mi]], w=[b_nd])
                    OP("dve", lambda e, l=l: e.scalar_tensor_tensor(out=dab[0:L, l:l + 1], in0=nd[0:L, l, 128:129], scalar=-1.0, in1=nd[0:L, l, 128:129], op0=ALU.mult, op1=ALU.max),
                       r=[b_nd], w=[b_dab])
                    OP("dve", lambda e, l=l: e.tensor_scalar(out=dab[0:L, l:l + 1], in0=dab[0:L, l:l + 1], scalar1=EMv[:, l:l + 1], scalar2=None, op0=ALU.max),
                       r=[b_dab, b_EM, b_EMs], w=[b_dab])
                OP("dve", lambda e: e.reciprocal(out=dab[0:L, :], in_=dab[0:L, :]), r=[b_dab], w=[b_dab])
                for l in range(2):
                    OP("act", lambda e, l=l: e.activation(out=junk[0:L, :], in_=nd[0:L, l, 0:128], func=AF.Square, scale=dab[0:L, l:l + 1], accum_out=ssh[0:L, l:l + 1]),
                       r=[b_nd, b_dab], w=[b_junk, b_ssh])
                OP("act", lambda e: e.activation(out=ssh[0:L, :], in_=ssh[0:L, :], func=AF.Sqrt, scale=1.0 / 128, bias=EPS), r=[b_ssh], w=[b_ssh])
                OP("dve", lambda e: e.reciprocal(out=ssh[0:L, :], in_=ssh[0:L, :]), r=[b_ssh], w=[b_ssh])
                OP("dve", lambda e: e.tensor_tensor(out=ssh[0:L, :], in0=ssh[0:L, :], in1=dab[0:L, :], op=ALU.mult), r=[b_ssh, b_dab], w=[b_ssh])
                for l in range(2):
                    OP("dve", lambda e, l=l: e.scalar_tensor_tensor(out=mlb[mi][0:L, l * 128:(l + 1) * 128], in0=nd[0:L, l, 0:128], scalar=ssh[0:L, l:l + 1], in1=sg_v(l),
                                                                    op0=ALU.mult, op1=ALU.mult), r=[b_nd, b_ssh, b_sgm, b_sgm_s], w=[b_mlb[mi]])
                kb.dma("sp", lambda e: e.dma_start(out=mix[mix_rows, 512 + h0 * 128:512 + h0 * 128 + 256], in_=mlb[mi][0:L, :]), reads=[b_mlb[mi]], writes=[b_mix2])

            def chunkC(L, ck_cols, UTv, GTv, EMv, kT_v, qT_v, kt_v, va_v, sg_v, full, mix_rows, mi, gcol):
                gend_bcast(gcol)
                OP("dve", lambda e: e.tensor_tensor(out=gtok[0:L, :], in0=UTv, in1=gend[mi][0:L, :], op=ALU.subtract), r=[b_UT, b_UTs, b_gend[mi]], w=[b_gtok])
                OP("act", lambda e: e.activation(out=gtok[0:L, :], in_=gtok[0:L, :], func=AF.Exp), r=[b_gtok], w=[b_gtok])
                for l in range(2):
                    OP("dve", lambda e, l=l: e.tensor_scalar(out=gv[0:L, l, :], in0=va_v(l), scalar1=gtok[0:L, l:l + 1], scalar2=None, op0=ALU.mult),
                       r=[b_mva, b_mva_s, b_gtok], w=[b_gv])
                    OP("pe", lambda e, l=l: e.matmul(PF[3][:, l * 129:(l + 1) * 129], lhsT=kt_v(l), rhs=gv[0:L, l, :], start=True, stop=True), r=[b_mkt, b_mkt_s, b_gv], w=[bPF[3]])
                OP("dve", lambda e: e.tensor_tensor(out=g2[:, :], in0=gprev[:, :], in1=gendb[:, :], op=ALU.subtract), r=[b_gprev, b_gendb], w=[b_g2])
                OP("act", lambda e: e.activation(out=gst[:, :], in_=g2[:, :], func=AF.Exp), r=[b_g2], w=[b_gst])
                for l in range(2):
                    OP("dve", lambda e, l=l: e.scalar_tensor_tensor(out=Cf[:, l, :], in0=Cf[:, l, :], scalar=gst[:, l:l + 1], in1=PF[3][:, l * 129:(l + 1) * 129],
                                                                    op0=ALU.mult, op1=ALU.add), r=[b_Cf, b_gst, bPF[3]], w=[b_Cf])
                OP("act", lambda e: e.copy(out=Cb[:], in_=Cf[:]), r=[b_Cf], w=[b_Cb])
                OP("dve", lambda e: e.tensor_copy(out=gprev[:], in_=gendb[:]), r=[b_gendb], w=[b_gprev])

            gendb = sbt(f"h{hp}_" "gendb", [128, 2], F32, p2); b_gendb = B("gendb")

            def gend_bcast(col):
                for l in range(2):
                    OP("pe", lambda e, l=l: e.matmul(PF[5][:, l:l + 1], lhsT=oh2[:, l, :], rhs=Grow[:, col:col + 1], start=True, stop=True), r=[b_oh2, b_Grow], w=[bPF[5]])
                OP("dve", lambda e: e.tensor_copy(out=gendb[:], in_=PF[5][:, 0:2]), r=[bPF[5]], w=[b_gendb])

            OP("pool", lambda e: e.memset(Cf[:], 0.0), w=[b_Cf])
            OP("pool", lambda e: e.memset(Cb[:], 0.0), w=[b_Cb])
            OP("pool", lambda e: e.memset(gprev[:], 0.0), w=[b_gprev])
            def pargs(ck):
                full = ck >= 16
                tm = ck - 16
                cols = slice(ck * 128, (ck + 1) * 128)
                fc = slice(tm * 128, (tm + 1) * 128)
                return (128, cols, UT[:, ck, :], GT[:, ck, :], EM[:, ck, :],
                        lambda l, fc=fc: mkT[:, l, fc], lambda l, fc=fc: mqT[:, l, fc], lambda l, ck=ck: mkt[:, ck, l * 128:(l + 1) * 128],
                        lambda l, ck=ck: mva[:, ck, l, :], lambda l, tm=tm: sgm[:, tm, l * 128:(l + 1) * 128], full,
                        slice(tm * 128, (tm + 1) * 128), ck % 2, ck * 128 + 127)
            chunkA(*pargs(0))
            for ck in range(32):
                if ck + 1 < 32:
                    chunkA(*pargs(ck + 1))
                chunkB(*pargs(ck))
                chunkC(*pargs(ck))
            if stop_after == "ml_c":
                kb.final_wait("sp"); kb.emit(); p2.close(); raise StopIteration
            for l in range(2):
                h = h0 + l
                store(C_p[h * 128:(h + 1) * 128, :], Cf[:, l, 0:128], b_Cf)
                store(n_p[h * 128:(h + 1) * 128, :], Cf[:, l, 128:129], b_Cf)
            store(m_p[h0:h0 + 2, :], Brow[:, 4095:4096], b_Brow)
            for sbi in range(4):
                for l in range(2):
                    h = h0 + l
                    r0 = (sbi * 4 + h) * 128
                    kb.dma("sp", lambda e, l=l, r0=r0: e.dma_start(out=Cf[:, l, 0:128], in_=sC[r0:r0 + 128, :]), writes=[b_Cf])
                    kb.dma("sp", lambda e, l=l, r0=r0: e.dma_start(out=Cf[:, l, 128:129], in_=sn[r0:r0 + 128, :]), writes=[b_Cf])
                    kb.dma("sp", lambda e, l=l, h=h, sbi=sbi: e.dma_start(out=gprev[:, l:l + 1], in_=smi[h:h + 1, sbi:sbi + 1].partition_broadcast(128)), writes=[b_gprev])
                OP("act", lambda e: e.copy(out=Cb[:], in_=Cf[:]), r=[b_Cf], w=[b_Cb])
                c0 = 4096 + sbi * 8
                fc = slice(NM + sbi * 8, NM + sbi * 8 + 8)
                sargs = (8, slice(c0, c0 + 8), UTs[:, sbi, :], GTs[:, sbi, :], EMs[:, sbi, :],
                         lambda l, fc=fc: mkT[:, l, fc], lambda l, fc=fc: mqT[:, l, fc], lambda l, sbi=sbi: mkt_s[:, sbi, l * 128:(l + 1) * 128],
                         lambda l, sbi=sbi: mva_s[:, sbi, l, :], lambda l, sbi=sbi: sgm_s[:, sbi, l * 128:(l + 1) * 128], True,
                         slice(NM + sbi * 8, NM + sbi * 8 + 8), sbi % 2, c0 + 7)
                chunkA(*sargs)
                chunkB(*sargs)
                chunkC(*sargs)
                for l in range(2):
                    h = h0 + l
                    r0 = (sbi * 4 + h) * 128
                    store(C_s[r0:r0 + 128, :], Cf[:, l, 0:128], b_Cf)
                    store(n_s[r0:r0 + 128, :], Cf[:, l, 128:129], b_Cf)
                store(m_s[sbi * 4 + h0:sbi * 4 + h0 + 2, :], Brow[:, c0 + 7:c0 + 8], b_Brow)
            kb.barrier()
            p2.close()
        try:
            for hp in range(2):
                ml_pass(hp)
        except StopIteration:
            return nc, dbg_outs
        if stop_after == "ml":
            kb.final_wait("sp")
            kb.emit()
            return nc, dbg_outs

        p3 = contextlib.ExitStack()
        wO = sbt("wO", [128, 8, 1024], BF16, p3); b_wO = B("wO")
        wU = sbt("wU", [128, 8, 4096], BF16, p3); b_wU = B("wU")
        wD = sbt("wD", [128, 32, 1024], BF16, p3); b_wD = B("wD")
        for c in range(8):
            load_w(wO[:, c, :], w_out[c * 128:(c + 1) * 128, :], b_wO, 1024)
        for c in range(8):
            load_w(wU[:, c, :], w_up[c * 128:(c + 1) * 128, :], b_wU, 4096)
        for f in range(32):
            load_w(wD[:, f, :], w_down[f * 128:(f + 1) * 128, :], b_wD, 1024)
        kb.dma("sp", lambda e: e.dma_start(out=g1[:], in_=nffn[0:1, :].partition_broadcast(128)), writes=[b_g1])
        g3 = sbt("g3", [128, D], F32, p3); b_g3 = B("g3")
        kb.dma("sp", lambda e: e.dma_start(out=g3[:], in_=nfin[0:1, :].partition_broadcast(128)), writes=[b_g3])
        mixb = [sbt(f"mixb{i}", [128, D], BF16, p3) for i in range(2)]; b_mixb = [B(f"mixb{i}") for i in range(2)]
        mixT = sbt("mixT", [128, 8, 256], BF16, p3); b_mixT = B("mixT")
        xn2T = sbt("xn2T", [128, 8, 256], BF16, p3); b_xn2T = B("xn2T")
        uT = sbt("uT", [128, 32, 256], BF16, p3); b_uT = B("uT")
        ur = [sbt(f"ur{i}", [128, 256], F32, p3) for i in range(2)]; b_ur = [B(f"ur{i}") for i in range(2)]
        yo = sbt("yo", [128, D], F32, p3); b_yo = B("yo")
        all_mix = [b_mix, b_mix2, b_mix3]
        fgroups = [("main", gi) for gi in range(8)] + [("smp", 0)]
        for kind, gi in fgroups:
            if kind == "main":
                tiles = [(gi * 256 + t * 128, 128, t * 128) for t in range(2)]
                xsrc, ydst = xm, y_m
            else:
                tiles = [(0, NS, 0)]
                xsrc, ydst = xs, y_s
            ntok = sum(n for _, n, _ in tiles)
            for ti, (r0, n, c0) in enumerate(tiles):
                mr0 = r0 if kind == "main" else NM
                kb.dma("sp", lambda e, ti=ti, r0=r0, n=n, xsrc=xsrc: e.dma_start(out=xt[ti][0:n, :], in_=xsrc[r0:r0 + n, :]), writes=[b_xt[ti]])
                kb.dma("sp", lambda e, ti=ti, mr0=mr0, n=n: e.dma_start(out=mixb[ti][0:n, :], in_=mix[mr0:mr0 + n, :]), reads=all_mix, writes=[b_mixb[ti]])
                to_featmajor(mixb[ti], b_mixb[ti], n, mixT[:, :, c0:c0 + n], b_mixT)
            for ti, (r0, n, c0) in enumerate(tiles):
                for hf in range(2):
                    for c in range(8):
                        OP("pe", lambda e, c=c, hf=hf, n=n, c0=c0: e.matmul(PF[hf][0:n, :], lhsT=mixT[:, c, c0:c0 + n], rhs=wO[:, c, hf * 512:(hf + 1) * 512], start=(c == 0), stop=(c == 7)),
                           r=[b_mixT, b_wO], w=[bPF[hf]])
                    OP("dve", lambda e, ti=ti, hf=hf, n=n: e.tensor_tensor(out=xt[ti][0:n, hf * 512:(hf + 1) * 512], in0=PF[hf][0:n, :], in1=xt[ti][0:n, hf * 512:(hf + 1) * 512], op=ALU.add),
                       r=[bPF[hf], b_xt[ti]], w=[b_xt[ti]])
                rms_scale(xt[ti][0:n, :], b_xt[ti], n, ti, xnb[ti][0:n, :], b_xnb[ti])
                OP("dve", lambda e, ti=ti, n=n: e.scalar_tensor_tensor(out=xnb[ti][0:n, :], in0=xt[ti][0:n, :], scalar=rstd[ti][0:n, 0:1], in1=g1[0:n, :], op0=ALU.mult, op1=ALU.mult),
                   r=[b_xt[ti], b_rstd[ti], b_g1], w=[b_xnb[ti]])
                to_featmajor(xnb[ti], b_xnb[ti], n, xn2T[:, :, c0:c0 + n], b_xn2T)
            for f in range(32):
                pf = 2 + f % 2
                ui = f % 2
                for c in range(8):
                    OP("pe", lambda e, c=c, f=f, pf=pf, ntok=ntok: e.matmul(PF[pf][:, 0:ntok], lhsT=wU[:, c, f * 128:(f + 1) * 128], rhs=xn2T[:, c, 0:ntok], start=(c == 0), stop=(c == 7)),
                       r=[b_wU, b_xn2T], w=[bPF[pf]])
                OP("dve", lambda e, pf=pf, ui=ui, ntok=ntok: e.tensor_scalar(out=ur[ui][:, 0:ntok], in0=PF[pf][:, 0:ntok], scalar1=0.0, scalar2=None, op0=ALU.max), r=[bPF[pf]], w=[b_ur[ui]])
                OP("act", lambda e, f=f, ui=ui, ntok=ntok: e.activation(out=uT[:, f, 0:ntok], in_=ur[ui][:, 0:ntok], func=AF.Square), r=[b_ur[ui]], w=[b_uT])
            for ti, (r0, n, c0) in enumerate(tiles):
                for hf in range(2):
                    for f in range(32):
                        OP("pe", lambda e, f=f, hf=hf, n=n, c0=c0: e.matmul(PF[hf][0:n, :], lhsT=uT[:, f, c0:c0 + n], rhs=wD[:, f, hf * 512:(hf + 1) * 512], start=(f == 0), stop=(f == 31)),
                           r=[b_uT, b_wD], w=[bPF[hf]])
                    OP("dve", lambda e, ti=ti, hf=hf, n=n: e.tensor_tensor(out=xt[ti][0:n, hf * 512:(hf + 1) * 512], in0=PF[hf][0:n, :], in1=xt[ti][0:n, hf * 512:(hf + 1) * 512], op=ALU.add),
                       r=[bPF[hf], b_xt[ti]], w=[b_xt[ti]])
                rms_scale(xt[ti][0:n, :], b_xt[ti], n, ti, xnb[ti][0:n, :], b_xnb[ti])
                OP("dve", lambda e, ti=ti, n=n: e.scalar_tensor_tensor(out=yo[0:n, :], in0=xt[ti][0:n, :], scalar=rstd[ti][0:n, 0:1], in1=g3[0:n, :], op0=ALU.mult, op1=ALU.mult),
                   r=[b_xt[ti], b_rstd[ti], b_g3], w=[b_yo])
                store(ydst[r0:r0 + n, :], yo[0:n, :], b_yo)
        kb.barrier()
        p3.close()
        kb.final_wait("sp")
        kb.emit()
        return nc, dbg_outs


def make_in_maps(inp):
    f = lambda a: np.ascontiguousarray(a, dtype=np.float32)
    xp = np.asarray(inp["x_prompt"]); xsamp = np.asarray(inp["x_sample"])
    ckf = f(np.asarray(inp["cache_k"]).reshape(2560 * 128, 512))
    cvf = f(np.asarray(inp["cache_v"]).reshape(2560 * 128, 512))
    ptab = np.asarray(inp["page_table"]).astype(np.int32)
    sCf = np.asarray(inp["state_C"])[0]; snf = np.asarray(inp["state_n"])[0]; smf = np.asarray(inp["state_m"])[0]
    shared = {
        "w_in": f(inp["w_in"][0]), "w_out": f(inp["w_out"][0]), "w_up": f(inp["w_up"][0]), "w_down": f(inp["w_down"][0]),
        "nmix": f(inp["norm_mix"]).reshape(1, D), "nffn": f(inp["norm_ffn"]).reshape(1, D), "nfin": f(inp["norm_final"]).reshape(1, D),
        "mlg": f(inp["ml_norm"]).reshape(1, 512),
        "bg": f(np.concatenate([np.asarray(inp["b_ig"]).reshape(-1), np.asarray(inp["b_fg"]).reshape(-1)])).reshape(8, 1),
        "rb": f(inp["rel_bias"]).reshape(1, 256), "ck": ckf, "cv": cvf,
    }
    maps = []
    for c in range(8):
        b, half = c // 2, c % 2
        m = dict(shared)
        m["xm"] = f(xp[b, half * NM:(half + 1) * NM])
        m["xc"] = f(xp[b, 0:NM]) if half else np.zeros((NM, D), np.float32)
        m["xs"] = f(xsamp[4 * c:4 * c + 4].reshape(NS, D))
        m["pt"] = np.ascontiguousarray(ptab[4 * c:4 * c + 4].reshape(1, 256))
        m["sC"] = f(sCf[4 * c:4 * c + 4].reshape(16 * 128, 128))
        m["sn"] = f(snf[4 * c:4 * c + 4].reshape(16 * 128, 1))
        m["smi"] = f(smf[4 * c:4 * c + 4].T)
        m["cf"] = np.array([[float(half), (float(half) - 1.0) * 30000.0, 0.0, 0.0]], np.float32)
        cd = np.zeros((16, 2, 16), np.float32)
        for qt in range(16):
            own = 8 + qt // 2
            for n in range(16):
                is_cand = (n < 8 and half == 1) or (8 <= n < own)
                cd[qt, 0, n] = NEG if is_cand else 0.0
                cd[qt, 1, n] = 0.0 if (is_cand or n == own) else NEG
        m["cand"] = cd.reshape(1, 512)
        maps.append(m)
    return maps


STOP_AFTER = "all"
CACHE_ROWS = 2560 * 128


def kernel(**inputs):
    maps = make_in_maps(inputs)
    for m in maps:
        m["ck"] = m["ck"][:CACHE_ROWS]
        m["cv"] = m["cv"][:CACHE_ROWS]
    nc, _ = build_program(stop_after=STOP_AFTER, cache_rows=CACHE_ROWS)
    res = run_bass_kernel_spmd(nc, maps, core_ids=list(range(8)))
    R = res.results
    f32 = np.float32
    y_prompt = np.zeros((4, 4096, D), f32); y_sample = np.zeros((32, 8, D), f32)
    nkp = np.zeros((1, 4, 4096, 8, 64), f32); nvp = np.zeros((1, 4, 4096, 8, 64), f32)
    nCp = np.zeros((1, 4, 4, 128, 128), f32); nnp_ = np.zeros((1, 4, 4, 128), f32); nmp = np.zeros((1, 4, 4), f32)
    nks = np.zeros((1, 32, 8, 8, 64), f32); nvs = np.zeros((1, 32, 8, 8, 64), f32)
    nCs = np.zeros((1, 32, 4, 128, 128), f32); nns = np.zeros((1, 32, 4, 128), f32); nms = np.zeros((1, 32, 4), f32)
    for c in range(8):
        b, half = c // 2, c % 2
        r = R[c]
        sl = slice(half * NM, (half + 1) * NM)
        y_prompt[b, sl] = r["y_m"]
        y_sample[4 * c:4 * c + 4] = r["y_s"].reshape(4, 8, D)
        nkp[0, b, sl] = r["k_m"].reshape(NM, 8, 64)
        nvp[0, b, sl] = r["v_m"].reshape(NM, 8, 64)
        if half == 1:
            nCp[0, b] = r["C_p"].reshape(4, 128, 128)
            nnp_[0, b] = r["n_p"].reshape(4, 128)
            nmp[0, b] = r["m_p"].reshape(4)
        nks[0, 4 * c:4 * c + 4] = r["k_s"].reshape(4, 8, 8, 64)
        nvs[0, 4 * c:4 * c + 4] = r["v_s"].reshape(4, 8, 8, 64)
        nCs[0, 4 * c:4 * c + 4] = r["C_s"].reshape(4, 4, 128, 128)
        nns[0, 4 * c:4 * c + 4] = r["n_s"].reshape(4, 4, 128)
        nms[0, 4 * c:4 * c + 4] = r["m_s"].reshape(4, 4)
    return (y_prompt, y_sample, nkp, nvp, nCp, nnp_, nmp, nks, nvs, nCs, nns, nms)
```

```python
import contextlib
import math
import numpy as np
import concourse.bass as bass
import concourse.mybir as mybir
from concourse.alu_op_type import AluOpType as ALU
from concourse.bass_utils import run_bass_kernel_spmd

F32 = mybir.dt.float32
BF16 = mybir.dt.bfloat16
I32 = mybir.dt.int32
AF = mybir.ActivationFunctionType
AX = mybir.AxisListType

ENGS = ("pe", "act", "dve", "pool", "sp")
NEG = -30000.0
D = 1024
NM = 2048
NS = 32
EPS = 1e-6


class Buf:
    __slots__ = ("name", "w", "rs", "dsem", "dcnt")

    def __init__(self, name):
        self.name = name
        self.w = None
        self.rs = {}
        self.dsem = None
        self.dcnt = 0


class KB:
    def __init__(self, nc, stack):
        self.nc = nc
        self.stack = stack
        self.sems = {}
        self.cnt = {e: 0 for e in ENGS}
        self.known = {e: {} for e in ENGS}
        self.prog = {e: [] for e in ENGS}
        for e in ENGS:
            self._newsem("E_" + e)
        self.nd = 0
        self.dbufs = []

    def _newsem(self, key):
        self.sems[key] = self.stack.enter_context(self.nc.semaphore(key))
        return key

    def buf(self, name):
        return Buf(name)

    def _collect(self, e, reads, writes):
        waits = {}
        kn = self.known[e]
        own = "E_" + e

        def need(tok, same_ok):
            if tok is None:
                return
            k, v = tok
            if k == own and same_ok:
                return
            if kn.get(k, 0) >= v:
                return
            if waits.get(k, 0) < v:
                waits[k] = v

        for b in reads:
            need(b.w, False)
        for b in writes:
            need(b.w, True)
            for k, v in b.rs.items():
                need((k, v), True)
        for k, v in waits.items():
            kn[k] = v
        return list(waits.items())

    def op(self, e, fn, reads=(), writes=()):
        waits = self._collect(e, reads, writes)
        self.cnt[e] += 1
        tok = ("E_" + e, self.cnt[e])
        for b in reads:
            if b.rs.get(tok[0], 0) < tok[1]:
                b.rs[tok[0]] = tok[1]
        for b in writes:
            b.w = tok
            b.rs = {}
        self.prog[e].append((waits, fn, (tok[0], 1)))

    def dma(self, q, fn, reads=(), writes=()):
        waits = self._collect(q, reads, writes)
        tgt = writes[0] if writes else reads[0]
        if tgt.dsem is None:
            self.nd += 1
            tgt.dsem = self._newsem(f"D{self.nd}")
            self.dbufs.append(tgt)
        tgt.dcnt += 16
        tok = (tgt.dsem, tgt.dcnt)
        for b in reads:
            if b.rs.get(tok[0], 0) < tok[1]:
                b.rs[tok[0]] = tok[1]
        for b in writes:
            b.w = tok
            b.rs = {}
        self.prog[q].append((waits, fn, (tok[0], 16)))

    def barrier(self):
        for e in ENGS:
            waits = [("E_" + x, self.cnt[x]) for x in ENGS if x != e and self.cnt[x] > 0]
            waits += [(b.dsem, b.dcnt) for b in self.dbufs]
            for k, v in waits:
                self.known[e][k] = max(self.known[e].get(k, 0), v)
            self.prog[e].append((waits, None, None))

    def final_wait(self, e):
        waits = [(b.dsem, b.dcnt) for b in self.dbufs]
        self.prog[e].append((waits, None, None))

    def emit(self):
        nc = self.nc
        sems = self.sems
        prog = self.prog
        with nc.Block() as block:
            def run(name):
                def body(eng):
                    for waits, fn, inc in prog[name]:
                        for k, v in waits:
                            eng.wait_ge(sems[k], v)
                        if fn is not None:
                            fn(eng).then_inc(sems[inc[0]], inc[1])
                return body
            block.tensor(run("pe"))
            block.scalar(run("act"))
            block.vector(run("dve"))
            block.gpsimd(run("pool"))
            block.sync(run("sp"))


def t5_thresholds():
    n = np.arange(0, 600)
    nf = np.maximum(n, 1).astype(np.float32)
    large = 16 + (np.log(nf / np.float32(16)) / np.float32(math.log(128 / 16)) * np.float32(16)).astype(np.int32)
    large = np.minimum(large, 31)
    bucket = np.where(n < 16, n, large)
    return [int(np.argmax(bucket >= b)) for b in range(1, 32)]


def build_program(dbg=None, stop_after=None, cache_rows=2560 * 128):
    nc = bass.Bass("TRN2", target_bir_lowering=False)
    din = lambda name, shape, dt=F32: nc.dram_tensor(name, shape, dt, kind="ExternalInput").ap()
    dout = lambda name, shape, dt=F32: nc.dram_tensor(name, shape, dt, kind="ExternalOutput").ap()
    xm = din("xm", [NM, D]); xc = din("xc", [NM, D]); xs = din("xs", [NS, D])
    w_in = din("w_in", [D, 3592]); w_out = din("w_out", [D, D]); w_up = din("w_up", [D, 4096]); w_down = din("w_down", [4096, D])
    nmix = din("nmix", [1, D]); nffn = din("nffn", [1, D]); nfin = din("nfin", [1, D]); mlg = din("mlg", [1, 512])
    bg = din("bg", [8, 1]); rb = din("rb", [1, 256])
    ck = din("ck", [cache_rows, 512]); cv = din("cv", [cache_rows, 512])
    pt = din("pt", [1, 256], I32)
    sC = din("sC", [16 * 128, 128]); sn = din("sn", [16 * 128, 1]); smi = din("smi", [4, 4])
    cf = din("cf", [1, 4]); cand = din("cand", [1, 512])
    y_m = dout("y_m", [NM, D]); y_s = dout("y_s", [NS, D])
    k_m = dout("k_m", [NM, 512]); v_m = dout("v_m", [NM, 512]); k_s = dout("k_s", [NS, 512]); v_s = dout("v_s", [NS, 512])
    C_p = dout("C_p", [512, 128]); n_p = dout("n_p", [512, 1]); m_p = dout("m_p", [4, 1])
    C_s = dout("C_s", [2048, 128]); n_s = dout("n_s", [2048, 1]); m_s = dout("m_s", [16, 1])
    mix = nc.dram_tensor("mix", [NM + NS, D], BF16, kind="ExternalOutput").ap()
    xnTd = nc.dram_tensor("xnTd", [9, 128, 4096], BF16).ap()
    dbg_outs = {}

    with contextlib.ExitStack() as st:
        kb = KB(nc, st)
        B = kb.buf
        out_bufs = []

        def sbt(name, shape, dt, stack=st):
            return stack.enter_context(nc.sbuf_tensor(name, shape, dt))

        PT = [st.enter_context(nc.psum_tensor(f"PT{i}", [128, 8, 128], BF16)) for i in range(2)]
        bPT = [B(f"PT{i}") for i in range(2)]
        PF = [st.enter_context(nc.psum_tensor(f"PF{i}", [128, 512], F32)) for i in range(6)]
        bPF = [B(f"PF{i}") for i in range(6)]

        TQ = {"on": False, "q": []}

        def OP(e, fn, r=(), w=()):
            if TQ["on"]:
                TQ["q"].append((e, fn, list(r), list(w)))
            else:
                kb.op(e, fn, r, w)

        def DMA(q, fn, reads=(), writes=()):
            if TQ["on"]:
                TQ["q"].append(("dma", q, fn, list(reads), list(writes)))
            else:
                kb.dma(q, fn, reads, writes)

        def flush_tq(n):
            for _ in range(min(n, len(TQ["q"]))):
                it = TQ["q"].pop(0)
                if it[0] == "dma":
                    kb.dma(it[1], it[2], it[3], it[4])
                else:
                    kb.op(it[0], it[1], it[2], it[3])

        def store(dst_ap, src_ap, src_buf, q="sp"):
            import os
            if os.environ.get("NO_ST") is not None:
                return
            kb.dma(q, lambda e: e.dma_start(out=dst_ap, in_=src_ap), reads=[src_buf], writes=[])

        identf = sbt("identf", [128, 128], F32); b_idf = B("idf")
        ident = sbt("ident", [128, 128], BF16); b_id = B("id")
        OP("pool", lambda e: e.memset(identf[:], 1.0), w=[b_idf])
        OP("pool", lambda e: e.affine_select(out=identf[:], in_=identf[:], pattern=[[-1, 128]], compare_op=ALU.is_equal,
                                             fill=0.0, base=0, channel_multiplier=1), r=[b_idf], w=[b_idf])
        OP("dve", lambda e: e.tensor_copy(out=ident[:], in_=identf[:]), r=[b_idf], w=[b_id])
        g1 = sbt("g1", [128, D], F32); b_g1 = B("g1")
        kb.dma("sp", lambda e: e.dma_start(out=g1[:], in_=nmix[0:1, :].partition_broadcast(128)), writes=[b_g1])
        cfb = sbt("cfb", [128, 4], F32); b_cfb = B("cfb")
        kb.dma("sp", lambda e: e.dma_start(out=cfb[:], in_=cf[0:1, :].partition_broadcast(128)), writes=[b_cfb])
        rbb = sbt("rbb", [128, 32, 8], F32); b_rbb = B("rbb")
        kb.dma("sp", lambda e: e.dma_start(out=rbb[:].rearrange("p b h -> p (b h)"), in_=rb[0:1, :].partition_broadcast(128)), writes=[b_rbb])
        candb = sbt("candb", [128, 16, 2, 16], F32); b_cand = B("cand")
        kb.dma("sp", lambda e: e.dma_start(out=candb[:].rearrange("p a b c -> p (a b c)"), in_=cand[0:1, :].partition_broadcast(128)), writes=[b_cand])

        if stop_after == "c0":
            kb.final_wait("sp")
            kb.emit()
            return nc, dbg_outs
        xt = [sbt(f"xt{i}", [128, D], F32) for i in range(2)]; b_xt = [B(f"xt{i}") for i in range(2)]
        ssq = [sbt(f"ssq{i}", [128, 1], F32) for i in range(2)]; b_ssq = [B(f"ssq{i}") for i in range(2)]
        rstd = [sbt(f"rstd{i}", [128, 1], F32) for i in range(2)]; b_rstd = [B(f"rstd{i}") for i in range(2)]
        xnb = [sbt(f"xnb{i}", [128, D], BF16) for i in range(2)]; b_xnb = [B(f"xnb{i}") for i in range(2)]
        cnt = {"x": 0}

        def rms_scale(src, b_src, n, i, junk, b_junk, dim=D):
            OP("act", lambda e: e.activation(out=junk, in_=src, func=AF.Square, accum_out=ssq[i][0:n, :]),
               r=[b_src], w=[b_junk, b_ssq[i]])
            OP("act", lambda e: e.activation(out=rstd[i][0:n, :], in_=ssq[i][0:n, :], func=AF.Sqrt, scale=1.0 / dim, bias=EPS),
               r=[b_ssq[i]], w=[b_rstd[i]])
            OP("dve", lambda e: e.reciprocal(out=rstd[i][0:n, :], in_=rstd[i][0:n, :]), r=[b_rstd[i]], w=[b_rstd[i]])

        def to_featmajor(src_bf, b_src, n, dst, b_dst, ncol=8):
            i = cnt["x"] % 2
            cnt["x"] += 1
            for c in range(ncol):
                OP("pe", lambda e, c=c: e.transpose(out=PT[i][:, c, 0:n], in_=src_bf[0:n, c * 128:(c + 1) * 128], identity=ident[0:n, 0:n]),
                   r=[b_src, b_id], w=[bPT[i]])
            OP("act", lambda e: e.copy(out=dst, in_=PT[i][:, 0:ncol, 0:n]), r=[bPT[i]], w=[b_dst])

        def norm_tile(src_ap, n, gt, b_gt, dst, b_dst, q="sp"):
            i = cnt["x"] % 2
            kb.dma(q, lambda e: e.dma_start(out=xt[i][0:n, :], in_=src_ap), writes=[b_xt[i]])
            rms_scale(xt[i][0:n, :], b_xt[i], n, i, xnb[i][0:n, :], b_xnb[i])
            OP("dve", lambda e: e.scalar_tensor_tensor(out=xnb[i][0:n, :], in0=xt[i][0:n, :], scalar=rstd[i][0:n, 0:1], in1=gt[0:n, :],
                                                       op0=ALU.mult, op1=ALU.mult), r=[b_xt[i], b_rstd[i], b_gt], w=[b_xnb[i]])
            to_featmajor(xnb[i], b_xnb[i], n, dst, b_dst)

        def load_w(dst, src, b_dst, ncols):
            for c0 in range(0, ncols, 2048):
                c1 = min(ncols, c0 + 2048)
                kb.dma("pool", lambda e, c0=c0, c1=c1: e.dma_start(out=dst[:, c0:c1], in_=src[:, c0:c1]), writes=[b_dst])

        evq = {"i": 0}

        def evac(out, in_, r, w, scale=None):
            evq["i"] += 1
            if evq["i"] % 2:
                if scale is None:
                    OP("act", lambda e: e.copy(out=out, in_=in_), r=r, w=w)
                else:
                    OP("dve", lambda e: e.tensor_scalar(out=out, in0=in_, scalar1=scale, scalar2=None, op0=ALU.mult), r=r, w=w)
            else:
                if scale is None:
                    OP("dve", lambda e: e.tensor_copy(out=out, in_=in_), r=r, w=w)
                else:
                    OP("dve", lambda e: e.tensor_scalar(out=out, in0=in_, scalar1=scale, scalar2=None, op0=ALU.mult), r=r, w=w)

        qTs = sbt("qTs", [128, 4, NS], BF16); b_qTs = B("qTs")
        kTs_own = sbt("kTs_own", [128, 4, NS], BF16); b_kTs_own = B("kTs_own")
        Vs_own = sbt("Vs_own", [8, 4, 512], BF16); b_Vs_own = B("Vs_own")
        TS128 = sbt("TS128", [128, 8, 8], F32); b_TS128 = B("TS128")
        TS0 = sbt("TS0", [8, 8, 8], F32); b_TS0 = B("TS0")
        p1 = contextlib.ExitStack()
        qTa = [sbt(f"qTa{h}", [81, NM], BF16, p1) for h in range(8)]; b_qTa = [[B(f"qTa{h}_{g}") for g in range(4)] for h in range(8)]
        b_qTaP = [[B(f"qTaP{h}_{g}") for g in range(4)] for h in range(8)]
        kTa = [sbt(f"kTa{h}", [81, 4096], BF16, p1) for h in range(8)]; b_kTa = [[B(f"kTa{h}_{g}") for g in range(8)] for h in range(8)]
        b_kTaI = [B(f"kTaI{h}") for h in range(8)]
        Va = sbt("Va", [128, 32, 8, 65], BF16, p1); b_Va = [B(f"Va{t}") for t in range(32)]; b_Va1 = B("Va1")
        OP("pool", lambda e: e.memset(Va[:, :, :, 64:65], 1.0), w=[b_Va1])
        for h in range(8):
            OP("pool", lambda e, h=h: e.memset(kTa[h][64:81, :], 1.0), w=[b_kTaI[h]])
            OP("pool", lambda e, h=h: e.affine_select(out=kTa[h][64:80, :].rearrange("p (b k) -> p b k", k=256),
                                                      in_=kTa[h][64:80, :].rearrange("p (b k) -> p b k", k=256),
                                                      pattern=[[1, 16], [0, 256]], compare_op=ALU.is_equal, fill=0.0, base=0,
                                                      channel_multiplier=-1), r=[b_kTaI[h]], w=[b_kTaI[h]])

        if stop_after == "c1":
            kb.final_wait("sp")
            kb.emit()
            p1.close()
            return nc, dbg_outs
        thr = t5_thresholds()
        Tt = {0: sbt("T0", [128, 8, 128], F32, p1), 128: sbt("T128", [128, 8, 128], F32, p1)}
        b_Tt = {0: [B(f"T0_{h}") for h in range(8)], 128: [B(f"T128_{h}") for h in range(8)]}
        drb = sbt("drb", [128, 32, 8], F32, p1); b_drb = B("drb")
        OP("dve", lambda e: e.tensor_tensor(out=drb[:, 1:32, :], in0=rbb[:, 1:32, :], in1=rbb[:, 0:31, :], op=ALU.subtract), r=[b_rbb], w=[b_drb])
        OP("dve", lambda e: e.tensor_tensor(out=drb[:, 0:1, :], in0=rbb[:, 0:1, :], in1=rbb[:, 31:32, :], op=ALU.subtract), r=[b_rbb], w=[b_drb])
        rb31x8 = sbt("rb31x8", [128, 8], F32, p1); b_rb31 = B("rb31")
        OP("dve", lambda e: e.tensor_scalar(out=rb31x8[:], in0=rbb[:, 31, :], scalar1=8.0, scalar2=None, op0=ALU.mult), r=[b_rbb], w=[b_rb31])
        disti = sbt("disti", [128, 128], I32, p1); b_disti = B("disti")
        distf = sbt("distf", [128, 128], F32, p1); b_distf = B("distf")
        gef = sbt("gef", [128, 128], F32, p1); b_gef = B("gef")
        TQ["on"] = True
        for delta in (0, 128):
            OP("pool", lambda e, delta=delta: e.iota(out=disti[:], pattern=[[1, 128]], base=delta, channel_multiplier=-1), w=[b_disti])
            OP("dve", lambda e: e.tensor_copy(out=distf[:], in_=disti[:]), r=[b_disti], w=[b_distf])
            for h in range(8):
                OP("dve", lambda e, h=h, delta=delta: e.tensor_scalar(out=Tt[delta][:, h, :], in0=distf[:], scalar1=0.0, scalar2=drb[:, 0, h:h + 1],
                                                                      op0=ALU.mult, op1=ALU.add), r=[b_distf, b_drb], w=[b_Tt[delta][h]])
            steps = [(float(thr[b - 1]), b) for b in range(1, 32)]
            for tv, b in steps:
                OP("dve", lambda e, tv=tv: e.tensor_scalar(out=gef[:], in0=distf[:], scalar1=tv, scalar2=None, op0=ALU.is_ge), r=[b_distf], w=[b_gef])
                for h in range(8):
                    OP("dve", lambda e, h=h, b=b, delta=delta: e.scalar_tensor_tensor(out=Tt[delta][:, h, :], in0=gef[:], scalar=drb[:, b, h:h + 1], in1=Tt[delta][:, h, :],
                                                                                     op0=ALU.mult, op1=ALU.add), r=[b_gef, b_drb, b_Tt[delta][h]], w=[b_Tt[delta][h]])
            if delta == 0:
                OP("dve", lambda e: e.tensor_scalar(out=gef[:], in0=distf[:], scalar1=0.0, scalar2=None, op0=ALU.is_lt), r=[b_distf], w=[b_gef])
                for h in range(8):
                    OP("dve", lambda e, h=h: e.scalar_tensor_tensor(out=Tt[0][:, h, :], in0=gef[:], scalar=NEG, in1=Tt[0][:, h, :],
                                                                    op0=ALU.mult, op1=ALU.add), r=[b_gef, b_Tt[0][h]], w=[b_Tt[0][h]])
        OP("dve", lambda e: e.tensor_copy(out=TS128[:], in_=Tt[128][:, :, 0:8]), r=b_Tt[128], w=[b_TS128])
        OP("dve", lambda e: e.tensor_copy(out=TS0[:], in_=Tt[0][0:8, :, 0:8]), r=b_Tt[0], w=[b_TS0])
        TQ["on"] = False
        tq_slice = (len(TQ["q"]) + 7) // 8
        negT = sbt("negT", [128, 128], F32, p1); b_negT = B("negT")
        OP("pool", lambda e: e.memset(negT[:], NEG), w=[b_negT])

        if stop_after == "c2":
            kb.final_wait("sp")
            kb.emit()
            p1.close()
            return nc, dbg_outs
        p1a = contextlib.ExitStack()
        wA = sbt("wA", [128, 8, 1536], BF16, p1a); b_wA = B("wA")
        for c in range(8):
            load_w(wA[:, c, :], w_in[c * 128:(c + 1) * 128, 0:1536], b_wA, 1536)
        xnT = [sbt(f"xnT{i}", [128, 8, 512], BF16, p1a) for i in range(1)]; b_xnT = [B(f"xnT{i}") for i in range(1)]
        kvst = [sbt(f"kvst{i}", [128, 1024], F32, p1a) for i in range(2)]; b_kvst = [B(f"kvst{i}") for i in range(2)]

        if stop_after == "s0":
            kb.final_wait("sp"); kb.emit(); p1a.close(); p1.close(); return nc, dbg_outs
        b_xnTd = B("xnTd")
        for kind in ("ctx", "main"):
            src = xc if kind == "ctx" else xm
            for g in range(4):
                xi = 0
                for t in range(4):
                    r0 = g * 512 + t * 128
                    norm_tile(src[r0:r0 + 128, :], 128, g1, b_g1, xnT[xi][:, :, t * 128:(t + 1) * 128], b_xnT[xi])
                if stop_after == "s1":
                    kb.final_wait("sp"); kb.emit(); p1a.close(); p1.close(); return nc, dbg_outs
                flush_tq(tq_slice)
                kg = g if kind == "ctx" else 4 + g
                kb.dma("sp", lambda e, kg=kg, xi=xi: e.dma_start(out=xnTd[kg], in_=xnT[xi][:].rearrange("p c t -> p (c t)")), reads=[b_xnT[xi]], writes=[b_xnTd])
                import os
                for h in (range(8) if os.environ.get("SKIP_FM") is None else ()):
                    for which in (("k",) if kind == "ctx" else ("q", "k")):
                        col0 = (0 if which == "q" else 512) + h * 64
                        pf = (2 * h + (which == "k")) % 2
                        for c in range(8):
                            OP("pe", lambda e, c=c, pf=pf, col0=col0, xi=xi: e.matmul(PF[pf][0:64, :], lhsT=wA[:, c, col0:col0 + 64], rhs=xnT[xi][:, c, :],
                                                                                       start=(c == 0), stop=(c == 7)),
                               r=[b_wA, b_xnT[xi]], w=[bPF[pf]])
                        if os.environ.get("NO_EV") is not None:
                            pass
                        elif which == "q":
                            evac(qTa[h][0:64, g * 512:(g + 1) * 512], PF[pf][0:64, :], [bPF[pf]], [b_qTa[h][g]])
                        else:
                            evac(kTa[h][0:64, kg * 512:(kg + 1) * 512], PF[pf][0:64, :], [bPF[pf]], [b_kTa[h][kg]])
                if stop_after == "s2":
                    kb.final_wait("sp"); kb.emit(); p1a.close(); p1.close(); return nc, dbg_outs
                for t in (range(4) if os.environ.get("SKIP_TM") is None else ()):
                    ta = kg * 4 + t
                    si = ta % 2
                    for which in (("v",) if kind == "ctx" else ("k", "v")):
                        col0 = 512 if which == "k" else 1024
                        pf = 3 if which == "v" else 4
                        for c in range(8):
                            OP("pe", lambda e, c=c, pf=pf, col0=col0, xi=xi, t=t: e.matmul(PF[pf][:, :], lhsT=xnT[xi][:, c, t * 128:(t + 1) * 128], rhs=wA[:, c, col0:col0 + 512],
                                                                                          start=(c == 0), stop=(c == 7)),
                               r=[b_wA, b_xnT[xi]], w=[bPF[pf]])
                        if which == "v" and os.environ.get("NO_VA") is None:
                            OP("dve", lambda e, ta=ta, pf=pf: e.tensor_copy(out=Va[:, ta, :, 0:64], in_=PF[pf][:, :].rearrange("p (h d) -> p h d", d=64)),
                               r=[bPF[pf]], w=[b_Va[ta]])
                        if kind == "main" and os.environ.get("NO_KV") is None:
                            o0 = 0 if which == "k" else 512
                            OP("dve", lambda e, si=si, pf=pf, o0=o0: e.tensor_copy(out=kvst[si][:, o0:o0 + 512], in_=PF[pf][:, :]), r=[bPF[pf]], w=[b_kvst[si]])
                    if kind == "main":
                        r0 = g * 512 + t * 128
                        store(k_m[r0:r0 + 128, :], kvst[si][:, 0:512], b_kvst[si])
                        store(v_m[r0:r0 + 128, :], kvst[si][:, 512:1024], b_kvst[si])
                if stop_after == "s3" or (stop_after == "s4" and kind == "main") or (stop_after == "s5" and kind == "ctx" and g == 3) or (stop_after == "s6" and kind == "ctx" and g == 1):
                    kb.final_wait("sp"); kb.emit(); p1a.close(); p1.close(); return nc, dbg_outs
        if stop_after == "c3":
            kb.final_wait("sp")
            kb.emit()
            p1a.close()
            p1.close()
            return nc, dbg_outs
        flush_tq(10 ** 9)
        xi = 0
        norm_tile(xs[0:NS, :], NS, g1, b_g1, xnT[xi][:, :, 0:NS], b_xnT[xi])
        kb.dma("sp", lambda e, xi=xi: e.dma_start(out=xnTd[8].rearrange("p (c t) -> p c t", t=512)[:, :, 0:NS], in_=xnT[xi][:, :, 0:NS]), reads=[b_xnT[xi]], writes=[b_xnTd])
        for which in ("q", "k"):
            for ch in range(4):
                col0 = (0 if which == "q" else 512) + ch * 128
                pf = ch % 2
                for c in range(8):
                    OP("pe", lambda e, c=c, pf=pf, col0=col0, xi=xi: e.matmul(PF[pf][:, 0:NS], lhsT=wA[:, c, col0:col0 + 128], rhs=xnT[xi][:, c, 0:NS],
                                                                               start=(c == 0), stop=(c == 7)), r=[b_wA, b_xnT[xi]], w=[bPF[pf]])
                dst = qTs if which == "q" else kTs_own
                bd = b_qTs if which == "q" else b_kTs_own
                evac(dst[:, ch, :], PF[pf][:, 0:NS], [bPF[pf]], [bd])
        for sbi in range(4):
            si = sbi % 2
            for which in ("k", "v"):
                col0 = 512 if which == "k" else 1024
                pf = 2 + (which == "v")
                for c in range(8):
                    OP("pe", lambda e, c=c, pf=pf, col0=col0, xi=xi, sbi=sbi: e.matmul(PF[pf][0:8, :], lhsT=xnT[xi][:, c, sbi * 8:(sbi + 1) * 8], rhs=wA[:, c, col0:col0 + 512],
                                                                                      start=(c == 0), stop=(c == 7)), r=[b_wA, b_xnT[xi]], w=[bPF[pf]])
                o0 = 0 if which == "k" else 512
                if which == "v":
                    OP("dve", lambda e, sbi=sbi, pf=pf: e.tensor_copy(out=Vs_own[:, sbi, :], in_=PF[pf][0:8, :]), r=[bPF[pf]], w=[b_Vs_own])
                OP("dve", lambda e, si=si, pf=pf, o0=o0: e.tensor_copy(out=kvst[si][0:8, o0:o0 + 512], in_=PF[pf][0:8, :]), r=[bPF[pf]], w=[b_kvst[si]])
            store(k_s[sbi * 8:(sbi + 1) * 8, :], kvst[si][0:8, 0:512], b_kvst[si])
            store(v_s[sbi * 8:(sbi + 1) * 8, :], kvst[si][0:8, 512:1024], b_kvst[si])
        kb.barrier()
        p1a.close()

        if stop_after == "p1":
            kb.final_wait("sp")
            kb.emit()
            p1.close()
            return nc, dbg_outs

        p1b = contextlib.ExitStack()
        kmf = sbt("kmf", [64, 8, 16], F32, p1b); b_kmf = B("kmf")
        kmT = sbt("kmT", [64, 8, 16], BF16, p1b); b_kmT = B("kmT")
        for h in range(8):
            OP("dve", lambda e, h=h: e.tensor_reduce(out=kmf[:, h, :], in_=kTa[h][0:64, :].rearrange("p (b k) -> p b k", k=256), axis=AX.X, op=ALU.add),
               r=b_kTa[h], w=[b_kmf])
        OP("dve", lambda e: e.tensor_copy(out=kmT[:], in_=kmf[:]), r=[b_kmf], w=[b_kmT])
        selm = sbt("selm", [128, 16, 16], F32, p1b); b_selm = B("selm")
        OP("dve", lambda e: e.tensor_scalar(out=selm[:], in0=candb[:, :, 0, :], scalar1=-1.0, scalar2=NEG, op0=ALU.mult, op1=ALU.add), r=[b_cand], w=[b_selm])
        selW = sbt("selW", [128, 4, 8, 81], F32, p1b); b_selW = [B(f"selW{j}") for j in range(4)]
        OP("pool", lambda e: e.memset(selW[:], 0.0), w=b_selW)
        for j in range(4):
            OP("dve", lambda e, j=j: e.tensor_copy(out=selW[:, j, :, 80:81], in_=rb31x8[:].rearrange("p (h o) -> p h o", o=1)), r=[b_rb31], w=[b_selW[j]])
        smk = sbt("smk", [128, 8, 16], F32, p1b); b_smk = B("smk")
        top8 = sbt("top8", [128, 8, 8], F32, p1b); b_top8 = B("top8")
        PTt = [sbt(f"PTt{i}", [128, 512], BF16, p1b) for i in range(2)]; b_PTt = [B(f"PTt{i}") for i in range(2)]
        tmpS = [sbt(f"tmpS{i}", [128, 512], F32, p1b) for i in range(2)]; b_tmpS = [B(f"tmpS{i}") for i in range(2)]
        ot = [sbt(f"ot{i}", [65, 512], F32, p1b) for i in range(2)]; b_ot = [B(f"ot{i}") for i in range(2)]
        rc4 = sbt("rc4", [128, 4, 1], F32, p1b); b_rc4 = B("rc4")
        attb = [sbt(f"attb{i}", [128, 4, 512], BF16, p1b) for i in range(2)]; b_attb = [B(f"attb{i}") for i in range(2)]
        b_mix = B("mix")
        sidx = 0
        for g in range(4):
            for j in range(4):
                qt = 4 * g + j
                for h in range(8):
                    OP("pe", lambda e, h=h, qt=qt: e.matmul(PF[4][:, h * 16:(h + 1) * 16], lhsT=qTa[h][0:64, qt * 128:(qt + 1) * 128], rhs=kmT[:, h, :],
                                                            start=True, stop=True), r=[b_qTa[h][g], b_kmT], w=[bPF[4]])
                OP("dve", lambda e, qt=qt: e.tensor_tensor(out=smk[:], in0=PF[4][:, 0:128].rearrange("p (h n) -> p h n", n=16),
                                                           in1=selm[:, qt:qt + 1, :].to_broadcast([128, 8, 16]), op=ALU.add), r=[bPF[4], b_selm], w=[b_smk])
                for h in range(8):
                    OP("dve", lambda e, h=h: e.max(out=top8[:, h, :], in_=smk[:, h, :]), r=[b_smk], w=[b_top8])
                for h in range(8):
                    OP("dve", lambda e, h=h, j=j, qt=qt: e.scalar_tensor_tensor(out=selW[:, j, h, 64:80], in0=smk[:, h, :], scalar=top8[:, h, 2:3], in1=candb[:, qt, 0, :],
                                                                               op0=ALU.is_lt, op1=ALU.mult), r=[b_smk, b_top8, b_cand], w=[b_selW[j]])
                OP("dve", lambda e, j=j, qt=qt: e.tensor_tensor(out=selW[:, j, :, 64:80], in0=selW[:, j, :, 64:80],
                                                                in1=candb[:, qt:qt + 1, 1, :].to_broadcast([128, 8, 16]), op=ALU.add), r=[b_cand, b_selW[j]], w=[b_selW[j]])
            for h in range(8):
                for j in range(4):
                    OP("pe", lambda e, h=h, j=j: e.transpose(out=PF[5][0:81, j * 128:(j + 1) * 128], in_=selW[:, j, h, :], identity=identf[:]),
                       r=[b_selW[j], b_idf], w=[bPF[5]])
                OP("act", lambda e, h=h, g=g: e.copy(out=qTa[h][64:81, g * 512:(g + 1) * 512], in_=PF[5][64:81, :]), r=[bPF[5]], w=[b_qTaP[h][g]])
            ab = g % 2
            for h in range(8):
                po = 2 + h % 2
                nk = 16 + 4 * g + 4

                def emit_S(kt, h=h, g=g):
                    si = kt % 2
                    OP("pe", lambda e, h=h, kt=kt, g=g, si=si: e.matmul(PF[si][:, :], lhsT=kTa[h][0:81, kt * 128:(kt + 1) * 128], rhs=qTa[h][0:81, g * 512:(g + 1) * 512],
                                                                         start=True, stop=True),
                       r=[b_kTa[h][kt // 4], b_kTaI[h], b_qTa[h][g], b_qTaP[h][g]], w=[bPF[si]])

                def emit_exp(kt, h=h, g=g):
                    si = kt % 2
                    rel = kt - (16 + 4 * g)
                    if rel < -1:
                        OP("act", lambda e, si=si: e.activation(out=PTt[si][:], in_=PF[si][:, :], func=AF.Exp, scale=0.125), r=[bPF[si]], w=[b_PTt[si]])
                    else:
                        for j in range(4):
                            d = j - rel
                            cs = slice(j * 128, (j + 1) * 128)
                            if d == 0:
                                Tm, bTm = Tt[0][:, h, :], b_Tt[0][h]
                            elif d == 1:
                                Tm, bTm = Tt[128][:, h, :], b_Tt[128][h]
                            elif d == -1 and j % 2 == 0:
                                Tm, bTm = negT[:], b_negT
                            else:
                                Tm = None
                            if Tm is not None:
                                OP("dve", lambda e, si=si, cs=cs, Tm=Tm: e.scalar_tensor_tensor(out=tmpS[si][:, cs], in0=PF[si][:, cs], scalar=0.125, in1=Tm,
                                                                                                 op0=ALU.mult, op1=ALU.add), r=[bPF[si], bTm], w=[b_tmpS[si]])
                            else:
                                OP("dve", lambda e, si=si, cs=cs: e.tensor_scalar(out=tmpS[si][:, cs], in0=PF[si][:, cs], scalar1=0.125, scalar2=None, op0=ALU.mult),
                                   r=[bPF[si]], w=[b_tmpS[si]])
                        OP("act", lambda e, si=si: e.activation(out=PTt[si][:], in_=tmpS[si][:], func=AF.Exp), r=[b_tmpS[si]], w=[b_PTt[si]])

                def emit_PV(kt, h=h, po=po, nk=nk):
                    si = kt % 2
                    OP("pe", lambda e, h=h, kt=kt, si=si, po=po, nk=nk: e.matmul(PF[po][0:65, :], lhsT=Va[:, kt, h, :], rhs=PTt[si][:], start=(kt == 0), stop=(kt == nk - 1)),
                       r=[b_Va[kt], b_Va1, b_PTt[si]], w=[bPF[po]])

                emit_S(0)
                for kt in range(nk):
                    if kt + 1 < nk:
                        emit_S(kt + 1)
                    emit_exp(kt)
                    emit_PV(kt)
                oi = h % 2
                OP("dve", lambda e, oi=oi, po=po: e.tensor_copy(out=ot[oi][:], in_=PF[po][0:65, :]), r=[bPF[po]], w=[b_ot[oi]])
                for j in range(4):
                    OP("pe", lambda e, oi=oi, j=j: e.transpose(out=PF[4][:, j * 128:j * 128 + 65], in_=ot[oi][0:65, j * 128:(j + 1) * 128], identity=identf[0:65, 0:65]),
                       r=[b_ot[oi], b_idf], w=[bPF[4]])
                pv = PF[4][:, :].rearrange("p (j c) -> p j c", c=128)
                OP("dve", lambda e, pv=pv: e.reciprocal(out=rc4[:], in_=pv[:, :, 64:65]), r=[bPF[4]], w=[b_rc4])
                OP("dve", lambda e, pv=pv, h=h, ab=ab: e.tensor_tensor(out=attb[ab][:, :, h * 64:(h + 1) * 64], in0=pv[:, :, 0:64], in1=rc4[:].to_broadcast([128, 4, 64]), op=ALU.mult),
                   r=[bPF[4], b_rc4], w=[b_attb[ab]])
            for j in range(4):
                r0 = g * 512 + j * 128
                kb.dma("sp", lambda e, ab=ab, j=j, r0=r0: e.dma_start(out=mix[r0:r0 + 128, 0:512], in_=attb[ab][:, j, :]), reads=[b_attb[ab]], writes=[b_mix])
        if dbg == "att":
            def dump(name, shape, dt, src, bufs):
                o = dout("d_" + name, shape, dt)
                bb = B("dmp_" + name)
                kb.dma("sp", lambda e: e.dma_start(out=o, in_=src), reads=bufs, writes=[bb])
            dump("qTa0", [81, NM], BF16, qTa[0][:, :], b_qTa[0] + b_qTaP[0])
            dump("kTa0", [81, 4096], BF16, kTa[0][:, :], b_kTa[0] + [b_kTaI[0]])
            dump("T0", [128, 128], F32, Tt[0][:, 0, :], [b_Tt[0][0]])
            dump("T128", [128, 128], F32, Tt[128][:, 0, :], [b_Tt[128][0]])
            dump("Va16", [128, 8 * 65], BF16, Va[:, 16, :, :].rearrange("p h d -> p (h d)"), [b_Va[16], b_Va1])
            dump("kmf", [64, 128], F32, kmf[:].rearrange("p h n -> p (h n)"), [b_kmf])
            dump("selW", [128, 4 * 8 * 81], F32, selW[:].rearrange("p a b c -> p (a b c)"), b_selW)
        kb.barrier()
        p1b.close()
        p1.close()
        if stop_after == "att":
            kb.final_wait("sp")
            kb.emit()
            return nc, dbg_outs

        ps_ = contextlib.ExitStack()
        ptb = sbt("ptb", [128, 256], I32, ps_); b_ptb = B("ptb")
        DMA("sp", lambda e: e.dma_start(out=ptb[:], in_=pt[0:1, :].partition_broadcast(128)), writes=[b_ptb])
        pio = sbt("pio", [128, 1], I32, ps_); b_pio = B("pio")
        OP("pool", lambda e: e.iota(out=pio[:], pattern=[[0, 1]], base=0, channel_multiplier=1), w=[b_pio])
        piof = sbt("piof", [128, 1], F32, ps_); b_piof = B("piof")
        OP("dve", lambda e: e.tensor_copy(out=piof[:], in_=pio[:]), r=[b_pio], w=[b_piof])
        ptf = sbt("ptf", [128, 256], F32, ps_); b_ptf = B("ptf")
        OP("dve", lambda e: e.tensor_copy(out=ptf[:], in_=ptb[:]), r=[b_ptb], w=[b_ptf])
        ridx = sbt("ridx", [128, 256], I32, ps_); b_ridx = B("ridx")
        OP("dve", lambda e: e.tensor_scalar(out=ridx[:], in0=ptf[:], scalar1=128.0, scalar2=piof[:, 0:1], op0=ALU.mult, op1=ALU.add), r=[b_ptf, b_piof], w=[b_ridx])
        onesb = sbt("onesb", [128, 1], BF16, ps_); b_onesb = B("onesb")
        OP("pool", lambda e: e.memset(onesb[:], 1.0), w=[b_onesb])
        ohS = sbt("ohS", [33, 33, 128], BF16, ps_); b_ohS = B("ohS")
        OP("pool", lambda e: e.memset(ohS[:], 1.0), w=[b_ohS])
        OP("pool", lambda e: e.affine_select(out=ohS[:], in_=ohS[:], pattern=[[1, 33], [0, 128]], compare_op=ALU.is_equal, fill=0.0, base=0, channel_multiplier=-1),
           r=[b_ohS], w=[b_ohS])
        rbcol = sbt("rbcol", [64, 1], F32, ps_); b_rbcol = B("rbcol")
        for h in range(8):
            DMA("sp", lambda e, h=h: e.dma_start(out=rbcol[h * 8:(h + 1) * 8, :], in_=rb[0:1, 248 + h:248 + h + 1].partition_broadcast(8)), writes=[b_rbcol])
        OP("dve", lambda e: e.tensor_scalar(out=rbcol[:], in0=rbcol[:], scalar1=8.0, scalar2=None, op0=ALU.mult), r=[b_rbcol], w=[b_rbcol])
        kTsp2 = [sbt(f"kTsp{i}", [128, 4, 8192], BF16, ps_) for i in range(2)]; b_kTsp2 = [B(f"kTsp{i}") for i in range(2)]
        kpf = [sbt(f"kpf{i}", [128, 2, 512], F32, ps_) for i in range(4)]; b_kpf = [B(f"kpf{i}") for i in range(4)]
        kpb = [sbt(f"kpb{i}", [128, 2, 512], BF16, ps_) for i in range(4)]; b_kpb = [B(f"kpb{i}") for i in range(4)]
        vpf = [sbt(f"vpf{i}", [128, 512], F32, ps_) for i in range(4)]; b_vpf = [B(f"vpf{i}") for i in range(4)]
        vpb = [sbt(f"vpb{i}", [128, 512], BF16, ps_) for i in range(4)]; b_vpb = [B(f"vpb{i}") for i in range(4)]
        kms = sbt("kms", [128, 4, 32], BF16, ps_); b_kms = B("kms")
        kmsf = sbt("kmsf", [128, 4, 32], F32, ps_); b_kmsf = B("kmsf")
        Qbd = sbt("Qbd", [128, 4, 64], BF16, ps_); b_Qbd = B("Qbd")
        scs = sbt("scs", [64, 32], F32, ps_); b_scs = B("scs")
        top8s = sbt("top8s", [64, 8], F32, ps_); b_top8s = B("top8s")
        penF = sbt("penF", [64, 33], F32, ps_); b_penF = B("penF")
        penTb = sbt("penTb", [33, 64], BF16, ps_); b_penTb = B("penTb")
        PTs = [sbt(f"PTs{i}", [128, 64], BF16, ps_) for i in range(2)]; b_PTs = [B(f"PTs{i}") for i in range(2)]
        tmS = sbt("tmS", [128, 64], F32, ps_); b_tmS = B("tmS")
        osb = sbt("osb", [64, 512], BF16, ps_); b_osb = B("osb")
        recs = sbt("recs", [64, 1], F32, ps_); b_recs = B("recs")
        b_mix3 = B("mix3")
        xc_ = {"n": 0}
        def k_phase(sbi):
            kTsp = kTsp2[sbi % 2]; b_kTsp = b_kTsp2[sbi % 2]
            for pr in range(32):
                bi = pr % 4
                for a in range(2):
                    col = sbi * 64 + pr * 2 + a
                    DMA("pool", lambda e, bi=bi, a=a, col=col: e.indirect_dma_start(out=kpf[bi][:, a, :], out_offset=None, in_=ck[:, :],
                                                                                      in_offset=bass.IndirectOffsetOnAxis(ap=ridx[:, col:col + 1], axis=0)),
                           reads=[b_ridx], writes=[b_kpf[bi]])
                OP("dve", lambda e, bi=bi: e.tensor_copy(out=kpb[bi][:], in_=kpf[bi][:]), r=[b_kpf[bi]], w=[b_kpb[bi]])
                for a in range(2):
                    ti_ = xc_["n"] % 2
                    xc_["n"] += 1
                    pg = pr * 2 + a
                    for ch in range(4):
                        OP("pe", lambda e, ti_=ti_, bi=bi, a=a, ch=ch: e.transpose(out=PT[ti_][:, ch, :], in_=kpb[bi][:, a, ch * 128:(ch + 1) * 128], identity=ident[:]),
                           r=[b_kpb[bi], b_id], w=[bPT[ti_]])
                    OP("act", lambda e, ti_=ti_, pg=pg: e.copy(out=kTsp[:, :, pg * 128:(pg + 1) * 128], in_=PT[ti_][:, 0:4, :]), r=[bPT[ti_]], w=[b_kTsp])
            for ch in range(4):
                OP("dve", lambda e, ch=ch: e.tensor_reduce(out=kmsf[:, ch, :], in_=kTsp[:, ch, :].rearrange("p (n k) -> p n k", k=256), axis=AX.X, op=ALU.add), r=[b_kTsp], w=[b_kmsf])
            OP("dve", lambda e: e.tensor_copy(out=kms[:], in_=kmsf[:]), r=[b_kmsf], w=[b_kms])

        TQ["on"] = False
        k_phase(0)
        for sbi in range(4):
            kTsp = kTsp2[sbi % 2]; b_kTsp = b_kTsp2[sbi % 2]
            if sbi + 1 < 4:
                TQ["on"] = True
                k_phase(sbi + 1)
                TQ["on"] = False
            kq_slice = (len(TQ["q"]) + 64) // 65
            OP("pool", lambda e: e.memset(Qbd[:], 0.0), w=[b_Qbd])
            for h in range(8):
                ch, hh = h // 2, h % 2
                OP("dve", lambda e, h=h, ch=ch, hh=hh, sbi=sbi: e.tensor_copy(out=Qbd[hh * 64:(hh + 1) * 64, ch, h * 8:(h + 1) * 8], in_=qTs[hh * 64:(hh + 1) * 64, ch, sbi * 8:(sbi + 1) * 8]),
                   r=[b_qTs], w=[b_Qbd])
            for ch in range(4):
                OP("pe", lambda e, ch=ch: e.matmul(PF[4][0:64, 0:32], lhsT=Qbd[:, ch, :], rhs=kms[:, ch, :], start=(ch == 0), stop=(ch == 3)), r=[b_Qbd, b_kms], w=[bPF[4]])
            OP("dve", lambda e: e.tensor_copy(out=scs[:], in_=PF[4][0:64, 0:32]), r=[bPF[4]], w=[b_scs])
            OP("dve", lambda e: e.max(out=top8s[:], in_=scs[:]), r=[b_scs], w=[b_top8s])
            OP("dve", lambda e: e.tensor_scalar(out=penF[:, 0:32], in0=scs[:], scalar1=top8s[:, 2:3], scalar2=NEG, op0=ALU.is_lt, op1=ALU.mult), r=[b_scs, b_top8s], w=[b_penF])
            OP("pool", lambda e: e.memset(penF[:, 32:33], 0.0), w=[b_penF])
            OP("dve", lambda e: e.tensor_scalar(out=penF[:], in0=penF[:], scalar1=rbcol[:, 0:1], scalar2=None, op0=ALU.add), r=[b_penF, b_rbcol], w=[b_penF])
            OP("pe", lambda e: e.transpose(out=PF[4][0:33, 64:128], in_=penF[:], identity=identf[0:64, 0:64]), r=[b_penF, b_idf], w=[bPF[4]])
            OP("act", lambda e: e.copy(out=penTb[:], in_=PF[4][0:33, 64:128]), r=[bPF[4]], w=[b_penTb])
            def s_load(kt, sbi=sbi):
                if kt >= 64:
                    return
                bi = kt % 4
                col = sbi * 64 + kt
                DMA("pool", lambda e, bi=bi, col=col: e.indirect_dma_start(out=vpf[bi][:, :], out_offset=None, in_=cv[:, :],
                                                                              in_offset=bass.IndirectOffsetOnAxis(ap=ridx[:, col:col + 1], axis=0)),
                       reads=[b_ridx], writes=[b_vpf[bi]])
                OP("dve", lambda e, bi=bi: e.tensor_copy(out=vpb[bi][:, :], in_=vpf[bi][:, :]), r=[b_vpf[bi]], w=[b_vpb[bi]])

            def s_S(kt, sbi=sbi, kTsp=kTsp, b_kTsp=b_kTsp):
                own = kt == 64
                si = kt % 2
                L = 8 if own else 128
                n = 32 if own else kt // 2
                for ch in range(4):
                    lhs = kTs_own[:, ch, sbi * 8:(sbi + 1) * 8] if own else kTsp[:, ch, kt * 128:(kt + 1) * 128]
                    OP("pe", lambda e, si=si, ch=ch, lhs=lhs, L=L: e.matmul(PF[si][0:L, 0:64], lhsT=lhs, rhs=Qbd[:, ch, :], start=(ch == 0), stop=False),
                       r=[b_kTsp, b_kTs_own, b_Qbd], w=[bPF[si]])
                OP("pe", lambda e, si=si, n=n, L=L: e.matmul(PF[si][0:L, 0:64], lhsT=ohS[:, n, 0:L], rhs=penTb[:], start=False, stop=True), r=[b_ohS, b_penTb], w=[bPF[si]])

            def s_exp(kt):
                own = kt == 64
                si = kt % 2
                L = 8 if own else 128
                if kt >= 63:
                    Tm = TS0[:].rearrange("p h q -> p (h q)") if own else TS128[:].rearrange("p h q -> p (h q)")
                    OP("dve", lambda e, si=si, L=L, Tm=Tm: e.scalar_tensor_tensor(out=tmS[0:L, :], in0=PF[si][0:L, 0:64], scalar=0.125, in1=Tm, op0=ALU.mult, op1=ALU.add),
                       r=[bPF[si], b_TS0, b_TS128], w=[b_tmS])
                    OP("act", lambda e, si=si, L=L: e.activation(out=PTs[si][0:L, :], in_=tmS[0:L, :], func=AF.Exp), r=[b_tmS], w=[b_PTs[si]])
                else:
                    OP("act", lambda e, si=si: e.activation(out=PTs[si][:], in_=PF[si][:, 0:64], func=AF.Exp, scale=0.125), r=[bPF[si]], w=[b_PTs[si]])

            def s_PV(kt, sbi=sbi):
                own = kt == 64
                si = kt % 2
                L = 8 if own else 128
                rhsv = Vs_own[:, sbi, :] if own else vpb[kt % 4][:, :]
                OP("pe", lambda e, si=si, L=L, rhsv=rhsv, kt=kt: e.matmul(PF[2][0:64, :], lhsT=PTs[si][0:L, :], rhs=rhsv, start=(kt == 0), stop=(kt == 64)),
                   r=[b_PTs[si], b_Vs_own, b_vpb[kt % 4]], w=[bPF[2]])
                OP("pe", lambda e, si=si, L=L, kt=kt: e.matmul(PF[3][0:64, 0:1], lhsT=PTs[si][0:L, :], rhs=onesb[0:L, 0:1], start=(kt == 0), stop=(kt == 64)),
                   r=[b_PTs[si], b_onesb], w=[bPF[3]])

            s_load(0); s_load(1); s_load(2)
            s_S(0)
            for kt in range(65):
                flush_tq(kq_slice)
                s_load(kt + 3)
                if kt + 1 < 65:
                    s_S(kt + 1)
                s_exp(kt)
                s_PV(kt)
            flush_tq(10 ** 9)
            OP("dve", lambda e: e.reciprocal(out=recs[:], in_=PF[3][0:64, 0:1]), r=[bPF[3]], w=[b_recs])
            OP("dve", lambda e: e.tensor_scalar(out=osb[:], in0=PF[2][0:64, :], scalar1=recs[:, 0:1], scalar2=None, op0=ALU.mult), r=[bPF[2], b_recs], w=[b_osb])
            for h in range(8):
                DMA("sp", lambda e, h=h, sbi=sbi: e.dma_start(out=mix[NM + sbi * 8:NM + sbi * 8 + 8, h * 64:(h + 1) * 64], in_=osb[h * 8:(h + 1) * 8, h * 64:(h + 1) * 64]),
                       reads=[b_osb], writes=[b_mix3])
        kb.barrier()
        ps_.close()
        if stop_after == "satt":
            kb.final_wait("sp")
            kb.emit()
            return nc, dbg_outs

        b_mix2 = B("mix2")
        NT = 4096 + NS
        def ml_pass(hp):
            h0 = 2 * hp
            p2 = contextlib.ExitStack()
            mqT = sbt(f"h{hp}_" "mqT", [128, 2, NM + NS], BF16, p2); b_mqT = B("mqT")
            mkT = sbt(f"h{hp}_" "mkT", [128, 2, NM + NS], BF16, p2); b_mkT = B("mkT")
            mkt = sbt(f"h{hp}_" "mkt", [128, 32, 256], BF16, p2); b_mkt = B("mkt")
            mva = sbt(f"h{hp}_" "mva", [128, 32, 2, 129], BF16, p2); b_mva = B("mva")
            sgm = sbt(f"h{hp}_" "sgm", [128, 16, 256], BF16, p2); b_sgm = B("sgm")
            mkt_s = sbt(f"h{hp}_" "mkt_s", [8, 4, 256], BF16, p2); b_mkt_s = B("mkt_s")
            mva_s = sbt(f"h{hp}_" "mva_s", [8, 4, 2, 129], BF16, p2); b_mva_s = B("mva_s")
            sgm_s = sbt(f"h{hp}_" "sgm_s", [8, 4, 256], BF16, p2); b_sgm_s = B("sgm_s")
            OP("pool", lambda e: e.memset(mva[:, :, :, 128:129], 1.0), w=[b_mva])
            OP("pool", lambda e: e.memset(mva_s[:, :, :, 128:129], 1.0), w=[b_mva_s])
            Grow = sbt(f"h{hp}_" "Grow", [2, NT], F32, p2); b_Grow = B("Grow")
            Urow = sbt(f"h{hp}_" "Urow", [2, NT], F32, p2); b_Urow = B("Urow")
            Brow = sbt(f"h{hp}_" "Brow", [2, NT], F32, p2); b_Brow = B("Brow")
            mlgb = sbt(f"h{hp}_" "mlgb", [128, 256], F32, p2); b_mlgb = B("mlgb")
            kb.dma("sp", lambda e, h0=h0: e.dma_start(out=mlgb[:], in_=mlg[0:1, h0 * 128:h0 * 128 + 256].partition_broadcast(128)), writes=[b_mlgb])
            bgt = sbt(f"h{hp}_" "bgt", [2, 2], F32, p2); b_bgt = B("bgt")
            kb.dma("sp", lambda e, h0=h0: e.dma_start(out=bgt[:, 0:1], in_=bg[h0:h0 + 2, :]), writes=[b_bgt])
            kb.dma("sp", lambda e, h0=h0: e.dma_start(out=bgt[:, 1:2], in_=bg[4 + h0:4 + h0 + 2, :]), writes=[b_bgt])
            sm0 = sbt(f"h{hp}_" "sm0", [2, 4], F32, p2); b_sm0 = B("sm0")
            kb.dma("sp", lambda e, h0=h0: e.dma_start(out=sm0[:], in_=smi[h0:h0 + 2, :]), writes=[b_sm0])
            ones2 = sbt(f"h{hp}_" "ones2", [2, 512], F32, p2); b_ones2 = B("ones2")
            OP("pool", lambda e: e.memset(ones2[:], 1.0), w=[b_ones2])
            oh2 = sbt(f"h{hp}_" "oh2", [2, 2, 128], F32, p2); b_oh2 = B("oh2")
            OP("pool", lambda e: e.memset(oh2[:], 1.0), w=[b_oh2])
            OP("pool", lambda e: e.affine_select(out=oh2[:], in_=oh2[:], pattern=[[1, 2], [0, 128]], compare_op=ALU.is_equal, fill=0.0, base=0,
                                                 channel_multiplier=-1), r=[b_oh2], w=[b_oh2])
            cmask = sbt(f"h{hp}_" "cmask", [128, 128], F32, p2); b_cmask = B("cmask")
            OP("pool", lambda e: e.memset(cmask[:], 0.0), w=[b_cmask])
            OP("pool", lambda e: e.affine_select(out=cmask[:], in_=cmask[:], pattern=[[1, 128]], compare_op=ALU.is_ge, fill=-NEG, base=0,
                                                 channel_multiplier=-1), r=[b_cmask], w=[b_cmask])
            UT = sbt(f"h{hp}_" "UT", [128, 33, 2], F32, p2); GT = sbt(f"h{hp}_" "GT", [128, 33, 2], F32, p2); mT = sbt(f"h{hp}_" "mT", [128, 33, 2], F32, p2); EM = sbt(f"h{hp}_" "EM", [128, 33, 2], F32, p2)
            b_UT = B("UT"); b_GT = B("GT"); b_mT = B("mT"); b_EM = B("EM")
            UTs = sbt(f"h{hp}_" "UTs", [8, 4, 2], F32, p2); GTs = sbt(f"h{hp}_" "GTs", [8, 4, 2], F32, p2); mTs = sbt(f"h{hp}_" "mTs", [8, 4, 2], F32, p2); EMs = sbt(f"h{hp}_" "EMs", [8, 4, 2], F32, p2)
            b_UTs = B("UTs"); b_GTs = B("GTs"); b_mTs = B("mTs"); b_EMs = B("EMs")

            p2a = contextlib.ExitStack()
            wB = sbt(f"h{hp}_" "wB", [128, 8, 1032], BF16, p2a); b_wB = B("wB")
            for c in range(8):
                rows = slice(c * 128, (c + 1) * 128)
                for k4 in range(4):
                    kb.dma("pool", lambda e, c=c, k4=k4, rows=rows, h0=h0: e.dma_start(out=wB[:, c, k4 * 256:(k4 + 1) * 256],
                                                                                    in_=w_in[rows, 1536 + k4 * 512 + h0 * 128:1536 + k4 * 512 + h0 * 128 + 256]), writes=[b_wB])
                kb.dma("pool", lambda e, c=c, rows=rows: e.dma_start(out=wB[:, c, 1024:1032], in_=w_in[rows, 3584:3592]), writes=[b_wB])
            xnT2 = sbt(f"h{hp}_" "xnT2", [128, 8, 512], BF16, p2a); b_xnT2 = B("xnT2")
            gTs = sbt(f"h{hp}_" "gTs", [8, 512], F32, p2a); b_gTs = B("gTs")
            sgt = sbt(f"h{hp}_" "sgt", [128, 256], F32, p2a); b_sgt = B("sgt")
            KS = 128.0 ** -0.5
            groups = [("ctx", g) for g in range(4)] + [("main", g) for g in range(4)] + [("smp", 0)]
            for kind, g in groups:
                ntok = NS if kind == "smp" else 512
                gidx = {"ctx": g, "main": 4 + g, "smp": 8}[kind]
                kb.dma("sp", lambda e, gidx=gidx, ntok=ntok: e.dma_start(out=xnT2[:, :, 0:ntok], in_=xnTd[gidx].rearrange("p (c t) -> p c t", t=512)[:, :, 0:ntok]),
                       reads=[b_xnTd], writes=[b_xnT2])
                gcol0 = {"ctx": g * 512, "main": 2048 + g * 512, "smp": 4096}[kind]
                for c in range(8):
                    OP("pe", lambda e, c=c, ntok=ntok: e.matmul(PF[0][0:8, 0:ntok], lhsT=wB[:, c, 1024:1032], rhs=xnT2[:, c, 0:ntok], start=(c == 0), stop=(c == 7)),
                       r=[b_wB, b_xnT2], w=[bPF[0]])
                OP("dve", lambda e, ntok=ntok: e.tensor_copy(out=gTs[:, 0:ntok], in_=PF[0][0:8, 0:ntok]), r=[bPF[0]], w=[b_gTs])
                kb.dma("sp", lambda e, ntok=ntok, gcol0=gcol0, h0=h0: e.dma_start(out=Urow[:, gcol0:gcol0 + ntok], in_=gTs[h0:h0 + 2, 0:ntok]), reads=[b_gTs], writes=[b_Urow])
                kb.dma("sp", lambda e, ntok=ntok, gcol0=gcol0, h0=h0: e.dma_start(out=Brow[:, gcol0:gcol0 + ntok], in_=gTs[4 + h0:4 + h0 + 2, 0:ntok]), reads=[b_gTs], writes=[b_Brow])
                if kind != "ctx":
                    fcol0 = g * 512 if kind == "main" else NM
                    for l in range(2):
                        for which in ("q", "k"):
                            wc0 = (0 if which == "q" else 256) + l * 128
                            pf = 1 + (which == "k")
                            for c in range(8):
                                OP("pe", lambda e, c=c, pf=pf, wc0=wc0, ntok=ntok: e.matmul(PF[pf][:, 0:ntok], lhsT=wB[:, c, wc0:wc0 + 128], rhs=xnT2[:, c, 0:ntok],
                                                                                          start=(c == 0), stop=(c == 7)), r=[b_wB, b_xnT2], w=[bPF[pf]])
                            if which == "q":
                                evac(mqT[:, l, fcol0:fcol0 + ntok], PF[pf][:, 0:ntok], [bPF[pf]], [b_mqT])
                            else:
                                evac(mkT[:, l, fcol0:fcol0 + ntok], PF[pf][:, 0:ntok], [bPF[pf]], [b_mkT], scale=KS)
                if kind == "smp":
                    tiles = [(sbi * 8, 8, sbi) for sbi in range(4)]
                else:
                    tiles = [(t * 128, 128, (g if kind == "ctx" else 4 + g) * 4 + t) for t in range(4)]
                for (c0, n, ta) in tiles:
                    for c in range(8):
                        OP("pe", lambda e, c=c, c0=c0, n=n: e.matmul(PF[3][0:n, :], lhsT=xnT2[:, c, c0:c0 + n], rhs=wB[:, c, 256:768], start=(c == 0), stop=(c == 7)),
                           r=[b_wB, b_xnT2], w=[bPF[3]])
                    if kind == "smp":
                        kdst, b_kd = mkt_s[:, ta, :], b_mkt_s
                        vdst, b_vd = mva_s[:, ta, :, 0:128], b_mva_s
                    else:
                        kdst, b_kd = mkt[:, ta, :], b_mkt
                        vdst, b_vd = mva[:, ta, :, 0:128], b_mva
                    OP("dve", lambda e, n=n, kdst=kdst: e.tensor_scalar(out=kdst, in0=PF[3][0:n, 0:256], scalar1=KS, scalar2=None, op0=ALU.mult), r=[bPF[3]], w=[b_kd])
                    OP("dve", lambda e, n=n, vdst=vdst: e.tensor_copy(out=vdst, in_=PF[3][0:n, 256:512].rearrange("p (l d) -> p l d", d=128)), r=[bPF[3]], w=[b_vd])
                    if kind != "ctx":
                        for c in range(8):
                            OP("pe", lambda e, c=c, c0=c0, n=n: e.matmul(PF[4][0:n, 0:256], lhsT=xnT2[:, c, c0:c0 + n], rhs=wB[:, c, 768:1024], start=(c == 0), stop=(c == 7)),
                               r=[b_wB, b_xnT2], w=[bPF[4]])
                        OP("dve", lambda e, n=n: e.tensor_copy(out=sgt[0:n, :], in_=PF[4][0:n, 0:256]), r=[bPF[4]], w=[b_sgt])
                        OP("act", lambda e, n=n: e.activation(out=sgt[0:n, :], in_=sgt[0:n, :], func=AF.Sigmoid), r=[b_sgt], w=[b_sgt])
                        if kind == "smp":
                            sdst, b_sd = sgm_s[:, ta, :], b_sgm_s
                        else:
                            sdst, b_sd = sgm[:, ta - 16, :], b_sgm
                        OP("dve", lambda e, n=n, sdst=sdst: e.tensor_tensor(out=sdst, in0=sgt[0:n, :], in1=mlgb[0:n, :], op=ALU.mult), r=[b_sgt, b_mlgb], w=[b_sd])
            kb.barrier()
            p2a.close()
            if stop_after == "ml_a":
                kb.final_wait("sp"); kb.emit(); p2.close(); raise StopIteration

            nbf = sbt(f"h{hp}_" "nbf", [2, 1], F32, p2); b_nbf = B("nbf")
            OP("dve", lambda e: e.tensor_scalar(out=nbf[:], in0=bgt[:, 1:2], scalar1=-1.0, scalar2=None, op0=ALU.mult), r=[b_bgt], w=[b_nbf])
            OP("act", lambda e: e.activation(out=Brow[:], in_=Brow[:], func=AF.Exp, scale=-1.0, bias=nbf[:, 0:1]), r=[b_Brow, b_nbf], w=[b_Brow])
            OP("act", lambda e: e.activation(out=Brow[:], in_=Brow[:], func=AF.Ln, scale=1.0, bias=1.0), r=[b_Brow], w=[b_Brow])
            OP("dve", lambda e: e.tensor_scalar(out=Brow[:], in0=Brow[:], scalar1=-1.0, scalar2=None, op0=ALU.mult), r=[b_Brow], w=[b_Brow])
            OP("dve", lambda e: e.tensor_scalar(out=Urow[:], in0=Urow[:], scalar1=bgt[:, 0:1], scalar2=None, op0=ALU.add), r=[b_Urow, b_bgt], w=[b_Urow])
            OP("dve", lambda e: e.tensor_scalar(out=Brow[:, 0:2048], in0=Brow[:, 0:2048], scalar1=cfb[0:2, 0:1], scalar2=None, op0=ALU.mult), r=[b_Brow, b_cfb], w=[b_Brow])
            OP("dve", lambda e: e.tensor_scalar(out=Urow[:, 0:2048], in0=Urow[:, 0:2048], scalar1=cfb[0:2, 0:1], scalar2=cfb[0:2, 1:2], op0=ALU.mult, op1=ALU.add),
               r=[b_Urow, b_cfb], w=[b_Urow])
            segs = [(i * 512, 512, None if i == 0 else i * 512 - 1) for i in range(8)] + [(4096 + sbi * 8, 8, None) for sbi in range(4)]
            for (c0, n, prev) in segs:
                init = 0.0 if prev is None else Brow[:, prev:prev + 1]
                OP("dve", lambda e, c0=c0, n=n, init=init: e.tensor_tensor_scan(out=Brow[:, c0:c0 + n], data0=ones2[:, 0:n], data1=Brow[:, c0:c0 + n], initial=init,
                                                                                op0=ALU.mult, op1=ALU.add), r=[b_Brow, b_ones2], w=[b_Brow])
            OP("dve", lambda e: e.tensor_tensor(out=Urow[:], in0=Urow[:], in1=Brow[:], op=ALU.subtract), r=[b_Urow, b_Brow], w=[b_Urow])
            for si_, (c0, n, prev) in enumerate(segs):
                if c0 >= 4096:
                    sbi = (c0 - 4096) // 8
                    init = sm0[:, sbi:sbi + 1]
                else:
                    init = 0.0 if prev is None else Grow[:, prev:prev + 1]
                OP("dve", lambda e, c0=c0, n=n, init=init: e.tensor_tensor_scan(out=Grow[:, c0:c0 + n], data0=Urow[:, c0:c0 + n], data1=Urow[:, c0:c0 + n], initial=init,
                                                                                op0=ALU.max, op1=ALU.max), r=[b_Urow, b_Grow, b_sm0], w=[b_Grow])
            OP("dve", lambda e: e.tensor_tensor(out=Brow[:], in0=Brow[:], in1=Grow[:], op=ALU.add), r=[b_Grow, b_Brow], w=[b_Brow])
            for (row, b_row, colt, b_colt, colts, b_colts) in ((Urow, b_Urow, UT, b_UT, UTs, b_UTs), (Grow, b_Grow, GT, b_GT, GTs, b_GTs), (Brow, b_Brow, mT, b_mT, mTs, b_mTs)):
                for ck in range(32):
                    OP("pe", lambda e, ck=ck, row=row: e.transpose(out=PF[5][:, ck * 2:ck * 2 + 2], in_=row[:, ck * 128:(ck + 1) * 128], identity=identf[0:2, 0:2]),
                       r=[b_row, b_idf], w=[bPF[5]])
                OP("dve", lambda e, colt=colt: e.tensor_copy(out=colt[:, 0:32, :], in_=PF[5][:, 0:64].rearrange("p (c l) -> p c l", l=2)), r=[bPF[5]], w=[b_colt])
                for sbi in range(4):
                    OP("pe", lambda e, sbi=sbi, row=row: e.transpose(out=PF[5][0:8, sbi * 2:sbi * 2 + 2], in_=row[:, 4096 + sbi * 8:4096 + sbi * 8 + 8], identity=identf[0:2, 0:2]),
                       r=[b_row, b_idf], w=[bPF[5]])
                OP("dve", lambda e, colts=colts: e.tensor_copy(out=colts[:], in_=PF[5][0:8, 0:8].rearrange("p (c l) -> p c l", l=2)), r=[bPF[5]], w=[b_colts])
            OP("act", lambda e: e.activation(out=EM[:, 0:32, :], in_=mT[:, 0:32, :], func=AF.Exp, scale=-1.0), r=[b_mT], w=[b_EM])
            OP("act", lambda e: e.activation(out=EMs[:], in_=mTs[:], func=AF.Exp, scale=-1.0), r=[b_mTs], w=[b_EMs])

            if stop_after == "ml_b":
                kb.final_wait("sp"); kb.emit(); p2.close(); raise StopIteration
            Cf = sbt(f"h{hp}_" "Cf", [128, 2, 129], F32, p2); b_Cf = B("Cf")
            Cb = sbt(f"h{hp}_" "Cb", [128, 2, 129], BF16, p2); b_Cb = B("Cb")
            gprev = sbt(f"h{hp}_" "gprev", [128, 2], F32, p2); b_gprev = B("gprev")
            gend = [sbt(f"h{hp}_" f"gend{i}", [128, 2], F32, p2) for i in range(2)]; b_gend = [B(f"gend{i}") for i in range(2)]
            g2 = sbt(f"h{hp}_" "g2", [128, 2], F32, p2); b_g2 = B("g2")
            gtok = sbt(f"h{hp}_" "gtok", [128, 2], F32, p2); b_gtok = B("gtok")
            gst = sbt(f"h{hp}_" "gst", [128, 2], F32, p2); b_gst = B("gst")
            wst = sbt(f"h{hp}_" "wst", [128, 2], F32, p2); b_wst = B("wst")
            tmpD = sbt(f"h{hp}_" "tmpD", [128, 2, 128], F32, p2); b_tmpD = B("tmpD")
            sT = sbt(f"h{hp}_" "sT", [128, 2, 128], BF16, p2); b_sT = B("sT")
            hs = [sbt(f"h{hp}_" f"hs{i}", [128, 2, 129], F32, p2) for i in range(2)]; b_hs = [B(f"hs{i}") for i in range(2)]
            nd = sbt(f"h{hp}_" "nd", [128, 2, 129], F32, p2); b_nd = B("nd")
            dab = sbt(f"h{hp}_" "dab", [128, 2], F32, p2); b_dab = B("dab")
            ssh = sbt(f"h{hp}_" "ssh", [128, 2], F32, p2); b_ssh = B("ssh")
            junk = sbt(f"h{hp}_" "junk", [128, 128], F32, p2); b_junk = B("junk")
            gv = sbt(f"h{hp}_" "gv", [128, 2, 129], BF16, p2); b_gv = B("gv")
            mlb = [sbt(f"h{hp}_" f"mlb{i}", [128, 256], BF16, p2) for i in range(2)]; b_mlb = [B(f"mlb{i}") for i in range(2)]

            def chunkA(L, ck_cols, UTv, GTv, EMv, kT_v, qT_v, kt_v, va_v, sg_v, full, mix_rows, mi, gcol):
                for l in range(2):
                    OP("pe", lambda e, l=l: e.matmul(PF[4][0:L, l * 128:l * 128 + L], lhsT=oh2[:, l, 0:L], rhs=Grow[:, ck_cols], start=True, stop=True),
                       r=[b_oh2, b_Grow], w=[bPF[4]])
                OP("dve", lambda e: e.tensor_copy(out=gend[mi][0:L, :].rearrange("p (l o) -> p l o", o=1), in_=PF[4][0:L, 0:256].rearrange("p (l t) -> p l t", t=128)[:, :, L - 1:L]),
                   r=[bPF[4]], w=[b_gend[mi]])
                if not full:
                    return
                for l in range(2):
                    OP("pe", lambda e, l=l: e.matmul(PF[0][0:L, l * 128:l * 128 + L], lhsT=kT_v(l), rhs=qT_v(l), start=True, stop=True), r=[b_mkT, b_mqT], w=[bPF[0]])
                    OP("dve", lambda e, l=l: e.scalar_tensor_tensor(out=tmpD[0:L, l, 0:L], in0=PF[4][0:L, l * 128:l * 128 + L], scalar=UTv[:, l:l + 1], in1=cmask[0:L, 0:L],
                                                                    op0=ALU.subtract, op1=ALU.add), r=[bPF[4], b_UT, b_UTs, b_cmask], w=[b_tmpD])
                OP("act", lambda e: e.activation(out=tmpD[0:L, :, 0:L], in_=tmpD[0:L, :, 0:L], func=AF.Exp, scale=-1.0), r=[b_tmpD], w=[b_tmpD])
                OP("dve", lambda e: e.tensor_tensor(out=sT[0:L, :, 0:L], in0=PF[0][0:L, 0:256].rearrange("p (l t) -> p l t", t=128)[:, :, 0:L], in1=tmpD[0:L, :, 0:L], op=ALU.mult),
                   r=[bPF[0], b_tmpD], w=[b_sT])
                for l in range(2):
                    OP("pe", lambda e, l=l: e.matmul(PF[1][0:L, l * 129:(l + 1) * 129], lhsT=sT[0:L, l, 0:L], rhs=va_v(l), start=True, stop=True),
                       r=[b_sT, b_mva, b_mva_s], w=[bPF[1]])
                OP("dve", lambda e: e.tensor_copy(out=hs[mi][0:L, :, :], in_=PF[1][0:L, 0:258].rearrange("p (l c) -> p l c", c=129)), r=[bPF[1]], w=[b_hs[mi]])

            def chunkB(L, ck_cols, UTv, GTv, EMv, kT_v, qT_v, kt_v, va_v, sg_v, full, mix_rows, mi, gcol):
                if not full:
                    return
                for l in range(2):
                    OP("pe", lambda e, l=l: e.matmul(PF[2][0:L, l * 129:(l + 1) * 129], lhsT=qT_v(l), rhs=Cb[:, l, :], start=True, stop=True),
                       r=[b_mqT, b_Cb], w=[bPF[2]])
                OP("dve", lambda e: e.tensor_tensor(out=g2[0:L, :], in0=gprev[0:L, :], in1=GTv, op=ALU.subtract), r=[b_gprev, b_GT, b_GTs], w=[b_g2])
                OP("act", lambda e: e.activation(out=wst[0:L, :], in_=g2[0:L, :], func=AF.Exp), r=[b_g2], w=[b_wst])
                for l in range(2):
                    OP("dve", lambda e, l=l: e.scalar_tensor_tensor(out=nd[0:L, l, :], in0=PF[2][0:L, l * 129:(l + 1) * 129], scalar=wst[0:L, l:l + 1], in1=hs[mi][0:L, l, :],
                                                                    op0=ALU.mult, op1=ALU.add), r=[bPF[2], b_wst, b_hs[mi]], w=[b_nd])
                    OP("dve", lambda e, l=l: e.scalar_tensor_tensor(out=dab[0:L, l:l + 1], in0=nd[0:L, l, 128:129], scalar=-1.0, in1=nd[0:L, l, 128:129], op0=ALU.mult, op1=ALU.max),
                       r=[b_nd], w=[b_dab])
                    OP("dve", lambda e, l=l: e.tensor_scalar(out=dab[0:L, l:l + 1], in0=dab[0:L, l:l + 1], scalar1=EMv[:, l:l + 1], scalar2=None, op0=ALU.max),
                       r=[b_dab, b_EM, b_EMs], w=[b_dab])
                OP("dve", lambda e: e.reciprocal(out=dab[0:L, :], in_=dab[0:L, :]), r=[b_dab], w=[b_dab])
                for l in range(2):
                    OP("act", lambda e, l=l: e.activation(out=junk[0:L, :], in_=nd[0:L, l, 0:128], func=AF.Square, scale=dab[0:L, l:l + 1], accum_out=ssh[0:L, l:l + 1]),
                       r=[b_nd, b_dab], w=[b_junk, b_ssh])
                OP("act", lambda e: e.activation(out=ssh[0:L, :], in_=ssh[0:L, :], func=AF.Sqrt, scale=1.0 / 128, bias=EPS), r=[b_ssh], w=[b_ssh])
                OP("dve", lambda e: e.reciprocal(out=ssh[0:L, :], in_=ssh[0:L, :]), r=[b_ssh], w=[b_ssh])
                OP("dve", lambda e: e.tensor_tensor(out=ssh[0:L, :], in0=ssh[0:L, :], in1=dab[0:L, :], op=ALU.mult), r=[b_ssh, b_dab], w=[b_ssh])
                for l in range(2):
                    OP("dve", lambda e, l=l: e.scalar_tensor_tensor(out=mlb[mi][0:L, l * 128:(l + 1) * 128], in0=nd[0:L, l, 0:128], scalar=ssh[0:L, l:l + 1], in1=sg_v(l),
                                                                    op0=ALU.mult, op1=ALU.mult), r=[b_nd, b_ssh, b_sgm, b_sgm_s], w=[b_mlb[mi]])
                kb.dma("sp", lambda e: e.dma_start(out=mix[mix_rows, 512 + h0 * 128:512 + h0 * 128 + 256], in_=mlb[mi][0:L, :]), reads=[b_mlb[mi]], writes=[b_mix2])

            def chunkC(L, ck_cols, UTv, GTv, EMv, kT_v, qT_v, kt_v, va_v, sg_v, full, mix_rows, mi, gcol):
                if L == 128:
                    gb, bgb = gend[mi], b_gend[mi]
                else:
                    gend_bcast(gcol)
                    gb, bgb = gendb, b_gendb
                OP("dve", lambda e: e.tensor_tensor(out=gtok[0:L, :], in0=UTv, in1=gend[mi][0:L, :], op=ALU.subtract), r=[b_UT, b_UTs, b_gend[mi]], w=[b_gtok])
                OP("act", lambda e: e.activation(out=gtok[0:L, :], in_=gtok[0:L, :], func=AF.Exp), r=[b_gtok], w=[b_gtok])
                for l in range(2):
                    OP("dve", lambda e, l=l: e.tensor_scalar(out=gv[0:L, l, :], in0=va_v(l), scalar1=gtok[0:L, l:l + 1], scalar2=None, op0=ALU.mult),
                       r=[b_mva, b_mva_s, b_gtok], w=[b_gv])
                    OP("pe", lambda e, l=l: e.matmul(PF[3][:, l * 129:(l + 1) * 129], lhsT=kt_v(l), rhs=gv[0:L, l, :], start=True, stop=True), r=[b_mkt, b_mkt_s, b_gv], w=[bPF[3]])
                OP("dve", lambda e: e.tensor_tensor(out=g2[:, :], in0=gprev[:, :], in1=gb[:, :], op=ALU.subtract), r=[b_gprev, bgb], w=[b_g2])
                OP("act", lambda e: e.activation(out=gst[:, :], in_=g2[:, :], func=AF.Exp), r=[b_g2], w=[b_gst])
                for l in range(2):
                    OP("dve", lambda e, l=l: e.scalar_tensor_tensor(out=Cf[:, l, :], in0=Cf[:, l, :], scalar=gst[:, l:l + 1], in1=PF[3][:, l * 129:(l + 1) * 129],
                                                                    op0=ALU.mult, op1=ALU.add), r=[b_Cf, b_gst, bPF[3]], w=[b_Cf])
                OP("act", lambda e: e.copy(out=Cb[:], in_=Cf[:]), r=[b_Cf], w=[b_Cb])
                OP("dve", lambda e: e.tensor_copy(out=gprev[:], in_=gb[:]), r=[bgb], w=[b_gprev])

            gendb = sbt(f"h{hp}_" "gendb", [128, 2], F32, p2); b_gendb = B("gendb")

            def gend_bcast(col):
                for l in range(2):
                    OP("pe", lambda e, l=l: e.matmul(PF[5][:, l:l + 1], lhsT=oh2[:, l, :], rhs=Grow[:, col:col + 1], start=True, stop=True), r=[b_oh2, b_Grow], w=[bPF[5]])
                OP("dve", lambda e: e.tensor_copy(out=gendb[:], in_=PF[5][:, 0:2]), r=[bPF[5]], w=[b_gendb])

            OP("pool", lambda e: e.memset(Cf[:], 0.0), w=[b_Cf])
            OP("pool", lambda e: e.memset(Cb[:], 0.0), w=[b_Cb])
            OP("pool", lambda e: e.memset(gprev[:], 0.0), w=[b_gprev])
            def pargs(ck):
                full = ck >= 16
                tm = ck - 16
                cols = slice(ck * 128, (ck + 1) * 128)
                fc = slice(tm * 128, (tm + 1) * 128)
                return (128, cols, UT[:, ck, :], GT[:, ck, :], EM[:, ck, :],
                        lambda l, fc=fc: mkT[:, l, fc], lambda l, fc=fc: mqT[:, l, fc], lambda l, ck=ck: mkt[:, ck, l * 128:(l + 1) * 128],
                        lambda l, ck=ck: mva[:, ck, l, :], lambda l, tm=tm: sgm[:, tm, l * 128:(l + 1) * 128], full,
                        slice(tm * 128, (tm + 1) * 128), ck % 2, ck * 128 + 127)
            chunkA(*pargs(0))
            for ck in range(32):
                if ck + 1 < 32:
                    chunkA(*pargs(ck + 1))
                chunkB(*pargs(ck))
                chunkC(*pargs(ck))
            if stop_after == "ml_c":
                kb.final_wait("sp"); kb.emit(); p2.close(); raise StopIteration
            for l in range(2):
                h = h0 + l
                store(C_p[h * 128:(h + 1) * 128, :], Cf[:, l, 0:128], b_Cf)
                store(n_p[h * 128:(h + 1) * 128, :], Cf[:, l, 128:129], b_Cf)
            store(m_p[h0:h0 + 2, :], Brow[:, 4095:4096], b_Brow)
            for sbi in range(4):
                for l in range(2):
                    h = h0 + l
                    r0 = (sbi * 4 + h) * 128
                    kb.dma("sp", lambda e, l=l, r0=r0: e.dma_start(out=Cf[:, l, 0:128], in_=sC[r0:r0 + 128, :]), writes=[b_Cf])
                    kb.dma("sp", lambda e, l=l, r0=r0: e.dma_start(out=Cf[:, l, 128:129], in_=sn[r0:r0 + 128, :]), writes=[b_Cf])
                    kb.dma("sp", lambda e, l=l, h=h, sbi=sbi: e.dma_start(out=gprev[:, l:l + 1], in_=smi[h:h + 1, sbi:sbi + 1].partition_broadcast(128)), writes=[b_gprev])
                OP("act", lambda e: e.copy(out=Cb[:], in_=Cf[:]), r=[b_Cf], w=[b_Cb])
                c0 = 4096 + sbi * 8
                fc = slice(NM + sbi * 8, NM + sbi * 8 + 8)
                sargs = (8, slice(c0, c0 + 8), UTs[:, sbi, :], GTs[:, sbi, :], EMs[:, sbi, :],
                         lambda l, fc=fc: mkT[:, l, fc], lambda l, fc=fc: mqT[:, l, fc], lambda l, sbi=sbi: mkt_s[:, sbi, l * 128:(l + 1) * 128],
                         lambda l, sbi=sbi: mva_s[:, sbi, l, :], lambda l, sbi=sbi: sgm_s[:, sbi, l * 128:(l + 1) * 128], True,
                         slice(NM + sbi * 8, NM + sbi * 8 + 8), sbi % 2, c0 + 7)
                chunkA(*sargs)
                chunkB(*sargs)
                chunkC(*sargs)
                for l in range(2):
                    h = h0 + l
                    r0 = (sbi * 4 + h) * 128
                    store(C_s[r0:r0 + 128, :], Cf[:, l, 0:128], b_Cf)
                    store(n_s[r0:r0 + 128, :], Cf[:, l, 128:129], b_Cf)
                store(m_s[sbi * 4 + h0:sbi * 4 + h0 + 2, :], Brow[:, c0 + 7:c0 + 8], b_Brow)
            kb.barrier()
            p2.close()
        try:
            for hp in range(2):
                ml_pass(hp)
        except StopIteration:
            return nc, dbg_outs
        if stop_after == "ml":
            kb.final_wait("sp")
            kb.emit()
            return nc, dbg_outs

        p3 = contextlib.ExitStack()
        wO = sbt("wO", [128, 8, 1024], BF16, p3); b_wO = B("wO")
        wU = sbt("wU", [128, 8, 4096], BF16, p3); b_wU = B("wU")
        wD = sbt("wD", [128, 32, 1024], BF16, p3); b_wD = B("wD")
        for c in range(8):
            load_w(wO[:, c, :], w_out[c * 128:(c + 1) * 128, :], b_wO, 1024)
        for c in range(8):
            load_w(wU[:, c, :], w_up[c * 128:(c + 1) * 128, :], b_wU, 4096)
        for f in range(32):
            load_w(wD[:, f, :], w_down[f * 128:(f + 1) * 128, :], b_wD, 1024)
        kb.dma("sp", lambda e: e.dma_start(out=g1[:], in_=nffn[0:1, :].partition_broadcast(128)), writes=[b_g1])
        g3 = sbt("g3", [128, D], F32, p3); b_g3 = B("g3")
        kb.dma("sp", lambda e: e.dma_start(out=g3[:], in_=nfin[0:1, :].partition_broadcast(128)), writes=[b_g3])
        mixb = [sbt(f"mixb{i}", [128, D], BF16, p3) for i in range(2)]; b_mixb = [B(f"mixb{i}") for i in range(2)]
        mixT = sbt("mixT", [128, 8, 256], BF16, p3); b_mixT = B("mixT")
        xn2T = sbt("xn2T", [128, 8, 256], BF16, p3); b_xn2T = B("xn2T")
        uT = sbt("uT", [128, 32, 256], BF16, p3); b_uT = B("uT")
        ur = [sbt(f"ur{i}", [128, 256], F32, p3) for i in range(2)]; b_ur = [B(f"ur{i}") for i in range(2)]
        yo = sbt("yo", [128, D], F32, p3); b_yo = B("yo")
        all_mix = [b_mix, b_mix2, b_mix3]
        fgroups = [("main", gi) for gi in range(8)] + [("smp", 0)]
        for kind, gi in fgroups:
            if kind == "main":
                tiles = [(gi * 256 + t * 128, 128, t * 128) for t in range(2)]
                xsrc, ydst = xm, y_m
            else:
                tiles = [(0, NS, 0)]
                xsrc, ydst = xs, y_s
            ntok = sum(n for _, n, _ in tiles)
            for ti, (r0, n, c0) in enumerate(tiles):
                mr0 = r0 if kind == "main" else NM
                kb.dma("sp", lambda e, ti=ti, r0=r0, n=n, xsrc=xsrc: e.dma_start(out=xt[ti][0:n, :], in_=xsrc[r0:r0 + n, :]), writes=[b_xt[ti]])
                kb.dma("sp", lambda e, ti=ti, mr0=mr0, n=n: e.dma_start(out=mixb[ti][0:n, :], in_=mix[mr0:mr0 + n, :]), reads=all_mix, writes=[b_mixb[ti]])
                to_featmajor(mixb[ti], b_mixb[ti], n, mixT[:, :, c0:c0 + n], b_mixT)
            for ti, (r0, n, c0) in enumerate(tiles):
                for hf in range(2):
                    for c in range(8):
                        OP("pe", lambda e, c=c, hf=hf, n=n, c0=c0: e.matmul(PF[hf][0:n, :], lhsT=mixT[:, c, c0:c0 + n], rhs=wO[:, c, hf * 512:(hf + 1) * 512], start=(c == 0), stop=(c == 7)),
                           r=[b_mixT, b_wO], w=[bPF[hf]])
                    OP("dve", lambda e, ti=ti, hf=hf, n=n: e.tensor_tensor(out=xt[ti][0:n, hf * 512:(hf + 1) * 512], in0=PF[hf][0:n, :], in1=xt[ti][0:n, hf * 512:(hf + 1) * 512], op=ALU.add),
                       r=[bPF[hf], b_xt[ti]], w=[b_xt[ti]])
                rms_scale(xt[ti][0:n, :], b_xt[ti], n, ti, xnb[ti][0:n, :], b_xnb[ti])
                OP("dve", lambda e, ti=ti, n=n: e.scalar_tensor_tensor(out=xnb[ti][0:n, :], in0=xt[ti][0:n, :], scalar=rstd[ti][0:n, 0:1], in1=g1[0:n, :], op0=ALU.mult, op1=ALU.mult),
                   r=[b_xt[ti], b_rstd[ti], b_g1], w=[b_xnb[ti]])
                to_featmajor(xnb[ti], b_xnb[ti], n, xn2T[:, :, c0:c0 + n], b_xn2T)
            for f in range(32):
                pf = 2 + f % 2
                ui = f % 2
                for c in range(8):
                    OP("pe", lambda e, c=c, f=f, pf=pf, ntok=ntok: e.matmul(PF[pf][:, 0:ntok], lhsT=wU[:, c, f * 128:(f + 1) * 128], rhs=xn2T[:, c, 0:ntok], start=(c == 0), stop=(c == 7)),
                       r=[b_wU, b_xn2T], w=[bPF[pf]])
                OP("dve", lambda e, pf=pf, ui=ui, ntok=ntok: e.tensor_scalar(out=ur[ui][:, 0:ntok], in0=PF[pf][:, 0:ntok], scalar1=0.0, scalar2=None, op0=ALU.max), r=[bPF[pf]], w=[b_ur[ui]])
                OP("act", lambda e, f=f, ui=ui, ntok=ntok: e.activation(out=uT[:, f, 0:ntok], in_=ur[ui][:, 0:ntok], func=AF.Square), r=[b_ur[ui]], w=[b_uT])
            for ti, (r0, n, c0) in enumerate(tiles):
                for hf in range(2):
                    for f in range(32):
                        OP("pe", lambda e, f=f, hf=hf, n=n, c0=c0: e.matmul(PF[hf][0:n, :], lhsT=uT[:, f, c0:c0 + n], rhs=wD[:, f, hf * 512:(hf + 1) * 512], start=(f == 0), stop=(f == 31)),
                           r=[b_uT, b_wD], w=[bPF[hf]])
                    OP("dve", lambda e, ti=ti, hf=hf, n=n: e.tensor_tensor(out=xt[ti][0:n, hf * 512:(hf + 1) * 512], in0=PF[hf][0:n, :], in1=xt[ti][0:n, hf * 512:(hf + 1) * 512], op=ALU.add),
                       r=[bPF[hf], b_xt[ti]], w=[b_xt[ti]])
                rms_scale(xt[ti][0:n, :], b_xt[ti], n, ti, xnb[ti][0:n, :], b_xnb[ti])
                OP("dve", lambda e, ti=ti, n=n: e.scalar_tensor_tensor(out=yo[0:n, :], in0=xt[ti][0:n, :], scalar=rstd[ti][0:n, 0:1], in1=g3[0:n, :], op0=ALU.mult, op1=ALU.mult),
                   r=[b_xt[ti], b_rstd[ti], b_g3], w=[b_yo])
                store(ydst[r0:r0 + n, :], yo[0:n, :], b_yo)
        kb.barrier()
        p3.close()
        kb.final_wait("sp")
        kb.emit()
        return nc, dbg_outs


def make_in_maps(inp):
    f = lambda a: np.ascontiguousarray(a, dtype=np.float32)
    xp = np.asarray(inp["x_prompt"]); xsamp = np.asarray(inp["x_sample"])
    ckf = f(np.asarray(inp["cache_k"]).reshape(2560 * 128, 512))
    cvf = f(np.asarray(inp["cache_v"]).reshape(2560 * 128, 512))
    ptab = np.asarray(inp["page_table"]).astype(np.int32)
    sCf = np.asarray(inp["state_C"])[0]; snf = np.asarray(inp["state_n"])[0]; smf = np.asarray(inp["state_m"])[0]
    shared = {
        "w_in": f(inp["w_in"][0]), "w_out": f(inp["w_out"][0]), "w_up": f(inp["w_up"][0]), "w_down": f(inp["w_down"][0]),
        "nmix": f(inp["norm_mix"]).reshape(1, D), "nffn": f(inp["norm_ffn"]).reshape(1, D), "nfin": f(inp["norm_final"]).reshape(1, D),
        "mlg": f(inp["ml_norm"]).reshape(1, 512),
        "bg": f(np.concatenate([np.asarray(inp["b_ig"]).reshape(-1), np.asarray(inp["b_fg"]).reshape(-1)])).reshape(8, 1),
        "rb": f(inp["rel_bias"]).reshape(1, 256), "ck": ckf, "cv": cvf,
    }
    maps = []
    for c in range(8):
        b, half = c // 2, c % 2
        m = dict(shared)
        m["xm"] = f(xp[b, half * NM:(half + 1) * NM])
        m["xc"] = f(xp[b, 0:NM]) if half else np.zeros((NM, D), np.float32)
        m["xs"] = f(xsamp[4 * c:4 * c + 4].reshape(NS, D))
        m["pt"] = np.ascontiguousarray(ptab[4 * c:4 * c + 4].reshape(1, 256))
        m["sC"] = f(sCf[4 * c:4 * c + 4].reshape(16 * 128, 128))
        m["sn"] = f(snf[4 * c:4 * c + 4].reshape(16 * 128, 1))
        m["smi"] = f(smf[4 * c:4 * c + 4].T)
        m["cf"] = np.array([[float(half), (float(half) - 1.0) * 30000.0, 0.0, 0.0]], np.float32)
        cd = np.zeros((16, 2, 16), np.float32)
        for qt in range(16):
            own = 8 + qt // 2
            for n in range(16):
                is_cand = (n < 8 and half == 1) or (8 <= n < own)
                cd[qt, 0, n] = NEG if is_cand else 0.0
                cd[qt, 1, n] = 0.0 if (is_cand or n == own) else NEG
        m["cand"] = cd.reshape(1, 512)
        maps.append(m)
    return maps


STOP_AFTER = "all"
CACHE_ROWS = 2560 * 128


def kernel(**inputs):
    maps = make_in_maps(inputs)
    for m in maps:
        m["ck"] = m["ck"][:CACHE_ROWS]
        m["cv"] = m["cv"][:CACHE_ROWS]
    nc, _ = build_program(stop_after=STOP_AFTER, cache_rows=CACHE_ROWS)
    res = run_bass_kernel_spmd(nc, maps, core_ids=list(range(8)))
    R = res.results
    f32 = np.float32
    y_prompt = np.zeros((4, 4096, D), f32); y_sample = np.zeros((32, 8, D), f32)
    nkp = np.zeros((1, 4, 4096, 8, 64), f32); nvp = np.zeros((1, 4, 4096, 8, 64), f32)
    nCp = np.zeros((1, 4, 4, 128, 128), f32); nnp_ = np.zeros((1, 4, 4, 128), f32); nmp = np.zeros((1, 4, 4), f32)
    nks = np.zeros((1, 32, 8, 8, 64), f32); nvs = np.zeros((1, 32, 8, 8, 64), f32)
    nCs = np.zeros((1, 32, 4, 128, 128), f32); nns = np.zeros((1, 32, 4, 128), f32); nms = np.zeros((1, 32, 4), f32)
    for c in range(8):
        b, half = c // 2, c % 2
        r = R[c]
        sl = slice(half * NM, (half + 1) * NM)
        y_prompt[b, sl] = r["y_m"]
        y_sample[4 * c:4 * c + 4] = r["y_s"].reshape(4, 8, D)
        nkp[0, b, sl] = r["k_m"].reshape(NM, 8, 64)
        nvp[0, b, sl] = r["v_m"].reshape(NM, 8, 64)
        if half == 1:
            nCp[0, b] = r["C_p"].reshape(4, 128, 128)
            nnp_[0, b] = r["n_p"].reshape(4, 128)
            nmp[0, b] = r["m_p"].reshape(4)
        nks[0, 4 * c:4 * c + 4] = r["k_s"].reshape(4, 8, 8, 64)
        nvs[0, 4 * c:4 * c + 4] = r["v_s"].reshape(4, 8, 8, 64)
        nCs[0, 4 * c:4 * c + 4] = r["C_s"].reshape(4, 4, 128, 128)
        nns[0, 4 * c:4 * c + 4] = r["n_s"].reshape(4, 4, 128)
        nms[0, 4 * c:4 * c + 4] = r["m_s"].reshape(4, 4)
    return (y_prompt, y_sample, nkp, nvp, nCp, nnp_, nmp, nks, nvs, nCs, nns, nms)
```

```python
import contextlib
import math
import numpy as np
import concourse.bass as bass
import concourse.mybir as mybir
from concourse.alu_op_type import AluOpType as ALU
from concourse.bass_utils import run_bass_kernel_spmd

F32 = mybir.dt.float32
BF16 = mybir.dt.bfloat16
I32 = mybir.dt.int32
AF = mybir.ActivationFunctionType
AX = mybir.AxisListType

ENGS = ("pe", "act", "dve", "pool", "sp")
NEG = -30000.0
D = 1024
NM = 2048
NS = 32
EPS = 1e-6


class Buf:
    __slots__ = ("name", "w", "rs", "dsem", "dcnt")

    def __init__(self, name):
        self.name = name
        self.w = None
        self.rs = {}
        self.dsem = None
        self.dcnt = 0


class KB:
    def __init__(self, nc, stack):
        self.nc = nc
        self.stack = stack
        self.sems = {}
        self.cnt = {e: 0 for e in ENGS}
        self.known = {e: {} for e in ENGS}
        self.prog = {e: [] for e in ENGS}
        for e in ENGS:
            self._newsem("E_" + e)
        self.nd = 0
        self.dbufs = []

    def _newsem(self, key):
        self.sems[key] = self.stack.enter_context(self.nc.semaphore(key))
        return key

    def buf(self, name):
        return Buf(name)

    def _collect(self, e, reads, writes):
        waits = {}
        kn = self.known[e]
        own = "E_" + e

        def need(tok, same_ok):
            if tok is None:
                return
            k, v = tok
            if k == own and same_ok:
                return
            if kn.get(k, 0) >= v:
                return
            if waits.get(k, 0) < v:
                waits[k] = v

        for b in reads:
            need(b.w, False)
        for b in writes:
            need(b.w, True)
            for k, v in b.rs.items():
                need((k, v), True)
        for k, v in waits.items():
            kn[k] = v
        return list(waits.items())

    def op(self, e, fn, reads=(), writes=()):
        waits = self._collect(e, reads, writes)
        self.cnt[e] += 1
        tok = ("E_" + e, self.cnt[e])
        for b in reads:
            if b.rs.get(tok[0], 0) < tok[1]:
                b.rs[tok[0]] = tok[1]
        for b in writes:
            b.w = tok
            b.rs = {}
        self.prog[e].append((waits, fn, (tok[0], 1)))

    def dma(self, q, fn, reads=(), writes=()):
        waits = self._collect(q, reads, writes)
        tgt = writes[0] if writes else reads[0]
        if tgt.dsem is None:
            self.nd += 1
            tgt.dsem = self._newsem(f"D{self.nd}")
            self.dbufs.append(tgt)
        tgt.dcnt += 16
        tok = (tgt.dsem, tgt.dcnt)
        for b in reads:
            if b.rs.get(tok[0], 0) < tok[1]:
                b.rs[tok[0]] = tok[1]
        for b in writes:
            b.w = tok
            b.rs = {}
        self.prog[q].append((waits, fn, (tok[0], 16)))

    def barrier(self):
        for e in ENGS:
            waits = [("E_" + x, self.cnt[x]) for x in ENGS if x != e and self.cnt[x] > 0]
            waits += [(b.dsem, b.dcnt) for b in self.dbufs]
            for k, v in waits:
                self.known[e][k] = max(self.known[e].get(k, 0), v)
            self.prog[e].append((waits, None, None))

    def final_wait(self, e):
        waits = [(b.dsem, b.dcnt) for b in self.dbufs]
        self.prog[e].append((waits, None, None))

    def emit(self):
        nc = self.nc
        sems = self.sems
        prog = self.prog
        with nc.Block() as block:
            def run(name):
                def body(eng):
                    for waits, fn, inc in prog[name]:
                        for k, v in waits:
                            eng.wait_ge(sems[k], v)
                        if fn is not None:
                            fn(eng).then_inc(sems[inc[0]], inc[1])
                return body
            block.tensor(run("pe"))
            block.scalar(run("act"))
            block.vector(run("dve"))
            block.gpsimd(run("pool"))
            block.sync(run("sp"))


def t5_thresholds():
    n = np.arange(0, 600)
    nf = np.maximum(n, 1).astype(np.float32)
    large = 16 + (np.log(nf / np.float32(16)) / np.float32(math.log(128 / 16)) * np.float32(16)).astype(np.int32)
    large = np.minimum(large, 31)
    bucket = np.where(n < 16, n, large)
    return [int(np.argmax(bucket >= b)) for b in range(1, 32)]


def build_program(dbg=None, stop_after=None, cache_rows=2560 * 128):
    nc = bass.Bass("TRN2", target_bir_lowering=False)
    din = lambda name, shape, dt=F32: nc.dram_tensor(name, shape, dt, kind="ExternalInput").ap()
    dout = lambda name, shape, dt=F32: nc.dram_tensor(name, shape, dt, kind="ExternalOutput").ap()
    xm = din("xm", [NM, D]); xc = din("xc", [NM, D]); xs = din("xs", [NS, D])
    w_in = din("w_in", [D, 3592]); w_out = din("w_out", [D, D]); w_up = din("w_up", [D, 4096]); w_down = din("w_down", [4096, D])
    nmix = din("nmix", [1, D]); nffn = din("nffn", [1, D]); nfin = din("nfin", [1, D]); mlg = din("mlg", [1, 512])
    bg = din("bg", [8, 1]); rb = din("rb", [1, 256])
    ck = din("ck", [cache_rows, 512]); cv = din("cv", [cache_rows, 512])
    pt = din("pt", [1, 256], I32)
    sC = din("sC", [16 * 128, 128]); sn = din("sn", [16 * 128, 1]); smi = din("smi", [4, 4])
    cf = din("cf", [1, 4]); cand = din("cand", [1, 512])
    y_m = dout("y_m", [NM, D]); y_s = dout("y_s", [NS, D])
    k_m = dout("k_m", [NM, 512]); v_m = dout("v_m", [NM, 512]); k_s = dout("k_s", [NS, 512]); v_s = dout("v_s", [NS, 512])
    C_p = dout("C_p", [512, 128]); n_p = dout("n_p", [512, 1]); m_p = dout("m_p", [4, 1])
    C_s = dout("C_s", [2048, 128]); n_s = dout("n_s", [2048, 1]); m_s = dout("m_s", [16, 1])
    mix = nc.dram_tensor("mix", [NM + NS, D], BF16, kind="ExternalOutput").ap()
    xnTd = nc.dram_tensor("xnTd", [9, 128, 4096], BF16).ap()
    dbg_outs = {}

    with contextlib.ExitStack() as st:
        kb = KB(nc, st)
        B = kb.buf
        out_bufs = []

        def sbt(name, shape, dt, stack=st):
            return stack.enter_context(nc.sbuf_tensor(name, shape, dt))

        PT = [st.enter_context(nc.psum_tensor(f"PT{i}", [128, 8, 128], BF16)) for i in range(2)]
        bPT = [B(f"PT{i}") for i in range(2)]
        PF = [st.enter_context(nc.psum_tensor(f"PF{i}", [128, 512], F32)) for i in range(6)]
        bPF = [B(f"PF{i}") for i in range(6)]

        TQ = {"on": False, "q": []}

        def OP(e, fn, r=(), w=()):
            if TQ["on"]:
                TQ["q"].append((e, fn, list(r), list(w)))
            else:
                kb.op(e, fn, r, w)

        def DMA(q, fn, reads=(), writes=()):
            if TQ["on"]:
                TQ["q"].append(("dma", q, fn, list(reads), list(writes)))
            else:
                kb.dma(q, fn, reads, writes)

        def flush_tq(n):
            for _ in range(min(n, len(TQ["q"]))):
                it = TQ["q"].pop(0)
                if it[0] == "dma":
                    kb.dma(it[1], it[2], it[3], it[4])
                else:
                    kb.op(it[0], it[1], it[2], it[3])

        def store(dst_ap, src_ap, src_buf, q="sp"):
            import os
            if os.environ.get("NO_ST") is not None:
                return
            kb.dma(q, lambda e: e.dma_start(out=dst_ap, in_=src_ap), reads=[src_buf], writes=[])

        identf = sbt("identf", [128, 128], F32); b_idf = B("idf")
        ident = sbt("ident", [128, 128], BF16); b_id = B("id")
        OP("pool", lambda e: e.memset(identf[:], 1.0), w=[b_idf])
        OP("pool", lambda e: e.affine_select(out=identf[:], in_=identf[:], pattern=[[-1, 128]], compare_op=ALU.is_equal,
                                             fill=0.0, base=0, channel_multiplier=1), r=[b_idf], w=[b_idf])
        OP("dve", lambda e: e.tensor_copy(out=ident[:], in_=identf[:]), r=[b_idf], w=[b_id])
        g1 = sbt("g1", [128, D], F32); b_g1 = B("g1")
        kb.dma("sp", lambda e: e.dma_start(out=g1[:], in_=nmix[0:1, :].partition_broadcast(128)), writes=[b_g1])
        cfb = sbt("cfb", [128, 4], F32); b_cfb = B("cfb")
        kb.dma("sp", lambda e: e.dma_start(out=cfb[:], in_=cf[0:1, :].partition_broadcast(128)), writes=[b_cfb])
        rbb = sbt("rbb", [128, 32, 8], F32); b_rbb = B("rbb")
        kb.dma("sp", lambda e: e.dma_start(out=rbb[:].rearrange("p b h -> p (b h)"), in_=rb[0:1, :].partition_broadcast(128)), writes=[b_rbb])
        candb = sbt("candb", [128, 16, 2, 16], F32); b_cand = B("cand")
        kb.dma("sp", lambda e: e.dma_start(out=candb[:].rearrange("p a b c -> p (a b c)"), in_=cand[0:1, :].partition_broadcast(128)), writes=[b_cand])

        if stop_after == "c0":
            kb.final_wait("sp")
            kb.emit()
            return nc, dbg_outs
        xt = [sbt(f"xt{i}", [128, D], F32) for i in range(2)]; b_xt = [B(f"xt{i}") for i in range(2)]
        ssq = [sbt(f"ssq{i}", [128, 1], F32) for i in range(2)]; b_ssq = [B(f"ssq{i}") for i in range(2)]
        rstd = [sbt(f"rstd{i}", [128, 1], F32) for i in range(2)]; b_rstd = [B(f"rstd{i}") for i in range(2)]
        xnb = [sbt(f"xnb{i}", [128, D], BF16) for i in range(2)]; b_xnb = [B(f"xnb{i}") for i in range(2)]
        cnt = {"x": 0}

        def rms_scale(src, b_src, n, i, junk, b_junk, dim=D):
            OP("act", lambda e: e.activation(out=junk, in_=src, func=AF.Square, accum_out=ssq[i][0:n, :]),
               r=[b_src], w=[b_junk, b_ssq[i]])
            OP("act", lambda e: e.activation(out=rstd[i][0:n, :], in_=ssq[i][0:n, :], func=AF.Sqrt, scale=1.0 / dim, bias=EPS),
               r=[b_ssq[i]], w=[b_rstd[i]])
            OP("dve", lambda e: e.reciprocal(out=rstd[i][0:n, :], in_=rstd[i][0:n, :]), r=[b_rstd[i]], w=[b_rstd[i]])

        def to_featmajor(src_bf, b_src, n, dst, b_dst, ncol=8):
            i = cnt["x"] % 2
            cnt["x"] += 1
            for c in range(ncol):
                OP("pe", lambda e, c=c: e.transpose(out=PT[i][:, c, 0:n], in_=src_bf[0:n, c * 128:(c + 1) * 128], identity=ident[0:n, 0:n]),
                   r=[b_src, b_id], w=[bPT[i]])
            OP("act", lambda e: e.copy(out=dst, in_=PT[i][:, 0:ncol, 0:n]), r=[bPT[i]], w=[b_dst])

        def norm_tile(src_ap, n, gt, b_gt, dst, b_dst, q="sp"):
            i = cnt["x"] % 2
            kb.dma(q, lambda e: e.dma_start(out=xt[i][0:n, :], in_=src_ap), writes=[b_xt[i]])
            rms_scale(xt[i][0:n, :], b_xt[i], n, i, xnb[i][0:n, :], b_xnb[i])
            OP("dve", lambda e: e.scalar_tensor_tensor(out=xnb[i][0:n, :], in0=xt[i][0:n, :], scalar=rstd[i][0:n, 0:1], in1=gt[0:n, :],
                                                       op0=ALU.mult, op1=ALU.mult), r=[b_xt[i], b_rstd[i], b_gt], w=[b_xnb[i]])
            to_featmajor(xnb[i], b_xnb[i], n, dst, b_dst)

        def load_w(dst, src, b_dst, ncols):
            for c0 in range(0, ncols, 2048):
                c1 = min(ncols, c0 + 2048)
                kb.dma("pool", lambda e, c0=c0, c1=c1: e.dma_start(out=dst[:, c0:c1], in_=src[:, c0:c1]), writes=[b_dst])

        evq = {"i": 0}

        def evac(out, in_, r, w, scale=None):
            evq["i"] += 1
            if evq["i"] % 2:
                if scale is None:
                    OP("act", lambda e: e.copy(out=out, in_=in_), r=r, w=w)
                else:
                    OP("dve", lambda e: e.tensor_scalar(out=out, in0=in_, scalar1=scale, scalar2=None, op0=ALU.mult), r=r, w=w)
            else:
                if scale is None:
                    OP("dve", lambda e: e.tensor_copy(out=out, in_=in_), r=r, w=w)
                else:
                    OP("dve", lambda e: e.tensor_scalar(out=out, in0=in_, scalar1=scale, scalar2=None, op0=ALU.mult), r=r, w=w)

        qTs = sbt("qTs", [128, 4, NS], BF16); b_qTs = B("qTs")
        kTs_own = sbt("kTs_own", [128, 4, NS], BF16); b_kTs_own = B("kTs_own")
        Vs_own = sbt("Vs_own", [8, 4, 512], BF16); b_Vs_own = B("Vs_own")
        TS128 = sbt("TS128", [128, 8, 8], F32); b_TS128 = B("TS128")
        TS0 = sbt("TS0", [8, 8, 8], F32); b_TS0 = B("TS0")
        p1 = contextlib.ExitStack()
        qTa = [sbt(f"qTa{h}", [81, NM], BF16, p1) for h in range(8)]; b_qTa = [[B(f"qTa{h}_{g}") for g in range(4)] for h in range(8)]
        b_qTaP = [[B(f"qTaP{h}_{g}") for g in range(4)] for h in range(8)]
        kTa = [sbt(f"kTa{h}", [81, 4096], BF16, p1) for h in range(8)]; b_kTa = [[B(f"kTa{h}_{g}") for g in range(8)] for h in range(8)]
        b_kTaI = [B(f"kTaI{h}") for h in range(8)]
        Va = sbt("Va", [128, 32, 8, 65], BF16, p1); b_Va = [B(f"Va{t}") for t in range(32)]; b_Va1 = B("Va1")
        OP("pool", lambda e: e.memset(Va[:, :, :, 64:65], 1.0), w=[b_Va1])
        for h in range(8):
            OP("pool", lambda e, h=h: e.memset(kTa[h][64:81, :], 1.0), w=[b_kTaI[h]])
            OP("pool", lambda e, h=h: e.affine_select(out=kTa[h][64:80, :].rearrange("p (b k) -> p b k", k=256),
                                                      in_=kTa[h][64:80, :].rearrange("p (b k) -> p b k", k=256),
                                                      pattern=[[1, 16], [0, 256]], compare_op=ALU.is_equal, fill=0.0, base=0,
                                                      channel_multiplier=-1), r=[b_kTaI[h]], w=[b_kTaI[h]])

        if stop_after == "c1":
            kb.final_wait("sp")
            kb.emit()
            p1.close()
            return nc, dbg_outs
        thr = t5_thresholds()
        Tt = {0: sbt("T0", [128, 8, 128], F32, p1), 128: sbt("T128", [128, 8, 128], F32, p1)}
        b_Tt = {0: [B(f"T0_{h}") for h in range(8)], 128: [B(f"T128_{h}") for h in range(8)]}
        drb = sbt("drb", [128, 32, 8], F32, p1); b_drb = B("drb")
        OP("dve", lambda e: e.tensor_tensor(out=drb[:, 1:32, :], in0=rbb[:, 1:32, :], in1=rbb[:, 0:31, :], op=ALU.subtract), r=[b_rbb], w=[b_drb])
        OP("dve", lambda e: e.tensor_tensor(out=drb[:, 0:1, :], in0=rbb[:, 0:1, :], in1=rbb[:, 31:32, :], op=ALU.subtract), r=[b_rbb], w=[b_drb])
        rb31x8 = sbt("rb31x8", [128, 8], F32, p1); b_rb31 = B("rb31")
        OP("dve", lambda e: e.tensor_scalar(out=rb31x8[:], in0=rbb[:, 31, :], scalar1=8.0, scalar2=None, op0=ALU.mult), r=[b_rbb], w=[b_rb31])
        disti = sbt("disti", [128, 128], I32, p1); b_disti = B("disti")
        distf = sbt("distf", [128, 128], F32, p1); b_distf = B("distf")
        gef = sbt("gef", [128, 128], F32, p1); b_gef = B("gef")
        TQ["on"] = True
        for delta in (0, 128):
            OP("pool", lambda e, delta=delta: e.iota(out=disti[:], pattern=[[1, 128]], base=delta, channel_multiplier=-1), w=[b_disti])
            OP("dve", lambda e: e.tensor_copy(out=distf[:], in_=disti[:]), r=[b_disti], w=[b_distf])
            for h in range(8):
                OP("dve", lambda e, h=h, delta=delta: e.tensor_scalar(out=Tt[delta][:, h, :], in0=distf[:], scalar1=0.0, scalar2=drb[:, 0, h:h + 1],
                                                                      op0=ALU.mult, op1=ALU.add), r=[b_distf, b_drb], w=[b_Tt[delta][h]])
            steps = [(float(thr[b - 1]), b) for b in range(1, 32)]
            for tv, b in steps:
                OP("dve", lambda e, tv=tv: e.tensor_scalar(out=gef[:], in0=distf[:], scalar1=tv, scalar2=None, op0=ALU.is_ge), r=[b_distf], w=[b_gef])
                for h in range(8):
                    OP("dve", lambda e, h=h, b=b, delta=delta: e.scalar_tensor_tensor(out=Tt[delta][:, h, :], in0=gef[:], scalar=drb[:, b, h:h + 1], in1=Tt[delta][:, h, :],
                                                                                     op0=ALU.mult, op1=ALU.add), r=[b_gef, b_drb, b_Tt[delta][h]], w=[b_Tt[delta][h]])
            if delta == 0:
                OP("dve", lambda e: e.tensor_scalar(out=gef[:], in0=distf[:], scalar1=0.0, scalar2=None, op0=ALU.is_lt), r=[b_distf], w=[b_gef])
                for h in range(8):
                    OP("dve", lambda e, h=h: e.scalar_tensor_tensor(out=Tt[0][:, h, :], in0=gef[:], scalar=NEG, in1=Tt[0][:, h, :],
                                                                    op0=ALU.mult, op1=ALU.add), r=[b_gef, b_Tt[0][h]], w=[b_Tt[0][h]])
        OP("dve", lambda e: e.tensor_copy(out=TS128[:], in_=Tt[128][:, :, 0:8]), r=b_Tt[128], w=[b_TS128])
        OP("dve", lambda e: e.tensor_copy(out=TS0[:], in_=Tt[0][0:8, :, 0:8]), r=b_Tt[0], w=[b_TS0])
        TQ["on"] = False
        tq_slice = (len(TQ["q"]) + 7) // 8
        negT = sbt("negT", [128, 128], F32, p1); b_negT = B("negT")
        OP("pool", lambda e: e.memset(negT[:], NEG), w=[b_negT])

        if stop_after == "c2":
            kb.final_wait("sp")
            kb.emit()
            p1.close()
            return nc, dbg_outs
        p1a = contextlib.ExitStack()
        wA = sbt("wA", [128, 8, 1536], BF16, p1a); b_wA = B("wA")
        for c in range(8):
            load_w(wA[:, c, :], w_in[c * 128:(c + 1) * 128, 0:1536], b_wA, 1536)
        xnT = [sbt(f"xnT{i}", [128, 8, 512], BF16, p1a) for i in range(1)]; b_xnT = [B(f"xnT{i}") for i in range(1)]
        kvst = [sbt(f"kvst{i}", [128, 1024], F32, p1a) for i in range(2)]; b_kvst = [B(f"kvst{i}") for i in range(2)]

        if stop_after == "s0":
            kb.final_wait("sp"); kb.emit(); p1a.close(); p1.close(); return nc, dbg_outs
        b_xnTd = B("xnTd")
        for kind in ("ctx", "main"):
            src = xc if kind == "ctx" else xm
            for g in range(4):
                xi = 0
                for t in range(4):
                    r0 = g * 512 + t * 128
                    norm_tile(src[r0:r0 + 128, :], 128, g1, b_g1, xnT[xi][:, :, t * 128:(t + 1) * 128], b_xnT[xi])
                if stop_after == "s1":
                    kb.final_wait("sp"); kb.emit(); p1a.close(); p1.close(); return nc, dbg_outs
                flush_tq(tq_slice)
                kg = g if kind == "ctx" else 4 + g
                kb.dma("sp", lambda e, kg=kg, xi=xi: e.dma_start(out=xnTd[kg], in_=xnT[xi][:].rearrange("p c t -> p (c t)")), reads=[b_xnT[xi]], writes=[b_xnTd])
                import os
                for h in (range(8) if os.environ.get("SKIP_FM") is None else ()):
                    for which in (("k",) if kind == "ctx" else ("q", "k")):
                        col0 = (0 if which == "q" else 512) + h * 64
                        pf = (2 * h + (which == "k")) % 2
                        for c in range(8):
                            OP("pe", lambda e, c=c, pf=pf, col0=col0, xi=xi: e.matmul(PF[pf][0:64, :], lhsT=wA[:, c, col0:col0 + 64], rhs=xnT[xi][:, c, :],
                                                                                       start=(c == 0), stop=(c == 7)),
                               r=[b_wA, b_xnT[xi]], w=[bPF[pf]])
                        if os.environ.get("NO_EV") is not None:
                            pass
                        elif which == "q":
                            evac(qTa[h][0:64, g * 512:(g + 1) * 512], PF[pf][0:64, :], [bPF[pf]], [b_qTa[h][g]])
                        else:
                            evac(kTa[h][0:64, kg * 512:(kg + 1) * 512], PF[pf][0:64, :], [bPF[pf]], [b_kTa[h][kg]])
                if stop_after == "s2":
                    kb.final_wait("sp"); kb.emit(); p1a.close(); p1.close(); return nc, dbg_outs
                for t in (range(4) if os.environ.get("SKIP_TM") is None else ()):
                    ta = kg * 4 + t
                    si = ta % 2
                    for which in (("v",) if kind == "ctx" else ("k", "v")):
                        col0 = 512 if which == "k" else 1024
                        pf = 3 if which == "v" else 4
                        for c in range(8):
                            OP("pe", lambda e, c=c, pf=pf, col0=col0, xi=xi, t=t: e.matmul(PF[pf][:, :], lhsT=xnT[xi][:, c, t * 128:(t + 1) * 128], rhs=wA[:, c, col0:col0 + 512],
                                                                                          start=(c == 0), stop=(c == 7)),
                               r=[b_wA, b_xnT[xi]], w=[bPF[pf]])
                        if which == "v" and os.environ.get("NO_VA") is None:
                            OP("dve", lambda e, ta=ta, pf=pf: e.tensor_copy(out=Va[:, ta, :, 0:64], in_=PF[pf][:, :].rearrange("p (h d) -> p h d", d=64)),
                               r=[bPF[pf]], w=[b_Va[ta]])
                        if kind == "main" and os.environ.get("NO_KV") is None:
                            o0 = 0 if which == "k" else 512
                            OP("dve", lambda e, si=si, pf=pf, o0=o0: e.tensor_copy(out=kvst[si][:, o0:o0 + 512], in_=PF[pf][:, :]), r=[bPF[pf]], w=[b_kvst[si]])
                    if kind == "main":
                        r0 = g * 512 + t * 128
                        store(k_m[r0:r0 + 128, :], kvst[si][:, 0:512], b_kvst[si])
                        store(v_m[r0:r0 + 128, :], kvst[si][:, 512:1024], b_kvst[si])
                if stop_after == "s3" or (stop_after == "s4" and kind == "main") or (stop_after == "s5" and kind == "ctx" and g == 3) or (stop_after == "s6" and kind == "ctx" and g == 1):
                    kb.final_wait("sp"); kb.emit(); p1a.close(); p1.close(); return nc, dbg_outs
        if stop_after == "c3":
            kb.final_wait("sp")
            kb.emit()
            p1a.close()
            p1.close()
            return nc, dbg_outs
        flush_tq(10 ** 9)
        xi = 0
        norm_tile(xs[0:NS, :], NS, g1, b_g1, xnT[xi][:, :, 0:NS], b_xnT[xi])
        kb.dma("sp", lambda e, xi=xi: e.dma_start(out=xnTd[8].rearrange("p (c t) -> p c t", t=512)[:, :, 0:NS], in_=xnT[xi][:, :, 0:NS]), reads=[b_xnT[xi]], writes=[b_xnTd])
        for which in ("q", "k"):
            for ch in range(4):
                col0 = (0 if which == "q" else 512) + ch * 128
                pf = ch % 2
                for c in range(8):
                    OP("pe", lambda e, c=c, pf=pf, col0=col0, xi=xi: e.matmul(PF[pf][:, 0:NS], lhsT=wA[:, c, col0:col0 + 128], rhs=xnT[xi][:, c, 0:NS],
                                                                               start=(c == 0), stop=(c == 7)), r=[b_wA, b_xnT[xi]], w=[bPF[pf]])
                dst = qTs if which == "q" else kTs_own
                bd = b_qTs if which == "q" else b_kTs_own
                evac(dst[:, ch, :], PF[pf][:, 0:NS], [bPF[pf]], [bd])
        for sbi in range(4):
            si = sbi % 2
            for which in ("k", "v"):
                col0 = 512 if which == "k" else 1024
                pf = 2 + (which == "v")
                for c in range(8):
                    OP("pe", lambda e, c=c, pf=pf, col0=col0, xi=xi, sbi=sbi: e.matmul(PF[pf][0:8, :], lhsT=xnT[xi][:, c, sbi * 8:(sbi + 1) * 8], rhs=wA[:, c, col0:col0 + 512],
                                                                                      start=(c == 0), stop=(c == 7)), r=[b_wA, b_xnT[xi]], w=[bPF[pf]])
                o0 = 0 if which == "k" else 512
                if which == "v":
                    OP("dve", lambda e, sbi=sbi, pf=pf: e.tensor_copy(out=Vs_own[:, sbi, :], in_=PF[pf][0:8, :]), r=[bPF[pf]], w=[b_Vs_own])
                OP("dve", lambda e, si=si, pf=pf, o0=o0: e.tensor_copy(out=kvst[si][0:8, o0:o0 + 512], in_=PF[pf][0:8, :]), r=[bPF[pf]], w=[b_kvst[si]])
            store(k_s[sbi * 8:(sbi + 1) * 8, :], kvst[si][0:8, 0:512], b_kvst[si])
            store(v_s[sbi * 8:(sbi + 1) * 8, :], kvst[si][0:8, 512:1024], b_kvst[si])
        kb.barrier()
        p1a.close()

        if stop_after == "p1":
            kb.final_wait("sp")
            kb.emit()
            p1.close()
            return nc, dbg_outs

        p1b = contextlib.ExitStack()
        kmf = sbt("kmf", [64, 8, 16], F32, p1b); b_kmf = B("kmf")
        kmT = sbt("kmT", [64, 8, 16], BF16, p1b); b_kmT = B("kmT")
        for h in range(8):
            OP("dve", lambda e, h=h: e.tensor_reduce(out=kmf[:, h, :], in_=kTa[h][0:64, :].rearrange("p (b k) -> p b k", k=256), axis=AX.X, op=ALU.add),
               r=b_kTa[h], w=[b_kmf])
        OP("dve", lambda e: e.tensor_copy(out=kmT[:], in_=kmf[:]), r=[b_kmf], w=[b_kmT])
        selm = sbt("selm", [128, 16, 16], F32, p1b); b_selm = B("selm")
        OP("dve", lambda e: e.tensor_scalar(out=selm[:], in0=candb[:, :, 0, :], scalar1=-1.0, scalar2=NEG, op0=ALU.mult, op1=ALU.add), r=[b_cand], w=[b_selm])
        selW = sbt("selW", [128, 4, 8, 81], F32, p1b); b_selW = [B(f"selW{j}") for j in range(4)]
        OP("pool", lambda e: e.memset(selW[:], 0.0), w=b_selW)
        for j in range(4):
            OP("dve", lambda e, j=j: e.tensor_copy(out=selW[:, j, :, 80:81], in_=rb31x8[:].rearrange("p (h o) -> p h o", o=1)), r=[b_rb31], w=[b_selW[j]])
        smk = sbt("smk", [128, 8, 16], F32, p1b); b_smk = B("smk")
        top8 = sbt("top8", [128, 8, 8], F32, p1b); b_top8 = B("top8")
        PTt = [sbt(f"PTt{i}", [128, 512], BF16, p1b) for i in range(2)]; b_PTt = [B(f"PTt{i}") for i in range(2)]
        tmpS = [sbt(f"tmpS{i}", [128, 512], F32, p1b) for i in range(2)]; b_tmpS = [B(f"tmpS{i}") for i in range(2)]
        ot = [sbt(f"ot{i}", [65, 512], F32, p1b) for i in range(2)]; b_ot = [B(f"ot{i}") for i in range(2)]
        rc4 = sbt("rc4", [128, 4, 1], F32, p1b); b_rc4 = B("rc4")
        attb = [sbt(f"attb{i}", [128, 4, 512], BF16, p1b) for i in range(2)]; b_attb = [B(f"attb{i}") for i in range(2)]
        b_mix = B("mix")
        sidx = 0
        for g in range(4):
            for j in range(4):
                qt = 4 * g + j
                for h in range(8):
                    OP("pe", lambda e, h=h, qt=qt: e.matmul(PF[4][:, h * 16:(h + 1) * 16], lhsT=qTa[h][0:64, qt * 128:(qt + 1) * 128], rhs=kmT[:, h, :],
                                                            start=True, stop=True), r=[b_qTa[h][g], b_kmT], w=[bPF[4]])
                OP("dve", lambda e, qt=qt: e.tensor_tensor(out=smk[:], in0=PF[4][:, 0:128].rearrange("p (h n) -> p h n", n=16),
                                                           in1=selm[:, qt:qt + 1, :].to_broadcast([128, 8, 16]), op=ALU.add), r=[bPF[4], b_selm], w=[b_smk])
                for h in range(8):
                    OP("dve", lambda e, h=h: e.max(out=top8[:, h, :], in_=smk[:, h, :]), r=[b_smk], w=[b_top8])
                for h in range(8):
                    OP("dve", lambda e, h=h, j=j, qt=qt: e.scalar_tensor_tensor(out=selW[:, j, h, 64:80], in0=smk[:, h, :], scalar=top8[:, h, 2:3], in1=candb[:, qt, 0, :],
                                                                               op0=ALU.is_lt, op1=ALU.mult), r=[b_smk, b_top8, b_cand], w=[b_selW[j]])
                OP("dve", lambda e, j=j, qt=qt: e.tensor_tensor(out=selW[:, j, :, 64:80], in0=selW[:, j, :, 64:80],
                                                                in1=candb[:, qt:qt + 1, 1, :].to_broadcast([128, 8, 16]), op=ALU.add), r=[b_cand, b_selW[j]], w=[b_selW[j]])
            for h in range(8):
                for j in range(4):
                    OP("pe", lambda e, h=h, j=j: e.transpose(out=PF[5][0:81, j * 128:(j + 1) * 128], in_=selW[:, j, h, :], identity=identf[:]),
                       r=[b_selW[j], b_idf], w=[bPF[5]])
                OP("act", lambda e, h=h, g=g: e.copy(out=qTa[h][64:81, g * 512:(g + 1) * 512], in_=PF[5][64:81, :]), r=[bPF[5]], w=[b_qTaP[h][g]])
            ab = g % 2
            for h in range(8):
                po = 2 + h % 2
                nk = 16 + 4 * g + 4

                def emit_S(kt, h=h, g=g):
                    si = kt % 2
                    OP("pe", lambda e, h=h, kt=kt, g=g, si=si: e.matmul(PF[si][:, :], lhsT=kTa[h][0:81, kt * 128:(kt + 1) * 128], rhs=qTa[h][0:81, g * 512:(g + 1) * 512],
                                                                         start=True, stop=True),
                       r=[b_kTa[h][kt // 4], b_kTaI[h], b_qTa[h][g], b_qTaP[h][g]], w=[bPF[si]])

                def emit_exp(kt, h=h, g=g):
                    si = kt % 2
                    rel = kt - (16 + 4 * g)
                    if rel < -1:
                        OP("act", lambda e, si=si: e.activation(out=PTt[si][:], in_=PF[si][:, :], func=AF.Exp, scale=0.125), r=[bPF[si]], w=[b_PTt[si]])
                    else:
                        for j in range(4):
                            d = j - rel
                            cs = slice(j * 128, (j + 1) * 128)
                            if d == 0:
                                Tm, bTm = Tt[0][:, h, :], b_Tt[0][h]
                            elif d == 1:
                                Tm, bTm = Tt[128][:, h, :], b_Tt[128][h]
                            elif d == -1 and j % 2 == 0:
                                Tm, bTm = negT[:], b_negT
                            else:
                                Tm = None
                            if Tm is not None:
                                OP("dve", lambda e, si=si, cs=cs, Tm=Tm: e.scalar_tensor_tensor(out=tmpS[si][:, cs], in0=PF[si][:, cs], scalar=0.125, in1=Tm,
                                                                                                 op0=ALU.mult, op1=ALU.add), r=[bPF[si], bTm], w=[b_tmpS[si]])
                            else:
                                OP("dve", lambda e, si=si, cs=cs: e.tensor_scalar(out=tmpS[si][:, cs], in0=PF[si][:, cs], scalar1=0.125, scalar2=None, op0=ALU.mult),
                                   r=[bPF[si]], w=[b_tmpS[si]])
                        OP("act", lambda e, si=si: e.activation(out=PTt[si][:], in_=tmpS[si][:], func=AF.Exp), r=[b_tmpS[si]], w=[b_PTt[si]])

                def emit_PV(kt, h=h, po=po, nk=nk):
                    si = kt % 2
                    OP("pe", lambda e, h=h, kt=kt, si=si, po=po, nk=nk: e.matmul(PF[po][0:65, :], lhsT=Va[:, kt, h, :], rhs=PTt[si][:], start=(kt == 0), stop=(kt == nk - 1)),
                       r=[b_Va[kt], b_Va1, b_PTt[si]], w=[bPF[po]])

                emit_S(0)
                for kt in range(nk):
                    if kt + 1 < nk:
                        emit_S(kt + 1)
                    emit_exp(kt)
                    emit_PV(kt)
                oi = h % 2
                OP("dve", lambda e, oi=oi, po=po: e.tensor_copy(out=ot[oi][:], in_=PF[po][0:65, :]), r=[bPF[po]], w=[b_ot[oi]])
                for j in range(4):
                    OP("pe", lambda e, oi=oi, j=j: e.transpose(out=PF[4][:, j * 128:j * 128 + 65], in_=ot[oi][0:65, j * 128:(j + 1) * 128], identity=identf[0:65, 0:65]),
                       r=[b_ot[oi], b_idf], w=[bPF[4]])
                pv = PF[4][:, :].rearrange("p (j c) -> p j c", c=128)
                OP("dve", lambda e, pv=pv: e.reciprocal(out=rc4[:], in_=pv[:, :, 64:65]), r=[bPF[4]], w=[b_rc4])
                OP("dve", lambda e, pv=pv, h=h, ab=ab: e.tensor_tensor(out=attb[ab][:, :, h * 64:(h + 1) * 64], in0=pv[:, :, 0:64], in1=rc4[:].to_broadcast([128, 4, 64]), op=ALU.mult),
                   r=[bPF[4], b_rc4], w=[b_attb[ab]])
            for j in range(4):
                r0 = g * 512 + j * 128
                kb.dma("sp", lambda e, ab=ab, j=j, r0=r0: e.dma_start(out=mix[r0:r0 + 128, 0:512], in_=attb[ab][:, j, :]), reads=[b_attb[ab]], writes=[b_mix])
        if dbg == "att":
            def dump(name, shape, dt, src, bufs):
                o = dout("d_" + name, shape, dt)
                bb = B("dmp_" + name)
                kb.dma("sp", lambda e: e.dma_start(out=o, in_=src), reads=bufs, writes=[bb])
            dump("qTa0", [81, NM], BF16, qTa[0][:, :], b_qTa[0] + b_qTaP[0])
            dump("kTa0", [81, 4096], BF16, kTa[0][:, :], b_kTa[0] + [b_kTaI[0]])
            dump("T0", [128, 128], F32, Tt[0][:, 0, :], [b_Tt[0][0]])
            dump("T128", [128, 128], F32, Tt[128][:, 0, :], [b_Tt[128][0]])
            dump("Va16", [128, 8 * 65], BF16, Va[:, 16, :, :].rearrange("p h d -> p (h d)"), [b_Va[16], b_Va1])
            dump("kmf", [64, 128], F32, kmf[:].rearrange("p h n -> p (h n)"), [b_kmf])
            dump("selW", [128, 4 * 8 * 81], F32, selW[:].rearrange("p a b c -> p (a b c)"), b_selW)
        kb.barrier()
        p1b.close()
        p1.close()
        if stop_after == "att":
            kb.final_wait("sp")
            kb.emit()
            return nc, dbg_outs

        ps_ = contextlib.ExitStack()
        ptb = sbt("ptb", [128, 256], I32, ps_); b_ptb = B("ptb")
        DMA("sp", lambda e: e.dma_start(out=ptb[:], in_=pt[0:1, :].partition_broadcast(128)), writes=[b_ptb])
        pio = sbt("pio", [128, 1], I32, ps_); b_pio = B("pio")
        OP("pool", lambda e: e.iota(out=pio[:], pattern=[[0, 1]], base=0, channel_multiplier=1), w=[b_pio])
        piof = sbt("piof", [128, 1], F32, ps_); b_piof = B("piof")
        OP("dve", lambda e: e.tensor_copy(out=piof[:], in_=pio[:]), r=[b_pio], w=[b_piof])
        ptf = sbt("ptf", [128, 256], F32, ps_); b_ptf = B("ptf")
        OP("dve", lambda e: e.tensor_copy(out=ptf[:], in_=ptb[:]), r=[b_ptb], w=[b_ptf])
        ridx = sbt("ridx", [128, 256], I32, ps_); b_ridx = B("ridx")
        OP("dve", lambda e: e.tensor_scalar(out=ridx[:], in0=ptf[:], scalar1=128.0, scalar2=piof[:, 0:1], op0=ALU.mult, op1=ALU.add), r=[b_ptf, b_piof], w=[b_ridx])
        onesb = sbt("onesb", [128, 1], BF16, ps_); b_onesb = B("onesb")
        OP("pool", lambda e: e.memset(onesb[:], 1.0), w=[b_onesb])
        ohS = sbt("ohS", [33, 33, 128], BF16, ps_); b_ohS = B("ohS")
        OP("pool", lambda e: e.memset(ohS[:], 1.0), w=[b_ohS])
        OP("pool", lambda e: e.affine_select(out=ohS[:], in_=ohS[:], pattern=[[1, 33], [0, 128]], compare_op=ALU.is_equal, fill=0.0, base=0, channel_multiplier=-1),
           r=[b_ohS], w=[b_ohS])
        rbcol = sbt("rbcol", [64, 1], F32, ps_); b_rbcol = B("rbcol")
        for h in range(8):
            DMA("sp", lambda e, h=h: e.dma_start(out=rbcol[h * 8:(h + 1) * 8, :], in_=rb[0:1, 248 + h:248 + h + 1].partition_broadcast(8)), writes=[b_rbcol])
        OP("dve", lambda e: e.tensor_scalar(out=rbcol[:], in0=rbcol[:], scalar1=8.0, scalar2=None, op0=ALU.mult), r=[b_rbcol], w=[b_rbcol])
        kTsp2 = [sbt(f"kTsp{i}", [128, 4, 8192], BF16, ps_) for i in range(2)]; b_kTsp2 = [B(f"kTsp{i}") for i in range(2)]
        kpf = [sbt(f"kpf{i}", [128, 2, 512], F32, ps_) for i in range(4)]; b_kpf = [B(f"kpf{i}") for i in range(4)]
        kpb = [sbt(f"kpb{i}", [128, 2, 512], BF16, ps_) for i in range(4)]; b_kpb = [B(f"kpb{i}") for i in range(4)]
        vpf = [sbt(f"vpf{i}", [128, 512], F32, ps_) for i in range(4)]; b_vpf = [B(f"vpf{i}") for i in range(4)]
        vpb = [sbt(f"vpb{i}", [128, 512], BF16, ps_) for i in range(4)]; b_vpb = [B(f"vpb{i}") for i in range(4)]
        kms = sbt("kms", [128, 4, 32], BF16, ps_); b_kms = B("kms")
        kmsf = sbt("kmsf", [128, 4, 32], F32, ps_); b_kmsf = B("kmsf")
        Qbd = sbt("Qbd", [128, 4, 64], BF16, ps_); b_Qbd = B("Qbd")
        scs = sbt("scs", [64, 32], F32, ps_); b_scs = B("scs")
        top8s = sbt("top8s", [64, 8], F32, ps_); b_top8s = B("top8s")
        penF = sbt("penF", [64, 33], F32, ps_); b_penF = B("penF")
        penTb = sbt("penTb", [33, 64], BF16, ps_); b_penTb = B("penTb")
        PTs = [sbt(f"PTs{i}", [128, 64], BF16, ps_) for i in range(2)]; b_PTs = [B(f"PTs{i}") for i in range(2)]
        tmS = sbt("tmS", [128, 64], F32, ps_); b_tmS = B("tmS")
        osb = sbt("osb", [64, 512], BF16, ps_); b_osb = B("osb")
        recs = sbt("recs", [64, 1], F32, ps_); b_recs = B("recs")
        b_mix3 = B("mix3")
        xc_ = {"n": 0}
        def k_phase(sbi):
            kTsp = kTsp2[sbi % 2]; b_kTsp = b_kTsp2[sbi % 2]
            for pr in range(32):
                bi = pr % 4
                for a in range(2):
                    col = sbi * 64 + pr * 2 + a
                    DMA("pool", lambda e, bi=bi, a=a, col=col: e.indirect_dma_start(out=kpf[bi][:, a, :], out_offset=None, in_=ck[:, :],
                                                                                      in_offset=bass.IndirectOffsetOnAxis(ap=ridx[:, col:col + 1], axis=0)),
                           reads=[b_ridx], writes=[b_kpf[bi]])
                OP("dve", lambda e, bi=bi: e.tensor_copy(out=kpb[bi][:], in_=kpf[bi][:]), r=[b_kpf[bi]], w=[b_kpb[bi]])
                for a in range(2):
                    ti_ = xc_["n"] % 2
                    xc_["n"] += 1
                    pg = pr * 2 + a
                    for ch in range(4):
                        OP("pe", lambda e, ti_=ti_, bi=bi, a=a, ch=ch: e.transpose(out=PT[ti_][:, ch, :], in_=kpb[bi][:, a, ch * 128:(ch + 1) * 128], identity=ident[:]),
                           r=[b_kpb[bi], b_id], w=[bPT[ti_]])
                    OP("act", lambda e, ti_=ti_, pg=pg: e.copy(out=kTsp[:, :, pg * 128:(pg + 1) * 128], in_=PT[ti_][:, 0:4, :]), r=[bPT[ti_]], w=[b_kTsp])
            for ch in range(4):
                OP("dve", lambda e, ch=ch: e.tensor_reduce(out=kmsf[:, ch, :], in_=kTsp[:, ch, :].rearrange("p (n k) -> p n k", k=256), axis=AX.X, op=ALU.add), r=[b_kTsp], w=[b_kmsf])
            OP("dve", lambda e: e.tensor_copy(out=kms[:], in_=kmsf[:]), r=[b_kmsf], w=[b_kms])

        TQ["on"] = False
        k_phase(0)
        for sbi in range(4):
            kTsp = kTsp2[sbi % 2]; b_kTsp = b_kTsp2[sbi % 2]
            if sbi + 1 < 4:
                TQ["on"] = True
                k_phase(sbi + 1)
                TQ["on"] = False
            kq_slice = (len(TQ["q"]) + 64) // 65
            OP("pool", lambda e: e.memset(Qbd[:], 0.0), w=[b_Qbd])
            for h in range(8):
                ch, hh = h // 2, h % 2
                OP("dve", lambda e, h=h, ch=ch, hh=hh, sbi=sbi: e.tensor_copy(out=Qbd[hh * 64:(hh + 1) * 64, ch, h * 8:(h + 1) * 8], in_=qTs[hh * 64:(hh + 1) * 64, ch, sbi * 8:(sbi + 1) * 8]),
                   r=[b_qTs], w=[b_Qbd])
            for ch in range(4):
                OP("pe", lambda e, ch=ch: e.matmul(PF[4][0:64, 0:32], lhsT=Qbd[:, ch, :], rhs=kms[:, ch, :], start=(ch == 0), stop=(ch == 3)), r=[b_Qbd, b_kms], w=[bPF[4]])
            OP("dve", lambda e: e.tensor_copy(out=scs[:], in_=PF[4][0:64, 0:32]), r=[bPF[4]], w=[b_scs])
            OP("dve", lambda e: e.max(out=top8s[:], in_=scs[:]), r=[b_scs], w=[b_top8s])
            OP("dve", lambda e: e.tensor_scalar(out=penF[:, 0:32], in0=scs[:], scalar1=top8s[:, 2:3], scalar2=NEG, op0=ALU.is_lt, op1=ALU.mult), r=[b_scs, b_top8s], w=[b_penF])
            OP("pool", lambda e: e.memset(penF[:, 32:33], 0.0), w=[b_penF])
            OP("dve", lambda e: e.tensor_scalar(out=penF[:], in0=penF[:], scalar1=rbcol[:, 0:1], scalar2=None, op0=ALU.add), r=[b_penF, b_rbcol], w=[b_penF])
            OP("pe", lambda e: e.transpose(out=PF[4][0:33, 64:128], in_=penF[:], identity=identf[0:64, 0:64]), r=[b_penF, b_idf], w=[bPF[4]])
            OP("act", lambda e: e.copy(out=penTb[:], in_=PF[4][0:33, 64:128]), r=[bPF[4]], w=[b_penTb])
            def s_load(kt, sbi=sbi):
                if kt >= 64:
                    return
                bi = kt % 4
                col = sbi * 64 + kt
                DMA("pool", lambda e, bi=bi, col=col: e.indirect_dma_start(out=vpf[bi][:, :], out_offset=None, in_=cv[:, :],
                                                                              in_offset=bass.IndirectOffsetOnAxis(ap=ridx[:, col:col + 1], axis=0)),
                       reads=[b_ridx], writes=[b_vpf[bi]])
                OP("dve", lambda e, bi=bi: e.tensor_copy(out=vpb[bi][:, :], in_=vpf[bi][:, :]), r=[b_vpf[bi]], w=[b_vpb[bi]])

            def s_S(kt, sbi=sbi, kTsp=kTsp, b_kTsp=b_kTsp):
                own = kt == 64
                si = kt % 2
                L = 8 if own else 128
                n = 32 if own else kt // 2
                for ch in range(4):
                    lhs = kTs_own[:, ch, sbi * 8:(sbi + 1) * 8] if own else kTsp[:, ch, kt * 128:(kt + 1) * 128]
                    OP("pe", lambda e, si=si, ch=ch, lhs=lhs, L=L: e.matmul(PF[si][0:L, 0:64], lhsT=lhs, rhs=Qbd[:, ch, :], start=(ch == 0), stop=False),
                       r=[b_kTsp, b_kTs_own, b_Qbd], w=[bPF[si]])
                OP("pe", lambda e, si=si, n=n, L=L: e.matmul(PF[si][0:L, 0:64], lhsT=ohS[:, n, 0:L], rhs=penTb[:], start=False, stop=True), r=[b_ohS, b_penTb], w=[bPF[si]])

            def s_exp(kt):
                own = kt == 64
                si = kt % 2
                L = 8 if own else 128
                if kt >= 63:
                    Tm = TS0[:].rearrange("p h q -> p (h q)") if own else TS128[:].rearrange("p h q -> p (h q)")
                    OP("dve", lambda e, si=si, L=L, Tm=Tm: e.scalar_tensor_tensor(out=tmS[0:L, :], in0=PF[si][0:L, 0:64], scalar=0.125, in1=Tm, op0=ALU.mult, op1=ALU.add),
                       r=[bPF[si], b_TS0, b_TS128], w=[b_tmS])
                    OP("act", lambda e, si=si, L=L: e.activation(out=PTs[si][0:L, :], in_=tmS[0:L, :], func=AF.Exp), r=[b_tmS], w=[b_PTs[si]])
                else:
                    OP("act", lambda e, si=si: e.activation(out=PTs[si][:], in_=PF[si][:, 0:64], func=AF.Exp, scale=0.125), r=[bPF[si]], w=[b_PTs[si]])

            def s_PV(kt, sbi=sbi):
                own = kt == 64
                si = kt % 2
                L = 8 if own else 128
                rhsv = Vs_own[:, sbi, :] if own else vpb[kt % 4][:, :]
                OP("pe", lambda e, si=si, L=L, rhsv=rhsv, kt=kt: e.matmul(PF[2][0:64, :], lhsT=PTs[si][0:L, :], rhs=rhsv, start=(kt == 0), stop=(kt == 64)),
                   r=[b_PTs[si], b_Vs_own, b_vpb[kt % 4]], w=[bPF[2]])
                OP("pe", lambda e, si=si, L=L, kt=kt: e.matmul(PF[3][0:64, 0:1], lhsT=PTs[si][0:L, :], rhs=onesb[0:L, 0:1], start=(kt == 0), stop=(kt == 64)),
                   r=[b_PTs[si], b_onesb], w=[bPF[3]])

            s_load(0); s_load(1); s_load(2)
            s_S(0)
            for kt in range(65):
                flush_tq(kq_slice)
                s_load(kt + 3)
                if kt + 1 < 65:
                    s_S(kt + 1)
                s_exp(kt)
                s_PV(kt)
            flush_tq(10 ** 9)
            OP("dve", lambda e: e.reciprocal(out=recs[:], in_=PF[3][0:64, 0:1]), r=[bPF[3]], w=[b_recs])
            OP("dve", lambda e: e.tensor_scalar(out=osb[:], in0=PF[2][0:64, :], scalar1=recs[:, 0:1], scalar2=None, op0=ALU.mult), r=[bPF[2], b_recs], w=[b_osb])
            for h in range(8):
                DMA("sp", lambda e, h=h, sbi=sbi: e.dma_start(out=mix[NM + sbi * 8:NM + sbi * 8 + 8, h * 64:(h + 1) * 64], in_=osb[h * 8:(h + 1) * 8, h * 64:(h + 1) * 64]),
                       reads=[b_osb], writes=[b_mix3])
        kb.barrier()
        ps_.close()
        if stop_after == "satt":
            kb.final_wait("sp")
            kb.emit()
            return nc, dbg_outs

        b_mix2 = B("mix2")
        NT = 4096 + NS
        def ml_pass(hp):
            h0 = 2 * hp
            p2 = contextlib.ExitStack()
            mqT = sbt(f"h{hp}_" "mqT", [128, 2, NM + NS], BF16, p2); b_mqT = B("mqT")
            mkT = sbt(f"h{hp}_" "mkT", [128, 2, NM + NS], BF16, p2); b_mkT = B("mkT")
            mkt = sbt(f"h{hp}_" "mkt", [128, 32, 256], BF16, p2); b_mkt = B("mkt")
            mva = sbt(f"h{hp}_" "mva", [128, 32, 2, 129], BF16, p2); b_mva = B("mva")
            sgm = sbt(f"h{hp}_" "sgm", [128, 16, 256], BF16, p2); b_sgm = B("sgm")
            mkt_s = sbt(f"h{hp}_" "mkt_s", [8, 4, 256], BF16, p2); b_mkt_s = B("mkt_s")
            mva_s = sbt(f"h{hp}_" "mva_s", [8, 4, 2, 129], BF16, p2); b_mva_s = B("mva_s")
            sgm_s = sbt(f"h{hp}_" "sgm_s", [8, 4, 256], BF16, p2); b_sgm_s = B("sgm_s")
            OP("pool", lambda e: e.memset(mva[:, :, :, 128:129], 1.0), w=[b_mva])
            OP("pool", lambda e: e.memset(mva_s[:, :, :, 128:129], 1.0), w=[b_mva_s])
            Grow = sbt(f"h{hp}_" "Grow", [2, NT], F32, p2); b_Grow = B("Grow")
            Urow = sbt(f"h{hp}_" "Urow", [2, NT], F32, p2); b_Urow = B("Urow")
            Brow = sbt(f"h{hp}_" "Brow", [2, NT], F32, p2); b_Brow = B("Brow")
            mlgb = sbt(f"h{hp}_" "mlgb", [128, 256], F32, p2); b_mlgb = B("mlgb")
            kb.dma("sp", lambda e, h0=h0: e.dma_start(out=mlgb[:], in_=mlg[0:1, h0 * 128:h0 * 128 + 256].partition_broadcast(128)), writes=[b_mlgb])
            bgt = sbt(f"h{hp}_" "bgt", [2, 2], F32, p2); b_bgt = B("bgt")
            kb.dma("sp", lambda e, h0=h0: e.dma_start(out=bgt[:, 0:1], in_=bg[h0:h0 + 2, :]), writes=[b_bgt])
            kb.dma("sp", lambda e, h0=h0: e.dma_start(out=bgt[:, 1:2], in_=bg[4 + h0:4 + h0 + 2, :]), writes=[b_bgt])
            sm0 = sbt(f"h{hp}_" "sm0", [2, 4], F32, p2); b_sm0 = B("sm0")
            kb.dma("sp", lambda e, h0=h0: e.dma_start(out=sm0[:], in_=smi[h0:h0 + 2, :]), writes=[b_sm0])
            ones2 = sbt(f"h{hp}_" "ones2", [2, 512], F32, p2); b_ones2 = B("ones2")
            OP("pool", lambda e: e.memset(ones2[:], 1.0), w=[b_ones2])
            oh2 = sbt(f"h{hp}_" "oh2", [2, 2, 128], F32, p2); b_oh2 = B("oh2")
            OP("pool", lambda e: e.memset(oh2[:], 1.0), w=[b_oh2])
            OP("pool", lambda e: e.affine_select(out=oh2[:], in_=oh2[:], pattern=[[1, 2], [0, 128]], compare_op=ALU.is_equal, fill=0.0, base=0,
                                                 channel_multiplier=-1), r=[b_oh2], w=[b_oh2])
            cmask = sbt(f"h{hp}_" "cmask", [128, 128], F32, p2); b_cmask = B("cmask")
            OP("pool", lambda e: e.memset(cmask[:], 0.0), w=[b_cmask])
            OP("pool", lambda e: e.affine_select(out=cmask[:], in_=cmask[:], pattern=[[1, 128]], compare_op=ALU.is_ge, fill=-NEG, base=0,
                                                 channel_multiplier=-1), r=[b_cmask], w=[b_cmask])
            UT = sbt(f"h{hp}_" "UT", [128, 33, 2], F32, p2); GT = sbt(f"h{hp}_" "GT", [128, 33, 2], F32, p2); mT = sbt(f"h{hp}_" "mT", [128, 33, 2], F32, p2); EM = sbt(f"h{hp}_" "EM", [128, 33, 2], F32, p2)
            b_UT = B("UT"); b_GT = B("GT"); b_mT = B("mT"); b_EM = B("EM")
            UTs = sbt(f"h{hp}_" "UTs", [8, 4, 2], F32, p2); GTs = sbt(f"h{hp}_" "GTs", [8, 4, 2], F32, p2); mTs = sbt(f"h{hp}_" "mTs", [8, 4, 2], F32, p2); EMs = sbt(f"h{hp}_" "EMs", [8, 4, 2], F32, p2)
            b_UTs = B("UTs"); b_GTs = B("GTs"); b_mTs = B("mTs"); b_EMs = B("EMs")

            p2a = contextlib.ExitStack()
            wB = sbt(f"h{hp}_" "wB", [128, 8, 1032], BF16, p2a); b_wB = B("wB")
            for c in range(8):
                rows = slice(c * 128, (c + 1) * 128)
                for k4 in range(4):
                    kb.dma("pool", lambda e, c=c, k4=k4, rows=rows, h0=h0: e.dma_start(out=wB[:, c, k4 * 256:(k4 + 1) * 256],
                                                                                    in_=w_in[rows, 1536 + k4 * 512 + h0 * 128:1536 + k4 * 512 + h0 * 128 + 256]), writes=[b_wB])
                kb.dma("pool", lambda e, c=c, rows=rows: e.dma_start(out=wB[:, c, 1024:1032], in_=w_in[rows, 3584:3592]), writes=[b_wB])
            xnT2l = [sbt(f"h{hp}_" f"xnT2_{i}", [128, 8, 512], BF16, p2a) for i in range(2)]; b_xnT2l = [B(f"xnT2_{i}") for i in range(2)]
            gcount = 0
            gTs = sbt(f"h{hp}_" "gTs", [8, 512], F32, p2a); b_gTs = B("gTs")
            sgt = sbt(f"h{hp}_" "sgt", [128, 256], F32, p2a); b_sgt = B("sgt")
            KS = 128.0 ** -0.5
            groups = [("ctx", g) for g in range(4)] + [("main", g) for g in range(4)] + [("smp", 0)]
            for kind, g in groups:
                ntok = NS if kind == "smp" else 512
                gidx = {"ctx": g, "main": 4 + g, "smp": 8}[kind]
                X2 = xnT2l[gcount % 2]; bX2 = b_xnT2l[gcount % 2]
                gcount += 1
                kb.dma("sp", lambda e, X2=X2, gidx=gidx, ntok=ntok: e.dma_start(out=X2[:, :, 0:ntok], in_=xnTd[gidx].rearrange("p (c t) -> p c t", t=512)[:, :, 0:ntok]),
                       reads=[b_xnTd], writes=[bX2])
                gcol0 = {"ctx": g * 512, "main": 2048 + g * 512, "smp": 4096}[kind]
                for c in range(8):
                    OP("pe", lambda e, X2=X2, c=c, ntok=ntok: e.matmul(PF[0][0:8, 0:ntok], lhsT=wB[:, c, 1024:1032], rhs=X2[:, c, 0:ntok], start=(c == 0), stop=(c == 7)),
                       r=[b_wB, bX2], w=[bPF[0]])
                OP("dve", lambda e, ntok=ntok: e.tensor_copy(out=gTs[:, 0:ntok], in_=PF[0][0:8, 0:ntok]), r=[bPF[0]], w=[b_gTs])
                kb.dma("sp", lambda e, ntok=ntok, gcol0=gcol0, h0=h0: e.dma_start(out=Urow[:, gcol0:gcol0 + ntok], in_=gTs[h0:h0 + 2, 0:ntok]), reads=[b_gTs], writes=[b_Urow])
                kb.dma("sp", lambda e, ntok=ntok, gcol0=gcol0, h0=h0: e.dma_start(out=Brow[:, gcol0:gcol0 + ntok], in_=gTs[4 + h0:4 + h0 + 2, 0:ntok]), reads=[b_gTs], writes=[b_Brow])
                if kind != "ctx":
                    fcol0 = g * 512 if kind == "main" else NM
                    for l in range(2):
                        for which in ("q", "k"):
                            wc0 = (0 if which == "q" else 256) + l * 128
                            pf = 1 + (which == "k")
                            for c in range(8):
                                OP("pe", lambda e, X2=X2, c=c, pf=pf, wc0=wc0, ntok=ntok: e.matmul(PF[pf][:, 0:ntok], lhsT=wB[:, c, wc0:wc0 + 128], rhs=X2[:, c, 0:ntok],
                                                                                          start=(c == 0), stop=(c == 7)), r=[b_wB, bX2], w=[bPF[pf]])
                            if which == "q":
                                evac(mqT[:, l, fcol0:fcol0 + ntok], PF[pf][:, 0:ntok], [bPF[pf]], [b_mqT])
                            else:
                                evac(mkT[:, l, fcol0:fcol0 + ntok], PF[pf][:, 0:ntok], [bPF[pf]], [b_mkT], scale=KS)
                if kind == "smp":
                    tiles = [(sbi * 8, 8, sbi) for sbi in range(4)]
                else:
                    tiles = [(t * 128, 128, (g if kind == "ctx" else 4 + g) * 4 + t) for t in range(4)]
                for (c0, n, ta) in tiles:
                    for c in range(8):
                        OP("pe", lambda e, X2=X2, c=c, c0=c0, n=n: e.matmul(PF[3][0:n, :], lhsT=X2[:, c, c0:c0 + n], rhs=wB[:, c, 256:768], start=(c == 0), stop=(c == 7)),
                           r=[b_wB, bX2], w=[bPF[3]])
                    if kind == "smp":
                        kdst, b_kd = mkt_s[:, ta, :], b_mkt_s
                        vdst, b_vd = mva_s[:, ta, :, 0:128], b_mva_s
                    else:
                        kdst, b_kd = mkt[:, ta, :], b_mkt
                        vdst, b_vd = mva[:, ta, :, 0:128], b_mva
                    OP("dve", lambda e, n=n, kdst=kdst: e.tensor_scalar(out=kdst, in0=PF[3][0:n, 0:256], scalar1=KS, scalar2=None, op0=ALU.mult), r=[bPF[3]], w=[b_kd])
                    OP("dve", lambda e, n=n, vdst=vdst: e.tensor_copy(out=vdst, in_=PF[3][0:n, 256:512].rearrange("p (l d) -> p l d", d=128)), r=[bPF[3]], w=[b_vd])
                    if kind != "ctx":
                        for c in range(8):
                            OP("pe", lambda e, X2=X2, c=c, c0=c0, n=n: e.matmul(PF[4][0:n, 0:256], lhsT=X2[:, c, c0:c0 + n], rhs=wB[:, c, 768:1024], start=(c == 0), stop=(c == 7)),
                               r=[b_wB, bX2], w=[bPF[4]])
                        OP("dve", lambda e, n=n: e.tensor_copy(out=sgt[0:n, :], in_=PF[4][0:n, 0:256]), r=[bPF[4]], w=[b_sgt])
                        OP("act", lambda e, n=n: e.activation(out=sgt[0:n, :], in_=sgt[0:n, :], func=AF.Sigmoid), r=[b_sgt], w=[b_sgt])
                        if kind == "smp":
                            sdst, b_sd = sgm_s[:, ta, :], b_sgm_s
                        else:
                            sdst, b_sd = sgm[:, ta - 16, :], b_sgm
                        OP("dve", lambda e, n=n, sdst=sdst: e.tensor_tensor(out=sdst, in0=sgt[0:n, :], in1=mlgb[0:n, :], op=ALU.mult), r=[b_sgt, b_mlgb], w=[b_sd])
            kb.barrier()
            p2a.close()
            if stop_after == "ml_a":
                kb.final_wait("sp"); kb.emit(); p2.close(); raise StopIteration

            nbf = sbt(f"h{hp}_" "nbf", [2, 1], F32, p2); b_nbf = B("nbf")
            OP("dve", lambda e: e.tensor_scalar(out=nbf[:], in0=bgt[:, 1:2], scalar1=-1.0, scalar2=None, op0=ALU.mult), r=[b_bgt], w=[b_nbf])
            OP("act", lambda e: e.activation(out=Brow[:], in_=Brow[:], func=AF.Exp, scale=-1.0, bias=nbf[:, 0:1]), r=[b_Brow, b_nbf], w=[b_Brow])
            OP("act", lambda e: e.activation(out=Brow[:], in_=Brow[:], func=AF.Ln, scale=1.0, bias=1.0), r=[b_Brow], w=[b_Brow])
            OP("dve", lambda e: e.tensor_scalar(out=Brow[:], in0=Brow[:], scalar1=-1.0, scalar2=None, op0=ALU.mult), r=[b_Brow], w=[b_Brow])
            OP("dve", lambda e: e.tensor_scalar(out=Urow[:], in0=Urow[:], scalar1=bgt[:, 0:1], scalar2=None, op0=ALU.add), r=[b_Urow, b_bgt], w=[b_Urow])
            OP("dve", lambda e: e.tensor_scalar(out=Brow[:, 0:2048], in0=Brow[:, 0:2048], scalar1=cfb[0:2, 0:1], scalar2=None, op0=ALU.mult), r=[b_Brow, b_cfb], w=[b_Brow])
            OP("dve", lambda e: e.tensor_scalar(out=Urow[:, 0:2048], in0=Urow[:, 0:2048], scalar1=cfb[0:2, 0:1], scalar2=cfb[0:2, 1:2], op0=ALU.mult, op1=ALU.add),
               r=[b_Urow, b_cfb], w=[b_Urow])
            segs = [(i * 512, 512, None if i == 0 else i * 512 - 1) for i in range(8)] + [(4096 + sbi * 8, 8, None) for sbi in range(4)]
            for (c0, n, prev) in segs:
                init = 0.0 if prev is None else Brow[:, prev:prev + 1]
                OP("dve", lambda e, c0=c0, n=n, init=init: e.tensor_tensor_scan(out=Brow[:, c0:c0 + n], data0=ones2[:, 0:n], data1=Brow[:, c0:c0 + n], initial=init,
                                                                                op0=ALU.mult, op1=ALU.add), r=[b_Brow, b_ones2], w=[b_Brow])
            OP("dve", lambda e: e.tensor_tensor(out=Urow[:], in0=Urow[:], in1=Brow[:], op=ALU.subtract), r=[b_Urow, b_Brow], w=[b_Urow])
            for si_, (c0, n, prev) in enumerate(segs):
                if c0 >= 4096:
                    sbi = (c0 - 4096) // 8
                    init = sm0[:, sbi:sbi + 1]
                else:
                    init = 0.0 if prev is None else Grow[:, prev:prev + 1]
                OP("dve", lambda e, c0=c0, n=n, init=init: e.tensor_tensor_scan(out=Grow[:, c0:c0 + n], data0=Urow[:, c0:c0 + n], data1=Urow[:, c0:c0 + n], initial=init,
                                                                                op0=ALU.max, op1=ALU.max), r=[b_Urow, b_Grow, b_sm0], w=[b_Grow])
            OP("dve", lambda e: e.tensor_tensor(out=Brow[:], in0=Brow[:], in1=Grow[:], op=ALU.add), r=[b_Grow, b_Brow], w=[b_Brow])
            for (row, b_row, colt, b_colt, colts, b_colts) in ((Urow, b_Urow, UT, b_UT, UTs, b_UTs), (Grow, b_Grow, GT, b_GT, GTs, b_GTs), (Brow, b_Brow, mT, b_mT, mTs, b_mTs)):
                for ck in range(32):
                    OP("pe", lambda e, ck=ck, row=row: e.transpose(out=PF[5][:, ck * 2:ck * 2 + 2], in_=row[:, ck * 128:(ck + 1) * 128], identity=identf[0:2, 0:2]),
                       r=[b_row, b_idf], w=[bPF[5]])
                OP("dve", lambda e, colt=colt: e.tensor_copy(out=colt[:, 0:32, :], in_=PF[5][:, 0:64].rearrange("p (c l) -> p c l", l=2)), r=[bPF[5]], w=[b_colt])
                for sbi in range(4):
                    OP("pe", lambda e, sbi=sbi, row=row: e.transpose(out=PF[5][0:8, sbi * 2:sbi * 2 + 2], in_=row[:, 4096 + sbi * 8:4096 + sbi * 8 + 8], identity=identf[0:2, 0:2]),
                       r=[b_row, b_idf], w=[bPF[5]])
                OP("dve", lambda e, colts=colts: e.tensor_copy(out=colts[:], in_=PF[5][0:8, 0:8].rearrange("p (c l) -> p c l", l=2)), r=[bPF[5]], w=[b_colts])
            OP("act", lambda e: e.activation(out=EM[:, 0:32, :], in_=mT[:, 0:32, :], func=AF.Exp, scale=-1.0), r=[b_mT], w=[b_EM])
            OP("act", lambda e: e.activation(out=EMs[:], in_=mTs[:], func=AF.Exp, scale=-1.0), r=[b_mTs], w=[b_EMs])

            if stop_after == "ml_b":
                kb.final_wait("sp"); kb.emit(); p2.close(); raise StopIteration
            Cf = sbt(f"h{hp}_" "Cf", [128, 2, 129], F32, p2); b_Cf = B("Cf")
            Cb = sbt(f"h{hp}_" "Cb", [128, 2, 129], BF16, p2); b_Cb = B("Cb")
            gprev = sbt(f"h{hp}_" "gprev", [128, 2], F32, p2); b_gprev = B("gprev")
            gend = [sbt(f"h{hp}_" f"gend{i}", [128, 2], F32, p2) for i in range(2)]; b_gend = [B(f"gend{i}") for i in range(2)]
            g2 = sbt(f"h{hp}_" "g2", [128, 2], F32, p2); b_g2 = B("g2")
            gtok = sbt(f"h{hp}_" "gtok", [128, 2], F32, p2); b_gtok = B("gtok")
            gst = sbt(f"h{hp}_" "gst", [128, 2], F32, p2); b_gst = B("gst")
            wst = sbt(f"h{hp}_" "wst", [128, 2], F32, p2); b_wst = B("wst")
            tmpD = sbt(f"h{hp}_" "tmpD", [128, 2, 128], F32, p2); b_tmpD = B("tmpD")
            sT = sbt(f"h{hp}_" "sT", [128, 2, 128], BF16, p2); b_sT = B("sT")
            hs = [sbt(f"h{hp}_" f"hs{i}", [128, 2, 129], F32, p2) for i in range(2)]; b_hs = [B(f"hs{i}") for i in range(2)]
            nd = sbt(f"h{hp}_" "nd", [128, 2, 129], F32, p2); b_nd = B("nd")
            dab = sbt(f"h{hp}_" "dab", [128, 2], F32, p2); b_dab = B("dab")
            ssh = sbt(f"h{hp}_" "ssh", [128, 2], F32, p2); b_ssh = B("ssh")
            junk = sbt(f"h{hp}_" "junk", [128, 128], F32, p2); b_junk = B("junk")
            gv = sbt(f"h{hp}_" "gv", [128, 2, 129], BF16, p2); b_gv = B("gv")
            mlb = [sbt(f"h{hp}_" f"mlb{i}", [128, 256], BF16, p2) for i in range(2)]; b_mlb = [B(f"mlb{i}") for i in range(2)]

            def chunkA(L, ck_cols, UTv, GTv, EMv, kT_v, qT_v, kt_v, va_v, sg_v, full, mix_rows, mi, gcol):
                for l in range(2):
                    OP("pe", lambda e, l=l: e.matmul(PF[4][0:L, l * 128:l * 128 + L], lhsT=oh2[:, l, 0:L], rhs=Grow[:, ck_cols], start=True, stop=True),
                       r=[b_oh2, b_Grow], w=[bPF[4]])
                OP("dve", lambda e: e.tensor_copy(out=gend[mi][0:L, :].rearrange("p (l o) -> p l o", o=1), in_=PF[4][0:L, 0:256].rearrange("p (l t) -> p l t", t=128)[:, :, L - 1:L]),
                   r=[bPF[4]], w=[b_gend[mi]])
                if not full:
                    return
                for l in range(2):
                    OP("pe", lambda e, l=l: e.matmul(PF[0][0:L, l * 128:l * 128 + L], lhsT=kT_v(l), rhs=qT_v(l), start=True, stop=True), r=[b_mkT, b_mqT], w=[bPF[0]])
                    OP("dve", lambda e, l=l: e.scalar_tensor_tensor(out=tmpD[0:L, l, 0:L], in0=PF[4][0:L, l * 128:l * 128 + L], scalar=UTv[:, l:l + 1], in1=cmask[0:L, 0:L],
                                                                    op0=ALU.subtract, op1=ALU.add), r=[bPF[4], b_UT, b_UTs, b_cmask], w=[b_tmpD])
                OP("act", lambda e: e.activation(out=tmpD[0:L, :, 0:L], in_=tmpD[0:L, :, 0:L], func=AF.Exp, scale=-1.0), r=[b_tmpD], w=[b_tmpD])
                OP("dve", lambda e: e.tensor_tensor(out=sT[0:L, :, 0:L], in0=PF[0][0:L, 0:256].rearrange("p (l t) -> p l t", t=128)[:, :, 0:L], in1=tmpD[0:L, :, 0:L], op=ALU.mult),
                   r=[bPF[0], b_tmpD], w=[b_sT])
                for l in range(2):
                    OP("pe", lambda e, l=l: e.matmul(PF[1][0:L, l * 129:(l + 1) * 129], lhsT=sT[0:L, l, 0:L], rhs=va_v(l), start=True, stop=True),
                       r=[b_sT, b_mva, b_mva_s], w=[bPF[1]])
                OP("dve", lambda e: e.tensor_copy(out=hs[mi][0:L, :, :], in_=PF[1][0:L, 0:258].rearrange("p (l c) -> p l c", c=129)), r=[bPF[1]], w=[b_hs[mi]])

            def chunkB(L, ck_cols, UTv, GTv, EMv, kT_v, qT_v, kt_v, va_v, sg_v, full, mix_rows, mi, gcol):
                if not full:
                    return
                for l in range(2):
                    OP("pe", lambda e, l=l: e.matmul(PF[2][0:L, l * 129:(l + 1) * 129], lhsT=qT_v(l), rhs=Cb[:, l, :], start=True, stop=True),
                       r=[b_mqT, b_Cb], w=[bPF[2]])
                OP("dve", lambda e: e.tensor_tensor(out=g2[0:L, :], in0=gprev[0:L, :], in1=GTv, op=ALU.subtract), r=[b_gprev, b_GT, b_GTs], w=[b_g2])
                OP("act", lambda e: e.activation(out=wst[0:L, :], in_=g2[0:L, :], func=AF.Exp), r=[b_g2], w=[b_wst])
                for l in range(2):
                    OP("dve", lambda e, l=l: e.scalar_tensor_tensor(out=nd[0:L, l, :], in0=PF[2][0:L, l * 129:(l + 1) * 129], scalar=wst[0:L, l:l + 1], in1=hs[mi][0:L, l, :],
                                                                    op0=ALU.mult, op1=ALU.add), r=[bPF[2], b_wst, b_hs[mi]], w=[b_nd])
                    OP("dve", lambda e, l=l: e.scalar_tensor_tensor(out=dab[0:L, l:l + 1], in0=nd[0:L, l, 128:129], scalar=-1.0, in1=nd[0:L, l, 128:129], op0=ALU.mult, op1=ALU.max),
                       r=[b_nd], w=[b_dab])
                    OP("dve", lambda e, l=l: e.tensor_scalar(out=dab[0:L, l:l + 1], in0=dab[0:L, l:l + 1], scalar1=EMv[:, l:l + 1], scalar2=None, op0=ALU.max),
                       r=[b_dab, b_EM, b_EMs], w=[b_dab])
                OP("dve", lambda e: e.reciprocal(out=dab[0:L, :], in_=dab[0:L, :]), r=[b_dab], w=[b_dab])
                for l in range(2):
                    OP("act", lambda e, l=l: e.activation(out=junk[0:L, :], in_=nd[0:L, l, 0:128], func=AF.Square, scale=dab[0:L, l:l + 1], accum_out=ssh[0:L, l:l + 1]),
                       r=[b_nd, b_dab], w=[b_junk, b_ssh])
                OP("act", lambda e: e.activation(out=ssh[0:L, :], in_=ssh[0:L, :], func=AF.Sqrt, scale=1.0 / 128, bias=EPS), r=[b_ssh], w=[b_ssh])
                OP("dve", lambda e: e.reciprocal(out=ssh[0:L, :], in_=ssh[0:L, :]), r=[b_ssh], w=[b_ssh])
                OP("dve", lambda e: e.tensor_tensor(out=ssh[0:L, :], in0=ssh[0:L, :], in1=dab[0:L, :], op=ALU.mult), r=[b_ssh, b_dab], w=[b_ssh])
                for l in range(2):
                    OP("dve", lambda e, l=l: e.scalar_tensor_tensor(out=mlb[mi][0:L, l * 128:(l + 1) * 128], in0=nd[0:L, l, 0:128], scalar=ssh[0:L, l:l + 1], in1=sg_v(l),
                                                                    op0=ALU.mult, op1=ALU.mult), r=[b_nd, b_ssh, b_sgm, b_sgm_s], w=[b_mlb[mi]])
                kb.dma("sp", lambda e: e.dma_start(out=mix[mix_rows, 512 + h0 * 128:512 + h0 * 128 + 256], in_=mlb[mi][0:L, :]), reads=[b_mlb[mi]], writes=[b_mix2])

            def chunkC(L, ck_cols, UTv, GTv, EMv, kT_v, qT_v, kt_v, va_v, sg_v, full, mix_rows, mi, gcol):
                if L == 128:
                    gb, bgb = gend[mi], b_gend[mi]
                else:
                    gend_bcast(gcol)
                    gb, bgb = gendb, b_gendb
                OP("dve", lambda e: e.tensor_tensor(out=gtok[0:L, :], in0=UTv, in1=gend[mi][0:L, :], op=ALU.subtract), r=[b_UT, b_UTs, b_gend[mi]], w=[b_gtok])
                OP("act", lambda e: e.activation(out=gtok[0:L, :], in_=gtok[0:L, :], func=AF.Exp), r=[b_gtok], w=[b_gtok])
                for l in range(2):
                    OP("dve", lambda e, l=l: e.tensor_scalar(out=gv[0:L, l, :], in0=va_v(l), scalar1=gtok[0:L, l:l + 1], scalar2=None, op0=ALU.mult),
                       r=[b_mva, b_mva_s, b_gtok], w=[b_gv])
                    OP("pe", lambda e, l=l: e.matmul(PF[3][:, l * 129:(l + 1) * 129], lhsT=kt_v(l), rhs=gv[0:L, l, :], start=True, stop=True), r=[b_mkt, b_mkt_s, b_gv], w=[bPF[3]])
                OP("dve", lambda e: e.tensor_tensor(out=g2[:, :], in0=gprev[:, :], in1=gb[:, :], op=ALU.subtract), r=[b_gprev, bgb], w=[b_g2])
                OP("act", lambda e: e.activation(out=gst[:, :], in_=g2[:, :], func=AF.Exp), r=[b_g2], w=[b_gst])
                for l in range(2):
                    OP("dve", lambda e, l=l: e.scalar_tensor_tensor(out=Cf[:, l, :], in0=Cf[:, l, :], scalar=gst[:, l:l + 1], in1=PF[3][:, l * 129:(l + 1) * 129],
                                                                    op0=ALU.mult, op1=ALU.add), r=[b_Cf, b_gst, bPF[3]], w=[b_Cf])
                OP("act", lambda e: e.copy(out=Cb[:], in_=Cf[:]), r=[b_Cf], w=[b_Cb])
                OP("dve", lambda e: e.tensor_copy(out=gprev[:], in_=gb[:]), r=[bgb], w=[b_gprev])

            gendb = sbt(f"h{hp}_" "gendb", [128, 2], F32, p2); b_gendb = B("gendb")

            def gend_bcast(col):
                for l in range(2):
                    OP("pe", lambda e, l=l: e.matmul(PF[5][:, l:l + 1], lhsT=oh2[:, l, :], rhs=Grow[:, col:col + 1], start=True, stop=True), r=[b_oh2, b_Grow], w=[bPF[5]])
                OP("dve", lambda e: e.tensor_copy(out=gendb[:], in_=PF[5][:, 0:2]), r=[bPF[5]], w=[b_gendb])

            OP("pool", lambda e: e.memset(Cf[:], 0.0), w=[b_Cf])
            OP("pool", lambda e: e.memset(Cb[:], 0.0), w=[b_Cb])
            OP("pool", lambda e: e.memset(gprev[:], 0.0), w=[b_gprev])
            def pargs(ck):
                full = ck >= 16
                tm = ck - 16
                cols = slice(ck * 128, (ck + 1) * 128)
                fc = slice(tm * 128, (tm + 1) * 128)
                return (128, cols, UT[:, ck, :], GT[:, ck, :], EM[:, ck, :],
                        lambda l, fc=fc: mkT[:, l, fc], lambda l, fc=fc: mqT[:, l, fc], lambda l, ck=ck: mkt[:, ck, l * 128:(l + 1) * 128],
                        lambda l, ck=ck: mva[:, ck, l, :], lambda l, tm=tm: sgm[:, tm, l * 128:(l + 1) * 128], full,
                        slice(tm * 128, (tm + 1) * 128), ck % 2, ck * 128 + 127)
            chunkA(*pargs(0))
            for ck in range(32):
                if ck + 1 < 32:
                    chunkA(*pargs(ck + 1))
                chunkB(*pargs(ck))
                chunkC(*pargs(ck))
            if stop_after == "ml_c":
                kb.final_wait("sp"); kb.emit(); p2.close(); raise StopIteration
            for l in range(2):
                h = h0 + l
                store(C_p[h * 128:(h + 1) * 128, :], Cf[:, l, 0:128], b_Cf)
                store(n_p[h * 128:(h + 1) * 128, :], Cf[:, l, 128:129], b_Cf)
            store(m_p[h0:h0 + 2, :], Brow[:, 4095:4096], b_Brow)
            for sbi in range(4):
                for l in range(2):
                    h = h0 + l
                    r0 = (sbi * 4 + h) * 128
                    kb.dma("sp", lambda e, l=l, r0=r0: e.dma_start(out=Cf[:, l, 0:128], in_=sC[r0:r0 + 128, :]), writes=[b_Cf])
                    kb.dma("sp", lambda e, l=l, r0=r0: e.dma_start(out=Cf[:, l, 128:129], in_=sn[r0:r0 + 128, :]), writes=[b_Cf])
                    kb.dma("sp", lambda e, l=l, h=h, sbi=sbi: e.dma_start(out=gprev[:, l:l + 1], in_=smi[h:h + 1, sbi:sbi + 1].partition_broadcast(128)), writes=[b_gprev])
                OP("act", lambda e: e.copy(out=Cb[:], in_=Cf[:]), r=[b_Cf], w=[b_Cb])
                c0 = 4096 + sbi * 8
                fc = slice(NM + sbi * 8, NM + sbi * 8 + 8)
                sargs = (8, slice(c0, c0 + 8), UTs[:, sbi, :], GTs[:, sbi, :], EMs[:, sbi, :],
                         lambda l, fc=fc: mkT[:, l, fc], lambda l, fc=fc: mqT[:, l, fc], lambda l, sbi=sbi: mkt_s[:, sbi, l * 128:(l + 1) * 128],
                         lambda l, sbi=sbi: mva_s[:, sbi, l, :], lambda l, sbi=sbi: sgm_s[:, sbi, l * 128:(l + 1) * 128], True,
                         slice(NM + sbi * 8, NM + sbi * 8 + 8), sbi % 2, c0 + 7)
                chunkA(*sargs)
                chunkB(*sargs)
                chunkC(*sargs)
                for l in range(2):
                    h = h0 + l
                    r0 = (sbi * 4 + h) * 128
                    store(C_s[r0:r0 + 128, :], Cf[:, l, 0:128], b_Cf)
                    store(n_s[r0:r0 + 128, :], Cf[:, l, 128:129], b_Cf)
                store(m_s[sbi * 4 + h0:sbi * 4 + h0 + 2, :], Brow[:, c0 + 7:c0 + 8], b_Brow)
            kb.barrier()
            p2.close()
        try:
            for hp in range(2):
                ml_pass(hp)
        except StopIteration:
            return nc, dbg_outs
        if stop_after == "ml":
            kb.final_wait("sp")
            kb.emit()
            return nc, dbg_outs

        p3 = contextlib.ExitStack()
        wO = sbt("wO", [128, 8, 1024], BF16, p3); b_wO = B("wO")
        wU = sbt("wU", [128, 8, 4096], BF16, p3); b_wU = B("wU")
        wD = sbt("wD", [128, 32, 1024], BF16, p3); b_wD = B("wD")
        for c in range(8):
            load_w(wO[:, c, :], w_out[c * 128:(c + 1) * 128, :], b_wO, 1024)
        for c in range(8):
            load_w(wU[:, c, :], w_up[c * 128:(c + 1) * 128, :], b_wU, 4096)
        for f in range(32):
            load_w(wD[:, f, :], w_down[f * 128:(f + 1) * 128, :], b_wD, 1024)
        kb.dma("sp", lambda e: e.dma_start(out=g1[:], in_=nffn[0:1, :].partition_broadcast(128)), writes=[b_g1])
        g3 = sbt("g3", [128, D], F32, p3); b_g3 = B("g3")
        kb.dma("sp", lambda e: e.dma_start(out=g3[:], in_=nfin[0:1, :].partition_broadcast(128)), writes=[b_g3])
        mixb = [sbt(f"mixb{i}", [128, D], BF16, p3) for i in range(2)]; b_mixb = [B(f"mixb{i}") for i in range(2)]
        mixT = sbt("mixT", [128, 8, 256], BF16, p3); b_mixT = B("mixT")
        xn2T = sbt("xn2T", [128, 8, 256], BF16, p3); b_xn2T = B("xn2T")
        uT = sbt("uT", [128, 32, 256], BF16, p3); b_uT = B("uT")
        ur = [sbt(f"ur{i}", [128, 256], F32, p3) for i in range(2)]; b_ur = [B(f"ur{i}") for i in range(2)]
        yo = sbt("yo", [128, D], F32, p3); b_yo = B("yo")
        all_mix = [b_mix, b_mix2, b_mix3]
        fgroups = [("main", gi) for gi in range(8)] + [("smp", 0)]
        for kind, gi in fgroups:
            if kind == "main":
                tiles = [(gi * 256 + t * 128, 128, t * 128) for t in range(2)]
                xsrc, ydst = xm, y_m
            else:
                tiles = [(0, NS, 0)]
                xsrc, ydst = xs, y_s
            ntok = sum(n for _, n, _ in tiles)
            for ti, (r0, n, c0) in enumerate(tiles):
                mr0 = r0 if kind == "main" else NM
                kb.dma("sp", lambda e, ti=ti, r0=r0, n=n, xsrc=xsrc: e.dma_start(out=xt[ti][0:n, :], in_=xsrc[r0:r0 + n, :]), writes=[b_xt[ti]])
                kb.dma("sp", lambda e, ti=ti, mr0=mr0, n=n: e.dma_start(out=mixb[ti][0:n, :], in_=mix[mr0:mr0 + n, :]), reads=all_mix, writes=[b_mixb[ti]])
                to_featmajor(mixb[ti], b_mixb[ti], n, mixT[:, :, c0:c0 + n], b_mixT)
            for ti, (r0, n, c0) in enumerate(tiles):
                for hf in range(2):
                    for c in range(8):
                        OP("pe", lambda e, c=c, hf=hf, n=n, c0=c0: e.matmul(PF[hf][0:n, :], lhsT=mixT[:, c, c0:c0 + n], rhs=wO[:, c, hf * 512:(hf + 1) * 512], start=(c == 0), stop=(c == 7)),
                           r=[b_mixT, b_wO], w=[bPF[hf]])
                    OP("dve", lambda e, ti=ti, hf=hf, n=n: e.tensor_tensor(out=xt[ti][0:n, hf * 512:(hf + 1) * 512], in0=PF[hf][0:n, :], in1=xt[ti][0:n, hf * 512:(hf + 1) * 512], op=ALU.add),
                       r=[bPF[hf], b_xt[ti]], w=[b_xt[ti]])
                rms_scale(xt[ti][0:n, :], b_xt[ti], n, ti, xnb[ti][0:n, :], b_xnb[ti])
                OP("dve", lambda e, ti=ti, n=n: e.scalar_tensor_tensor(out=xnb[ti][0:n, :], in0=xt[ti][0:n, :], scalar=rstd[ti][0:n, 0:1], in1=g1[0:n, :], op0=ALU.mult, op1=ALU.mult),
                   r=[b_xt[ti], b_rstd[ti], b_g1], w=[b_xnb[ti]])
                to_featmajor(xnb[ti], b_xnb[ti], n, xn2T[:, :, c0:c0 + n], b_xn2T)
            for f in range(32):
                pf = 2 + f % 2
                ui = f % 2
                for c in range(8):
                    OP("pe", lambda e, c=c, f=f, pf=pf, ntok=ntok: e.matmul(PF[pf][:, 0:ntok], lhsT=wU[:, c, f * 128:(f + 1) * 128], rhs=xn2T[:, c, 0:ntok], start=(c == 0), stop=(c == 7)),
                       r=[b_wU, b_xn2T], w=[bPF[pf]])
                OP("dve", lambda e, pf=pf, ui=ui, ntok=ntok: e.tensor_scalar(out=ur[ui][:, 0:ntok], in0=PF[pf][:, 0:ntok], scalar1=0.0, scalar2=None, op0=ALU.max), r=[bPF[pf]], w=[b_ur[ui]])
                OP("act", lambda e, f=f, ui=ui, ntok=ntok: e.activation(out=uT[:, f, 0:ntok], in_=ur[ui][:, 0:ntok], func=AF.Square), r=[b_ur[ui]], w=[b_uT])
            for ti, (r0, n, c0) in enumerate(tiles):
                for hf in range(2):
                    for f in range(32):
                        OP("pe", lambda e, f=f, hf=hf, n=n, c0=c0: e.matmul(PF[hf][0:n, :], lhsT=uT[:, f, c0:c0 + n], rhs=wD[:, f, hf * 512:(hf + 1) * 512], start=(f == 0), stop=(f == 31)),
                           r=[b_uT, b_wD], w=[bPF[hf]])
                    OP("dve", lambda e, ti=ti, hf=hf, n=n: e.tensor_tensor(out=xt[ti][0:n, hf * 512:(hf + 1) * 512], in0=PF[hf][0:n, :], in1=xt[ti][0:n, hf * 512:(hf + 1) * 512], op=ALU.add),
                       r=[bPF[hf], b_xt[ti]], w=[b_xt[ti]])
                rms_scale(xt[ti][0:n, :], b_xt[ti], n, ti, xnb[ti][0:n, :], b_xnb[ti])
                OP("dve", lambda e, ti=ti, n=n: e.scalar_tensor_tensor(out=yo[0:n, :], in0=xt[ti][0:n, :], scalar=rstd[ti][0:n, 0:1], in1=g3[0:n, :], op0=ALU.mult, op1=ALU.mult),
                   r=[b_xt[ti], b_rstd[ti], b_g3], w=[b_yo])
                store(ydst[r0:r0 + n, :], yo[0:n, :], b_yo)
        kb.barrier()
        p3.close()
        kb.final_wait("sp")
        kb.emit()
        return nc, dbg_outs


def make_in_maps(inp):
    f = lambda a: np.ascontiguousarray(a, dtype=np.float32)
    xp = np.asarray(inp["x_prompt"]); xsamp = np.asarray(inp["x_sample"])
    ckf = f(np.asarray(inp["cache_k"]).reshape(2560 * 128, 512))
    cvf = f(np.asarray(inp["cache_v"]).reshape(2560 * 128, 512))
    ptab = np.asarray(inp["page_table"]).astype(np.int32)
    sCf = np.asarray(inp["state_C"])[0]; snf = np.asarray(inp["state_n"])[0]; smf = np.asarray(inp["state_m"])[0]
    shared = {
        "w_in": f(inp["w_in"][0]), "w_out": f(inp["w_out"][0]), "w_up": f(inp["w_up"][0]), "w_down": f(inp["w_down"][0]),
        "nmix": f(inp["norm_mix"]).reshape(1, D), "nffn": f(inp["norm_ffn"]).reshape(1, D), "nfin": f(inp["norm_final"]).reshape(1, D),
        "mlg": f(inp["ml_norm"]).reshape(1, 512),
        "bg": f(np.concatenate([np.asarray(inp["b_ig"]).reshape(-1), np.asarray(inp["b_fg"]).reshape(-1)])).reshape(8, 1),
        "rb": f(inp["rel_bias"]).reshape(1, 256), "ck": ckf, "cv": cvf,
    }
    maps = []
    for c in range(8):
        b, half = c // 2, c % 2
        m = dict(shared)
        m["xm"] = f(xp[b, half * NM:(half + 1) * NM])
        m["xc"] = f(xp[b, 0:NM]) if half else np.zeros((NM, D), np.float32)
        m["xs"] = f(xsamp[4 * c:4 * c + 4].reshape(NS, D))
        m["pt"] = np.ascontiguousarray(ptab[4 * c:4 * c + 4].reshape(1, 256))
        m["sC"] = f(sCf[4 * c:4 * c + 4].reshape(16 * 128, 128))
        m["sn"] = f(snf[4 * c:4 * c + 4].reshape(16 * 128, 1))
        m["smi"] = f(smf[4 * c:4 * c + 4].T)
        m["cf"] = np.array([[float(half), (float(half) - 1.0) * 30000.0, 0.0, 0.0]], np.float32)
        cd = np.zeros((16, 2, 16), np.float32)
        for qt in range(16):
            own = 8 + qt // 2
            for n in range(16):
                is_cand = (n < 8 and half == 1) or (8 <= n < own)
                cd[qt, 0, n] = NEG if is_cand else 0.0
                cd[qt, 1, n] = 0.0 if (is_cand or n == own) else NEG
        m["cand"] = cd.reshape(1, 512)
        maps.append(m)
    return maps


STOP_AFTER = "all"
CACHE_ROWS = 2560 * 128


def kernel(**inputs):
    maps = make_in_maps(inputs)
    for m in maps:
        m["ck"] = m["ck"][:CACHE_ROWS]
        m["cv"] = m["cv"][:CACHE_ROWS]
    nc, _ = build_program(stop_after=STOP_AFTER, cache_rows=CACHE_ROWS)
    res = run_bass_kernel_spmd(nc, maps, core_ids=list(range(8)))
    R = res.results
    f32 = np.float32
    y_prompt = np.zeros((4, 4096, D), f32); y_sample = np.zeros((32, 8, D), f32)
    nkp = np.zeros((1, 4, 4096, 8, 64), f32); nvp = np.zeros((1, 4, 4096, 8, 64), f32)
    nCp = np.zeros((1, 4, 4, 128, 128), f32); nnp_ = np.zeros((1, 4, 4, 128), f32); nmp = np.zeros((1, 4, 4), f32)
    nks = np.zeros((1, 32, 8, 8, 64), f32); nvs = np.zeros((1, 32, 8, 8, 64), f32)
    nCs = np.zeros((1, 32, 4, 128, 128), f32); nns = np.zeros((1, 32, 4, 128), f32); nms = np.zeros((1, 32, 4), f32)
    for c in range(8):
        b, half = c // 2, c % 2
        r = R[c]
        sl = slice(half * NM, (half + 1) * NM)
        y_prompt[b, sl] = r["y_m"]
        y_sample[4 * c:4 * c + 4] = r["y_s"].reshape(4, 8, D)
        nkp[0, b, sl] = r["k_m"].reshape(NM, 8, 64)
        nvp[0, b, sl] = r["v_m"].reshape(NM, 8, 64)
        if half == 1:
            nCp[0, b] = r["C_p"].reshape(4, 128, 128)
            nnp_[0, b] = r["n_p"].reshape(4, 128)
            nmp[0, b] = r["m_p"].reshape(4)
        nks[0, 4 * c:4 * c + 4] = r["k_s"].reshape(4, 8, 8, 64)
        nvs[0, 4 * c:4 * c + 4] = r["v_s"].reshape(4, 8, 8, 64)
        nCs[0, 4 * c:4 * c + 4] = r["C_s"].reshape(4, 4, 128, 128)
        nns[0, 4 * c:4 * c + 4] = r["n_s"].reshape(4, 4, 128)
        nms[0, 4 * c:4 * c + 4] = r["m_s"].reshape(4, 4)
    return (y_prompt, y_sample, nkp, nvp, nCp, nnp_, nmp, nks, nvs, nCs, nns, nms)
```
